# Optimizing a Trainium2 kernel written in Bass

```python
import math
import jax, jax.numpy as jnp
from jax import lax
import numpy as np

D_MODEL = 1024
BATCH = 32
SEQ = 2048
DEPTH = 2
DEC_BATCH = 8
DEC_SEQ = 8192
PAST_LEN = 128

GRID_W = 64
EPS = 1e-6
MIX_WIDTH = D_MODEL
FOUR_GROUPS = 4
FOUR_DIM = 64
FOUR_WIDTH = FOUR_GROUPS * FOUR_DIM
DN_HEADS = 4
DN_HEAD_DIM = 64
DN_WIDTH = DN_HEADS * DN_HEAD_DIM
DN_CHUNK = 64
CONV_W = 3
N_HEADS = 8
N_KV_HEADS = 2
HEAD_DIM = 64
GROUP = N_HEADS // N_KV_HEADS
ATTN_WIDTH = N_HEADS * HEAD_DIM
KV_WIDTH = N_KV_HEADS * HEAD_DIM
Q_BLOCK = 128
ROPE_THETA = 10000.0
AXIS_DIM = HEAD_DIM // 2
ROPE_FREQS = AXIS_DIM // 2
MEM_TOKENS = 256
MEM_HEADS = 4
MEM_HEAD_DIM = D_MODEL // MEM_HEADS
D_FF = 2816
IN_SPLITS = (FOUR_WIDTH, 3 * DN_WIDTH, 2 * DN_HEADS, 2 * DN_HEADS, DN_WIDTH, ATTN_WIDTH, KV_WIDTH, KV_WIDTH)
IN_WIDTH = FOUR_WIDTH + 4 * DN_WIDTH + 4 * DN_HEADS + ATTN_WIDTH + 2 * KV_WIDTH

kernel_name = 'hybrid_fourier_deltanet_gqa_encoder'


def rmsnorm(x, w):
    xf = x.astype(jnp.float32)
    y = xf * lax.rsqrt(jnp.mean(xf * xf, axis=-1, keepdims=True) + EPS)
    return (y * w.astype(jnp.float32)).astype(x.dtype)


def l2norm(x):
    return x * lax.rsqrt(jnp.sum(x * x, axis=-1, keepdims=True) + EPS)


def split_cols(z, sizes):
    out, start = [], 0
    for s in sizes:
        out.append(z[..., start:start + s])
        start += s
    return out


def swiglu(x, w_gu, w_down):
    g, u = jnp.split(x @ w_gu, 2, axis=-1)
    return (jax.nn.silu(g) * u) @ w_down


def fourier_mix(u, w_four):
    B, S, _ = u.shape
    ug = u.reshape(B, S, FOUR_GROUPS, FOUR_DIM).astype(jnp.float32)
    f = jnp.fft.fft2(ug, axes=(1, 3), norm='ortho').real
    y = jnp.einsum('bsgc,gcd->bsgd', f.astype(u.dtype), w_four)
    return y.reshape(B, S, FOUR_WIDTH)


def short_conv(z, w):
    pad = CONV_W // 2
    S = z.shape[1]
    zp = jnp.pad(z, ((0, 0), (pad, pad), (0, 0)))
    out = zp[:, 0:S] * w[0]
    for i in range(1, CONV_W):
        out = out + zp[:, i:i + S] * w[i]
    return jax.nn.silu(out)


def gated_delta_chunked(q, k, v, g, beta):
    B, S, H, D = q.shape
    C = DN_CHUNK
    N = S // C

    def to_chunks(t):
        t = t.reshape((B, N, C, H) + t.shape[3:])
        return jnp.moveaxis(t, 3, 1)

    q, k, v = to_chunks(q), to_chunks(k), to_chunks(v)
    g, beta = to_chunks(g), to_chunks(beta)
    gc = jnp.cumsum(g, axis=-1)
    causal = jnp.tril(jnp.ones((C, C), dtype=bool))
    strict = jnp.tril(jnp.ones((C, C), dtype=bool), -1)
    decay = jnp.exp(jnp.where(causal, gc[..., :, None] - gc[..., None, :], -jnp.inf))
    k_beta = k * beta[..., None]
    kk = jnp.einsum('bhnid,bhnjd->bhnij', k_beta, k)
    A = jnp.where(strict, kk * decay, 0.0) + jnp.eye(C, dtype=q.dtype)
    rhs = jnp.concatenate([v * beta[..., None], k_beta * jnp.exp(gc)[..., None]], axis=-1)
    sol = lax.linalg.triangular_solve(A, rhs, left_side=True, lower=True, unit_diagonal=True)
    u_val, w_key = sol[..., :D], sol[..., D:]
    qk = jnp.where(causal, jnp.einsum('bhnid,bhnjd->bhnij', q, k) * decay, 0.0)
    g_last = gc[..., -1]
    k_tail = k * jnp.exp(g_last[..., None] - gc)[..., None]
    q_dec = q * jnp.exp(gc)[..., None]

    def step(state, xs):
        qd, qkc, uc, wc, kt, gl = xs
        v_new = uc - jnp.einsum('bhcd,bhde->bhce', wc, state)
        o = jnp.einsum('bhcd,bhde->bhce', qd, state) + jnp.einsum('bhij,bhje->bhie', qkc, v_new)
        state = state * jnp.exp(gl)[..., None, None] + jnp.einsum('bhcd,bhce->bhde', kt, v_new)
        return state, o

    xs = tuple(jnp.moveaxis(t, 2, 0) for t in (q_dec, qk, u_val, w_key, k_tail, g_last))
    state0 = jnp.zeros((B, H, D, D), q.dtype)
    _, o = lax.scan(step, state0, xs)
    o = jnp.moveaxis(o, 0, 2)
    return jnp.moveaxis(o, 1, 3).reshape(B, S, H, D)


def deltanet_mix(qkv, a, b, gate, conv_w, A_log, dt_bias, out_norm):
    B, S, _ = qkv.shape
    dt = qkv.dtype
    f32 = jnp.float32
    qkv = short_conv(qkv, conv_w).astype(f32)
    q, k, v = jnp.split(qkv, 3, axis=-1)
    q = l2norm(q.reshape(B, S, DN_HEADS, DN_HEAD_DIM)) * (DN_HEAD_DIM ** -0.5)
    k = l2norm(k.reshape(B, S, DN_HEADS, DN_HEAD_DIM))
    v = v.reshape(B, S, DN_HEADS, DN_HEAD_DIM)
    a = a.astype(f32).reshape(B, S, 2, DN_HEADS)
    b = b.astype(f32).reshape(B, S, 2, DN_HEADS)
    g = -jnp.exp(A_log.astype(f32)) * jax.nn.softplus(a + dt_bias.astype(f32))
    beta = jax.nn.sigmoid(b)
    o_f = gated_delta_chunked(q, k, v, g[:, :, 0], beta[:, :, 0])
    flip = lambda t: jnp.flip(t, axis=1)
    o_b = flip(gated_delta_chunked(flip(q), flip(k), flip(v), flip(g[:, :, 1]), flip(beta[:, :, 1])))
    o = rmsnorm(o_f + o_b, out_norm) * jax.nn.silu(gate.astype(f32).reshape(B, S, DN_HEADS, DN_HEAD_DIM))
    return o.reshape(B, S, DN_WIDTH).astype(dt)


def axial_rope(n_tokens):
    rows = n_tokens // GRID_W
    row = jnp.repeat(jnp.arange(rows), GRID_W).astype(jnp.float32)
    col = jnp.tile(jnp.arange(GRID_W), rows).astype(jnp.float32)
    inv = 1.0 / (ROPE_THETA ** (jnp.arange(ROPE_FREQS, dtype=jnp.float32) * (2.0 / AXIS_DIM)))
    ang = jnp.stack([row[:, None] * inv, col[:, None] * inv], axis=1)
    return jnp.cos(ang), jnp.sin(ang)


def apply_rope(x, cos, sin):
    B, S, H, _ = x.shape
    xf = x.astype(jnp.float32).reshape(B, S, H, 2, 2, ROPE_FREQS)
    x1, x2 = xf[..., 0, :], xf[..., 1, :]
    c, s = cos[:, None], sin[:, None]
    out = jnp.stack([x1 * c - x2 * s, x2 * c + x1 * s], axis=-2)
    return out.reshape(B, S, H, HEAD_DIM).astype(x.dtype)


def gqa_attention(q, k, v, q_norm, k_norm, cos, sin):
    B, S, _ = q.shape
    q = apply_rope(rmsnorm(q.reshape(B, S, N_HEADS, HEAD_DIM), q_norm), cos, sin)
    k = apply_rope(rmsnorm(k.reshape(B, S, N_KV_HEADS, HEAD_DIM), k_norm), cos, sin)
    v = v.reshape(B, S, N_KV_HEADS, HEAD_DIM)
    nb = S // Q_BLOCK
    qb = jnp.moveaxis(q.reshape(B, nb, Q_BLOCK, N_KV_HEADS, GROUP, HEAD_DIM), 1, 0)
    scale = HEAD_DIM ** -0.5

    def block(qi):
        s = jnp.einsum('bqkgd,bskd->bkgqs', qi, k).astype(jnp.float32) * scale
        p = jax.nn.softmax(s, axis=-1)
        return jnp.einsum('bkgqs,bskd->bqkgd', p.astype(v.dtype), v)

    o = lax.map(block, qb)
    return jnp.moveaxis(o, 0, 1).reshape(B, S, ATTN_WIDTH)


def memory_cross_attention(h, mem, wq, wkv, wo):
    B, S, _ = h.shape
    M = mem.shape[1]
    q = (h @ wq).reshape(B, S, MEM_HEADS, MEM_HEAD_DIM)
    k, v = jnp.split(mem @ wkv, 2, axis=-1)
    k = k.reshape(B, M, MEM_HEADS, MEM_HEAD_DIM)
    v = v.reshape(B, M, MEM_HEADS, MEM_HEAD_DIM)
    s = jnp.einsum('bshd,bmhd->bhsm', q, k).astype(jnp.float32) * (MEM_HEAD_DIM ** -0.5)
    p = jax.nn.softmax(s, axis=-1)
    o = jnp.einsum('bhsm,bmhd->bshd', p.astype(v.dtype), v).reshape(B, S, D_MODEL)
    return o @ wo


def encoder(x, mem, weights):
    (ffn1_norm, ffn1_w_gu, ffn1_w_down, mix_norm, w_in, four_w, dn_conv, dn_A_log, dn_dt_bias,
     dn_out_norm, attn_q_norm, attn_k_norm, w_out, mem_norm_x, mem_norm_m, mem_wq, mem_wkv, mem_wo,
     ffn2_norm, ffn2_w_gu, ffn2_w_down, final_norm) = weights
    S = x.shape[1]
    cos, sin = axial_rope(S)
    for l in range(DEPTH):
        x = x + 0.5 * swiglu(rmsnorm(x, ffn1_norm[l]), ffn1_w_gu[l], ffn1_w_down[l])
        n = rmsnorm(x, mix_norm[l])
        z = n @ w_in[l]
        u_f, dn_qkv, dn_a, dn_b, dn_gate, a_q, a_k, a_v = split_cols(z, IN_SPLITS)
        y_f = fourier_mix(u_f, four_w[l])
        y_d = deltanet_mix(dn_qkv, dn_a, dn_b, dn_gate, dn_conv[l], dn_A_log[l], dn_dt_bias[l], dn_out_norm[l])
        y_a = gqa_attention(a_q, a_k, a_v, attn_q_norm[l], attn_k_norm[l], cos, sin)
        x = x + jnp.concatenate([y_f, y_d, y_a], axis=-1) @ w_out[l]
        x = x + memory_cross_attention(rmsnorm(x, mem_norm_x[l]), rmsnorm(mem, mem_norm_m[l]),
                                       mem_wq[l], mem_wkv[l], mem_wo[l])
        x = x + 0.5 * swiglu(rmsnorm(x, ffn2_norm[l]), ffn2_w_gu[l], ffn2_w_down[l])
    return rmsnorm(x, final_norm)


def setup_inputs(seed: int = 0) -> dict:
    key = jax.random.key(seed)
    ks = jax.random.split(key, 32)
    f32 = jnp.float32
    L = DEPTH

    def dense(k, shape, fan_in):
        return jax.random.normal(k, shape, f32) * (fan_in ** -0.5)

    def gain(k, shape):
        return 1.0 + 0.02 * jax.random.normal(k, shape, f32)

    dt = jnp.exp(jax.random.uniform(ks[10], (L, 2, DN_HEADS), f32, math.log(1e-3), math.log(1e-1)))
    return {
        'x_prompt': jax.random.normal(ks[0], (BATCH, SEQ, D_MODEL), f32),
        'x_sample': jax.random.normal(ks[1], (DEC_BATCH, DEC_SEQ, D_MODEL), f32),
        'mem_prompt': jax.random.normal(ks[2], (BATCH, MEM_TOKENS, D_MODEL), f32),
        'mem_sample': jax.random.normal(ks[3], (DEC_BATCH, MEM_TOKENS, D_MODEL), f32),
        'ffn1_norm': gain(ks[4], (L, D_MODEL)),
        'ffn1_w_gu': dense(ks[5], (L, D_MODEL, 2 * D_FF), D_MODEL),
        'ffn1_w_down': dense(ks[6], (L, D_FF, D_MODEL), D_FF),
        'mix_norm': gain(ks[7], (L, D_MODEL)),
        'w_in': dense(ks[8], (L, D_MODEL, IN_WIDTH), D_MODEL),
        'four_w': dense(ks[9], (L, FOUR_GROUPS, FOUR_DIM, FOUR_DIM), FOUR_DIM),
        'dn_conv': dense(ks[11], (L, CONV_W, 3 * DN_WIDTH), CONV_W),
        'dn_A_log': jnp.log(jax.random.uniform(ks[12], (L, 2, DN_HEADS), f32, 1.0, 16.0)),
        'dn_dt_bias': dt + jnp.log(-jnp.expm1(-dt)),
        'dn_out_norm': gain(ks[13], (L, DN_HEAD_DIM)),
        'attn_q_norm': gain(ks[14], (L, HEAD_DIM)),
        'attn_k_norm': gain(ks[15], (L, HEAD_DIM)),
        'w_out': dense(ks[16], (L, MIX_WIDTH, D_MODEL), MIX_WIDTH),
        'mem_norm_x': gain(ks[17], (L, D_MODEL)),
        'mem_norm_m': gain(ks[18], (L, D_MODEL)),
        'mem_wq': dense(ks[19], (L, D_MODEL, D_MODEL), D_MODEL),
        'mem_wkv': dense(ks[20], (L, D_MODEL, 2 * D_MODEL), D_MODEL),
        'mem_wo': dense(ks[21], (L, D_MODEL, D_MODEL), D_MODEL),
        'ffn2_norm': gain(ks[22], (L, D_MODEL)),
        'ffn2_w_gu': dense(ks[23], (L, D_MODEL, 2 * D_FF), D_MODEL),
        'ffn2_w_down': dense(ks[24], (L, D_FF, D_MODEL), D_FF),
        'final_norm': gain(ks[25], (D_MODEL,)),
    }


def reference(x_prompt, x_sample, mem_prompt, mem_sample, ffn1_norm, ffn1_w_gu, ffn1_w_down, mix_norm,
              w_in, four_w, dn_conv, dn_A_log, dn_dt_bias, dn_out_norm, attn_q_norm, attn_k_norm, w_out,
              mem_norm_x, mem_norm_m, mem_wq, mem_wkv, mem_wo, ffn2_norm, ffn2_w_gu, ffn2_w_down, final_norm):
    weights = (ffn1_norm, ffn1_w_gu, ffn1_w_down, mix_norm, w_in, four_w, dn_conv, dn_A_log, dn_dt_bias,
               dn_out_norm, attn_q_norm, attn_k_norm, w_out, mem_norm_x, mem_norm_m, mem_wq, mem_wkv, mem_wo,
               ffn2_norm, ffn2_w_gu, ffn2_w_down, final_norm)
    y_prompt = encoder(x_prompt, mem_prompt, weights)
    y_sample = encoder(x_sample, mem_sample, weights)
    return (y_prompt, y_sample)
```

```python
import math
from contextlib import ExitStack
import numpy as np
import ml_dtypes
import concourse.bass as bass
import concourse.mybir as mybir
from concourse.bass_utils import run_bass_kernel_spmd

F32 = mybir.dt.float32
BF16 = mybir.dt.bfloat16
AF = mybir.ActivationFunctionType
ALU = mybir.AluOpType
AX = mybir.AxisListType

D = 1024
DFF = 2816
NFB = DFF // 128
INW = 2064
EPS = 1e-6
NCORES = 8
MEMT = 256
ENGS = ("pe", "act", "dve", "pool", "sp")


class Op:
    __slots__ = ("eng", "fn", "deps", "sig", "cnt", "dma", "sem", "semval")

    def __init__(self, eng, fn, dma):
        self.eng = eng; self.fn = fn; self.dma = dma
        self.deps = []; self.sig = False; self.cnt = 0; self.sem = None; self.semval = 0


class Rec:
    NDMASEM = 14

    def __init__(self):
        self.ops = {e: [] for e in ENGS}
        self.lastw = {}
        self.readers = {}
        self.dma_uses = {}
        self.dma_rr = {e: 0 for e in ENGS}
        self.pending = {e: [] for e in ENGS}
        self.last_compute = {}
        self.n = 0

    def add(self, eng, fn, reads=(), writes=(), dma=False):
        op = Op(eng, fn, dma)
        deps = {}
        for r in reads:
            w = self.lastw.get(r)
            if w is not None:
                deps[w] = "raw"
        for w_ in writes:
            lw = self.lastw.get(w_)
            if lw is not None and lw not in deps:
                deps[lw] = "waw"
            for rd in self.readers.get(w_, ()):
                if rd not in deps:
                    deps[rd] = "war"
        need = []
        for d, kind in deps.items():
            if d.dma or dma:
                need.append(d)
            elif d.eng == eng:
                if eng == "pe":
                    continue
                need.append(d)
            else:
                need.append(d)
        if dma:
            k = self.dma_rr[eng] % self.NDMASEM
            self.dma_rr[eng] += 1
            ent = self.dma_uses.setdefault((eng, k), [0, None])
            if ent[1] is not None:
                need.append(ent[1])
            ent[0] += 1
            op.sem = (eng, k); op.semval = 16 * ent[0]
            ent[1] = op
        if self.pending[eng]:
            need.extend(self.pending[eng]); self.pending[eng] = []
        for d in need:
            d.sig = True
        op.deps = need
        for r in reads:
            self.readers.setdefault(r, []).append(op)
        for w_ in writes:
            self.lastw[w_] = op
            self.readers[w_] = []
        self.ops[eng].append(op)
        if not dma:
            self.last_compute[eng] = op
        self.n += 1
        return op

    def barrier(self):
        lasts = []
        for e in ENGS:
            if self.last_compute.get(e) is not None:
                lasts.append(self.last_compute[e])
        for ent in self.dma_uses.values():
            if ent[1] is not None:
                lasts.append(ent[1])
        for e in ENGS:
            self.pending[e] = [o for o in lasts if not (o.eng == e and not o.dma)]
        self.lastw = {}; self.readers = {}

    def setup_sems(self, nc, stack):
        self.esem = {e: stack.enter_context(nc.semaphore("s_" + e)) for e in ENGS}
        self.dsem = {}
        for q in ("sp", "pool", "act"):
            for k in range(self.NDMASEM):
                self.dsem[(q, k)] = stack.enter_context(nc.semaphore("d_%s%d" % (q, k)))
        self.cnts = {e: 0 for e in ENGS}
        self.waited = {e: {} for e in ENGS}

    def emit(self, nc, final=False):
        esem, dsem = self.esem, self.dsem
        for e in ENGS:
            if self.last_compute.get(e) is not None:
                self.last_compute[e].sig = True
        for e in ENGS:
            c = self.cnts[e]
            for op in self.ops[e]:
                if op.sig and not op.dma:
                    c += 1; op.cnt = c
            self.cnts[e] = c
        finals = [(dsem[k], 16 * ent[0]) for k, ent in self.dma_uses.items()] if final else []
        ops = self.ops
        self.ops = {e: [] for e in ENGS}
        self.emitted_last = {e: (ops[e][-1] if ops[e] else None) for e in ENGS}
        with nc.Block() as block:
            def run(engname, e):
                waited = self.waited[engname]
                for op in ops[engname]:
                    for d in op.deps:
                        if d.dma:
                            s, v = dsem[d.sem], d.semval
                        else:
                            s, v = esem[d.eng], d.cnt
                            assert v > 0
                        if waited.get(id(s), 0) < v:
                            e.wait_ge(s, v); waited[id(s)] = v
                    ins = op.fn(e)
                    if op.dma:
                        ins.then_inc(dsem[op.sem], 16)
                    elif op.sig:
                        ins.then_inc(esem[engname], 1)
                if engname == "sp":
                    for s, v in finals:
                        e.wait_ge(s, v)

            @block.tensor
            def _(e):
                run("pe", e)

            @block.scalar
            def _(e):
                run("act", e)

            @block.vector
            def _(e):
                run("dve", e)

            @block.gpsimd
            def _(e):
                run("pool", e)

            @block.sync
            def _(e):
                run("sp", e)


class Cfg:
    def __init__(self, seqs, depth=2):
        self.seqs = list(seqs)
        self.depth = depth
        self.ntok = sum(seqs)
        self.starts = [sum(seqs[:i]) for i in range(len(seqs))]
        self.tt = min(1024, min(seqs))
        self.nsub = self.tt // 128
        self.tg = min(512, self.tt)
        self.ntg = self.tt // self.tg


FULL = Cfg([2048, 2048, 2048, 2048, 8192])

WNAMES = ["ffn1_norm", "ffn1_w_gu", "ffn1_w_down", "mix_norm", "w_in", "four_w", "dn_conv", "dn_A_log",
          "dn_dt_bias", "dn_out_norm", "attn_q_norm", "attn_k_norm", "w_out", "mem_norm_x", "mem_norm_m",
          "mem_wq", "mem_wkv", "mem_wo", "ffn2_norm", "ffn2_w_gu", "ffn2_w_down", "final_norm"]
WSHAPES = {
    "ffn1_norm": (2, D), "ffn1_w_gu": (2, D, 2 * DFF), "ffn1_w_down": (2, DFF, D), "mix_norm": (2, D),
    "w_in": (2, D, INW), "four_w": (2, 4, 64, 64), "dn_conv": (2, 3, 768), "dn_A_log": (2, 2, 4),
    "dn_dt_bias": (2, 2, 4), "dn_out_norm": (2, 64), "attn_q_norm": (2, 64), "attn_k_norm": (2, 64),
    "w_out": (2, D, D), "mem_norm_x": (2, D), "mem_norm_m": (2, D), "mem_wq": (2, D, D),
    "mem_wkv": (2, D, 2 * D), "mem_wo": (2, D, D), "ffn2_norm": (2, D), "ffn2_w_gu": (2, D, 2 * DFF),
    "ffn2_w_down": (2, DFF, D), "final_norm": (D,),
}
NORM_IDS = {"ffn1_norm": 0, "mix_norm": 1, "mem_norm_x": 2, "mem_norm_m": 3, "ffn2_norm": 4}


def host_consts(cfg):
    c = {}
    c["c_identb"] = np.eye(128, dtype=np.float32).astype(ml_dtypes.bfloat16)
    c["c_identf"] = np.eye(128, dtype=np.float32)
    smax = max(cfg.seqs)
    nblk = smax // 128
    pos = np.arange(smax)
    row = (pos // 64).astype(np.float32); col = (pos % 64).astype(np.float32)
    inv = (1.0 / (10000.0 ** (np.arange(16, dtype=np.float32) * (2.0 / 32)))).astype(np.float32)
    ang = np.stack([row[:, None] * inv, col[:, None] * inv], axis=1).astype(np.float32)
    cos = np.cos(ang).astype(np.float32).reshape(nblk, 128, 2, 16).transpose(1, 0, 2, 3)
    sin = np.sin(ang).astype(np.float32).reshape(nblk, 128, 2, 16).transpose(1, 0, 2, 3)
    c["c_cos"] = np.ascontiguousarray(cos); c["c_sin"] = np.ascontiguousarray(sin)
    bf = ml_dtypes.bfloat16
    cc = np.arange(64)
    th = 2 * np.pi * np.outer(cc, cc) / 64.0
    c["c_fc"] = np.stack([np.cos(th) / 8.0, np.sin(th) / 8.0], axis=1).astype(np.float32).astype(bf)
    for S in sorted(set(cfg.seqs)):
        N1 = S // 128
        n1 = np.arange(N1); t1 = 2 * np.pi * np.outer(n1, n1) / N1
        sc = 1.0 / np.sqrt(S)
        f1 = np.concatenate([np.cos(t1), -np.sin(t1), np.sin(t1)], axis=1) * sc
        c["c_f1_%d" % S] = f1.astype(np.float32).astype(bf)
        n2 = np.arange(128)[:, None, None]; k1 = np.arange(N1)[None, :, None]; k2 = np.arange(128)[None, None, :]
        ang = 2 * np.pi * ((n2 * (k1 + N1 * k2)) % S) / S
        h = np.stack([np.cos(ang), -np.sin(ang)], axis=2)
        c["c_h_%d" % S] = h.astype(np.float32).astype(bf)
    p = np.arange(64)[:, None]; f = np.arange(64)[None, :]
    NEG = -1.0e4
    mk = np.stack([np.where(f >= p, 0.0, NEG), np.where(f < p, 0.0, NEG), np.where(f <= p, 0.0, NEG), np.where(f > p, 0.0, NEG)], axis=1)
    c["c_dnmask"] = mk.astype(np.float32)
    tri = np.stack([(p <= f).astype(np.float32), (p >= f).astype(np.float32), np.ones((64, 64), np.float32)], axis=1)
    c["c_tri"] = tri.astype(np.float32)
    return c


class Builder:
    def __init__(self, cfg, dump=(), stop_after=None):
        self.cfg = cfg
        self.dump = set(dump)
        self.stop_after = stop_after
        self.skip = set()
        self.dn_stage = 9
        self.nc = bass.Bass("TRN2", target_bir_lowering=False)
        self.R = Rec()
        self.sb_off = 16512
        self.sb_mark = 16512

    def dram(self, name, shape, dt, kind=None):
        if kind is None:
            kind = "ExternalOutput" if name in self.dump else ("ExternalInput" if name in getattr(self, "ext_in", ()) else "Internal")
        return self.nc.dram_tensor(name, list(shape), dt, kind=kind).ap()

    def sb(self, name, shape, dt):
        self.tid = getattr(self, "tid", 0) + 1
        return self.pstack.enter_context(self.nc.sbuf_tensor("%s_%d" % (name, self.tid), list(shape), dt))

    def phase_begin(self):
        self.R.barrier()
        self.pstack = ExitStack()

    def phase_end(self, final=False):
        self.R.emit(self.nc, final=final)
        self.pstack.close()

    def pe(self, fn, r, w): return self.R.add("pe", fn, r, w)
    def act(self, fn, r, w): return self.R.add("act", fn, r, w)
    def dve(self, fn, r, w): return self.R.add("dve", fn, r, w)
    def pool(self, fn, r, w): return self.R.add("pool", fn, r, w)

    def dma(self, q, out, in_, r, w):
        return self.R.add(q, lambda e, o=out, i=in_: e.dma_start(out=o, in_=i), r, w, dma=True)

    def build(self):
        cfg = self.cfg; nc = self.nc
        NT = cfg.ntok
        self.inp = {}
        self.inp["x"] = self.dram("x", [NT, D], F32, "ExternalInput")
        self.inp["mem"] = self.dram("mem", [len(cfg.seqs) * MEMT, D], F32, "ExternalInput")
        for n in WNAMES:
            shp = WSHAPES[n]
            if getattr(self, "tiny_w", False) and int(np.prod(shp)) > 100000:
                shp = (2, 128, 128)
            self.inp[n] = self.dram(n, shp, F32, "ExternalInput")
        hc = host_consts(cfg)
        self.cin = {}
        for k, v in hc.items():
            self.cin[k] = self.dram(k, v.shape, BF16 if v.dtype == ml_dtypes.bfloat16 else F32, "ExternalInput")
        self.y = self.dram("y", [NT, D], F32, "ExternalOutput")
        self.XS = self.dram("XS", [NT, D], F32)
        self.YT = self.dram("YT", [D, NT], BF16)
        self.UF = self.dram("UF", [NT, 256], BF16)
        self.ZT = self.dram("ZT", [768, NT], BF16)
        self.ZAB = self.dram("ZAB", [NT, 272], F32)
        self.QT = self.dram("QT", [4, 128, NT], BF16)
        self.KT = self.dram("KT", [2, 128, NT], BF16)
        self.VV = self.dram("VV", [NT, 128], BF16)
        self.OF = self.dram("OF", [2, NT, 256], F32)

        with ExitStack() as stack:
            stack.enter_context(nc.allow_non_contiguous_dma(reason="strided weight/activation tiles"))
            self.ps = [stack.enter_context(nc.psum_tensor("ps%d" % b, [128, 512], F32)) for b in range(8)]
            self.psb = {b: self.ps[b][:].bitcast(BF16) for b in range(8)}
            self.R.setup_sems(nc, stack)
            self.pstack = stack
            self.setup_consts()
            self.R.emit(nc)
            self.program()
            self.pstack = stack
            self.dve(lambda e: e.memset(self.epsc[:, 0:1], EPS), [], ["epsc"])
            self.R.emit(nc, final=True)
        return nc

    def setup_consts(self):
        cfg = self.cfg
        self.identb = self.sb("identb", [128, 128], BF16)
        self.identf = self.sb("identf", [128, 128], F32)
        self.onesb = self.sb("onesb", [128, 128], BF16)
        self.gains = self.sb("gains", [128, 88], F32)
        self.gq = self.sb("gq", [128, 2, 64], F32)
        self.gk = self.sb("gk", [128, 2, 64], F32)
        gstage = self.sb("gstage", [88, 128], F32)
        self.dma("sp", self.identb[:], self.cin["c_identb"], [], ["identb"])
        self.dma("sp", self.identf[:], self.cin["c_identf"], [], ["identf"])
        self.dve(lambda e: e.memset(self.onesb[:], 1.0), [], ["onesb"])
        self.epsc = self.sb("epsc", [128, 8], F32)
        self.dve(lambda e: e.memset(self.epsc[:], EPS), [], ["epsc"])
        for nm, i in NORM_IDS.items():
            self.dma("sp", gstage[i * 16:(i + 1) * 16, :],
                     self.inp[nm].rearrange("l (c p) -> (l c) p", p=128), [], ["gstage"])
        self.dma("sp", gstage[80:88, :], self.inp["final_norm"].rearrange("(c p) -> c p", p=128), [], ["gstage"])
        self.pe(lambda e: e.transpose(self.ps[0][:, 0:88], gstage[0:88, :], self.identf[0:88, 0:88]),
                ["gstage", "identf"], ["ps0"])
        self.dve(lambda e: e.tensor_copy(self.gains[:], self.ps[0][:, 0:88]), ["ps0"], ["gains"])
        self.dma("sp", self.gq[:].rearrange("p l d -> p (l d)"),
                 self.inp["attn_q_norm"].rearrange("l d -> (l d)").partition_broadcast(128), [], ["gq"])
        self.dma("sp", self.gk[:].rearrange("p l d -> p (l d)"),
                 self.inp["attn_k_norm"].rearrange("l d -> (l d)").partition_broadcast(128), [], ["gk"])
        self.dve(lambda e: e.tensor_scalar(self.gq[:], self.gq[:], 0.125, None, ALU.mult), ["gq"], ["gq"])

    def rsqrt(self, out, in_, scale, rkeys, wkey):
        self.act(lambda e: e.activation(out, in_, AF.Sqrt, bias=self.epsc[:out.shape[0], 0:1], scale=scale), list(rkeys) + ["epsc"], [wkey + "_s"])
        self.dve(lambda e: e.reciprocal(out, out), [wkey + "_s"], [wkey])

    def gain(self, name, l, c):
        i = NORM_IDS[name] * 16 + l * 8 + c
        return self.gains[:, i:i + 1]

    def program(self):
        cfg = self.cfg
        L = cfg.depth
        sa = self.stop_after
        if sa == "consts":
            return
        if sa == "dnonly":
            self.m_deltanet(0)
            return
        self.t_phase(0)
        if sa in ("T0", "load", "ffn", "n1", "n2"):
            return
        for l in range(L):
            self.m_phase(l)
            if sa == "M%d" % l:
                return
            self.t_phase(l + 1)
            if sa == "T%d" % (l + 1):
                return

    def m_phase(self, l):
        if "gqa" not in self.skip:
            self.m_gqa(l)
        if "four" not in self.skip:
            self.m_fourier(l)
        if "dn" not in self.skip:
            self.m_deltanet(l)
        else:
            self.phase_begin()
            zz = self.sb("zz", [128, 2, self.cfg.ntok], BF16)
            self.dve(lambda e: e.memset(zz[:], 0.0), [], ["zz"])
            self.dma("sp", self.YT[256:512, :].rearrange("(c p) t -> p c t", p=128), zz[:], ["zz"], ["YT"])
            self.phase_end()

    def wplan_tile(self, ph, first_of_seq):
        L = self.cfg.depth
        blocks = []
        lp, ln = ph - 1, ph
        if ph >= 1:
            if first_of_seq:
                for j in range(8):
                    blocks.append(("mem_wkv", lp, "col", j))
                for k in range(8):
                    blocks.append(("mem_wkv", lp, "row", k, [(0, 1024, 1024)]))
            for k in range(8):
                blocks.append(("w_out", lp, "row", k, [(0, 0, 1024)]))
            for j in range(8):
                blocks.append(("mem_wq", lp, "col", j))
            for k in range(8):
                blocks.append(("mem_wo", lp, "row", k, [(0, 0, 1024)]))
            blocks += self.ffn_blocks("ffn2", lp)
        if ph < L:
            blocks += self.ffn_blocks("ffn1", ln)
            for k in range(8):
                blocks.append(("w_in", ln, "row", k, [(0, 0, 256), (256, 1024, 272)]))
            for k in range(8):
                blocks.append(("w_in", ln, "row", k, [(0, 1296, 768)]))
            for j in range(2, 8):
                blocks.append(("w_in", ln, "col", j))
        return blocks

    def ffn_blocks(self, pre, l):
        out = []
        for half in range(2):
            for jj in range(11 * half, 11 * half + 11):
                out.append((pre + "_w_gu", l, "col", jj))
                out.append((pre + "_w_gu", l, "col", NFB + jj))
            for k in range(11 * half, 11 * half + 11):
                out.append((pre + "_w_down", l, "row", k, [(0, 0, 1024)]))
        return out

    def ws_init(self, plan):
        self.ws_plan = plan
        self.ws_next_dma = 0
        self.ws_next_acq = 0
        self.ws_released = 0
        self.ws_fill()

    def ws_fill(self):
        while self.ws_next_dma < len(self.ws_plan) and self.ws_next_dma < self.ws_released + self.NS:
            b = self.ws_plan[self.ws_next_dma]
            s = self.ws_next_dma % self.NS
            slot = self.wslots[s]
            w = self.inp[b[0]]
            if b[2] == "col":
                j = b[3]
                src = w[b[1], :, j * 128:(j + 1) * 128].rearrange("(kc p) c -> p kc c", p=128)
                self.dma("pool", slot[:].rearrange("p (kc c) -> p kc c", kc=8), src, [], ["ws%d" % s])
            else:
                k = b[3]
                for (d0, c0, n) in b[4]:
                    self.dma("pool", slot[:, d0:d0 + n], w[b[1], k * 128:(k + 1) * 128, c0:c0 + n], [], ["ws%d" % s])
            self.ws_next_dma += 1

    def ws_acquire(self, name, kind):
        i = self.ws_next_acq
        b = self.ws_plan[i]
        assert b[0] == name and b[2] == kind, (b, name, kind)
        assert i < self.ws_next_dma, "weight block not yet scheduled (ring too small)"
        self.ws_next_acq += 1
        s = i % self.NS
        return s, self.wslots[s], "ws%d" % s

    def ws_release(self, n=1):
        self.ws_released += n
        self.ws_fill()

    def t_phase(self, ph):
        cfg = self.cfg
        L = cfg.depth
        self.phase_begin()
        TT, NSUB, TG, NTG = cfg.tt, cfg.nsub, cfg.tg, cfg.ntg
        self.NS = 16
        self.wslots = [self.sb("ws%d" % s, [128, 1024], BF16) for s in range(self.NS)]
        self.xt = self.sb("xt", [128, NSUB, D], F32)
        self.nT = self.sb("nT", [128, 8, TT], BF16)
        self.ov = self.sb("ov", [128, 24 * TT], BF16)
        self.hT = self.ov[:, 0:11 * TT].rearrange("p (c t) -> p c t", c=11)
        self.qT = self.ov[:, 0:8 * TT].rearrange("p (c t) -> p c t", c=8)
        self.pT = self.ov[:, 8 * TT:16 * TT].rearrange("p (c t) -> p c t", c=8)
        self.oT = self.ov[:, 16 * TT:24 * TT].rearrange("p (c t) -> p c t", c=8)
        self.yTt = self.ov[:, 8 * TT:16 * TT].rearrange("p (c t) -> p c t", c=8)
        self.memkT = self.sb("memkT", [128, 8, MEMT], BF16)
        self.memV = self.sb("memV", [128, 2, D], BF16)
        self.memnT = self.sb("memnT", [128, 8, MEMT], BF16)
        self.rbc = self.sb("rbc", [128, 2, 512], F32)
        self.cos = self.sb("cos", [128, NSUB, 2, 16], F32)
        self.sin = self.sb("sin", [128, NSUB, 2, 16], F32)
        if ph == L:
            self.gfin = self.sb("gfin", [128, D], F32)
            self.dma("sp", self.gfin[:], self.inp["final_norm"].partition_broadcast(128), [], ["gfin"])
        self.xn = self.sb("xn", [128, 2, D], BF16)
        self.junk = self.sb("junk", [128, D], BF16)
        self.ss = self.sb("ss", [128, 8], F32)
        self.rstd = self.sb("rstd", [128, 8], F32)
        self.sg = self.sb("sg", [128, 2, 512], F32)
        self.zt = self.sb("zt", [128, 1296], F32)
        self.zq = self.sb("zq", [128, 768], F32)
        self.zqb = self.sb("zqb", [128, 768], BF16)
        self.zsq = self.sb("zsq", [128, 640], F32)
        self.zn = self.sb("zn", [128, 640], F32)
        self.zr = self.sb("zr", [128, 640], F32)
        self.rt = self.sb("rt", [128, 4, 320], F32)
        self.zss = self.sb("zss", [128, 32], F32)
        self.qkst = self.sb("qkst", [128, 6, TT], BF16)
        self.vst = self.sb("vst", [128, 128], BF16)
        self.ufst = self.sb("ufst", [128, 256], BF16)
        self.zfm = self.sb("zfm", [128, 2, 512], BF16)
        tiles = []
        for si, (st, S) in enumerate(zip(cfg.starts, cfg.seqs)):
            for t0 in range(0, S, TT):
                tiles.append((si, st, S, t0))
        plan = []
        for (si, st, S, t0) in tiles:
            plan += self.wplan_tile(ph, t0 == 0)
        self.ws_init(plan)
        xsrc = self.inp["x"] if ph == 0 else self.XS
        xdst = self.y if ph == L else self.XS
        for (si, st, S, t0) in tiles:
            g0 = st + t0
            self.dma("sp", self.xt[:], xsrc[g0:g0 + TT, :].rearrange("(j p) d -> p j d", p=128),
                     [], ["x%d" % j for j in range(NSUB)])
            if ph >= 1:
                if t0 == 0:
                    self.mem_kv(ph - 1, si)
                self.wout(ph - 1, g0)
                self.mca(ph - 1)
                self.ffn("ffn2", ph - 1)
            if ph == L:
                self.final_norm()
            if ph < L:
                b0 = t0 // 128
                self.dma("sp", self.cos[:], self.cin["c_cos"][:, b0:b0 + NSUB], [], ["cos"])
                self.dma("sp", self.sin[:], self.cin["c_sin"][:, b0:b0 + NSUB], [], ["sin"])
                if self.stop_after != "load":
                    self.ffn("ffn1", ph)
                if self.stop_after not in ("load", "ffn", "n1", "n2"):
                    self.win(ph, g0, t0)
            self.dma("sp", xdst[g0:g0 + TT, :].rearrange("(j p) d -> p j d", p=128), self.xt[:],
                     ["x%d" % j for j in range(NSUB)], ["xdst"])
        self.phase_end()

    def norm_to_nT(self, name, l):
        cfg = self.cfg
        NSUB, TT = cfg.nsub, cfg.tt
        for j in range(NSUB):
            self.act(lambda e, j=j: e.activation(self.junk[:], self.xt[:, j, :], AF.Square,
                                                 accum_out=self.ss[:, j:j + 1]),
                     ["x%d" % j], ["junk", "ss%d" % j])
        self.rsqrt(self.rstd[:, 0:NSUB], self.ss[:, 0:NSUB], 1.0 / D, ["ss%d" % j for j in range(NSUB)], "rstd")
        if self.stop_after == "n1":
            return
        for j in range(NSUB):
            b = j % 2
            self.dve(lambda e, j=j, b=b: e.tensor_scalar(self.xn[:, b, :], self.xt[:, j, :], self.rstd[:, j:j + 1], None, ALU.mult),
                     ["x%d" % j, "rstd"], ["xn%d" % b])
            pb = 6 + (j % 2)
            psv = self.psb[pb]
            for c in range(8):
                self.pe(lambda e, c=c, b=b, psv=psv: e.transpose(psv[:, c * 128:(c + 1) * 128], self.xn[:, b, c * 128:(c + 1) * 128], self.identb[:]),
                        ["xn%d" % b, "identb"], ["ps%d" % pb])
            for c in range(8):
                self.dve(lambda e, c=c, j=j, psv=psv: e.tensor_scalar(self.nT[:, c, j * 128:(j + 1) * 128], psv[:, c * 128:(c + 1) * 128],
                                                                      self.gain(name, l, c), None, ALU.mult),
                         ["ps%d" % pb, "gains"], ["nT%d" % c])

    def ffn(self, pre, l):
        cfg = self.cfg
        NSUB, TT, TG, NTG = cfg.nsub, cfg.tt, cfg.tg, cfg.ntg
        self.norm_to_nT(pre + "_norm", l)
        nTr = ["nT%d" % c for c in range(8)]
        bank = 0
        if self.stop_after in ("n1", "n2"):
            return
        for half in range(2):
            for jl in range(11):
                sg_, slg, rg = self.ws_acquire(pre + "_w_gu", "col")
                su_, slu, ru = self.ws_acquire(pre + "_w_gu", "col")
                wg = slg[:].rearrange("p (kc c) -> p kc c", kc=8)
                wu = slu[:].rearrange("p (kc c) -> p kc c", kc=8)
                for tg in range(NTG):
                    bg, bu = bank % 6, (bank + 1) % 6
                    bank += 2
                    tsl = slice(tg * TG, (tg + 1) * TG)
                    for kc in range(8):
                        self.pe(lambda e, kc=kc, wg=wg, bg=bg, tsl=tsl: e.matmul(self.ps[bg][:, 0:TG], wg[:, kc, :], self.nT[:, kc, tsl], start=(kc == 0), stop=(kc == 7)),
                                [rg, "nT%d" % kc], ["ps%d" % bg])
                    for kc in range(8):
                        self.pe(lambda e, kc=kc, wu=wu, bu=bu, tsl=tsl: e.matmul(self.ps[bu][:, 0:TG], wu[:, kc, :], self.nT[:, kc, tsl], start=(kc == 0), stop=(kc == 7)),
                                [ru, "nT%d" % kc], ["ps%d" % bu])
                    sb_ = tg % 2
                    self.act(lambda e, bg=bg, sb_=sb_: e.activation(self.sg[:, sb_, 0:TG], self.ps[bg][:, 0:TG], AF.Silu),
                             ["ps%d" % bg], ["sg%d" % sb_])
                    self.dve(lambda e, bu=bu, sb_=sb_, jl=jl, tsl=tsl: e.tensor_tensor(self.hT[:, jl, tsl], self.sg[:, sb_, 0:TG], self.ps[bu][:, 0:TG], ALU.mult),
                             ["sg%d" % sb_, "ps%d" % bu], ["h%d" % jl])
                self.ws_release(2)
            slots = [self.ws_acquire(pre + "_w_down", "row") for _ in range(11)]
            for j in range(NSUB):
                for ch in range(2):
                    b = bank % 6; bank += 1
                    for k in range(11):
                        _, sl, rk = slots[k]
                        self.pe(lambda e, k=k, sl=sl, b=b, j=j, ch=ch: e.matmul(self.ps[b][:, :], self.hT[:, k, j * 128:(j + 1) * 128], sl[:, ch * 512:(ch + 1) * 512], start=(k == 0), stop=(k == 10)),
                                [rk, "h%d" % k], ["ps%d" % b])
                    self.dve(lambda e, b=b, j=j, ch=ch: e.scalar_tensor_tensor(self.xt[:, j, ch * 512:(ch + 1) * 512], self.ps[b][:, :], 0.5, self.xt[:, j, ch * 512:(ch + 1) * 512], ALU.mult, ALU.add),
                             ["ps%d" % b, "x%d" % j], ["x%d" % j])
            self.ws_release(11)

    def win(self, l, g0, t0):
        cfg = self.cfg
        NSUB, TT, TG, NTG = cfg.nsub, cfg.tt, cfg.tg, cfg.ntg
        self.norm_to_nT("mix_norm", l)
        slots = [self.ws_acquire("w_in", "row") for _ in range(8)]
        for j in range(NSUB):
            for (c0, n, zo) in ((0, 256, 0), (256, 272, 256)):
                b = (j * 2 + (c0 > 0)) % 6
                for kc in range(8):
                    _, sl, rk = slots[kc]
                    self.pe(lambda e, kc=kc, sl=sl, b=b, j=j, c0=c0, n=n: e.matmul(self.ps[b][:, 0:n], self.nT[:, kc, j * 128:(j + 1) * 128], sl[:, c0:c0 + n], start=(kc == 0), stop=(kc == 7)),
                            [rk, "nT%d" % kc], ["ps%d" % b])
                if c0 == 0:
                    self.act(lambda e, b=b: e.copy(self.ufst[:], self.ps[b][:, 0:256]), ["ps%d" % b], ["ufst"])
                    self.dma("sp", self.UF[g0 + j * 128:g0 + (j + 1) * 128, :], self.ufst[:], ["ufst"], ["UF"])
                else:
                    self.dve(lambda e, b=b: e.tensor_copy(self.zt[:, 0:272], self.ps[b][:, 0:272]), ["ps%d" % b], ["zab"])
                    self.dma("sp", self.ZAB[g0 + j * 128:g0 + (j + 1) * 128, :], self.zt[:, 0:272], ["zab"], ["ZAB"])
        self.ws_release(8)
        slots = [self.ws_acquire("w_in", "row") for _ in range(8)]
        for j in range(NSUB):
            for (c0, n) in ((0, 512), (512, 256)):
                b = (j * 2 + (c0 > 0)) % 6
                for kc in range(8):
                    _, sl, rk = slots[kc]
                    self.pe(lambda e, kc=kc, sl=sl, b=b, j=j, c0=c0, n=n: e.matmul(self.ps[b][:, 0:n], self.nT[:, kc, j * 128:(j + 1) * 128], sl[:, c0:c0 + n], start=(kc == 0), stop=(kc == 7)),
                            [rk, "nT%d" % kc], ["ps%d" % b])
                if c0 == 0:
                    self.act(lambda e, b=b: e.copy(self.zq[:, 0:512], self.ps[b][:, 0:512]), ["ps%d" % b], ["zq"])
                else:
                    self.dve(lambda e, b=b: e.tensor_copy(self.zq[:, 512:768], self.ps[b][:, 0:256]), ["ps%d" % b], ["zk"])
            self.qk_post(l, j, g0, t0)
        self.ws_release(8)
        for jb in range(6):
            _, sl, rk = self.ws_acquire("w_in", "col")
            wv = sl[:].rearrange("p (kc c) -> p kc c", kc=8)
            for tg in range(NTG):
                b = (jb * NTG + tg) % 6
                tsl = slice(tg * TG, (tg + 1) * TG)
                for kc in range(8):
                    self.pe(lambda e, kc=kc, wv=wv, b=b, tsl=tsl: e.matmul(self.ps[b][:, 0:TG], wv[:, kc, :], self.nT[:, kc, tsl], start=(kc == 0), stop=(kc == 7)),
                            [rk, "nT%d" % kc], ["ps%d" % b])
                s2 = (jb * NTG + tg) % 2
                self.act(lambda e, b=b, s2=s2: e.copy(self.zfm[:, s2, 0:TG], self.ps[b][:, 0:TG]), ["ps%d" % b], ["zfm%d" % s2])
                self.dma("sp", self.ZT[jb * 128:(jb + 1) * 128, g0 + tg * TG:g0 + (tg + 1) * TG], self.zfm[:, s2, 0:TG], ["zfm%d" % s2], ["ZT"])
            self.ws_release(1)
        for pr in range(4):
            self.dma("sp", self.QT[pr, :, g0:g0 + TT], self.qkst[:, pr, :], ["qkst"], ["QT"])
        for g in range(2):
            self.dma("sp", self.KT[g, :, g0:g0 + TT], self.qkst[:, 4 + g, :], ["qkst"], ["KT"])

    def qk_post(self, l, j, g0, t0):
        blk = (t0 // 128) + j
        zq, zsq, zn, zr, zqb, tt_ = self.zq, self.zsq, self.zn, self.zr, self.zqb, self.rt
        hv = lambda ap: ap.rearrange("p (h d) -> p h d", d=64)
        self.dve(lambda e: e.tensor_tensor(zsq[:, 0:640], zq[:, 0:640], zq[:, 0:640], ALU.mult), ["zq", "zk"], ["zsq"])
        self.dve(lambda e: e.tensor_reduce(self.zss[:, 0:10], hv(zsq[:, 0:640]), AX.X, ALU.add), ["zsq"], ["zss"])
        self.rsqrt(self.zss[:, 16:26], self.zss[:, 0:10], 1.0 / 64, ["zss"], "zrs")
        rs = self.zss[:, 16:26]
        self.dve(lambda e: e.tensor_tensor(hv(zn[:, 0:640]), hv(zq[:, 0:640]), rs.unsqueeze(2).to_broadcast([128, 10, 64]), ALU.mult),
                 ["zq", "zk", "zrs"], ["zn"])
        self.dve(lambda e: e.tensor_tensor(hv(zn[:, 0:512]), hv(zn[:, 0:512]), self.gq[:, l:l + 1, :].to_broadcast([128, 8, 64]), ALU.mult),
                 ["zn", "gq"], ["zn"])
        self.dve(lambda e: e.tensor_tensor(hv(zn[:, 512:640]), hv(zn[:, 512:640]), self.gk[:, l:l + 1, :].to_broadcast([128, 2, 64]), ALU.mult),
                 ["zn", "gk"], ["zn"])
        xv = zn[:, 0:640].rearrange("p (h a t f) -> p h a t f", h=10, a=2, t=2)
        ov = zr[:, 0:640].rearrange("p (h a t f) -> p h a t f", h=10, a=2, t=2)
        x1, x2 = xv[:, :, :, 0, :], xv[:, :, :, 1, :]
        cb = self.cos[:, j, :, :].unsqueeze(1).to_broadcast([128, 10, 2, 16])
        sb_ = self.sin[:, j, :, :].unsqueeze(1).to_broadcast([128, 10, 2, 16])
        t = [tt_[:, i, :].rearrange("p (h a f) -> p h a f", h=10, a=2) for i in range(4)]
        self.dve(lambda e: e.tensor_tensor(t[0], x1, cb, ALU.mult), ["zn", "cos"], ["rt0"])
        self.dve(lambda e: e.tensor_tensor(t[1], x2, sb_, ALU.mult), ["zn", "sin"], ["rt1"])
        self.dve(lambda e: e.tensor_tensor(t[2], x2, cb, ALU.mult), ["zn", "cos"], ["rt2"])
        self.dve(lambda e: e.tensor_tensor(t[3], x1, sb_, ALU.mult), ["zn", "sin"], ["rt3"])
        self.dve(lambda e: e.tensor_tensor(ov[:, :, :, 0, :], t[0], t[1], ALU.subtract), ["rt0", "rt1"], ["zr"])
        self.dve(lambda e: e.tensor_tensor(ov[:, :, :, 1, :], t[2], t[3], ALU.add), ["rt2", "rt3"], ["zr"])
        self.act(lambda e: e.copy(zqb[:, 0:512], zr[:, 0:512]), ["zr"], ["zqb"])
        self.act(lambda e: e.copy(zqb[:, 512:768].rearrange("p (g u d) -> p g u d", g=2, u=2),
                                  zr[:, 512:640].rearrange("p (g d) -> p g d", g=2).unsqueeze(2).to_broadcast([128, 2, 2, 64])),
                 ["zr"], ["zqb"])
        self.act(lambda e: e.copy(self.vst[:], zq[:, 640:768]), ["zk"], ["vst"])
        self.dma("sp", self.VV[g0 + j * 128:g0 + (j + 1) * 128, :], self.vst[:], ["vst"], ["VV"])
        pb = 6 + (j % 2)
        psv = self.psb[pb]
        for i in range(6):
            self.pe(lambda e, i=i, psv=psv: e.transpose(psv[:, i * 128:(i + 1) * 128], zqb[:, i * 128:(i + 1) * 128], self.identb[:]),
                    ["zqb", "identb"], ["ps%d" % pb])
        self.dve(lambda e, psv=psv: e.tensor_copy(self.qkst[:, :, j * 128:(j + 1) * 128], psv[:, 0:768].rearrange("p (i t) -> p i t", i=6)),
                 ["ps%d" % pb], ["qkst"])

    def mem_kv(self, l, si):
        mt = self.zt
        mnT = self.qkst
        mnTv = self.memnT
        for j in range(2):
            self.dma("sp", mt[:, 0:D], self.inp["mem"][si * MEMT + j * 128: si * MEMT + (j + 1) * 128, :], [], ["zab"])
            self.act(lambda e: e.activation(self.junk[:], mt[:, 0:D], AF.Square, accum_out=self.ss[:, 0:1]), ["zab"], ["junk", "ss0"])
            self.rsqrt(self.rstd[:, 0:1], self.ss[:, 0:1], 1.0 / D, ["ss0"], "rstd")
            self.dve(lambda e: e.tensor_scalar(self.xn[:, 0, :], mt[:, 0:D], self.rstd[:, 0:1], None, ALU.mult), ["zab", "rstd"], ["xn0"])
            pb = 6 + j
            psv = self.psb[pb]
            for c in range(8):
                self.pe(lambda e, c=c, psv=psv: e.transpose(psv[:, c * 128:(c + 1) * 128], self.xn[:, 0, c * 128:(c + 1) * 128], self.identb[:]),
                        ["xn0", "identb"], ["ps%d" % pb])
            for c in range(8):
                self.dve(lambda e, c=c, j=j, psv=psv: e.tensor_scalar(mnTv[:, c, j * 128:(j + 1) * 128], psv[:, c * 128:(c + 1) * 128],
                                                                      self.gain("mem_norm_m", l, c), None, ALU.mult),
                         ["ps%d" % pb, "gains"], ["qkst"])
        for jb in range(8):
            _, sl, rk = self.ws_acquire("mem_wkv", "col")
            wv = sl[:].rearrange("p (kc c) -> p kc c", kc=8)
            b = jb % 6
            for kc in range(8):
                self.pe(lambda e, kc=kc, wv=wv, b=b: e.matmul(self.ps[b][:, 0:MEMT], wv[:, kc, :], mnTv[:, kc, :], start=(kc == 0), stop=(kc == 7)),
                        [rk, "qkst"], ["ps%d" % b])
            self.act(lambda e, b=b, jb=jb: e.copy(self.memkT[:, jb, :], self.ps[b][:, 0:MEMT]), ["ps%d" % b], ["memkT"])
            self.ws_release(1)
        slots = [self.ws_acquire("mem_wkv", "row") for _ in range(8)]
        for mc in range(2):
            for ch in range(2):
                b = (mc * 2 + ch) % 6
                for kc in range(8):
                    _, sl, rk = slots[kc]
                    self.pe(lambda e, kc=kc, sl=sl, b=b, mc=mc, ch=ch: e.matmul(self.ps[b][:, :], mnTv[:, kc, mc * 128:(mc + 1) * 128], sl[:, ch * 512:(ch + 1) * 512], start=(kc == 0), stop=(kc == 7)),
                            [rk, "qkst"], ["ps%d" % b])
                self.act(lambda e, b=b, mc=mc, ch=ch: e.copy(self.memV[:, mc, ch * 512:(ch + 1) * 512], self.ps[b][:, :]), ["ps%d" % b], ["memV"])
        self.ws_release(8)

    def tm_linear_acc(self, name, aT, akey, nk, scale):
        cfg = self.cfg
        slots = [self.ws_acquire(name, "row") for _ in range(nk)]
        for j in range(cfg.nsub):
            for ch in range(2):
                b = (j * 2 + ch) % 6
                for k in range(nk):
                    _, sl, rk = slots[k]
                    self.pe(lambda e, k=k, sl=sl, b=b, j=j, ch=ch: e.matmul(self.ps[b][:, :], aT[:, k, j * 128:(j + 1) * 128], sl[:, ch * 512:(ch + 1) * 512], start=(k == 0), stop=(k == nk - 1)),
                            [rk, akey], ["ps%d" % b])
                self.dve(lambda e, b=b, j=j, ch=ch: e.scalar_tensor_tensor(self.xt[:, j, ch * 512:(ch + 1) * 512], self.ps[b][:, :], scale, self.xt[:, j, ch * 512:(ch + 1) * 512], ALU.mult, ALU.add),
                         ["ps%d" % b, "x%d" % j], ["x%d" % j])
        self.ws_release(nk)

    def wout(self, l, g0):
        TT = self.cfg.tt
        self.dma("sp", self.yTt[:], self.YT[:, g0:g0 + TT].rearrange("(c p) t -> p c t", p=128), [], ["yTt"])
        self.tm_linear_acc("w_out", self.yTt, "yTt", 8, 1.0)

    def mca(self, l):
        cfg = self.cfg
        NSUB, TT, TG, NTG = cfg.nsub, cfg.tt, cfg.tg, cfg.ntg
        self.norm_to_nT("mem_norm_x", l)
        for jb in range(8):
            _, sl, rk = self.ws_acquire("mem_wq", "col")
            wv = sl[:].rearrange("p (kc c) -> p kc c", kc=8)
            for tg in range(NTG):
                b = (jb * NTG + tg) % 6
                tsl = slice(tg * TG, (tg + 1) * TG)
                for kc in range(8):
                    self.pe(lambda e, kc=kc, wv=wv, b=b, tsl=tsl: e.matmul(self.ps[b][:, 0:TG], wv[:, kc, :], self.nT[:, kc, tsl], start=(kc == 0), stop=(kc == 7)),
                            [rk, "nT%d" % kc], ["ps%d" % b])
                self.act(lambda e, b=b, jb=jb, tsl=tsl: e.copy(self.qT[:, jb, tsl], self.ps[b][:, 0:TG]), ["ps%d" % b], ["qT%d" % jb])
            self.ws_release(1)
        bank = 0
        for tg in range(NTG):
            tsl = slice(tg * TG, (tg + 1) * TG)
            for h in range(4):
                for mc in range(2):
                    b = bank % 6; bank += 1
                    for dc in range(2):
                        self.pe(lambda e, b=b, h=h, mc=mc, dc=dc, tsl=tsl: e.matmul(self.ps[b][:, 0:TG], self.memkT[:, 2 * h + dc, mc * 128:(mc + 1) * 128], self.qT[:, 2 * h + dc, tsl], start=(dc == 0), stop=(dc == 1)),
                                ["memkT", "qT%d" % (2 * h + dc)], ["ps%d" % b])
                    self.act(lambda e, b=b, h=h, mc=mc, tsl=tsl: e.activation(self.pT[:, 2 * h + mc, tsl], self.ps[b][:, 0:TG], AF.Exp, scale=1.0 / 16.0),
                             ["ps%d" % b], ["pT%d" % (2 * h + mc)])
                b = bank % 6; bank += 1
                for mc in range(2):
                    self.pe(lambda e, b=b, h=h, mc=mc, tsl=tsl: e.matmul(self.ps[b][:, 0:TG], self.onesb[:], self.pT[:, 2 * h + mc, tsl], start=(mc == 0), stop=(mc == 1)),
                            ["onesb", "pT%d" % (2 * h + mc)], ["ps%d" % b])
                rb = h % 2
                self.dve(lambda e, b=b, rb=rb: e.reciprocal(self.rbc[:, rb, 0:TG], self.ps[b][:, 0:TG]), ["ps%d" % b], ["rbc%d" % rb])
                for dc in range(2):
                    b = bank % 6; bank += 1
                    for mc in range(2):
                        self.pe(lambda e, b=b, h=h, mc=mc, dc=dc, tsl=tsl: e.matmul(self.ps[b][:, 0:TG], self.memV[:, mc, (2 * h + dc) * 128:(2 * h + dc + 1) * 128], self.pT[:, 2 * h + mc, tsl], start=(mc == 0), stop=(mc == 1)),
                                ["memV", "pT%d" % (2 * h + mc)], ["ps%d" % b])
                    self.dve(lambda e, b=b, h=h, dc=dc, rb=rb, tsl=tsl: e.tensor_tensor(self.oT[:, 2 * h + dc, tsl], self.ps[b][:, 0:TG], self.rbc[:, rb, 0:TG], ALU.mult),
                             ["ps%d" % b, "rbc%d" % rb], ["oT"])
        self.tm_linear_acc("mem_wo", self.oT, "oT", 8, 1.0)

    def final_norm(self):
        cfg = self.cfg
        NSUB = cfg.nsub
        for j in range(NSUB):
            self.act(lambda e, j=j: e.activation(self.junk[:], self.xt[:, j, :], AF.Square, accum_out=self.ss[:, j:j + 1]),
                     ["x%d" % j], ["junk", "ss%d" % j])
        self.rsqrt(self.rstd[:, 0:NSUB], self.ss[:, 0:NSUB], 1.0 / D, ["ss%d" % j for j in range(NSUB)], "rstd")
        for j in range(NSUB):
            self.dve(lambda e, j=j: e.scalar_tensor_tensor(self.xt[:, j, :], self.xt[:, j, :], self.rstd[:, j:j + 1], self.gfin[:], ALU.mult, ALU.mult),
                     ["x%d" % j, "rstd", "gfin"], ["x%d" % j])

    def m_gqa(self, l):
        cfg = self.cfg
        self.phase_begin()
        SM = max(cfg.seqs)
        NBM = SM // 128
        KTs = self.sb("KTs", [128, 2, SM], BF16)
        Vx = self.sb("Vx", [128, NBM, 2, 65], BF16)
        onesf = self.sb("onesf", [128, 64], F32)
        QW = min(512, min(cfg.seqs))
        qt = self.sb("qtile", [128, 2, QW], BF16)
        pT = self.sb("pTa", [128, 3, QW], BF16)
        rr = self.sb("rr", [128, QW], F32)
        bcs = self.sb("bcs", [64, QW], F32)
        yst = self.sb("yst", [64, 2, QW], BF16)
        self.dve(lambda e: e.memset(onesf[:], 1.0), [], ["onesf"])
        self.dve(lambda e: e.memset(Vx[:], 1.0), [], ["Vx"])
        it = 0
        for si, (st, S) in enumerate(zip(cfg.starts, cfg.seqs)):
            NB = S // 128
            for g in range(2):
                self.dma("sp", KTs[:, g, 0:S], self.KT[g, :, st:st + S], [], ["KTs"])
            for g in range(2):
                self.dma("sp", Vx[:, 0:NB, g, 0:64], self.VV[st:st + S, g * 64:(g + 1) * 64].rearrange("(kb p) d -> p kb d", p=128), [], ["Vx"])
            for pr in range(4):
                g = pr // 2
                for qb in range(S // QW):
                    qs = it % 2; it += 1
                    self.dma("sp", qt[:, qs, :], self.QT[pr, :, st + qb * QW: st + (qb + 1) * QW], [], ["qt%d" % qs])
                    for hh in range(2):
                        h = 2 * pr + hh
                        base = hh * 64
                        ob = 4 + hh
                        for kb in range(NB + 1):
                            if kb < NB:
                                sbk = kb % 3
                                self.pe(lambda e, sbk=sbk, kb=kb, g=g, base=base, qs=qs: e.matmul(self.ps[sbk][:, 0:QW], KTs[base:base + 64, g, kb * 128:(kb + 1) * 128], qt[base:base + 64, qs, :], start=True, stop=True),
                                        ["KTs", "qt%d" % qs], ["ps%d" % sbk])
                                self.act(lambda e, sbk=sbk: e.activation(pT[:, sbk, :], self.ps[sbk][:, 0:QW], AF.Exp), ["ps%d" % sbk], ["pTa%d" % sbk])
                            if kb >= 1:
                                k2 = kb - 1
                                self.pe(lambda e, k2=k2, g=g, ob=ob, NB=NB: e.matmul(self.ps[ob][0:65, 0:QW], Vx[:, k2, g, :], pT[:, k2 % 3, :], start=(k2 == 0), stop=(k2 == NB - 1)),
                                        ["Vx", "pTa%d" % (k2 % 3)], ["ps%d" % ob])
                        self.dve(lambda e, ob=ob: e.reciprocal(rr[64:65, :], self.ps[ob][64:65, 0:QW]), ["ps%d" % ob], ["rr"])
                        self.pe(lambda e: e.matmul(self.ps[3][0:64, 0:QW], onesf[64:65, 0:64], rr[64:65, :], start=True, stop=True), ["onesf", "rr"], ["ps3"])
                        self.act(lambda e: e.copy(bcs[:, :], self.ps[3][0:64, 0:QW]), ["ps3"], ["bcs"])
                        self.dve(lambda e, ob=ob, hh=hh: e.tensor_tensor(yst[:, hh, :], self.ps[ob][0:64, 0:QW], bcs[:, :], ALU.mult), ["ps%d" % ob, "bcs"], ["yst%d" % hh])
                        self.dma("sp", self.YT[512 + h * 64: 512 + (h + 1) * 64, st + qb * QW: st + (qb + 1) * QW], yst[:, hh, :], ["yst%d" % hh], ["YT"])
        self.phase_end()

    def m_fourier(self, l):
        cfg = self.cfg
        self.phase_begin()
        SM = max(cfg.seqs)
        N1M = SM // 128
        fc = self.sb("fc", [64, 2, 64], BF16)
        wf = self.sb("wf", [64, 4, 64], BF16)
        AB = self.sb("AB", [64, 4, 2, 64], BF16)
        f1 = self.sb("f1", [64, 3 * N1M], BF16)
        Ht = self.sb("Ht", [128, N1M, 2, 128], BF16)
        Ut = self.sb("Ut", [64, 128, 64], BF16)
        Ot = self.sb("Ot", [128, N1M, 3, 64], BF16)
        PT = self.sb("PTf", [64, 2, SM], BF16)
        yst = self.sb("ystf", [64, 2, 512], BF16)
        self.dma("sp", fc[:], self.cin["c_fc"], [], ["fc"])
        self.dma("pool", wf[:], self.inp["four_w"][l].rearrange("g c d -> c g d"), [], ["wf"])
        for g in range(4):
            for x in range(2):
                self.pe(lambda e, g=g, x=x: e.matmul(self.ps[0][0:64, (g * 2 + x) * 64:(g * 2 + x + 1) * 64], fc[:, x, :], wf[:, g, :], start=True, stop=True),
                        ["fc", "wf"], ["ps0"])
        self.dve(lambda e: e.tensor_copy(AB[:].rearrange("c g x d -> c (g x d)"), self.ps[0][0:64, :]), ["ps0"], ["AB"])
        for si, (st, S) in enumerate(zip(cfg.starts, cfg.seqs)):
            N1 = S // 128
            self.dma("sp", f1[0:N1, 0:3 * N1], self.cin["c_f1_%d" % S], [], ["f1"])
            self.dma("sp", Ht[:, 0:N1, :, :], self.cin["c_h_%d" % S], [], ["Ht"])
            CB = 1
            while CB * 2 <= min(64, 512 // (3 * N1)):
                CB *= 2
            for g in range(4):
                self.dma("sp", Ut[0:N1, :, :], self.UF[st:st + S, g * 64:(g + 1) * 64].rearrange("(n1 n2) c -> n1 n2 c", n2=128), [], ["Ut"])
                for c0 in range(0, 64, CB):
                    b = (c0 // CB) % 3
                    for cl in range(CB):
                        c = c0 + cl
                        self.pe(lambda e, b=b, cl=cl, c=c, N1=N1: e.matmul(self.ps[b][:, cl * 3 * N1:(cl + 1) * 3 * N1], Ut[0:N1, :, c], f1[0:N1, 0:3 * N1], start=True, stop=True),
                                ["Ut", "f1"], ["ps%d" % b])
                    self.dve(lambda e, b=b, c0=c0, N1=N1, CB=CB: e.tensor_copy(Ot[:, 0:N1, :, c0:c0 + CB].rearrange("p k x c -> p c x k"),
                                                                             self.ps[b][:, 0:CB * 3 * N1].rearrange("p (c x k) -> p c x k", c=CB, x=3)),
                             ["ps%d" % b], ["Ot"])
                KB = min(4, N1)
                for k0 in range(0, N1, KB):
                    for x in range(2):
                        b = 3 + ((k0 // KB) * 2 + x) % 3
                        for kl in range(KB):
                            k1 = k0 + kl
                            o = self.ps[b][0:64, kl * 128:(kl + 1) * 128]
                            if x == 0:
                                self.pe(lambda e, o=o, k1=k1: e.matmul(o, Ot[:, k1, 0, :], Ht[:, k1, 0, :], start=True, stop=False), ["Ot", "Ht"], ["ps%d" % b])
                                self.pe(lambda e, o=o, k1=k1: e.matmul(o, Ot[:, k1, 2, :], Ht[:, k1, 1, :], start=False, stop=True), ["Ot", "Ht"], ["ps%d" % b])
                            else:
                                self.pe(lambda e, o=o, k1=k1: e.matmul(o, Ot[:, k1, 0, :], Ht[:, k1, 1, :], start=True, stop=False), ["Ot", "Ht"], ["ps%d" % b])
                                self.pe(lambda e, o=o, k1=k1: e.matmul(o, Ot[:, k1, 1, :], Ht[:, k1, 0, :], start=False, stop=True), ["Ot", "Ht"], ["ps%d" % b])
                        dst = PT[:, x, 0:S].rearrange("c (k2 k1) -> c k1 k2", k1=N1)[:, k0:k0 + KB, :]
                        eng = self.act if x == 0 else self.dve
                        if x == 0:
                            self.act(lambda e, b=b, dst=dst, KB=KB: e.copy(dst, self.ps[b][0:64, 0:KB * 128].rearrange("c (k f) -> c k f", k=KB)), ["ps%d" % b], ["PTf"])
                        else:
                            self.dve(lambda e, b=b, dst=dst, KB=KB: e.tensor_copy(dst, self.ps[b][0:64, 0:KB * 128].rearrange("c (k f) -> c k f", k=KB)), ["ps%d" % b], ["PTf"])
                SW = min(512, S)
                for s0 in range(0, S, SW):
                    b = 6 + (s0 // SW) % 2
                    ys = (s0 // SW) % 2
                    self.pe(lambda e, b=b, g=g, s0=s0, SW=SW: e.matmul(self.ps[b][0:64, 0:SW], AB[:, g, 0, :], PT[:, 0, s0:s0 + SW], start=True, stop=False), ["AB", "PTf"], ["ps%d" % b])
                    self.pe(lambda e, b=b, g=g, s0=s0, SW=SW: e.matmul(self.ps[b][0:64, 0:SW], AB[:, g, 1, :], PT[:, 1, s0:s0 + SW], start=False, stop=True), ["AB", "PTf"], ["ps%d" % b])
                    self.act(lambda e, b=b, ys=ys, SW=SW: e.copy(yst[:, ys, 0:SW], self.ps[b][0:64, 0:SW]), ["ps%d" % b], ["ystf%d" % ys])
                    self.dma("sp", self.YT[g * 64:(g + 1) * 64, st + s0: st + s0 + SW], yst[:, ys, 0:SW], ["ystf%d" % ys], ["YT"])
        self.phase_end()

    def nb(self):
        self._bank = (getattr(self, "_bank", -1) + 1) % 6
        return self._bank

    def nbt(self):
        self._bankt = (getattr(self, "_bankt", -1) + 1) % 2
        return 6 + self._bankt

    def m_deltanet(self, l):
        cfg = self.cfg
        self.phase_begin()
        W = min(512, min(cfg.seqs)); NCH = W // 64; NCP = NCH // 2; NU = NCH * 4
        idb = self.identb[0:64, 0:64]; idf = self.identf[0:64, 0:64]; onb = self.onesb[0:64, 0:64]
        sbt = lambda n, sh, dt: self.sb(n, sh, dt)
        cw = sbt("cw", [64, 3, 12], F32); dg = sbt("dg", [64, 12, 3, 64], BF16)
        mk = sbt("mk", [64, 4, 64], F32); tri = sbt("tri", [64, 3, 64], F32)
        nA = sbt("nA", [64, 8], F32); dtb = sbt("dtb", [64, 8], F32)
        zin = sbt("zin", [64, 12, W + 2], BF16); zab = sbt("zab", [64, NCH, 16], F32)
        cs = sbt("cs", [64, 2, W], F32); sq = sbt("sq", [64, 2, W], BF16); rs = sbt("rs", [64, 2, W], F32)
        qTn = sbt("qTn", [64, NU, 64], BF16); kTn = sbt("kTn", [64, NU, 64], BF16); vTn = sbt("vTn", [64, NU, 64], BF16)
        k_tm = sbt("k_tm", [64, NU, 64], BF16); v_tm = sbt("v_tm", [64, NU, 64], BF16)
        gt = sbt("gt", [64, 8, NU], F32)
        dgx = sbt("dgx", [64, 8, 64], F32); dl = sbt("dl", [64, 8, 64], F32); egr = sbt("egr", [64, 8, 64], F32)
        t1 = sbt("t1", [64, 2, 8, 64], F32); DTt = sbt("DTt", [64, 8, 64], F32); DLt = sbt("DLt", [64, 8, 64], F32)
        qdec = sbt("qdec", [64, NU, 64], BF16); qkTm = sbt("qkTm", [64, NU, 64], BF16)
        Lp = [sbt("Lp%d" % i, [64, NU, 64], BF16) for i in range(2)]
        Up = [sbt("Up%d" % i, [64, NU, 64], BF16) for i in range(2)]
        Rp = [sbt("Rp%d" % i, [64, NU, 64], BF16) for i in range(2)]
        vb = sbt("vb", [64, NU, 64], BF16); kbe = sbt("kbe", [64, NU, 64], BF16); ktl = sbt("ktl", [64, NU, 64], BF16)
        u_sb = sbt("u_sb", [64, NU, 64], F32); wT = sbt("wT", [64, NU, 64], BF16)
        Sf = sbt("Sf", [64, 4, 64], F32); Sb = sbt("Sb", [64, 4, 64], BF16)
        vnew = sbt("vnew", [64, 2, 4, 64], BF16); ost = sbt("ost", [64, 2, 256], F32)
        o2 = sbt("o2", [128, 2, 256], F32); gte = sbt("gte", [128, 256], F32); osum = sbt("osum", [128, 256], F32)
        pss = sbt("pss", [128, 8], F32); ydb = sbt("ydb", [128, 256], BF16); gdn = sbt("gdn", [128, 64], F32)
        ydT = sbt("ydT", [128, 2, W], BF16)
        self.dma("sp", cw[:], self.inp["dn_conv"][l].rearrange("t (xh d) -> d t xh", d=64), [], ["cw"])
        self.dma("sp", mk[:], self.cin["c_dnmask"], [], ["mk"])
        self.dma("sp", tri[:], self.cin["c_tri"], [], ["tri"])
        self.dma("sp", nA[:], self.inp["dn_A_log"][l].rearrange("a h -> (a h)").partition_broadcast(64), [], ["nA"])
        self.dma("sp", dtb[:], self.inp["dn_dt_bias"][l].rearrange("a h -> (a h)").partition_broadcast(64), [], ["dtb"])
        self.dma("sp", gdn[:], self.inp["dn_out_norm"][l].partition_broadcast(128), [], ["gdn"])
        self.act(lambda e: e.activation(nA[:], nA[:], AF.Exp), ["nA"], ["nA"])
        self.dve(lambda e: e.tensor_scalar(nA[:], nA[:], -1.0, None, ALU.mult), ["nA"], ["nA"])
        for xh in range(12):
            for t in range(3):
                self.dve(lambda e, xh=xh, t=t: e.tensor_scalar(dg[:, xh, t, :], idb, cw[:, t, xh:xh + 1], None, ALU.mult), ["cw", "identb"], ["dg"])
        one_col = tri[:, 2, 0:1]
        B8 = lambda ap: ap.unsqueeze(2).to_broadcast([64, 8, 64])
        B4 = lambda ap: ap.unsqueeze(2).to_broadcast([64, 4, 64])
        psv = lambda b: self.ps[b][0:64, :].rearrange("p (u f) -> p u f", f=64)
        psbv = lambda b: self.psb[b][0:64, 0:512].rearrange("p (u f) -> p u f", f=64)

        for si, (st, S) in enumerate(zip(cfg.starts, cfg.seqs)):
            NTL = S // W
            for d in range(2):
                kT_, kL_ = (0, 1) if d == 0 else (2, 3)
                self.dve(lambda e: e.memset(Sf[:], 0.0), [], ["Sf"])
                self.dve(lambda e: e.memset(Sb[:], 0.0), [], ["Sb"])
                order = range(NTL) if d == 0 else range(NTL - 1, -1, -1)
                for ti in order:
                    p0 = ti * W
                    lo, hi = max(p0 - 1, 0), min(p0 + W + 1, S)
                    c0 = lo - (p0 - 1)
                    if p0 == 0:
                        self.dve(lambda e: e.memset(zin[:, :, 0:1], 0.0), [], ["zin"])
                    if p0 + W == S:
                        self.dve(lambda e: e.memset(zin[:, :, W + 1:W + 2], 0.0), [], ["zin"])
                    for x in range(3):
                        self.dma("sp", zin[:, x * 4:(x + 1) * 4, c0:c0 + (hi - lo)],
                                 self.ZT[x * 256:(x + 1) * 256, st + lo:st + hi].rearrange("(h d) t -> d h t", d=64), [], ["zin"])
                    self.dma("sp", zab[:], self.ZAB[st + p0:st + p0 + W, 0:16].rearrange("(n c) x -> c n x", c=64), [], ["zab"])
                    for xh in range(12):
                        x, h = xh // 4, xh % 4
                        b = self.nb()
                        for t in range(3):
                            self.pe(lambda e, b=b, xh=xh, t=t: e.matmul(self.ps[b][0:64, 0:W], dg[:, xh, t, :], zin[:, xh, t:t + W], start=(t == 0), stop=(t == 2)),
                                    ["dg", "zin"], ["ps%d" % b])
                        if x == 2:
                            self.act(lambda e, b=b, h=h: e.activation(vTn[:].rearrange("p (n h) f -> p n h f", h=4)[:, :, h, :], self.ps[b][0:64, 0:W].rearrange("p (n f) -> p n f", f=64), AF.Silu),
                                     ["ps%d" % b], ["vTn"])
                            continue
                        s2 = xh % 2
                        self.act(lambda e, b=b, s2=s2: e.activation(cs[:, s2, :], self.ps[b][0:64, 0:W], AF.Silu), ["ps%d" % b], ["cs%d" % s2])
                        self.act(lambda e, s2=s2: e.activation(sq[:, s2, :], cs[:, s2, :], AF.Square), ["cs%d" % s2], ["sq%d" % s2])
                        b2 = self.nb()
                        self.pe(lambda e, b2=b2, s2=s2: e.matmul(self.ps[b2][0:64, 0:W], onb, sq[:, s2, :], start=True, stop=True), ["onesb", "sq%d" % s2], ["ps%d" % b2])
                        self.act(lambda e, b2=b2, s2=s2: e.activation(rs[:, s2, :], self.ps[b2][0:64, 0:W], AF.Sqrt, bias=self.epsc[0:64, 0:1], scale=1.0), ["ps%d" % b2, "epsc"], ["rs%d_s" % s2])
                        self.dve(lambda e, s2=s2: e.reciprocal(rs[:, s2, :], rs[:, s2, :]), ["rs%d_s" % s2], ["rs%d" % s2])
                        dstT = (qTn if x == 0 else kTn)[:].rearrange("p (n h) f -> p n h f", h=4)[:, :, h, :]
                        csv = cs[:, s2, :].rearrange("p (n f) -> p n f", f=64); rsv = rs[:, s2, :].rearrange("p (n f) -> p n f", f=64)
                        if x == 0:
                            self.dve(lambda e, dstT=dstT, csv=csv, rsv=rsv: e.scalar_tensor_tensor(dstT, csv, 0.125, rsv, ALU.mult, ALU.mult), ["cs%d" % s2, "rs%d" % s2], ["qTn"])
                        else:
                            self.dve(lambda e, dstT=dstT, csv=csv, rsv=rsv: e.tensor_tensor(dstT, csv, rsv, ALU.mult), ["cs%d" % s2, "rs%d" % s2], ["kTn"])
                    if self.dn_stage < 2:
                        continue
                    for cp in range(NCP):
                        for (srcT, dstm, key, dkey) in ((kTn, k_tm, "kTn", "k_tm"), (vTn, v_tm, "vTn", "v_tm")):
                            b = self.nbt()
                            for u in range(8):
                                self.pe(lambda e, b=b, u=u, cp=cp, srcT=srcT: e.transpose(self.psb[b][0:64, u * 64:(u + 1) * 64], srcT[:, cp * 8 + u, :], idb), [key, "identb"], ["ps%d" % b])
                            self.dve(lambda e, b=b, cp=cp, dstm=dstm: e.tensor_copy(dstm[:, cp * 8:(cp + 1) * 8, :], psbv(b)), ["ps%d" % b], [dkey])
                    if self.dn_stage < 3:
                        continue
                    G = lambda r: gt[:, r, :]
                    Gn = lambda r: gt[:, r, :].rearrange("p (n h) -> p n h", h=4)
                    self.dve(lambda e, d=d: e.tensor_tensor(Gn(0), zab[:, :, d * 4:(d + 1) * 4], dtb[:, d * 4:(d + 1) * 4].unsqueeze(1).to_broadcast([64, NCH, 4]), ALU.add), ["zab", "dtb"], ["g0"])
                    self.act(lambda e: e.activation(G(0), G(0), AF.Exp), ["g0"], ["g0"])
                    self.act(lambda e: e.activation(G(0), G(0), AF.Ln, bias=one_col, scale=1.0), ["g0", "tri"], ["g0"])
                    self.dve(lambda e, d=d: e.tensor_tensor(Gn(1), Gn(0), nA[:, d * 4:(d + 1) * 4].unsqueeze(1).to_broadcast([64, NCH, 4]), ALU.mult), ["g0", "nA"], ["g1"])
                    self.act(lambda e, d=d: e.activation(Gn(2), zab[:, :, 8 + d * 4:8 + (d + 1) * 4], AF.Sigmoid), ["zab"], ["g2"])
                    b = self.nb()
                    self.pe(lambda e, b=b, d=d: e.matmul(self.ps[b][0:64, 0:NU], tri[:, d, :], G(1), start=True, stop=True), ["tri", "g1"], ["ps%d" % b])
                    self.pe(lambda e, b=b: e.matmul(self.ps[b][0:64, 64:64 + NU], tri[:, 2, :], G(1), start=True, stop=True), ["tri", "g1"], ["ps%d" % b])
                    self.dve(lambda e, b=b: e.tensor_copy(G(3), self.ps[b][0:64, 0:NU]), ["ps%d" % b], ["g3"])
                    self.act(lambda e, b=b: e.activation(G(4), self.ps[b][0:64, 0:NU], AF.Exp), ["ps%d" % b], ["g4"])
                    self.act(lambda e, b=b: e.activation(G(5), self.ps[b][0:64, 64:64 + NU], AF.Exp), ["ps%d" % b], ["g5"])
                    self.dve(lambda e, b=b: e.tensor_tensor(G(6), self.ps[b][0:64, 64:64 + NU], G(3), ALU.subtract), ["ps%d" % b, "g3"], ["g6"])
                    self.act(lambda e: e.activation(G(6), G(6), AF.Exp), ["g6"], ["g6"])
                    self.dve(lambda e: e.tensor_tensor(G(7), G(2), G(4), ALU.mult), ["g2", "g4"], ["g7"])
                    if self.dn_stage < 4:
                        continue
                    for cp in range(NCP):
                        us = slice(cp * 8, (cp + 1) * 8)
                        bA = self.nb()
                        for u in range(8):
                            self.pe(lambda e, bA=bA, u=u, cp=cp: e.matmul(self.ps[bA][0:64, u * 64:(u + 1) * 64], kTn[:, cp * 8 + u, :], kTn[:, cp * 8 + u, :], start=True, stop=True), ["kTn"], ["ps%d" % bA])
                        bB = self.nb()
                        for u in range(8):
                            self.pe(lambda e, bB=bB, u=u, cp=cp: e.matmul(self.ps[bB][0:64, u * 64:(u + 1) * 64], kTn[:, cp * 8 + u, :], qTn[:, cp * 8 + u, :], start=True, stop=True), ["kTn", "qTn"], ["ps%d" % bB])
                        if self.dn_stage < 4.01:
                            continue
                        self.dve(lambda e, us=us: e.tensor_tensor(dgx[:], idf.unsqueeze(1).to_broadcast([64, 8, 64]), B8(gt[:, 3, us]), ALU.mult), ["identf", "g3"], ["dgx"])
                        if self.dn_stage < 4.02:
                            continue
                        bC = self.nb()
                        self.pe(lambda e, bC=bC: e.matmul(self.ps[bC][0:64, :], tri[:, 2, :], dgx[:].rearrange("p u f -> p (u f)"), start=True, stop=True), ["tri", "dgx"], ["ps%d" % bC])
                        if self.dn_stage < 4.03:
                            continue
                        self.dve(lambda e, bC=bC, us=us: e.tensor_tensor(dl[:], psv(bC), B8(gt[:, 3, us]), ALU.subtract), ["ps%d" % bC, "g3"], ["dl"])
                        if self.dn_stage < 4.04:
                            continue
                        self.dve(lambda e, bC=bC: e.tensor_copy(egr[:], psv(bC)), ["ps%d" % bC], ["egr"])
                        self.act(lambda e: e.activation(egr[:], egr[:], AF.Exp), ["egr"], ["egr"])
                        if self.dn_stage < 4.05:
                            continue
                        self.dve(lambda e, us=us: e.tensor_tensor(qdec[:, us, :], qTn[:, us, :], egr[:], ALU.mult), ["qTn", "egr"], ["qdec"])
                        if self.dn_stage < 4.25:
                            continue
                        self.dve(lambda e, kT_=kT_: e.scalar_tensor_tensor(t1[:, 0], dl[:], 0.0, mk[:, kT_, :].unsqueeze(1).to_broadcast([64, 8, 64]), ALU.min, ALU.add), ["dl", "mk"], ["t1a"])
                        self.act(lambda e: e.activation(DTt[:], t1[:, 0], AF.Exp), ["t1a"], ["DTt"])
                        self.dve(lambda e, kL_=kL_: e.scalar_tensor_tensor(t1[:, 1], dl[:], 0.0, mk[:, kL_, :].unsqueeze(1).to_broadcast([64, 8, 64]), ALU.max, ALU.subtract), ["dl", "mk"], ["t1b"])
                        self.act(lambda e: e.activation(DLt[:], t1[:, 1], AF.Exp, scale=-1.0), ["t1b"], ["DLt"])
                        self.dve(lambda e, bB=bB, us=us: e.tensor_tensor(qkTm[:, us, :], psv(bB), DTt[:], ALU.mult), ["ps%d" % bB, "DTt"], ["qkTm"])
                        self.dve(lambda e, bA=bA: e.tensor_tensor(DLt[:], psv(bA), DLt[:], ALU.mult), ["ps%d" % bA, "DLt"], ["DLt"])
                        self.dve(lambda e, us=us: e.tensor_tensor(Lp[0][:, us, :], DLt[:], B8(gt[:, 2, us]), ALU.mult), ["DLt", "g2"], ["L0"])
                        if self.dn_stage < 4.5:
                            continue
                        bD = self.nbt()
                        for u in range(8):
                            self.pe(lambda e, bD=bD, u=u, cp=cp: e.transpose(self.psb[bD][0:64, u * 64:(u + 1) * 64], Lp[0][:, cp * 8 + u, :], idb), ["L0", "identb"], ["ps%d" % bD])
                        self.dve(lambda e, bD=bD, us=us: e.tensor_copy(Up[0][:, us, :], psbv(bD)), ["ps%d" % bD], ["U0"])
                        self.dve(lambda e, us=us: e.tensor_tensor(Rp[0][:, us, :], idb.unsqueeze(1).to_broadcast([64, 8, 64]), Up[0][:, us, :], ALU.subtract), ["identb", "U0"], ["R0"])
                        if self.dn_stage < 4.75:
                            continue
                        self.dve(lambda e, us=us: e.tensor_tensor(vb[:, us, :], v_tm[:, us, :], B8(gt[:, 2, us]), ALU.mult), ["v_tm", "g2"], ["vb"])
                        self.dve(lambda e, us=us: e.tensor_tensor(kbe[:, us, :], k_tm[:, us, :], B8(gt[:, 7, us]), ALU.mult), ["k_tm", "g7"], ["kbe"])
                        self.dve(lambda e, us=us: e.tensor_tensor(ktl[:, us, :], k_tm[:, us, :], B8(gt[:, 6, us]), ALU.mult), ["k_tm", "g6"], ["ktl"])
                    if self.dn_stage < 5:
                        continue
                    cur = 0
                    for lev in range(5):
                        nxt = 1 - cur
                        for cp in range(NCP):
                            us = slice(cp * 8, (cp + 1) * 8)
                            b = self.nb()
                            for u in range(8):
                                self.pe(lambda e, b=b, u=u, cp=cp, cur=cur: e.matmul(self.ps[b][0:64, u * 64:(u + 1) * 64], Up[cur][:, cp * 8 + u, :], Lp[cur][:, cp * 8 + u, :], start=True, stop=True),
                                        ["U%d" % cur, "L%d" % cur], ["ps%d" % b])
                            self.act(lambda e, b=b, us=us, nxt=nxt: e.copy(Lp[nxt][:, us, :], psv(b)), ["ps%d" % b], ["L%d" % nxt])
                        if lev < 4:
                            for cp in range(NCP):
                                us = slice(cp * 8, (cp + 1) * 8)
                                b = self.nb()
                                for u in range(8):
                                    self.pe(lambda e, b=b, u=u, cp=cp, cur=cur: e.matmul(self.ps[b][0:64, u * 64:(u + 1) * 64], Lp[cur][:, cp * 8 + u, :], Up[cur][:, cp * 8 + u, :], start=True, stop=True),
                                            ["U%d" % cur, "L%d" % cur], ["ps%d" % b])
                                self.act(lambda e, b=b, us=us, nxt=nxt: e.copy(Up[nxt][:, us, :], psv(b)), ["ps%d" % b], ["U%d" % nxt])
                        for cp in range(NCP):
                            us = slice(cp * 8, (cp + 1) * 8)
                            b = self.nb()
                            for u in range(8):
                                self.pe(lambda e, b=b, u=u, cp=cp, cur=cur, nxt=nxt: e.matmul(self.ps[b][0:64, u * 64:(u + 1) * 64], Lp[nxt][:, cp * 8 + u, :], Rp[cur][:, cp * 8 + u, :], start=True, stop=True),
                                        ["L%d" % nxt, "R%d" % cur], ["ps%d" % b])
                            self.dve(lambda e, b=b, us=us, cur=cur, nxt=nxt: e.tensor_tensor(Rp[nxt][:, us, :], psv(b), Rp[cur][:, us, :], ALU.add), ["ps%d" % b, "R%d" % cur], ["R%d" % nxt])
                        cur = nxt
                    Rf_ = Rp[cur]; rkey = "R%d" % cur
                    if self.dn_stage < 6:
                        continue
                    for cp in range(NCP):
                        us = slice(cp * 8, (cp + 1) * 8)
                        b = self.nb()
                        for u in range(8):
                            self.pe(lambda e, b=b, u=u, cp=cp, Rf_=Rf_: e.matmul(self.ps[b][0:64, u * 64:(u + 1) * 64], Rf_[:, cp * 8 + u, :], vb[:, cp * 8 + u, :], start=True, stop=True), [rkey, "vb"], ["ps%d" % b])
                        self.act(lambda e, b=b, us=us: e.copy(u_sb[:, us, :], psv(b)), ["ps%d" % b], ["u_sb"])
                        b = self.nb()
                        for u in range(8):
                            self.pe(lambda e, b=b, u=u, cp=cp, Rf_=Rf_: e.matmul(self.ps[b][0:64, u * 64:(u + 1) * 64], kbe[:, cp * 8 + u, :], Rf_[:, cp * 8 + u, :], start=True, stop=True), [rkey, "kbe"], ["ps%d" % b])
                        self.dve(lambda e, b=b, us=us: e.tensor_copy(wT[:, us, :], psv(b)), ["ps%d" % b], ["wT"])
                    if self.dn_stage < 7:
                        continue
                    corder = range(NCH) if d == 0 else range(NCH - 1, -1, -1)
                    for n in corder:
                        hs = slice(n * 4, (n + 1) * 4)
                        par = n % 2
                        psh = lambda b: self.ps[b][0:64, 0:256].rearrange("p (h f) -> p h f", f=64)
                        bG = self.nb()
                        for h in range(4):
                            self.pe(lambda e, bG=bG, h=h, n=n: e.matmul(self.ps[bG][0:64, h * 64:(h + 1) * 64], wT[:, n * 4 + h, :], Sb[:, h, :], start=True, stop=True), ["wT", "Sb"], ["ps%d" % bG])
                        self.dve(lambda e, bG=bG, hs=hs, par=par: e.tensor_tensor(vnew[:, par], u_sb[:, hs, :], psh(bG), ALU.subtract), ["u_sb", "ps%d" % bG], ["vnew%d" % par])
                        bH = self.nb()
                        for h in range(4):
                            self.pe(lambda e, bH=bH, h=h, n=n: e.matmul(self.ps[bH][0:64, h * 64:(h + 1) * 64], qdec[:, n * 4 + h, :], Sb[:, h, :], start=True, stop=False), ["qdec", "Sb"], ["ps%d" % bH])
                            self.pe(lambda e, bH=bH, h=h, n=n, par=par: e.matmul(self.ps[bH][0:64, h * 64:(h + 1) * 64], qkTm[:, n * 4 + h, :], vnew[:, par, h, :], start=False, stop=True), ["qkTm", "vnew%d" % par], ["ps%d" % bH])
                        bI = self.nb()
                        for h in range(4):
                            self.pe(lambda e, bI=bI, h=h, n=n, par=par: e.matmul(self.ps[bI][0:64, h * 64:(h + 1) * 64], ktl[:, n * 4 + h, :], vnew[:, par, h, :], start=True, stop=True), ["ktl", "vnew%d" % par], ["ps%d" % bI])
                        self.dve(lambda e, hs=hs: e.tensor_tensor(Sf[:], Sf[:], B4(gt[:, 5, hs]), ALU.mult), ["Sf", "g5"], ["Sf"])
                        self.dve(lambda e, bI=bI: e.tensor_tensor(Sf[:], Sf[:], psh(bI), ALU.add), ["Sf", "ps%d" % bI], ["Sf"])
                        self.act(lambda e: e.copy(Sb[:], Sf[:]), ["Sf"], ["Sb"])
                        self.act(lambda e, bH=bH, par=par: e.copy(ost[:, par, :], self.ps[bH][0:64, 0:256]), ["ps%d" % bH], ["ost%d" % par])
                        self.dma("sp", self.OF[d, st + p0 + n * 64: st + p0 + (n + 1) * 64, :], ost[:, par, :], ["ost%d" % par], ["OF"])
            for p0 in (range(0, S, W) if self.dn_stage >= 8 else []):
                for jb in range(W // 128):
                    q0 = st + p0 + jb * 128
                    self.dma("sp", o2[:], self.OF[:, q0:q0 + 128, :].rearrange("a p f -> p a f"), ["OF"], ["o2"])
                    self.dma("sp", gte[:], self.ZAB[q0:q0 + 128, 16:272], [], ["gte"])
                    self.dve(lambda e: e.tensor_tensor(osum[:], o2[:, 0, :], o2[:, 1, :], ALU.add), ["o2"], ["osum"])
                    self.act(lambda e: e.activation(o2[:, 0, :], osum[:], AF.Square), ["osum"], ["o2"])
                    self.dve(lambda e: e.tensor_reduce(pss[:, 0:4], o2[:, 0, :].rearrange("p (h f) -> p h f", f=64), AX.X, ALU.add), ["o2"], ["pss"])
                    self.rsqrt(pss[:, 4:8], pss[:, 0:4], 1.0 / 64, ["pss"], "prs")
                    self.dve(lambda e: e.tensor_tensor(osum[:].rearrange("p (h f) -> p h f", f=64), osum[:].rearrange("p (h f) -> p h f", f=64), pss[:, 4:8].unsqueeze(2).to_broadcast([128, 4, 64]), ALU.mult), ["osum", "prs"], ["osum"])
                    self.dve(lambda e: e.tensor_tensor(osum[:].rearrange("p (h f) -> p h f", f=64), osum[:].rearrange("p (h f) -> p h f", f=64), gdn[:].unsqueeze(1).to_broadcast([128, 4, 64]), ALU.mult), ["osum", "gdn"], ["osum"])
                    self.act(lambda e: e.activation(gte[:], gte[:], AF.Silu), ["gte"], ["gte"])
                    self.dve(lambda e: e.tensor_tensor(ydb[:], osum[:], gte[:], ALU.mult), ["osum", "gte"], ["ydb"])
                    b = self.nbt()
                    for c in range(2):
                        self.pe(lambda e, b=b, c=c: e.transpose(self.psb[b][:, c * 128:(c + 1) * 128], ydb[:, c * 128:(c + 1) * 128], self.identb[:]), ["ydb", "identb"], ["ps%d" % b])
                    self.dve(lambda e, b=b, jb=jb: e.tensor_copy(ydT[:, :, jb * 128:(jb + 1) * 128], self.psb[b][:, 0:256].rearrange("p (c t) -> p c t", c=2)), ["ps%d" % b], ["ydT"])
                self.dma("sp", self.YT[256:512, st + p0: st + p0 + W].rearrange("(c p) t -> p c t", p=128), ydT[:], ["ydT"], ["YT"])
        self.phase_end()


_SKIP = set()


def kernel(**inputs):
    cfg = FULL
    b = Builder(cfg)
    b.skip = set(_SKIP)
    nc = b.build()
    consts = host_consts(cfg)
    xp = np.asarray(inputs["x_prompt"], dtype=np.float32)
    xs = np.asarray(inputs["x_sample"], dtype=np.float32)
    mp = np.asarray(inputs["mem_prompt"], dtype=np.float32)
    ms = np.asarray(inputs["mem_sample"], dtype=np.float32)
    w = {n: np.ascontiguousarray(np.asarray(inputs[n], dtype=np.float32)) for n in WNAMES}
    in_maps = []
    for c in range(NCORES):
        x = np.concatenate([xp[4 * c:4 * c + 4].reshape(-1, D), xs[c].reshape(-1, D)], axis=0)
        mem = np.concatenate([mp[4 * c:4 * c + 4].reshape(-1, D), ms[c].reshape(-1, D)], axis=0)
        m = {"x": np.ascontiguousarray(x), "mem": np.ascontiguousarray(mem)}
        m.update(w)
        m.update(consts)
        in_maps.append(m)
    res = run_bass_kernel_spmd(nc, in_maps, core_ids=list(range(NCORES)))
    yp = np.empty((32, 2048, D), np.float32)
    ys = np.empty((8, 8192, D), np.float32)
    for c in range(NCORES):
        y = np.asarray(res.results[c]["y"], dtype=np.float32)
        yp[4 * c:4 * c + 4] = y[0:8192].reshape(4, 2048, D)
        ys[c] = y[8192:16384]
    return (yp, ys)
```

```python
import math
from contextlib import ExitStack
import numpy as np
import ml_dtypes
import concourse.bass as bass
import concourse.mybir as mybir
from concourse.bass_utils import run_bass_kernel_spmd

F32 = mybir.dt.float32
BF16 = mybir.dt.bfloat16
AF = mybir.ActivationFunctionType
ALU = mybir.AluOpType
AX = mybir.AxisListType

D = 1024
DFF = 2816
NFB = DFF // 128
INW = 2064
EPS = 1e-6
NCORES = 8
MEMT = 256
ENGS = ("pe", "act", "dve", "pool", "sp")


class Op:
    __slots__ = ("eng", "fn", "deps", "sig", "cnt", "dma", "sem", "semval")

    def __init__(self, eng, fn, dma):
        self.eng = eng; self.fn = fn; self.dma = dma
        self.deps = []; self.sig = False; self.cnt = 0; self.sem = None; self.semval = 0


class Rec:
    NDMASEM = 14

    def __init__(self):
        self.ops = {e: [] for e in ENGS}
        self.lastw = {}
        self.readers = {}
        self.dma_uses = {}
        self.dma_rr = {}
        self.pending = {e: [] for e in ENGS}
        self.last_compute = {}
        self.n = 0
        self.seq = []
        self.sid = 0
        self.streams = None

    def add(self, eng, fn, reads=(), writes=(), dma=False):
        op = Op(eng, fn, dma)
        deps = {}
        for r in reads:
            w = self.lastw.get(r)
            if w is not None:
                deps[w] = "raw"
        for w_ in writes:
            lw = self.lastw.get(w_)
            if lw is not None and lw not in deps:
                deps[lw] = "waw"
            for rd in self.readers.get(w_, ()):
                if rd not in deps:
                    deps[rd] = "war"
        need = []
        for d, kind in deps.items():
            if d.dma or dma:
                need.append(d)
            elif d.eng == eng:
                if eng == "pe":
                    continue
                need.append(d)
            else:
                need.append(d)
        if dma:
            rk = (self.sid, eng)
            k = self.dma_rr.get(rk, 0) % self.NDMASEM + self.sid * self.NDMASEM
            self.dma_rr[rk] = self.dma_rr.get(rk, 0) + 1
            ent = self.dma_uses.setdefault((eng, k), [0, None])
            if ent[1] is not None:
                need.append(ent[1])
            ent[0] += 1
            op.sem = (eng, k); op.semval = 16 * ent[0]
            ent[1] = op
        if self.pending[eng]:
            need.extend(self.pending[eng]); self.pending[eng] = []
        for d in need:
            d.sig = True
        op.deps = need
        for r in reads:
            self.readers.setdefault(r, []).append(op)
        for w_ in writes:
            self.lastw[w_] = op
            self.readers[w_] = []
        self.seq.append(op)
        if not dma:
            self.last_compute[eng] = op
        self.n += 1
        return op

    def barrier(self):
        lasts = []
        for e in ENGS:
            if self.last_compute.get(e) is not None:
                lasts.append(self.last_compute[e])
        for ent in self.dma_uses.values():
            if ent[1] is not None:
                lasts.append(ent[1])
        for e in ENGS:
            self.pending[e] = [o for o in lasts if not (o.eng == e and not o.dma)]
        self.lastw = {}; self.readers = {}

    def fork(self, n=2):
        self.streams = []
        for i in range(n):
            self.streams.append({"seq": [], "lastw": {}, "readers": {}, "pending": {e: list(self.pending[e]) for e in ENGS}})
        self.main_seq = self.seq
        self.pending = {e: [] for e in ENGS}

    def use(self, i):
        st = self.streams[i]
        self.sid = i
        self.seq, self.lastw, self.readers, self.pending = st["seq"], st["lastw"], st["readers"], st["pending"]

    def join(self):
        lists = [s["seq"] for s in self.streams]
        idx = [0] * len(lists)
        merged = []
        total = sum(len(x) for x in lists)
        for _ in range(total):
            best, bf = None, None
            for i, lst in enumerate(lists):
                if idx[i] < len(lst):
                    f = idx[i] / len(lst)
                    if bf is None or f < bf:
                        best, bf = i, f
            merged.append(lists[best][idx[best]]); idx[best] += 1
        self.seq = self.main_seq
        self.seq.extend(merged)
        self.sid = 0
        self.lastw = {}; self.readers = {}
        self.pending = {e: [] for e in ENGS}
        for op in merged:
            if not op.dma:
                self.last_compute[op.eng] = op
        self.streams = None

    def setup_sems(self, nc, stack):
        self.esem = {e: stack.enter_context(nc.semaphore("s_" + e)) for e in ENGS}
        self.dsem = {}
        for q in ("sp", "pool"):
            for k in range(self.NDMASEM * 2):
                self.dsem[(q, k)] = stack.enter_context(nc.semaphore("d_%s%d" % (q, k)))
        self.cnts = {e: 0 for e in ENGS}
        self.waited = {e: {} for e in ENGS}

    def emit(self, nc, final=False):
        esem, dsem = self.esem, self.dsem
        for e in ENGS:
            if self.last_compute.get(e) is not None:
                self.last_compute[e].sig = True
        for op in self.seq:
            if op.sig and not op.dma:
                self.cnts[op.eng] += 1; op.cnt = self.cnts[op.eng]
        finals = [(dsem[k], 16 * ent[0]) for k, ent in self.dma_uses.items()] if final else []
        ops = {e: [] for e in ENGS}
        for op in self.seq:
            ops[op.eng].append(op)
        self.seq = []
        with nc.Block() as block:
            def run(engname, e):
                waited = self.waited[engname]
                for op in ops[engname]:
                    for d in op.deps:
                        if d.dma:
                            s, v = dsem[d.sem], d.semval
                        else:
                            s, v = esem[d.eng], d.cnt
                            assert v > 0
                        if waited.get(id(s), 0) < v:
                            e.wait_ge(s, v); waited[id(s)] = v
                    ins = op.fn(e)
                    if op.dma:
                        ins.then_inc(dsem[op.sem], 16)
                    elif op.sig:
                        ins.then_inc(esem[engname], 1)
                if engname == "sp":
                    for s, v in finals:
                        e.wait_ge(s, v)

            @block.tensor
            def _(e):
                run("pe", e)

            @block.scalar
            def _(e):
                run("act", e)

            @block.vector
            def _(e):
                run("dve", e)

            @block.gpsimd
            def _(e):
                run("pool", e)

            @block.sync
            def _(e):
                run("sp", e)


class Cfg:
    def __init__(self, seqs, depth=2):
        self.seqs = list(seqs)
        self.depth = depth
        self.ntok = sum(seqs)
        self.starts = [sum(seqs[:i]) for i in range(len(seqs))]
        self.tt = min(1024, min(seqs))
        self.nsub = self.tt // 128
        self.tg = min(512, self.tt)
        self.ntg = self.tt // self.tg


FULL = Cfg([2048, 2048, 2048, 2048, 8192])

WNAMES = ["ffn1_norm", "ffn1_w_gu", "ffn1_w_down", "mix_norm", "w_in", "four_w", "dn_conv", "dn_A_log",
          "dn_dt_bias", "dn_out_norm", "attn_q_norm", "attn_k_norm", "w_out", "mem_norm_x", "mem_norm_m",
          "mem_wq", "mem_wkv", "mem_wo", "ffn2_norm", "ffn2_w_gu", "ffn2_w_down", "final_norm"]
WSHAPES = {
    "ffn1_norm": (2, D), "ffn1_w_gu": (2, D, 2 * DFF), "ffn1_w_down": (2, DFF, D), "mix_norm": (2, D),
    "w_in": (2, D, INW), "four_w": (2, 4, 64, 64), "dn_conv": (2, 3, 768), "dn_A_log": (2, 2, 4),
    "dn_dt_bias": (2, 2, 4), "dn_out_norm": (2, 64), "attn_q_norm": (2, 64), "attn_k_norm": (2, 64),
    "w_out": (2, D, D), "mem_norm_x": (2, D), "mem_norm_m": (2, D), "mem_wq": (2, D, D),
    "mem_wkv": (2, D, 2 * D), "mem_wo": (2, D, D), "ffn2_norm": (2, D), "ffn2_w_gu": (2, D, 2 * DFF),
    "ffn2_w_down": (2, DFF, D), "final_norm": (D,),
}
NORM_IDS = {"ffn1_norm": 0, "mix_norm": 1, "mem_norm_x": 2, "mem_norm_m": 3, "ffn2_norm": 4}


def host_consts(cfg):
    c = {}
    c["c_identb"] = np.eye(128, dtype=np.float32).astype(ml_dtypes.bfloat16)
    c["c_identf"] = np.eye(128, dtype=np.float32)
    smax = max(cfg.seqs)
    nblk = smax // 128
    pos = np.arange(smax)
    row = (pos // 64).astype(np.float32); col = (pos % 64).astype(np.float32)
    inv = (1.0 / (10000.0 ** (np.arange(16, dtype=np.float32) * (2.0 / 32)))).astype(np.float32)
    ang = np.stack([row[:, None] * inv, col[:, None] * inv], axis=1).astype(np.float32)
    cos = np.cos(ang).astype(np.float32).reshape(nblk, 128, 2, 16).transpose(1, 0, 2, 3)
    sin = np.sin(ang).astype(np.float32).reshape(nblk, 128, 2, 16).transpose(1, 0, 2, 3)
    c["c_cos"] = np.ascontiguousarray(cos); c["c_sin"] = np.ascontiguousarray(sin)
    bf = ml_dtypes.bfloat16
    cc = np.arange(64)
    th = 2 * np.pi * np.outer(cc, cc) / 64.0
    c["c_fc"] = np.stack([np.cos(th) / 8.0, np.sin(th) / 8.0], axis=1).astype(np.float32).astype(bf)
    for S in sorted(set(cfg.seqs)):
        N1 = S // 128
        n1 = np.arange(N1); t1 = 2 * np.pi * np.outer(n1, n1) / N1
        sc = 1.0 / np.sqrt(S)
        f1 = np.concatenate([np.cos(t1), -np.sin(t1), np.sin(t1)], axis=1) * sc
        c["c_f1_%d" % S] = f1.astype(np.float32).astype(bf)
        n2 = np.arange(128)[:, None, None]; k1 = np.arange(N1)[None, :, None]; k2 = np.arange(128)[None, None, :]
        ang = 2 * np.pi * ((n2 * (k1 + N1 * k2)) % S) / S
        h = np.stack([np.cos(ang), -np.sin(ang)], axis=2)
        c["c_h_%d" % S] = h.astype(np.float32).astype(bf)
    p = np.arange(64)[:, None]; f = np.arange(64)[None, :]
    NEG = -1.0e4
    mk = np.stack([np.where(f >= p, 0.0, NEG), np.where(f < p, 0.0, NEG), np.where(f <= p, 0.0, NEG), np.where(f > p, 0.0, NEG)], axis=1)
    c["c_dnmask"] = mk.astype(np.float32)
    tri = np.stack([(p <= f).astype(np.float32), (p >= f).astype(np.float32), np.ones((64, 64), np.float32)], axis=1)
    c["c_tri"] = tri.astype(np.float32)
    return c


class Builder:
    def __init__(self, cfg, dump=(), stop_after=None):
        self.cfg = cfg
        self.dump = set(dump)
        self.stop_after = stop_after
        self.skip = set()
        self.dn_stage = 9
        self.overlap = True
        self.own_phase = True
        self.nc = bass.Bass("TRN2", target_bir_lowering=False)
        self.R = Rec()
        self.sb_off = 16512
        self.sb_mark = 16512

    def dram(self, name, shape, dt, kind=None):
        if kind is None:
            kind = "ExternalOutput" if name in self.dump else ("ExternalInput" if name in getattr(self, "ext_in", ()) else "Internal")
        return self.nc.dram_tensor(name, list(shape), dt, kind=kind).ap()

    def sb(self, name, shape, dt):
        self.tid = getattr(self, "tid", 0) + 1
        return self.pstack.enter_context(self.nc.sbuf_tensor("%s_%d" % (name, self.tid), list(shape), dt))

    def phase_begin(self):
        if not self.own_phase:
            return
        self.R.barrier()
        self.pstack = ExitStack()

    def phase_end(self, final=False):
        if not self.own_phase:
            return
        self.R.emit(self.nc, final=final)
        self.pstack.close()

    def pe(self, fn, r, w): return self.R.add("pe", fn, r, w)
    def act(self, fn, r, w): return self.R.add("act", fn, r, w)
    def dve(self, fn, r, w): return self.R.add("dve", fn, r, w)
    def pool(self, fn, r, w): return self.R.add("pool", fn, r, w)

    def dma(self, q, out, in_, r, w):
        return self.R.add(q, lambda e, o=out, i=in_: e.dma_start(out=o, in_=i), r, w, dma=True)

    def build(self):
        cfg = self.cfg; nc = self.nc
        NT = cfg.ntok
        self.inp = {}
        self.inp["x"] = self.dram("x", [NT, D], F32, "ExternalInput")
        self.inp["mem"] = self.dram("mem", [len(cfg.seqs) * MEMT, D], F32, "ExternalInput")
        for n in WNAMES:
            shp = WSHAPES[n]
            if getattr(self, "tiny_w", False) and int(np.prod(shp)) > 100000:
                shp = (2, 128, 128)
            self.inp[n] = self.dram(n, shp, F32, "ExternalInput")
        hc = host_consts(cfg)
        self.cin = {}
        for k, v in hc.items():
            self.cin[k] = self.dram(k, v.shape, BF16 if v.dtype == ml_dtypes.bfloat16 else F32, "ExternalInput")
        self.y = self.dram("y", [NT, D], F32, "ExternalOutput")
        self.XS = self.dram("XS", [NT, D], F32)
        self.YT = self.dram("YT", [D, NT], BF16)
        self.UF = self.dram("UF", [NT, 256], BF16)
        self.ZT = self.dram("ZT", [768, NT], BF16)
        self.ZAB = self.dram("ZAB", [NT, 272], F32)
        self.QT = self.dram("QT", [4, 128, NT], BF16)
        self.KT = self.dram("KT", [2, 128, NT], BF16)
        self.VV = self.dram("VV", [NT, 128], BF16)
        self.OF = self.dram("OF", [2, NT, 256], F32)

        with ExitStack() as stack:
            stack.enter_context(nc.allow_non_contiguous_dma(reason="strided weight/activation tiles"))
            self.ps = [stack.enter_context(nc.psum_tensor("ps%d" % b, [128, 512], F32)) for b in range(8)]
            self.psb = {b: self.ps[b][:].bitcast(BF16) for b in range(8)}
            self.R.setup_sems(nc, stack)
            self.pstack = stack
            self.setup_consts()
            self.R.emit(nc)
            self.program()
            self.pstack = stack
            self.dve(lambda e: e.memset(self.epsc[:, 0:1], EPS), [], ["epsc"])
            self.R.emit(nc, final=True)
        return nc

    def setup_consts(self):
        cfg = self.cfg
        self.identb = self.sb("identb", [128, 128], BF16)
        self.identf = self.sb("identf", [128, 128], F32)
        self.onesb = self.sb("onesb", [128, 128], BF16)
        self.gains = self.sb("gains", [128, 88], F32)
        self.gq = self.sb("gq", [128, 2, 64], F32)
        self.gk = self.sb("gk", [128, 2, 64], F32)
        gstage = self.sb("gstage", [88, 128], F32)
        self.dma("sp", self.identb[:], self.cin["c_identb"], [], ["identb"])
        self.dma("sp", self.identf[:], self.cin["c_identf"], [], ["identf"])
        self.dve(lambda e: e.memset(self.onesb[:], 1.0), [], ["onesb"])
        self.epsc = self.sb("epsc", [128, 8], F32)
        self.dve(lambda e: e.memset(self.epsc[:], EPS), [], ["epsc"])
        for nm, i in NORM_IDS.items():
            self.dma("sp", gstage[i * 16:(i + 1) * 16, :],
                     self.inp[nm].rearrange("l (c p) -> (l c) p", p=128), [], ["gstage"])
        self.dma("sp", gstage[80:88, :], self.inp["final_norm"].rearrange("(c p) -> c p", p=128), [], ["gstage"])
        self.pe(lambda e: e.transpose(self.ps[0][:, 0:88], gstage[0:88, :], self.identf[0:88, 0:88]),
                ["gstage", "identf"], ["ps0"])
        self.dve(lambda e: e.tensor_copy(self.gains[:], self.ps[0][:, 0:88]), ["ps0"], ["gains"])
        self.dma("sp", self.gq[:].rearrange("p l d -> p (l d)"),
                 self.inp["attn_q_norm"].rearrange("l d -> (l d)").partition_broadcast(128), [], ["gq"])
        self.dma("sp", self.gk[:].rearrange("p l d -> p (l d)"),
                 self.inp["attn_k_norm"].rearrange("l d -> (l d)").partition_broadcast(128), [], ["gk"])
        self.dve(lambda e: e.tensor_scalar(self.gq[:], self.gq[:], 0.125, None, ALU.mult), ["gq"], ["gq"])

    def rsqrt(self, out, in_, scale, rkeys, wkey):
        self.act(lambda e: e.activation(out, in_, AF.Sqrt, bias=self.epsc[:out.shape[0], 0:1], scale=scale), list(rkeys) + ["epsc"], [wkey + "_s"])
        self.dve(lambda e: e.reciprocal(out, out), [wkey + "_s"], [wkey])

    def gain(self, name, l, c):
        i = NORM_IDS[name] * 16 + l * 8 + c
        return self.gains[:, i:i + 1]

    def program(self):
        cfg = self.cfg
        L = cfg.depth
        sa = self.stop_after
        if sa == "consts":
            return
        if sa == "dnonly":
            self.m_deltanet(0)
            return
        self.t_phase(0)
        if sa in ("T0", "load", "ffn", "n1", "n2"):
            return
        for l in range(L):
            self.m_phase(l)
            if sa == "M%d" % l:
                return
            self.t_phase(l + 1)
            if sa == "T%d" % (l + 1):
                return

    def m_phase(self, l):
        if self.overlap and "gqa" not in self.skip and "dn" not in self.skip:
            self.phase_begin()
            self.R.fork(2)
            self.own_phase = False
            self.R.use(0); self.m_gqa(l)
            self.R.use(1); self.m_deltanet(l)
            self.own_phase = True
            self.R.join()
            self.phase_end()
            if "four" not in self.skip:
                self.m_fourier(l)
            return
        if "gqa" not in self.skip:
            self.m_gqa(l)
        if "four" not in self.skip:
            self.m_fourier(l)
        if "dn" not in self.skip:
            self.m_deltanet(l)
        else:
            self.phase_begin()
            zz = self.sb("zz", [128, 2, self.cfg.ntok], BF16)
            self.dve(lambda e: e.memset(zz[:], 0.0), [], ["zz"])
            self.dma("sp", self.YT[256:512, :].rearrange("(c p) t -> p c t", p=128), zz[:], ["zz"], ["YT"])
            self.phase_end()

    def wplan_tile(self, ph, first_of_seq):
        L = self.cfg.depth
        blocks = []
        lp, ln = ph - 1, ph
        if ph >= 1:
            if first_of_seq:
                for j in range(8):
                    blocks.append(("mem_wkv", lp, "col", j))
                for k in range(8):
                    blocks.append(("mem_wkv", lp, "row", k, [(0, 1024, 1024)]))
            for k in range(8):
                blocks.append(("w_out", lp, "row", k, [(0, 0, 1024)]))
            for j in range(8):
                blocks.append(("mem_wq", lp, "col", j))
            for k in range(8):
                blocks.append(("mem_wo", lp, "row", k, [(0, 0, 1024)]))
            blocks += self.ffn_blocks("ffn2", lp)
        if ph < L:
            blocks += self.ffn_blocks("ffn1", ln)
            for k in range(8):
                blocks.append(("w_in", ln, "row", k, [(0, 0, 256), (256, 1024, 272)]))
            for k in range(8):
                blocks.append(("w_in", ln, "row", k, [(0, 1296, 768)]))
            for j in range(2, 8):
                blocks.append(("w_in", ln, "col", j))
        return blocks

    def ffn_blocks(self, pre, l):
        out = []
        for half in range(2):
            for jj in range(11 * half, 11 * half + 11):
                out.append((pre + "_w_gu", l, "col", jj))
                out.append((pre + "_w_gu", l, "col", NFB + jj))
            for k in range(11 * half, 11 * half + 11):
                out.append((pre + "_w_down", l, "row", k, [(0, 0, 1024)]))
        return out

    def ws_init(self, plan):
        self.ws_plan = plan
        self.ws_next_dma = 0
        self.ws_next_acq = 0
        self.ws_released = 0
        self.ws_fill()

    def ws_fill(self):
        while self.ws_next_dma < len(self.ws_plan) and self.ws_next_dma < self.ws_released + self.NS:
            b = self.ws_plan[self.ws_next_dma]
            s = self.ws_next_dma % self.NS
            slot = self.wslots[s]
            w = self.inp[b[0]]
            if b[2] == "col":
                j = b[3]
                src = w[b[1], :, j * 128:(j + 1) * 128].rearrange("(kc p) c -> p kc c", p=128)
                self.dma("pool", slot[:].rearrange("p (kc c) -> p kc c", kc=8), src, [], ["ws%d" % s])
            else:
                k = b[3]
                for (d0, c0, n) in b[4]:
                    self.dma("pool", slot[:, d0:d0 + n], w[b[1], k * 128:(k + 1) * 128, c0:c0 + n], [], ["ws%d" % s])
            self.ws_next_dma += 1

    def ws_acquire(self, name, kind):
        i = self.ws_next_acq
        b = self.ws_plan[i]
        assert b[0] == name and b[2] == kind, (b, name, kind)
        assert i < self.ws_next_dma, "weight block not yet scheduled (ring too small)"
        self.ws_next_acq += 1
        s = i % self.NS
        return s, self.wslots[s], "ws%d" % s

    def ws_release(self, n=1):
        self.ws_released += n
        self.ws_fill()

    def t_phase(self, ph):
        cfg = self.cfg
        L = cfg.depth
        self.phase_begin()
        TT, NSUB, TG, NTG = cfg.tt, cfg.nsub, cfg.tg, cfg.ntg
        self.NS = 16
        self.wslots = [self.sb("ws%d" % s, [128, 1024], BF16) for s in range(self.NS)]
        self.xt = self.sb("xt", [128, NSUB, D], F32)
        self.nT = self.sb("nT", [128, 8, TT], BF16)
        self.ov = self.sb("ov", [128, 24 * TT], BF16)
        self.hT = self.ov[:, 0:11 * TT].rearrange("p (c t) -> p c t", c=11)
        self.qT = self.ov[:, 0:8 * TT].rearrange("p (c t) -> p c t", c=8)
        self.pT = self.ov[:, 8 * TT:16 * TT].rearrange("p (c t) -> p c t", c=8)
        self.oT = self.ov[:, 16 * TT:24 * TT].rearrange("p (c t) -> p c t", c=8)
        self.yTt = self.ov[:, 8 * TT:16 * TT].rearrange("p (c t) -> p c t", c=8)
        self.memkT = self.sb("memkT", [128, 8, MEMT], BF16)
        self.memV = self.sb("memV", [128, 2, D], BF16)
        self.memnT = self.sb("memnT", [128, 8, MEMT], BF16)
        self.rbc = self.sb("rbc", [128, 2, 512], F32)
        self.cos = self.sb("cos", [128, NSUB, 2, 16], F32)
        self.sin = self.sb("sin", [128, NSUB, 2, 16], F32)
        if ph == L:
            self.gfin = self.sb("gfin", [128, D], F32)
            self.dma("sp", self.gfin[:], self.inp["final_norm"].partition_broadcast(128), [], ["gfin"])
        self.xn = self.sb("xn", [128, 2, D], BF16)
        self.junk = self.sb("junk", [128, D], BF16)
        self.ss = self.sb("ss", [128, 8], F32)
        self.rstd = self.sb("rstd", [128, 8], F32)
        self.sg = self.sb("sg", [128, 2, 512], F32)
        self.zt = self.sb("zt", [128, 1296], F32)
        self.zq = self.sb("zq", [128, 768], F32)
        self.zqb = self.sb("zqb", [128, 768], BF16)
        self.zsq = self.sb("zsq", [128, 640], F32)
        self.zn = self.sb("zn", [128, 640], F32)
        self.zr = self.sb("zr", [128, 640], F32)
        self.rt = self.sb("rt", [128, 4, 320], F32)
        self.zss = self.sb("zss", [128, 32], F32)
        self.qkst = self.sb("qkst", [128, 6, TT], BF16)
        self.vst = self.sb("vst", [128, 128], BF16)
        self.ufst = self.sb("ufst", [128, 256], BF16)
        self.zfm = self.sb("zfm", [128, 2, 512], BF16)
        tiles = []
        for si, (st, S) in enumerate(zip(cfg.starts, cfg.seqs)):
            for t0 in range(0, S, TT):
                tiles.append((si, st, S, t0))
        plan = []
        for (si, st, S, t0) in tiles:
            plan += self.wplan_tile(ph, t0 == 0)
        self.ws_init(plan)
        xsrc = self.inp["x"] if ph == 0 else self.XS
        xdst = self.y if ph == L else self.XS
        for (si, st, S, t0) in tiles:
            g0 = st + t0
            self.dma("sp", self.xt[:], xsrc[g0:g0 + TT, :].rearrange("(j p) d -> p j d", p=128),
                     [], ["x%d" % j for j in range(NSUB)])
            if ph >= 1:
                if t0 == 0:
                    self.mem_kv(ph - 1, si)
                self.wout(ph - 1, g0)
                self.mca(ph - 1)
                self.ffn("ffn2", ph - 1)
            if ph == L:
                self.final_norm()
            if ph < L:
                b0 = t0 // 128
                self.dma("sp", self.cos[:], self.cin["c_cos"][:, b0:b0 + NSUB], [], ["cos"])
                self.dma("sp", self.sin[:], self.cin["c_sin"][:, b0:b0 + NSUB], [], ["sin"])
                if self.stop_after != "load":
                    self.ffn("ffn1", ph)
                if self.stop_after not in ("load", "ffn", "n1", "n2"):
                    self.win(ph, g0, t0)
            self.dma("sp", xdst[g0:g0 + TT, :].rearrange("(j p) d -> p j d", p=128), self.xt[:],
                     ["x%d" % j for j in range(NSUB)], ["xdst"])
        self.phase_end()

    def norm_to_nT(self, name, l):
        cfg = self.cfg
        NSUB, TT = cfg.nsub, cfg.tt
        for j in range(NSUB):
            self.act(lambda e, j=j: e.activation(self.junk[:], self.xt[:, j, :], AF.Square,
                                                 accum_out=self.ss[:, j:j + 1]),
                     ["x%d" % j], ["junk", "ss%d" % j])
        self.rsqrt(self.rstd[:, 0:NSUB], self.ss[:, 0:NSUB], 1.0 / D, ["ss%d" % j for j in range(NSUB)], "rstd")
        if self.stop_after == "n1":
            return
        for j in range(NSUB):
            b = j % 2
            self.dve(lambda e, j=j, b=b: e.tensor_scalar(self.xn[:, b, :], self.xt[:, j, :], self.rstd[:, j:j + 1], None, ALU.mult),
                     ["x%d" % j, "rstd"], ["xn%d" % b])
            pb = 6 + (j % 2)
            psv = self.psb[pb]
            for c in range(8):
                self.pe(lambda e, c=c, b=b, psv=psv: e.transpose(psv[:, c * 128:(c + 1) * 128], self.xn[:, b, c * 128:(c + 1) * 128], self.identb[:]),
                        ["xn%d" % b, "identb"], ["ps%d" % pb])
            for c in range(8):
                self.dve(lambda e, c=c, j=j, psv=psv: e.tensor_scalar(self.nT[:, c, j * 128:(j + 1) * 128], psv[:, c * 128:(c + 1) * 128],
                                                                      self.gain(name, l, c), None, ALU.mult),
                         ["ps%d" % pb, "gains"], ["nT%d" % c])

    def ffn(self, pre, l):
        cfg = self.cfg
        NSUB, TT, TG, NTG = cfg.nsub, cfg.tt, cfg.tg, cfg.ntg
        self.norm_to_nT(pre + "_norm", l)
        nTr = ["nT%d" % c for c in range(8)]
        bank = 0
        if self.stop_after in ("n1", "n2"):
            return
        for half in range(2):
            for jl in range(11):
                sg_, slg, rg = self.ws_acquire(pre + "_w_gu", "col")
                su_, slu, ru = self.ws_acquire(pre + "_w_gu", "col")
                wg = slg[:].rearrange("p (kc c) -> p kc c", kc=8)
                wu = slu[:].rearrange("p (kc c) -> p kc c", kc=8)
                for tg in range(NTG):
                    bg, bu = bank % 6, (bank + 1) % 6
                    bank += 2
                    tsl = slice(tg * TG, (tg + 1) * TG)
                    for kc in range(8):
                        self.pe(lambda e, kc=kc, wg=wg, bg=bg, tsl=tsl: e.matmul(self.ps[bg][:, 0:TG], wg[:, kc, :], self.nT[:, kc, tsl], start=(kc == 0), stop=(kc == 7)),
                                [rg, "nT%d" % kc], ["ps%d" % bg])
                    for kc in range(8):
                        self.pe(lambda e, kc=kc, wu=wu, bu=bu, tsl=tsl: e.matmul(self.ps[bu][:, 0:TG], wu[:, kc, :], self.nT[:, kc, tsl], start=(kc == 0), stop=(kc == 7)),
                                [ru, "nT%d" % kc], ["ps%d" % bu])
                    sb_ = tg % 2
                    self.act(lambda e, bg=bg, sb_=sb_: e.activation(self.sg[:, sb_, 0:TG], self.ps[bg][:, 0:TG], AF.Silu),
                             ["ps%d" % bg], ["sg%d" % sb_])
                    self.dve(lambda e, bu=bu, sb_=sb_, jl=jl, tsl=tsl: e.tensor_tensor(self.hT[:, jl, tsl], self.sg[:, sb_, 0:TG], self.ps[bu][:, 0:TG], ALU.mult),
                             ["sg%d" % sb_, "ps%d" % bu], ["h%d" % jl])
                self.ws_release(2)
            slots = [self.ws_acquire(pre + "_w_down", "row") for _ in range(11)]
            for j in range(NSUB):
                for ch in range(2):
                    b = bank % 6; bank += 1
                    for k in range(11):
                        _, sl, rk = slots[k]
                        self.pe(lambda e, k=k, sl=sl, b=b, j=j, ch=ch: e.matmul(self.ps[b][:, :], self.hT[:, k, j * 128:(j + 1) * 128], sl[:, ch * 512:(ch + 1) * 512], start=(k == 0), stop=(k == 10)),
                                [rk, "h%d" % k], ["ps%d" % b])
                    self.dve(lambda e, b=b, j=j, ch=ch: e.scalar_tensor_tensor(self.xt[:, j, ch * 512:(ch + 1) * 512], self.ps[b][:, :], 0.5, self.xt[:, j, ch * 512:(ch + 1) * 512], ALU.mult, ALU.add),
                             ["ps%d" % b, "x%d" % j], ["x%d" % j])
            self.ws_release(11)

    def win(self, l, g0, t0):
        cfg = self.cfg
        NSUB, TT, TG, NTG = cfg.nsub, cfg.tt, cfg.tg, cfg.ntg
        self.norm_to_nT("mix_norm", l)
        slots = [self.ws_acquire("w_in", "row") for _ in range(8)]
        for j in range(NSUB):
            for (c0, n, zo) in ((0, 256, 0), (256, 272, 256)):
                b = (j * 2 + (c0 > 0)) % 6
                for kc in range(8):
                    _, sl, rk = slots[kc]
                    self.pe(lambda e, kc=kc, sl=sl, b=b, j=j, c0=c0, n=n: e.matmul(self.ps[b][:, 0:n], self.nT[:, kc, j * 128:(j + 1) * 128], sl[:, c0:c0 + n], start=(kc == 0), stop=(kc == 7)),
                            [rk, "nT%d" % kc], ["ps%d" % b])
                if c0 == 0:
                    self.act(lambda e, b=b: e.copy(self.ufst[:], self.ps[b][:, 0:256]), ["ps%d" % b], ["ufst"])
                    self.dma("sp", self.UF[g0 + j * 128:g0 + (j + 1) * 128, :], self.ufst[:], ["ufst"], ["UF"])
                else:
                    self.dve(lambda e, b=b: e.tensor_copy(self.zt[:, 0:272], self.ps[b][:, 0:272]), ["ps%d" % b], ["zab"])
                    self.dma("sp", self.ZAB[g0 + j * 128:g0 + (j + 1) * 128, :], self.zt[:, 0:272], ["zab"], ["ZAB"])
        self.ws_release(8)
        slots = [self.ws_acquire("w_in", "row") for _ in range(8)]
        for j in range(NSUB):
            for (c0, n) in ((0, 512), (512, 256)):
                b = (j * 2 + (c0 > 0)) % 6
                for kc in range(8):
                    _, sl, rk = slots[kc]
                    self.pe(lambda e, kc=kc, sl=sl, b=b, j=j, c0=c0, n=n: e.matmul(self.ps[b][:, 0:n], self.nT[:, kc, j * 128:(j + 1) * 128], sl[:, c0:c0 + n], start=(kc == 0), stop=(kc == 7)),
                            [rk, "nT%d" % kc], ["ps%d" % b])
                if c0 == 0:
                    self.act(lambda e, b=b: e.copy(self.zq[:, 0:512], self.ps[b][:, 0:512]), ["ps%d" % b], ["zq"])
                else:
                    self.dve(lambda e, b=b: e.tensor_copy(self.zq[:, 512:768], self.ps[b][:, 0:256]), ["ps%d" % b], ["zk"])
            self.qk_post(l, j, g0, t0)
        self.ws_release(8)
        for jb in range(6):
            _, sl, rk = self.ws_acquire("w_in", "col")
            wv = sl[:].rearrange("p (kc c) -> p kc c", kc=8)
            for tg in range(NTG):
                b = (jb * NTG + tg) % 6
                tsl = slice(tg * TG, (tg + 1) * TG)
                for kc in range(8):
                    self.pe(lambda e, kc=kc, wv=wv, b=b, tsl=tsl: e.matmul(self.ps[b][:, 0:TG], wv[:, kc, :], self.nT[:, kc, tsl], start=(kc == 0), stop=(kc == 7)),
                            [rk, "nT%d" % kc], ["ps%d" % b])
                s2 = (jb * NTG + tg) % 2
                self.act(lambda e, b=b, s2=s2: e.copy(self.zfm[:, s2, 0:TG], self.ps[b][:, 0:TG]), ["ps%d" % b], ["zfm%d" % s2])
                self.dma("sp", self.ZT[jb * 128:(jb + 1) * 128, g0 + tg * TG:g0 + (tg + 1) * TG], self.zfm[:, s2, 0:TG], ["zfm%d" % s2], ["ZT"])
            self.ws_release(1)
        for pr in range(4):
            self.dma("sp", self.QT[pr, :, g0:g0 + TT], self.qkst[:, pr, :], ["qkst"], ["QT"])
        for g in range(2):
            self.dma("sp", self.KT[g, :, g0:g0 + TT], self.qkst[:, 4 + g, :], ["qkst"], ["KT"])

    def qk_post(self, l, j, g0, t0):
        blk = (t0 // 128) + j
        zq, zsq, zn, zr, zqb, tt_ = self.zq, self.zsq, self.zn, self.zr, self.zqb, self.rt
        hv = lambda ap: ap.rearrange("p (h d) -> p h d", d=64)
        self.dve(lambda e: e.tensor_tensor(zsq[:, 0:640], zq[:, 0:640], zq[:, 0:640], ALU.mult), ["zq", "zk"], ["zsq"])
        self.dve(lambda e: e.tensor_reduce(self.zss[:, 0:10], hv(zsq[:, 0:640]), AX.X, ALU.add), ["zsq"], ["zss"])
        self.rsqrt(self.zss[:, 16:26], self.zss[:, 0:10], 1.0 / 64, ["zss"], "zrs")
        rs = self.zss[:, 16:26]
        self.dve(lambda e: e.tensor_tensor(hv(zn[:, 0:640]), hv(zq[:, 0:640]), rs.unsqueeze(2).to_broadcast([128, 10, 64]), ALU.mult),
                 ["zq", "zk", "zrs"], ["zn"])
        self.dve(lambda e: e.tensor_tensor(hv(zn[:, 0:512]), hv(zn[:, 0:512]), self.gq[:, l:l + 1, :].to_broadcast([128, 8, 64]), ALU.mult),
                 ["zn", "gq"], ["zn"])
        self.dve(lambda e: e.tensor_tensor(hv(zn[:, 512:640]), hv(zn[:, 512:640]), self.gk[:, l:l + 1, :].to_broadcast([128, 2, 64]), ALU.mult),
                 ["zn", "gk"], ["zn"])
        xv = zn[:, 0:640].rearrange("p (h a t f) -> p h a t f", h=10, a=2, t=2)
        ov = zr[:, 0:640].rearrange("p (h a t f) -> p h a t f", h=10, a=2, t=2)
        x1, x2 = xv[:, :, :, 0, :], xv[:, :, :, 1, :]
        cb = self.cos[:, j, :, :].unsqueeze(1).to_broadcast([128, 10, 2, 16])
        sb_ = self.sin[:, j, :, :].unsqueeze(1).to_broadcast([128, 10, 2, 16])
        t = [tt_[:, i, :].rearrange("p (h a f) -> p h a f", h=10, a=2) for i in range(4)]
        self.dve(lambda e: e.tensor_tensor(t[0], x1, cb, ALU.mult), ["zn", "cos"], ["rt0"])
        self.dve(lambda e: e.tensor_tensor(t[1], x2, sb_, ALU.mult), ["zn", "sin"], ["rt1"])
        self.dve(lambda e: e.tensor_tensor(t[2], x2, cb, ALU.mult), ["zn", "cos"], ["rt2"])
        self.dve(lambda e: e.tensor_tensor(t[3], x1, sb_, ALU.mult), ["zn", "sin"], ["rt3"])
        self.dve(lambda e: e.tensor_tensor(ov[:, :, :, 0, :], t[0], t[1], ALU.subtract), ["rt0", "rt1"], ["zr"])
        self.dve(lambda e: e.tensor_tensor(ov[:, :, :, 1, :], t[2], t[3], ALU.add), ["rt2", "rt3"], ["zr"])
        self.act(lambda e: e.copy(zqb[:, 0:512], zr[:, 0:512]), ["zr"], ["zqb"])
        self.act(lambda e: e.copy(zqb[:, 512:768].rearrange("p (g u d) -> p g u d", g=2, u=2),
                                  zr[:, 512:640].rearrange("p (g d) -> p g d", g=2).unsqueeze(2).to_broadcast([128, 2, 2, 64])),
                 ["zr"], ["zqb"])
        self.act(lambda e: e.copy(self.vst[:], zq[:, 640:768]), ["zk"], ["vst"])
        self.dma("sp", self.VV[g0 + j * 128:g0 + (j + 1) * 128, :], self.vst[:], ["vst"], ["VV"])
        pb = 6 + (j % 2)
        psv = self.psb[pb]
        for i in range(6):
            self.pe(lambda e, i=i, psv=psv: e.transpose(psv[:, i * 128:(i + 1) * 128], zqb[:, i * 128:(i + 1) * 128], self.identb[:]),
                    ["zqb", "identb"], ["ps%d" % pb])
        self.dve(lambda e, psv=psv: e.tensor_copy(self.qkst[:, :, j * 128:(j + 1) * 128], psv[:, 0:768].rearrange("p (i t) -> p i t", i=6)),
                 ["ps%d" % pb], ["qkst"])

    def mem_kv(self, l, si):
        mt = self.zt
        mnT = self.qkst
        mnTv = self.memnT
        for j in range(2):
            self.dma("sp", mt[:, 0:D], self.inp["mem"][si * MEMT + j * 128: si * MEMT + (j + 1) * 128, :], [], ["zab"])
            self.act(lambda e: e.activation(self.junk[:], mt[:, 0:D], AF.Square, accum_out=self.ss[:, 0:1]), ["zab"], ["junk", "ss0"])
            self.rsqrt(self.rstd[:, 0:1], self.ss[:, 0:1], 1.0 / D, ["ss0"], "rstd")
            self.dve(lambda e: e.tensor_scalar(self.xn[:, 0, :], mt[:, 0:D], self.rstd[:, 0:1], None, ALU.mult), ["zab", "rstd"], ["xn0"])
            pb = 6 + j
            psv = self.psb[pb]
            for c in range(8):
                self.pe(lambda e, c=c, psv=psv: e.transpose(psv[:, c * 128:(c + 1) * 128], self.xn[:, 0, c * 128:(c + 1) * 128], self.identb[:]),
                        ["xn0", "identb"], ["ps%d" % pb])
            for c in range(8):
                self.dve(lambda e, c=c, j=j, psv=psv: e.tensor_scalar(mnTv[:, c, j * 128:(j + 1) * 128], psv[:, c * 128:(c + 1) * 128],
                                                                      self.gain("mem_norm_m", l, c), None, ALU.mult),
                         ["ps%d" % pb, "gains"], ["qkst"])
        for jb in range(8):
            _, sl, rk = self.ws_acquire("mem_wkv", "col")
            wv = sl[:].rearrange("p (kc c) -> p kc c", kc=8)
            b = jb % 6
            for kc in range(8):
                self.pe(lambda e, kc=kc, wv=wv, b=b: e.matmul(self.ps[b][:, 0:MEMT], wv[:, kc, :], mnTv[:, kc, :], start=(kc == 0), stop=(kc == 7)),
                        [rk, "qkst"], ["ps%d" % b])
            self.act(lambda e, b=b, jb=jb: e.copy(self.memkT[:, jb, :], self.ps[b][:, 0:MEMT]), ["ps%d" % b], ["memkT"])
            self.ws_release(1)
        slots = [self.ws_acquire("mem_wkv", "row") for _ in range(8)]
        for mc in range(2):
            for ch in range(2):
                b = (mc * 2 + ch) % 6
                for kc in range(8):
                    _, sl, rk = slots[kc]
                    self.pe(lambda e, kc=kc, sl=sl, b=b, mc=mc, ch=ch: e.matmul(self.ps[b][:, :], mnTv[:, kc, mc * 128:(mc + 1) * 128], sl[:, ch * 512:(ch + 1) * 512], start=(kc == 0), stop=(kc == 7)),
                            [rk, "qkst"], ["ps%d" % b])
                self.act(lambda e, b=b, mc=mc, ch=ch: e.copy(self.memV[:, mc, ch * 512:(ch + 1) * 512], self.ps[b][:, :]), ["ps%d" % b], ["memV"])
        self.ws_release(8)

    def tm_linear_acc(self, name, aT, akey, nk, scale):
        cfg = self.cfg
        slots = [self.ws_acquire(name, "row") for _ in range(nk)]
        for j in range(cfg.nsub):
            for ch in range(2):
                b = (j * 2 + ch) % 6
                for k in range(nk):
                    _, sl, rk = slots[k]
                    self.pe(lambda e, k=k, sl=sl, b=b, j=j, ch=ch: e.matmul(self.ps[b][:, :], aT[:, k, j * 128:(j + 1) * 128], sl[:, ch * 512:(ch + 1) * 512], start=(k == 0), stop=(k == nk - 1)),
                            [rk, akey], ["ps%d" % b])
                self.dve(lambda e, b=b, j=j, ch=ch: e.scalar_tensor_tensor(self.xt[:, j, ch * 512:(ch + 1) * 512], self.ps[b][:, :], scale, self.xt[:, j, ch * 512:(ch + 1) * 512], ALU.mult, ALU.add),
                         ["ps%d" % b, "x%d" % j], ["x%d" % j])
        self.ws_release(nk)

    def wout(self, l, g0):
        TT = self.cfg.tt
        self.dma("sp", self.yTt[:], self.YT[:, g0:g0 + TT].rearrange("(c p) t -> p c t", p=128), [], ["yTt"])
        self.tm_linear_acc("w_out", self.yTt, "yTt", 8, 1.0)

    def mca(self, l):
        cfg = self.cfg
        NSUB, TT, TG, NTG = cfg.nsub, cfg.tt, cfg.tg, cfg.ntg
        self.norm_to_nT("mem_norm_x", l)
        for jb in range(8):
            _, sl, rk = self.ws_acquire("mem_wq", "col")
            wv = sl[:].rearrange("p (kc c) -> p kc c", kc=8)
            for tg in range(NTG):
                b = (jb * NTG + tg) % 6
                tsl = slice(tg * TG, (tg + 1) * TG)
                for kc in range(8):
                    self.pe(lambda e, kc=kc, wv=wv, b=b, tsl=tsl: e.matmul(self.ps[b][:, 0:TG], wv[:, kc, :], self.nT[:, kc, tsl], start=(kc == 0), stop=(kc == 7)),
                            [rk, "nT%d" % kc], ["ps%d" % b])
                self.act(lambda e, b=b, jb=jb, tsl=tsl: e.copy(self.qT[:, jb, tsl], self.ps[b][:, 0:TG]), ["ps%d" % b], ["qT%d" % jb])
            self.ws_release(1)
        bank = 0
        for tg in range(NTG):
            tsl = slice(tg * TG, (tg + 1) * TG)
            for h in range(4):
                for mc in range(2):
                    b = bank % 6; bank += 1
                    for dc in range(2):
                        self.pe(lambda e, b=b, h=h, mc=mc, dc=dc, tsl=tsl: e.matmul(self.ps[b][:, 0:TG], self.memkT[:, 2 * h + dc, mc * 128:(mc + 1) * 128], self.qT[:, 2 * h + dc, tsl], start=(dc == 0), stop=(dc == 1)),
                                ["memkT", "qT%d" % (2 * h + dc)], ["ps%d" % b])
                    self.act(lambda e, b=b, h=h, mc=mc, tsl=tsl: e.activation(self.pT[:, 2 * h + mc, tsl], self.ps[b][:, 0:TG], AF.Exp, scale=1.0 / 16.0),
                             ["ps%d" % b], ["pT%d" % (2 * h + mc)])
                b = bank % 6; bank += 1
                for mc in range(2):
                    self.pe(lambda e, b=b, h=h, mc=mc, tsl=tsl: e.matmul(self.ps[b][:, 0:TG], self.onesb[:], self.pT[:, 2 * h + mc, tsl], start=(mc == 0), stop=(mc == 1)),
                            ["onesb", "pT%d" % (2 * h + mc)], ["ps%d" % b])
                rb = h % 2
                self.dve(lambda e, b=b, rb=rb: e.reciprocal(self.rbc[:, rb, 0:TG], self.ps[b][:, 0:TG]), ["ps%d" % b], ["rbc%d" % rb])
                for dc in range(2):
                    b = bank % 6; bank += 1
                    for mc in range(2):
                        self.pe(lambda e, b=b, h=h, mc=mc, dc=dc, tsl=tsl: e.matmul(self.ps[b][:, 0:TG], self.memV[:, mc, (2 * h + dc) * 128:(2 * h + dc + 1) * 128], self.pT[:, 2 * h + mc, tsl], start=(mc == 0), stop=(mc == 1)),
                                ["memV", "pT%d" % (2 * h + mc)], ["ps%d" % b])
                    self.dve(lambda e, b=b, h=h, dc=dc, rb=rb, tsl=tsl: e.tensor_tensor(self.oT[:, 2 * h + dc, tsl], self.ps[b][:, 0:TG], self.rbc[:, rb, 0:TG], ALU.mult),
                             ["ps%d" % b, "rbc%d" % rb], ["oT"])
        self.tm_linear_acc("mem_wo", self.oT, "oT", 8, 1.0)

    def final_norm(self):
        cfg = self.cfg
        NSUB = cfg.nsub
        for j in range(NSUB):
            self.act(lambda e, j=j: e.activation(self.junk[:], self.xt[:, j, :], AF.Square, accum_out=self.ss[:, j:j + 1]),
                     ["x%d" % j], ["junk", "ss%d" % j])
        self.rsqrt(self.rstd[:, 0:NSUB], self.ss[:, 0:NSUB], 1.0 / D, ["ss%d" % j for j in range(NSUB)], "rstd")
        for j in range(NSUB):
            self.dve(lambda e, j=j: e.scalar_tensor_tensor(self.xt[:, j, :], self.xt[:, j, :], self.rstd[:, j:j + 1], self.gfin[:], ALU.mult, ALU.mult),
                     ["x%d" % j, "rstd", "gfin"], ["x%d" % j])

    def m_gqa(self, l):
        cfg = self.cfg
        self.phase_begin()
        SM = max(cfg.seqs)
        NBM = SM // 128
        KTs = self.sb("KTs", [128, 2, SM], BF16)
        Vx = self.sb("Vx", [128, NBM, 2, 65], BF16)
        onesf = self.sb("onesf", [128, 64], F32)
        QW = min(512, min(cfg.seqs))
        qt = self.sb("qtile", [128, 2, QW], BF16)
        pT = self.sb("pTa", [128, 3, QW], BF16)
        rr = self.sb("rr", [128, QW], F32)
        bcs = self.sb("bcs", [64, QW], F32)
        yst = self.sb("yst", [64, 2, QW], BF16)
        self.dve(lambda e: e.memset(onesf[:], 1.0), [], ["onesf"])
        self.dve(lambda e: e.memset(Vx[:], 1.0), [], ["Vx"])
        it = 0
        for si, (st, S) in enumerate(zip(cfg.starts, cfg.seqs)):
            NB = S // 128
            for g in range(2):
                self.dma("sp", KTs[:, g, 0:S], self.KT[g, :, st:st + S], [], ["KTs"])
            for g in range(2):
                self.dma("sp", Vx[:, 0:NB, g, 0:64], self.VV[st:st + S, g * 64:(g + 1) * 64].rearrange("(kb p) d -> p kb d", p=128), [], ["Vx"])
            for pr in range(4):
                g = pr // 2
                for qb in range(S // QW):
                    qs = it % 2; it += 1
                    self.dma("sp", qt[:, qs, :], self.QT[pr, :, st + qb * QW: st + (qb + 1) * QW], [], ["qt%d" % qs])
                    for hh in range(2):
                        h = 2 * pr + hh
                        base = hh * 64
                        ob = 2 + hh
                        for kb in range(NB + 1):
                            if kb < NB:
                                sbk = kb % 2
                                self.pe(lambda e, sbk=sbk, kb=kb, g=g, base=base, qs=qs: e.matmul(self.ps[sbk][:, 0:QW], KTs[base:base + 64, g, kb * 128:(kb + 1) * 128], qt[base:base + 64, qs, :], start=True, stop=True),
                                        ["KTs", "qt%d" % qs], ["ps%d" % sbk])
                                self.act(lambda e, sbk=sbk, kb=kb: e.activation(pT[:, kb % 3, :], self.ps[sbk][:, 0:QW], AF.Exp), ["ps%d" % sbk], ["pTa%d" % (kb % 3)])
                            if kb >= 1:
                                k2 = kb - 1
                                self.pe(lambda e, k2=k2, g=g, ob=ob, NB=NB: e.matmul(self.ps[ob][0:65, 0:QW], Vx[:, k2, g, :], pT[:, k2 % 3, :], start=(k2 == 0), stop=(k2 == NB - 1)),
                                        ["Vx", "pTa%d" % (k2 % 3)], ["ps%d" % ob])
                        self.dve(lambda e, ob=ob: e.reciprocal(rr[64:65, :], self.ps[ob][64:65, 0:QW]), ["ps%d" % ob], ["rr"])
                        self.pe(lambda e: e.matmul(self.ps[0][0:64, 0:QW], onesf[64:65, 0:64], rr[64:65, :], start=True, stop=True), ["onesf", "rr"], ["ps0"])
                        self.act(lambda e: e.copy(bcs[:, :], self.ps[0][0:64, 0:QW]), ["ps0"], ["bcs"])
                        self.dve(lambda e, ob=ob, hh=hh: e.tensor_tensor(yst[:, hh, :], self.ps[ob][0:64, 0:QW], bcs[:, :], ALU.mult), ["ps%d" % ob, "bcs"], ["yst%d" % hh])
                        self.dma("sp", self.YT[512 + h * 64: 512 + (h + 1) * 64, st + qb * QW: st + (qb + 1) * QW], yst[:, hh, :], ["yst%d" % hh], ["YT"])
        self.phase_end()

    def m_fourier(self, l):
        cfg = self.cfg
        self.phase_begin()
        SM = max(cfg.seqs)
        N1M = SM // 128
        fc = self.sb("fc", [64, 2, 64], BF16)
        wf = self.sb("wf", [64, 4, 64], BF16)
        AB = self.sb("AB", [64, 4, 2, 64], BF16)
        f1 = self.sb("f1", [64, 3 * N1M], BF16)
        Ht = self.sb("Ht", [128, N1M, 2, 128], BF16)
        Ut = self.sb("Ut", [64, 128, 64], BF16)
        Ot = self.sb("Ot", [128, N1M, 3, 64], BF16)
        PT = self.sb("PTf", [64, 2, SM], BF16)
        yst = self.sb("ystf", [64, 2, 512], BF16)
        self.dma("sp", fc[:], self.cin["c_fc"], [], ["fc"])
        self.dma("pool", wf[:], self.inp["four_w"][l].rearrange("g c d -> c g d"), [], ["wf"])
        for g in range(4):
            for x in range(2):
                self.pe(lambda e, g=g, x=x: e.matmul(self.ps[0][0:64, (g * 2 + x) * 64:(g * 2 + x + 1) * 64], fc[:, x, :], wf[:, g, :], start=True, stop=True),
                        ["fc", "wf"], ["ps0"])
        self.dve(lambda e: e.tensor_copy(AB[:].rearrange("c g x d -> c (g x d)"), self.ps[0][0:64, :]), ["ps0"], ["AB"])
        for si, (st, S) in enumerate(zip(cfg.starts, cfg.seqs)):
            N1 = S // 128
            self.dma("sp", f1[0:N1, 0:3 * N1], self.cin["c_f1_%d" % S], [], ["f1"])
            self.dma("sp", Ht[:, 0:N1, :, :], self.cin["c_h_%d" % S], [], ["Ht"])
            CB = 1
            while CB * 2 <= min(64, 512 // (3 * N1)):
                CB *= 2
            for g in range(4):
                self.dma("sp", Ut[0:N1, :, :], self.UF[st:st + S, g * 64:(g + 1) * 64].rearrange("(n1 n2) c -> n1 n2 c", n2=128), [], ["Ut"])
                for c0 in range(0, 64, CB):
                    b = (c0 // CB) % 3
                    for cl in range(CB):
                        c = c0 + cl
                        self.pe(lambda e, b=b, cl=cl, c=c, N1=N1: e.matmul(self.ps[b][:, cl * 3 * N1:(cl + 1) * 3 * N1], Ut[0:N1, :, c], f1[0:N1, 0:3 * N1], start=True, stop=True),
                                ["Ut", "f1"], ["ps%d" % b])
                    self.dve(lambda e, b=b, c0=c0, N1=N1, CB=CB: e.tensor_copy(Ot[:, 0:N1, :, c0:c0 + CB].rearrange("p k x c -> p c x k"),
                                                                             self.ps[b][:, 0:CB * 3 * N1].rearrange("p (c x k) -> p c x k", c=CB, x=3)),
                             ["ps%d" % b], ["Ot"])
                KB = min(4, N1)
                for k0 in range(0, N1, KB):
                    for x in range(2):
                        b = 3 + ((k0 // KB) * 2 + x) % 3
                        for kl in range(KB):
                            k1 = k0 + kl
                            o = self.ps[b][0:64, kl * 128:(kl + 1) * 128]
                            if x == 0:
                                self.pe(lambda e, o=o, k1=k1: e.matmul(o, Ot[:, k1, 0, :], Ht[:, k1, 0, :], start=True, stop=False), ["Ot", "Ht"], ["ps%d" % b])
                                self.pe(lambda e, o=o, k1=k1: e.matmul(o, Ot[:, k1, 2, :], Ht[:, k1, 1, :], start=False, stop=True), ["Ot", "Ht"], ["ps%d" % b])
                            else:
                                self.pe(lambda e, o=o, k1=k1: e.matmul(o, Ot[:, k1, 0, :], Ht[:, k1, 1, :], start=True, stop=False), ["Ot", "Ht"], ["ps%d" % b])
                                self.pe(lambda e, o=o, k1=k1: e.matmul(o, Ot[:, k1, 1, :], Ht[:, k1, 0, :], start=False, stop=True), ["Ot", "Ht"], ["ps%d" % b])
                        dst = PT[:, x, 0:S].rearrange("c (k2 k1) -> c k1 k2", k1=N1)[:, k0:k0 + KB, :]
                        eng = self.act if x == 0 else self.dve
                        if x == 0:
                            self.act(lambda e, b=b, dst=dst, KB=KB: e.copy(dst, self.ps[b][0:64, 0:KB * 128].rearrange("c (k f) -> c k f", k=KB)), ["ps%d" % b], ["PTf"])
                        else:
                            self.dve(lambda e, b=b, dst=dst, KB=KB: e.tensor_copy(dst, self.ps[b][0:64, 0:KB * 128].rearrange("c (k f) -> c k f", k=KB)), ["ps%d" % b], ["PTf"])
                SW = min(512, S)
                for s0 in range(0, S, SW):
                    b = 6 + (s0 // SW) % 2
                    ys = (s0 // SW) % 2
                    self.pe(lambda e, b=b, g=g, s0=s0, SW=SW: e.matmul(self.ps[b][0:64, 0:SW], AB[:, g, 0, :], PT[:, 0, s0:s0 + SW], start=True, stop=False), ["AB", "PTf"], ["ps%d" % b])
                    self.pe(lambda e, b=b, g=g, s0=s0, SW=SW: e.matmul(self.ps[b][0:64, 0:SW], AB[:, g, 1, :], PT[:, 1, s0:s0 + SW], start=False, stop=True), ["AB", "PTf"], ["ps%d" % b])
                    self.act(lambda e, b=b, ys=ys, SW=SW: e.copy(yst[:, ys, 0:SW], self.ps[b][0:64, 0:SW]), ["ps%d" % b], ["ystf%d" % ys])
                    self.dma("sp", self.YT[g * 64:(g + 1) * 64, st + s0: st + s0 + SW], yst[:, ys, 0:SW], ["ystf%d" % ys], ["YT"])
        self.phase_end()

    def nb(self):
        self._bank = (getattr(self, "_bank", -1) + 1) % 3
        return 4 + self._bank

    def nbt(self):
        return 7

    def m_deltanet(self, l):
        cfg = self.cfg
        self.phase_begin()
        W = min(512, min(cfg.seqs)); NCH = W // 64; NCP = NCH // 2; NU = NCH * 4
        idb = self.identb[0:64, 0:64]; idf = self.identf[0:64, 0:64]; onb = self.onesb[0:64, 0:64]
        sbt = lambda n, sh, dt: self.sb(n, sh, dt)
        cw = sbt("cw", [64, 3, 12], F32); dg = sbt("dg", [64, 12, 3, 64], BF16)
        mk = sbt("mk", [64, 4, 64], F32); tri = sbt("tri", [64, 3, 64], F32)
        nA = sbt("nA", [64, 8], F32); dtb = sbt("dtb", [64, 8], F32)
        zin = sbt("zin", [64, 12, W + 2], BF16); zab = sbt("zab", [64, NCH, 16], F32)
        cs = sbt("cs", [64, 2, W], F32); sq = sbt("sq", [64, 2, W], BF16); rs = sbt("rs", [64, 2, W], F32)
        qTn = sbt("qTn", [64, NU, 64], BF16); kTn = sbt("kTn", [64, NU, 64], BF16); vTn = sbt("vTn", [64, NU, 64], BF16)
        k_tm = sbt("k_tm", [64, NU, 64], BF16); v_tm = sbt("v_tm", [64, NU, 64], BF16)
        gt = sbt("gt", [64, 8, NU], F32)
        dgx = sbt("dgx", [64, 8, 64], F32); dl = sbt("dl", [64, 8, 64], F32); egr = sbt("egr", [64, 8, 64], F32)
        t1 = sbt("t1", [64, 2, 8, 64], F32); DTt = sbt("DTt", [64, 8, 64], F32); DLt = sbt("DLt", [64, 8, 64], F32)
        qdec = sbt("qdec", [64, NU, 64], BF16); qkTm = sbt("qkTm", [64, NU, 64], BF16)
        Lp = [sbt("Lp%d" % i, [64, NU, 64], BF16) for i in range(2)]
        Up = [sbt("Up%d" % i, [64, NU, 64], BF16) for i in range(2)]
        Rp = [sbt("Rp%d" % i, [64, NU, 64], BF16) for i in range(2)]
        vb = sbt("vb", [64, NU, 64], BF16); kbe = sbt("kbe", [64, NU, 64], BF16); ktl = sbt("ktl", [64, NU, 64], BF16)
        u_sb = sbt("u_sb", [64, NU, 64], F32); wT = sbt("wT", [64, NU, 64], BF16)
        Sf = sbt("Sf", [64, 4, 64], F32); Sb = sbt("Sb", [64, 4, 64], BF16)
        vnew = sbt("vnew", [64, 2, 4, 64], BF16); ost = sbt("ost", [64, 2, 256], F32)
        o2 = sbt("o2", [128, 2, 256], F32); gte = sbt("gte", [128, 256], F32); osum = sbt("osum", [128, 256], F32)
        pss = sbt("pss", [128, 8], F32); ydb = sbt("ydb", [128, 256], BF16); gdn = sbt("gdn", [128, 64], F32)
        ydT = sbt("ydT", [128, 2, W], BF16)
        self.dma("sp", cw[:], self.inp["dn_conv"][l].rearrange("t (xh d) -> d t xh", d=64), [], ["cw"])
        self.dma("sp", mk[:], self.cin["c_dnmask"], [], ["mk"])
        self.dma("sp", tri[:], self.cin["c_tri"], [], ["tri"])
        self.dma("sp", nA[:], self.inp["dn_A_log"][l].rearrange("a h -> (a h)").partition_broadcast(64), [], ["nA"])
        self.dma("sp", dtb[:], self.inp["dn_dt_bias"][l].rearrange("a h -> (a h)").partition_broadcast(64), [], ["dtb"])
        self.dma("sp", gdn[:], self.inp["dn_out_norm"][l].partition_broadcast(128), [], ["gdn"])
        self.act(lambda e: e.activation(nA[:], nA[:], AF.Exp), ["nA"], ["nA"])
        self.dve(lambda e: e.tensor_scalar(nA[:], nA[:], -1.0, None, ALU.mult), ["nA"], ["nA"])
        for xh in range(12):
            for t in range(3):
                self.dve(lambda e, xh=xh, t=t: e.tensor_scalar(dg[:, xh, t, :], idb, cw[:, t, xh:xh + 1], None, ALU.mult), ["cw", "identb"], ["dg"])
        one_col = tri[:, 2, 0:1]
        B8 = lambda ap: ap.unsqueeze(2).to_broadcast([64, 8, 64])
        B4 = lambda ap: ap.unsqueeze(2).to_broadcast([64, 4, 64])
        psv = lambda b: self.ps[b][0:64, :].rearrange("p (u f) -> p u f", f=64)
        psbv = lambda b: self.psb[b][0:64, 0:512].rearrange("p (u f) -> p u f", f=64)

        for si, (st, S) in enumerate(zip(cfg.starts, cfg.seqs)):
            NTL = S // W
            for d in range(2):
                kT_, kL_ = (0, 1) if d == 0 else (2, 3)
                self.dve(lambda e: e.memset(Sf[:], 0.0), [], ["Sf"])
                self.dve(lambda e: e.memset(Sb[:], 0.0), [], ["Sb"])
                order = range(NTL) if d == 0 else range(NTL - 1, -1, -1)
                for ti in order:
                    p0 = ti * W
                    lo, hi = max(p0 - 1, 0), min(p0 + W + 1, S)
                    c0 = lo - (p0 - 1)
                    if p0 == 0:
                        self.dve(lambda e: e.memset(zin[:, :, 0:1], 0.0), [], ["zin"])
                    if p0 + W == S:
                        self.dve(lambda e: e.memset(zin[:, :, W + 1:W + 2], 0.0), [], ["zin"])
                    for x in range(3):
                        self.dma("sp", zin[:, x * 4:(x + 1) * 4, c0:c0 + (hi - lo)],
                                 self.ZT[x * 256:(x + 1) * 256, st + lo:st + hi].rearrange("(h d) t -> d h t", d=64), [], ["zin"])
                    self.dma("sp", zab[:], self.ZAB[st + p0:st + p0 + W, 0:16].rearrange("(n c) x -> c n x", c=64), [], ["zab"])
                    for xh in range(12):
                        x, h = xh // 4, xh % 4
                        b = self.nb()
                        for t in range(3):
                            self.pe(lambda e, b=b, xh=xh, t=t: e.matmul(self.ps[b][0:64, 0:W], dg[:, xh, t, :], zin[:, xh, t:t + W], start=(t == 0), stop=(t == 2)),
                                    ["dg", "zin"], ["ps%d" % b])
                        if x == 2:
                            self.act(lambda e, b=b, h=h: e.activation(vTn[:].rearrange("p (n h) f -> p n h f", h=4)[:, :, h, :], self.ps[b][0:64, 0:W].rearrange("p (n f) -> p n f", f=64), AF.Silu),
                                     ["ps%d" % b], ["vTn"])
                            continue
                        s2 = xh % 2
                        self.act(lambda e, b=b, s2=s2: e.activation(cs[:, s2, :], self.ps[b][0:64, 0:W], AF.Silu), ["ps%d" % b], ["cs%d" % s2])
                        self.act(lambda e, s2=s2: e.activation(sq[:, s2, :], cs[:, s2, :], AF.Square), ["cs%d" % s2], ["sq%d" % s2])
                        b2 = self.nb()
                        self.pe(lambda e, b2=b2, s2=s2: e.matmul(self.ps[b2][0:64, 0:W], onb, sq[:, s2, :], start=True, stop=True), ["onesb", "sq%d" % s2], ["ps%d" % b2])
                        self.act(lambda e, b2=b2, s2=s2: e.activation(rs[:, s2, :], self.ps[b2][0:64, 0:W], AF.Sqrt, bias=self.epsc[0:64, 0:1], scale=1.0), ["ps%d" % b2, "epsc"], ["rs%d_s" % s2])
                        self.dve(lambda e, s2=s2: e.reciprocal(rs[:, s2, :], rs[:, s2, :]), ["rs%d_s" % s2], ["rs%d" % s2])
                        dstT = (qTn if x == 0 else kTn)[:].rearrange("p (n h) f -> p n h f", h=4)[:, :, h, :]
                        csv = cs[:, s2, :].rearrange("p (n f) -> p n f", f=64); rsv = rs[:, s2, :].rearrange("p (n f) -> p n f", f=64)
                        if x == 0:
                            self.dve(lambda e, dstT=dstT, csv=csv, rsv=rsv: e.scalar_tensor_tensor(dstT, csv, 0.125, rsv, ALU.mult, ALU.mult), ["cs%d" % s2, "rs%d" % s2], ["qTn"])
                        else:
                            self.dve(lambda e, dstT=dstT, csv=csv, rsv=rsv: e.tensor_tensor(dstT, csv, rsv, ALU.mult), ["cs%d" % s2, "rs%d" % s2], ["kTn"])
                    if self.dn_stage < 2:
                        continue
                    for cp in range(NCP):
                        for (srcT, dstm, key, dkey) in ((kTn, k_tm, "kTn", "k_tm"), (vTn, v_tm, "vTn", "v_tm")):
                            b = self.nbt()
                            for u in range(8):
                                self.pe(lambda e, b=b, u=u, cp=cp, srcT=srcT: e.transpose(self.psb[b][0:64, u * 64:(u + 1) * 64], srcT[:, cp * 8 + u, :], idb), [key, "identb"], ["ps%d" % b])
                            self.dve(lambda e, b=b, cp=cp, dstm=dstm: e.tensor_copy(dstm[:, cp * 8:(cp + 1) * 8, :], psbv(b)), ["ps%d" % b], [dkey])
                    if self.dn_stage < 3:
                        continue
                    G = lambda r: gt[:, r, :]
                    Gn = lambda r: gt[:, r, :].rearrange("p (n h) -> p n h", h=4)
                    self.dve(lambda e, d=d: e.tensor_tensor(Gn(0), zab[:, :, d * 4:(d + 1) * 4], dtb[:, d * 4:(d + 1) * 4].unsqueeze(1).to_broadcast([64, NCH, 4]), ALU.add), ["zab", "dtb"], ["g0"])
                    self.act(lambda e: e.activation(G(0), G(0), AF.Exp), ["g0"], ["g0"])
                    self.act(lambda e: e.activation(G(0), G(0), AF.Ln, bias=one_col, scale=1.0), ["g0", "tri"], ["g0"])
                    self.dve(lambda e, d=d: e.tensor_tensor(Gn(1), Gn(0), nA[:, d * 4:(d + 1) * 4].unsqueeze(1).to_broadcast([64, NCH, 4]), ALU.mult), ["g0", "nA"], ["g1"])
                    self.act(lambda e, d=d: e.activation(Gn(2), zab[:, :, 8 + d * 4:8 + (d + 1) * 4], AF.Sigmoid), ["zab"], ["g2"])
                    b = self.nb()
                    self.pe(lambda e, b=b, d=d: e.matmul(self.ps[b][0:64, 0:NU], tri[:, d, :], G(1), start=True, stop=True), ["tri", "g1"], ["ps%d" % b])
                    self.pe(lambda e, b=b: e.matmul(self.ps[b][0:64, 64:64 + NU], tri[:, 2, :], G(1), start=True, stop=True), ["tri", "g1"], ["ps%d" % b])
                    self.dve(lambda e, b=b: e.tensor_copy(G(3), self.ps[b][0:64, 0:NU]), ["ps%d" % b], ["g3"])
                    self.act(lambda e, b=b: e.activation(G(4), self.ps[b][0:64, 0:NU], AF.Exp), ["ps%d" % b], ["g4"])
                    self.act(lambda e, b=b: e.activation(G(5), self.ps[b][0:64, 64:64 + NU], AF.Exp), ["ps%d" % b], ["g5"])
                    self.dve(lambda e, b=b: e.tensor_tensor(G(6), self.ps[b][0:64, 64:64 + NU], G(3), ALU.subtract), ["ps%d" % b, "g3"], ["g6"])
                    self.act(lambda e: e.activation(G(6), G(6), AF.Exp), ["g6"], ["g6"])
                    self.dve(lambda e: e.tensor_tensor(G(7), G(2), G(4), ALU.mult), ["g2", "g4"], ["g7"])
                    if self.dn_stage < 4:
                        continue
                    for cp in range(NCP):
                        us = slice(cp * 8, (cp + 1) * 8)
                        bA = self.nb()
                        for u in range(8):
                            self.pe(lambda e, bA=bA, u=u, cp=cp: e.matmul(self.ps[bA][0:64, u * 64:(u + 1) * 64], kTn[:, cp * 8 + u, :], kTn[:, cp * 8 + u, :], start=True, stop=True), ["kTn"], ["ps%d" % bA])
                        bB = self.nb()
                        for u in range(8):
                            self.pe(lambda e, bB=bB, u=u, cp=cp: e.matmul(self.ps[bB][0:64, u * 64:(u + 1) * 64], kTn[:, cp * 8 + u, :], qTn[:, cp * 8 + u, :], start=True, stop=True), ["kTn", "qTn"], ["ps%d" % bB])
                        if self.dn_stage < 4.01:
                            continue
                        self.dve(lambda e, us=us: e.tensor_tensor(dgx[:], idf.unsqueeze(1).to_broadcast([64, 8, 64]), B8(gt[:, 3, us]), ALU.mult), ["identf", "g3"], ["dgx"])
                        if self.dn_stage < 4.02:
                            continue
                        bC = self.nb()
                        self.pe(lambda e, bC=bC: e.matmul(self.ps[bC][0:64, :], tri[:, 2, :], dgx[:].rearrange("p u f -> p (u f)"), start=True, stop=True), ["tri", "dgx"], ["ps%d" % bC])
                        if self.dn_stage < 4.03:
                            continue
                        self.dve(lambda e, bC=bC, us=us: e.tensor_tensor(dl[:], psv(bC), B8(gt[:, 3, us]), ALU.subtract), ["ps%d" % bC, "g3"], ["dl"])
                        if self.dn_stage < 4.04:
                            continue
                        self.dve(lambda e, bC=bC: e.tensor_copy(egr[:], psv(bC)), ["ps%d" % bC], ["egr"])
                        self.act(lambda e: e.activation(egr[:], egr[:], AF.Exp), ["egr"], ["egr"])
                        if self.dn_stage < 4.05:
                            continue
                        self.dve(lambda e, us=us: e.tensor_tensor(qdec[:, us, :], qTn[:, us, :], egr[:], ALU.mult), ["qTn", "egr"], ["qdec"])
                        if self.dn_stage < 4.25:
                            continue
                        self.dve(lambda e, kT_=kT_: e.scalar_tensor_tensor(t1[:, 0], dl[:], 0.0, mk[:, kT_, :].unsqueeze(1).to_broadcast([64, 8, 64]), ALU.min, ALU.add), ["dl", "mk"], ["t1a"])
                        self.act(lambda e: e.activation(DTt[:], t1[:, 0], AF.Exp), ["t1a"], ["DTt"])
                        self.dve(lambda e, kL_=kL_: e.scalar_tensor_tensor(t1[:, 1], dl[:], 0.0, mk[:, kL_, :].unsqueeze(1).to_broadcast([64, 8, 64]), ALU.max, ALU.subtract), ["dl", "mk"], ["t1b"])
                        self.act(lambda e: e.activation(DLt[:], t1[:, 1], AF.Exp, scale=-1.0), ["t1b"], ["DLt"])
                        self.dve(lambda e, bB=bB, us=us: e.tensor_tensor(qkTm[:, us, :], psv(bB), DTt[:], ALU.mult), ["ps%d" % bB, "DTt"], ["qkTm"])
                        self.dve(lambda e, bA=bA: e.tensor_tensor(DLt[:], psv(bA), DLt[:], ALU.mult), ["ps%d" % bA, "DLt"], ["DLt"])
                        self.dve(lambda e, us=us: e.tensor_tensor(Lp[0][:, us, :], DLt[:], B8(gt[:, 2, us]), ALU.mult), ["DLt", "g2"], ["L0"])
                        if self.dn_stage < 4.5:
                            continue
                        bD = self.nbt()
                        for u in range(8):
                            self.pe(lambda e, bD=bD, u=u, cp=cp: e.transpose(self.psb[bD][0:64, u * 64:(u + 1) * 64], Lp[0][:, cp * 8 + u, :], idb), ["L0", "identb"], ["ps%d" % bD])
                        self.dve(lambda e, bD=bD, us=us: e.tensor_copy(Up[0][:, us, :], psbv(bD)), ["ps%d" % bD], ["U0"])
                        self.dve(lambda e, us=us: e.tensor_tensor(Rp[0][:, us, :], idb.unsqueeze(1).to_broadcast([64, 8, 64]), Up[0][:, us, :], ALU.subtract), ["identb", "U0"], ["R0"])
                        if self.dn_stage < 4.75:
                            continue
                        self.dve(lambda e, us=us: e.tensor_tensor(vb[:, us, :], v_tm[:, us, :], B8(gt[:, 2, us]), ALU.mult), ["v_tm", "g2"], ["vb"])
                        self.dve(lambda e, us=us: e.tensor_tensor(kbe[:, us, :], k_tm[:, us, :], B8(gt[:, 7, us]), ALU.mult), ["k_tm", "g7"], ["kbe"])
                        self.dve(lambda e, us=us: e.tensor_tensor(ktl[:, us, :], k_tm[:, us, :], B8(gt[:, 6, us]), ALU.mult), ["k_tm", "g6"], ["ktl"])
                    if self.dn_stage < 5:
                        continue
                    cur = 0
                    for lev in range(5):
                        nxt = 1 - cur
                        for cp in range(NCP):
                            us = slice(cp * 8, (cp + 1) * 8)
                            b = self.nb()
                            for u in range(8):
                                self.pe(lambda e, b=b, u=u, cp=cp, cur=cur: e.matmul(self.ps[b][0:64, u * 64:(u + 1) * 64], Up[cur][:, cp * 8 + u, :], Lp[cur][:, cp * 8 + u, :], start=True, stop=True),
                                        ["U%d" % cur, "L%d" % cur], ["ps%d" % b])
                            self.act(lambda e, b=b, us=us, nxt=nxt: e.copy(Lp[nxt][:, us, :], psv(b)), ["ps%d" % b], ["L%d" % nxt])
                        if lev < 4:
                            for cp in range(NCP):
                                us = slice(cp * 8, (cp + 1) * 8)
                                b = self.nb()
                                for u in range(8):
                                    self.pe(lambda e, b=b, u=u, cp=cp, cur=cur: e.matmul(self.ps[b][0:64, u * 64:(u + 1) * 64], Lp[cur][:, cp * 8 + u, :], Up[cur][:, cp * 8 + u, :], start=True, stop=True),
                                            ["U%d" % cur, "L%d" % cur], ["ps%d" % b])
                                self.act(lambda e, b=b, us=us, nxt=nxt: e.copy(Up[nxt][:, us, :], psv(b)), ["ps%d" % b], ["U%d" % nxt])
                        for cp in range(NCP):
                            us = slice(cp * 8, (cp + 1) * 8)
                            b = self.nb()
                            for u in range(8):
                                self.pe(lambda e, b=b, u=u, cp=cp, cur=cur, nxt=nxt: e.matmul(self.ps[b][0:64, u * 64:(u + 1) * 64], Lp[nxt][:, cp * 8 + u, :], Rp[cur][:, cp * 8 + u, :], start=True, stop=True),
                                        ["L%d" % nxt, "R%d" % cur], ["ps%d" % b])
                            self.dve(lambda e, b=b, us=us, cur=cur, nxt=nxt: e.tensor_tensor(Rp[nxt][:, us, :], psv(b), Rp[cur][:, us, :], ALU.add), ["ps%d" % b, "R%d" % cur], ["R%d" % nxt])
                        cur = nxt
                    Rf_ = Rp[cur]; rkey = "R%d" % cur
                    if self.dn_stage < 6:
                        continue
                    for cp in range(NCP):
                        us = slice(cp * 8, (cp + 1) * 8)
                        b = self.nb()
                        for u in range(8):
                            self.pe(lambda e, b=b, u=u, cp=cp, Rf_=Rf_: e.matmul(self.ps[b][0:64, u * 64:(u + 1) * 64], Rf_[:, cp * 8 + u, :], vb[:, cp * 8 + u, :], start=True, stop=True), [rkey, "vb"], ["ps%d" % b])
                        self.act(lambda e, b=b, us=us: e.copy(u_sb[:, us, :], psv(b)), ["ps%d" % b], ["u_sb"])
                        b = self.nb()
                        for u in range(8):
                            self.pe(lambda e, b=b, u=u, cp=cp, Rf_=Rf_: e.matmul(self.ps[b][0:64, u * 64:(u + 1) * 64], kbe[:, cp * 8 + u, :], Rf_[:, cp * 8 + u, :], start=True, stop=True), [rkey, "kbe"], ["ps%d" % b])
                        self.dve(lambda e, b=b, us=us: e.tensor_copy(wT[:, us, :], psv(b)), ["ps%d" % b], ["wT"])
                    if self.dn_stage < 7:
                        continue
                    corder = range(NCH) if d == 0 else range(NCH - 1, -1, -1)
                    for n in corder:
                        hs = slice(n * 4, (n + 1) * 4)
                        par = n % 2
                        psh = lambda b: self.ps[b][0:64, 0:256].rearrange("p (h f) -> p h f", f=64)
                        bG = self.nb()
                        for h in range(4):
                            self.pe(lambda e, bG=bG, h=h, n=n: e.matmul(self.ps[bG][0:64, h * 64:(h + 1) * 64], wT[:, n * 4 + h, :], Sb[:, h, :], start=True, stop=True), ["wT", "Sb"], ["ps%d" % bG])
                        self.dve(lambda e, bG=bG, hs=hs, par=par: e.tensor_tensor(vnew[:, par], u_sb[:, hs, :], psh(bG), ALU.subtract), ["u_sb", "ps%d" % bG], ["vnew%d" % par])
                        bH = self.nb()
                        for h in range(4):
                            self.pe(lambda e, bH=bH, h=h, n=n: e.matmul(self.ps[bH][0:64, h * 64:(h + 1) * 64], qdec[:, n * 4 + h, :], Sb[:, h, :], start=True, stop=False), ["qdec", "Sb"], ["ps%d" % bH])
                            self.pe(lambda e, bH=bH, h=h, n=n, par=par: e.matmul(self.ps[bH][0:64, h * 64:(h + 1) * 64], qkTm[:, n * 4 + h, :], vnew[:, par, h, :], start=False, stop=True), ["qkTm", "vnew%d" % par], ["ps%d" % bH])
                        bI = self.nb()
                        for h in range(4):
                            self.pe(lambda e, bI=bI, h=h, n=n, par=par: e.matmul(self.ps[bI][0:64, h * 64:(h + 1) * 64], ktl[:, n * 4 + h, :], vnew[:, par, h, :], start=True, stop=True), ["ktl", "vnew%d" % par], ["ps%d" % bI])
                        self.dve(lambda e, hs=hs: e.tensor_tensor(Sf[:], Sf[:], B4(gt[:, 5, hs]), ALU.mult), ["Sf", "g5"], ["Sf"])
                        self.dve(lambda e, bI=bI: e.tensor_tensor(Sf[:], Sf[:], psh(bI), ALU.add), ["Sf", "ps%d" % bI], ["Sf"])
                        self.act(lambda e: e.copy(Sb[:], Sf[:]), ["Sf"], ["Sb"])
                        self.act(lambda e, bH=bH, par=par: e.copy(ost[:, par, :], self.ps[bH][0:64, 0:256]), ["ps%d" % bH], ["ost%d" % par])
                        self.dma("sp", self.OF[d, st + p0 + n * 64: st + p0 + (n + 1) * 64, :], ost[:, par, :], ["ost%d" % par], ["OF"])
            for p0 in (range(0, S, W) if self.dn_stage >= 8 else []):
                for jb in range(W // 128):
                    q0 = st + p0 + jb * 128
                    self.dma("sp", o2[:], self.OF[:, q0:q0 + 128, :].rearrange("a p f -> p a f"), ["OF"], ["o2"])
                    self.dma("sp", gte[:], self.ZAB[q0:q0 + 128, 16:272], [], ["gte"])
                    self.dve(lambda e: e.tensor_tensor(osum[:], o2[:, 0, :], o2[:, 1, :], ALU.add), ["o2"], ["osum"])
                    self.act(lambda e: e.activation(o2[:, 0, :], osum[:], AF.Square), ["osum"], ["o2"])
                    self.dve(lambda e: e.tensor_reduce(pss[:, 0:4], o2[:, 0, :].rearrange("p (h f) -> p h f", f=64), AX.X, ALU.add), ["o2"], ["pss"])
                    self.rsqrt(pss[:, 4:8], pss[:, 0:4], 1.0 / 64, ["pss"], "prs")
                    self.dve(lambda e: e.tensor_tensor(osum[:].rearrange("p (h f) -> p h f", f=64), osum[:].rearrange("p (h f) -> p h f", f=64), pss[:, 4:8].unsqueeze(2).to_broadcast([128, 4, 64]), ALU.mult), ["osum", "prs"], ["osum"])
                    self.dve(lambda e: e.tensor_tensor(osum[:].rearrange("p (h f) -> p h f", f=64), osum[:].rearrange("p (h f) -> p h f", f=64), gdn[:].unsqueeze(1).to_broadcast([128, 4, 64]), ALU.mult), ["osum", "gdn"], ["osum"])
                    self.act(lambda e: e.activation(gte[:], gte[:], AF.Silu), ["gte"], ["gte"])
                    self.dve(lambda e: e.tensor_tensor(ydb[:], osum[:], gte[:], ALU.mult), ["osum", "gte"], ["ydb"])
                    b = self.nbt()
                    for c in range(2):
                        self.pe(lambda e, b=b, c=c: e.transpose(self.psb[b][:, c * 128:(c + 1) * 128], ydb[:, c * 128:(c + 1) * 128], self.identb[:]), ["ydb", "identb"], ["ps%d" % b])
                    self.dve(lambda e, b=b, jb=jb: e.tensor_copy(ydT[:, :, jb * 128:(jb + 1) * 128], self.psb[b][:, 0:256].rearrange("p (c t) -> p c t", c=2)), ["ps%d" % b], ["ydT"])
                self.dma("sp", self.YT[256:512, st + p0: st + p0 + W].rearrange("(c p) t -> p c t", p=128), ydT[:], ["ydT"], ["YT"])
        self.phase_end()


_SKIP = set()


def kernel(**inputs):
    cfg = FULL
    b = Builder(cfg)
    b.skip = set(_SKIP)
    nc = b.build()
    consts = host_consts(cfg)
    xp = np.asarray(inputs["x_prompt"], dtype=np.float32)
    xs = np.asarray(inputs["x_sample"], dtype=np.float32)
    mp = np.asarray(inputs["mem_prompt"], dtype=np.float32)
    ms = np.asarray(inputs["mem_sample"], dtype=np.float32)
    w = {n: np.ascontiguousarray(np.asarray(inputs[n], dtype=np.float32)) for n in WNAMES}
    in_maps = []
    for c in range(NCORES):
        x = np.concatenate([xp[4 * c:4 * c + 4].reshape(-1, D), xs[c].reshape(-1, D)], axis=0)
        mem = np.concatenate([mp[4 * c:4 * c + 4].reshape(-1, D), ms[c].reshape(-1, D)], axis=0)
        m = {"x": np.ascontiguousarray(x), "mem": np.ascontiguousarray(mem)}
        m.update(w)
        m.update(consts)
        in_maps.append(m)
    res = run_bass_kernel_spmd(nc, in_maps, core_ids=list(range(NCORES)))
    yp = np.empty((32, 2048, D), np.float32)
    ys = np.empty((8, 8192, D), np.float32)
    for c in range(NCORES):
        y = np.asarray(res.results[c]["y"], dtype=np.float32)
        yp[4 * c:4 * c + 4] = y[0:8192].reshape(4, 2048, D)
        ys[c] = y[8192:16384]
    return (yp, ys)
```

```python
import math
from contextlib import ExitStack
import numpy as np
import ml_dtypes
import concourse.bass as bass
import concourse.mybir as mybir
from concourse.bass_utils import run_bass_kernel_spmd

F32 = mybir.dt.float32
BF16 = mybir.dt.bfloat16
AF = mybir.ActivationFunctionType
ALU = mybir.AluOpType
AX = mybir.AxisListType

D = 1024
DFF = 2816
NFB = DFF // 128
INW = 2064
EPS = 1e-6
NCORES = 8
MEMT = 256
ENGS = ("pe", "act", "dve", "pool", "sp")


class Op:
    __slots__ = ("eng", "fn", "deps", "sig", "cnt", "dma", "sem", "semval")

    def __init__(self, eng, fn, dma):
        self.eng = eng; self.fn = fn; self.dma = dma
        self.deps = []; self.sig = False; self.cnt = 0; self.sem = None; self.semval = 0


class Rec:
    NDMASEM = 14

    def __init__(self):
        self.ops = {e: [] for e in ENGS}
        self.lastw = {}
        self.readers = {}
        self.dma_uses = {}
        self.dma_rr = {}
        self.pending = {e: [] for e in ENGS}
        self.last_compute = {}
        self.n = 0
        self.seq = []
        self.sid = 0
        self.streams = None

    def add(self, eng, fn, reads=(), writes=(), dma=False):
        op = Op(eng, fn, dma)
        deps = {}
        for r in reads:
            w = self.lastw.get(r)
            if w is not None:
                deps[w] = "raw"
        for w_ in writes:
            lw = self.lastw.get(w_)
            if lw is not None and lw not in deps:
                deps[lw] = "waw"
            for rd in self.readers.get(w_, ()):
                if rd not in deps:
                    deps[rd] = "war"
        need = []
        for d, kind in deps.items():
            if d.dma or dma:
                need.append(d)
            elif d.eng == eng:
                if eng == "pe":
                    continue
                need.append(d)
            else:
                need.append(d)
        if dma:
            rk = (self.sid, eng)
            k = self.dma_rr.get(rk, 0) % self.NDMASEM + self.sid * self.NDMASEM
            self.dma_rr[rk] = self.dma_rr.get(rk, 0) + 1
            ent = self.dma_uses.setdefault((eng, k), [0, None])
            if ent[1] is not None:
                need.append(ent[1])
            ent[0] += 1
            op.sem = (eng, k); op.semval = 16 * ent[0]
            ent[1] = op
        if self.pending[eng]:
            need.extend(self.pending[eng]); self.pending[eng] = []
        for d in need:
            d.sig = True
        op.deps = need
        for r in reads:
            self.readers.setdefault(r, []).append(op)
        for w_ in writes:
            self.lastw[w_] = op
            self.readers[w_] = []
        self.seq.append(op)
        if not dma:
            self.last_compute[eng] = op
        self.n += 1
        return op

    def barrier(self):
        lasts = []
        for e in ENGS:
            if self.last_compute.get(e) is not None:
                lasts.append(self.last_compute[e])
        for ent in self.dma_uses.values():
            if ent[1] is not None:
                lasts.append(ent[1])
        for e in ENGS:
            self.pending[e] = [o for o in lasts if not (o.eng == e and not o.dma)]
        self.lastw = {}; self.readers = {}

    def fork(self, n=2):
        self.streams = []
        for i in range(n):
            self.streams.append({"seq": [], "lastw": {}, "readers": {}, "pending": {e: list(self.pending[e]) for e in ENGS}})
        self.main_seq = self.seq
        self.pending = {e: [] for e in ENGS}

    def use(self, i):
        st = self.streams[i]
        self.sid = i
        self.seq, self.lastw, self.readers, self.pending = st["seq"], st["lastw"], st["readers"], st["pending"]

    def join(self):
        lists = [s["seq"] for s in self.streams]
        idx = [0] * len(lists)
        merged = []
        total = sum(len(x) for x in lists)
        for _ in range(total):
            best, bf = None, None
            for i, lst in enumerate(lists):
                if idx[i] < len(lst):
                    f = idx[i] / len(lst)
                    if bf is None or f < bf:
                        best, bf = i, f
            merged.append(lists[best][idx[best]]); idx[best] += 1
        self.seq = self.main_seq
        self.seq.extend(merged)
        self.sid = 0
        self.lastw = {}; self.readers = {}
        self.pending = {e: [] for e in ENGS}
        for op in merged:
            if not op.dma:
                self.last_compute[op.eng] = op
        self.streams = None

    def setup_sems(self, nc, stack):
        self.esem = {e: stack.enter_context(nc.semaphore("s_" + e)) for e in ENGS}
        self.dsem = {}
        for q in ("sp", "pool"):
            for k in range(self.NDMASEM * 2):
                self.dsem[(q, k)] = stack.enter_context(nc.semaphore("d_%s%d" % (q, k)))
        self.cnts = {e: 0 for e in ENGS}
        self.waited = {e: {} for e in ENGS}

    def emit(self, nc, final=False):
        esem, dsem = self.esem, self.dsem
        for e in ENGS:
            if self.last_compute.get(e) is not None:
                self.last_compute[e].sig = True
        for op in self.seq:
            if op.sig and not op.dma:
                self.cnts[op.eng] += 1; op.cnt = self.cnts[op.eng]
        finals = [(dsem[k], 16 * ent[0]) for k, ent in self.dma_uses.items()] if final else []
        ops = {e: [] for e in ENGS}
        for op in self.seq:
            ops[op.eng].append(op)
        self.seq = []
        with nc.Block() as block:
            def run(engname, e):
                waited = self.waited[engname]
                for op in ops[engname]:
                    for d in op.deps:
                        if d.dma:
                            s, v = dsem[d.sem], d.semval
                        else:
                            s, v = esem[d.eng], d.cnt
                            assert v > 0
                        if waited.get(id(s), 0) < v:
                            e.wait_ge(s, v); waited[id(s)] = v
                    ins = op.fn(e)
                    if op.dma:
                        ins.then_inc(dsem[op.sem], 16)
                    elif op.sig:
                        ins.then_inc(esem[engname], 1)
                if engname == "sp":
                    for s, v in finals:
                        e.wait_ge(s, v)

            @block.tensor
            def _(e):
                run("pe", e)

            @block.scalar
            def _(e):
                run("act", e)

            @block.vector
            def _(e):
                run("dve", e)

            @block.gpsimd
            def _(e):
                run("pool", e)

            @block.sync
            def _(e):
                run("sp", e)


class Cfg:
    def __init__(self, seqs, depth=2):
        self.seqs = list(seqs)
        self.depth = depth
        self.ntok = sum(seqs)
        self.starts = [sum(seqs[:i]) for i in range(len(seqs))]
        self.tt = min(1024, min(seqs))
        self.nsub = self.tt // 128
        self.tg = min(512, self.tt)
        self.ntg = self.tt // self.tg


FULL = Cfg([2048, 2048, 2048, 2048, 8192])

WNAMES = ["ffn1_norm", "ffn1_w_gu", "ffn1_w_down", "mix_norm", "w_in", "four_w", "dn_conv", "dn_A_log",
          "dn_dt_bias", "dn_out_norm", "attn_q_norm", "attn_k_norm", "w_out", "mem_norm_x", "mem_norm_m",
          "mem_wq", "mem_wkv", "mem_wo", "ffn2_norm", "ffn2_w_gu", "ffn2_w_down", "final_norm"]
WSHAPES = {
    "ffn1_norm": (2, D), "ffn1_w_gu": (2, D, 2 * DFF), "ffn1_w_down": (2, DFF, D), "mix_norm": (2, D),
    "w_in": (2, D, INW), "four_w": (2, 4, 64, 64), "dn_conv": (2, 3, 768), "dn_A_log": (2, 2, 4),
    "dn_dt_bias": (2, 2, 4), "dn_out_norm": (2, 64), "attn_q_norm": (2, 64), "attn_k_norm": (2, 64),
    "w_out": (2, D, D), "mem_norm_x": (2, D), "mem_norm_m": (2, D), "mem_wq": (2, D, D),
    "mem_wkv": (2, D, 2 * D), "mem_wo": (2, D, D), "ffn2_norm": (2, D), "ffn2_w_gu": (2, D, 2 * DFF),
    "ffn2_w_down": (2, DFF, D), "final_norm": (D,),
}
NORM_IDS = {"ffn1_norm": 0, "mix_norm": 1, "mem_norm_x": 2, "mem_norm_m": 3, "ffn2_norm": 4}


def host_consts(cfg):
    c = {}
    c["c_identb"] = np.eye(128, dtype=np.float32).astype(ml_dtypes.bfloat16)
    c["c_identf"] = np.eye(128, dtype=np.float32)
    smax = max(cfg.seqs)
    nblk = smax // 128
    pos = np.arange(smax)
    row = (pos // 64).astype(np.float32); col = (pos % 64).astype(np.float32)
    inv = (1.0 / (10000.0 ** (np.arange(16, dtype=np.float32) * (2.0 / 32)))).astype(np.float32)
    ang = np.stack([row[:, None] * inv, col[:, None] * inv], axis=1).astype(np.float32)
    cos = np.cos(ang).astype(np.float32).reshape(nblk, 128, 2, 16).transpose(1, 0, 2, 3)
    sin = np.sin(ang).astype(np.float32).reshape(nblk, 128, 2, 16).transpose(1, 0, 2, 3)
    c["c_cos"] = np.ascontiguousarray(cos); c["c_sin"] = np.ascontiguousarray(sin)
    bf = ml_dtypes.bfloat16
    cc = np.arange(64)
    th = 2 * np.pi * np.outer(cc, cc) / 64.0
    c["c_fc"] = np.stack([np.cos(th) / 8.0, np.sin(th) / 8.0], axis=1).astype(np.float32).astype(bf)
    for S in sorted(set(cfg.seqs)):
        N1 = S // 128
        n1 = np.arange(N1); t1 = 2 * np.pi * np.outer(n1, n1) / N1
        sc = 1.0 / np.sqrt(S)
        f1 = np.concatenate([np.cos(t1), -np.sin(t1), np.sin(t1)], axis=1) * sc
        c["c_f1_%d" % S] = f1.astype(np.float32).astype(bf)
        n2 = np.arange(128)[:, None, None]; k1 = np.arange(N1)[None, :, None]; k2 = np.arange(128)[None, None, :]
        ang = 2 * np.pi * ((n2 * (k1 + N1 * k2)) % S) / S
        h = np.stack([np.cos(ang), -np.sin(ang)], axis=2)
        c["c_h_%d" % S] = h.astype(np.float32).astype(bf)
    p = np.arange(64)[:, None]; f = np.arange(64)[None, :]
    NEG = -1.0e4
    mk = np.stack([np.where(f >= p, 0.0, NEG), np.where(f < p, 0.0, NEG), np.where(f <= p, 0.0, NEG), np.where(f > p, 0.0, NEG)], axis=1)
    c["c_dnmask"] = mk.astype(np.float32)
    tri = np.stack([(p <= f).astype(np.float32), (p >= f).astype(np.float32), np.ones((64, 64), np.float32)], axis=1)
    c["c_tri"] = tri.astype(np.float32)
    return c


class Builder:
    def __init__(self, cfg, dump=(), stop_after=None):
        self.cfg = cfg
        self.dump = set(dump)
        self.stop_after = stop_after
        self.skip = set()
        self.dn_stage = 9
        self.overlap = False
        self.own_phase = True
        self.nc = bass.Bass("TRN2", target_bir_lowering=False)
        self.R = Rec()
        self.sb_off = 16512
        self.sb_mark = 16512

    def dram(self, name, shape, dt, kind=None):
        if kind is None:
            kind = "ExternalOutput" if name in self.dump else ("ExternalInput" if name in getattr(self, "ext_in", ()) else "Internal")
        return self.nc.dram_tensor(name, list(shape), dt, kind=kind).ap()

    def sb(self, name, shape, dt):
        self.tid = getattr(self, "tid", 0) + 1
        return self.pstack.enter_context(self.nc.sbuf_tensor("%s_%d" % (name, self.tid), list(shape), dt))

    def phase_begin(self):
        if not self.own_phase:
            return
        self.R.barrier()
        self.pstack = ExitStack()

    def phase_end(self, final=False):
        if not self.own_phase:
            return
        self.R.emit(self.nc, final=final)
        self.pstack.close()

    def pe(self, fn, r, w): return self.R.add("pe", fn, r, w)
    def act(self, fn, r, w): return self.R.add("act", fn, r, w)
    def dve(self, fn, r, w): return self.R.add("dve", fn, r, w)
    def pool(self, fn, r, w): return self.R.add("pool", fn, r, w)

    def dma(self, q, out, in_, r, w):
        return self.R.add(q, lambda e, o=out, i=in_: e.dma_start(out=o, in_=i), r, w, dma=True)

    def build(self):
        cfg = self.cfg; nc = self.nc
        NT = cfg.ntok
        self.inp = {}
        self.inp["x"] = self.dram("x", [NT, D], F32, "ExternalInput")
        self.inp["mem"] = self.dram("mem", [len(cfg.seqs) * MEMT, D], F32, "ExternalInput")
        for n in WNAMES:
            shp = WSHAPES[n]
            if getattr(self, "tiny_w", False) and int(np.prod(shp)) > 100000:
                shp = (2, 128, 128)
            self.inp[n] = self.dram(n, shp, F32, "ExternalInput")
        hc = host_consts(cfg)
        self.cin = {}
        for k, v in hc.items():
            self.cin[k] = self.dram(k, v.shape, BF16 if v.dtype == ml_dtypes.bfloat16 else F32, "ExternalInput")
        self.y = self.dram("y", [NT, D], F32, "ExternalOutput")
        self.XS = self.dram("XS", [NT, D], F32)
        self.YT = self.dram("YT", [D, NT], BF16)
        self.UF = self.dram("UF", [NT, 256], BF16)
        self.ZT = self.dram("ZT", [768, NT], BF16)
        self.ZAB = self.dram("ZAB", [NT, 272], F32)
        self.QT = self.dram("QT", [4, 128, NT], BF16)
        self.KT = self.dram("KT", [2, 128, NT], BF16)
        self.VV = self.dram("VV", [NT, 128], BF16)
        self.OF = self.dram("OF", [2, NT, 256], F32)

        with ExitStack() as stack:
            stack.enter_context(nc.allow_non_contiguous_dma(reason="strided weight/activation tiles"))
            self.ps = [stack.enter_context(nc.psum_tensor("ps%d" % b, [128, 512], F32)) for b in range(8)]
            self.psb = {b: self.ps[b][:].bitcast(BF16) for b in range(8)}
            self.R.setup_sems(nc, stack)
            self.pstack = stack
            self.setup_consts()
            self.R.emit(nc)
            self.program()
            self.pstack = stack
            self.dve(lambda e: e.memset(self.epsc[:, 0:1], EPS), [], ["epsc"])
            self.R.emit(nc, final=True)
        return nc

    def setup_consts(self):
        cfg = self.cfg
        self.identb = self.sb("identb", [128, 128], BF16)
        self.identf = self.sb("identf", [128, 128], F32)
        self.onesb = self.sb("onesb", [128, 128], BF16)
        self.gains = self.sb("gains", [128, 88], F32)
        self.gq = self.sb("gq", [128, 2, 64], F32)
        self.gk = self.sb("gk", [128, 2, 64], F32)
        gstage = self.sb("gstage", [88, 128], F32)
        self.dma("sp", self.identb[:], self.cin["c_identb"], [], ["identb"])
        self.dma("sp", self.identf[:], self.cin["c_identf"], [], ["identf"])
        self.dve(lambda e: e.memset(self.onesb[:], 1.0), [], ["onesb"])
        self.epsc = self.sb("epsc", [128, 8], F32)
        self.dve(lambda e: e.memset(self.epsc[:], EPS), [], ["epsc"])
        for nm, i in NORM_IDS.items():
            self.dma("sp", gstage[i * 16:(i + 1) * 16, :],
                     self.inp[nm].rearrange("l (c p) -> (l c) p", p=128), [], ["gstage"])
        self.dma("sp", gstage[80:88, :], self.inp["final_norm"].rearrange("(c p) -> c p", p=128), [], ["gstage"])
        self.pe(lambda e: e.transpose(self.ps[0][:, 0:88], gstage[0:88, :], self.identf[0:88, 0:88]),
                ["gstage", "identf"], ["ps0"])
        self.dve(lambda e: e.tensor_copy(self.gains[:], self.ps[0][:, 0:88]), ["ps0"], ["gains"])
        self.dma("sp", self.gq[:].rearrange("p l d -> p (l d)"),
                 self.inp["attn_q_norm"].rearrange("l d -> (l d)").partition_broadcast(128), [], ["gq"])
        self.dma("sp", self.gk[:].rearrange("p l d -> p (l d)"),
                 self.inp["attn_k_norm"].rearrange("l d -> (l d)").partition_broadcast(128), [], ["gk"])
        self.dve(lambda e: e.tensor_scalar(self.gq[:], self.gq[:], 0.125, None, ALU.mult), ["gq"], ["gq"])

    def rsqrt(self, out, in_, scale, rkeys, wkey):
        self.act(lambda e: e.activation(out, in_, AF.Sqrt, bias=self.epsc[:out.shape[0], 0:1], scale=scale), list(rkeys) + ["epsc"], [wkey + "_s"])
        self.dve(lambda e: e.reciprocal(out, out), [wkey + "_s"], [wkey])

    def gain(self, name, l, c):
        i = NORM_IDS[name] * 16 + l * 8 + c
        return self.gains[:, i:i + 1]

    def program(self):
        cfg = self.cfg
        L = cfg.depth
        sa = self.stop_after
        if sa == "consts":
            return
        if sa == "dnonly":
            self.m_deltanet(0)
            return
        self.t_phase(0)
        if sa in ("T0", "load", "ffn", "n1", "n2"):
            return
        for l in range(L):
            self.m_phase(l)
            if sa == "M%d" % l:
                return
            self.t_phase(l + 1)
            if sa == "T%d" % (l + 1):
                return

    def m_phase(self, l):
        if self.overlap and "gqa" not in self.skip and "dn" not in self.skip:
            self.phase_begin()
            self.R.fork(2)
            self.own_phase = False
            self.R.use(0); self.m_gqa(l)
            self.R.use(1); self.m_deltanet(l)
            self.own_phase = True
            self.R.join()
            self.phase_end()
            if "four" not in self.skip:
                self.m_fourier(l)
            return
        if "gqa" not in self.skip:
            self.m_gqa(l)
        if "four" not in self.skip:
            self.m_fourier(l)
        if "dn" not in self.skip:
            self.m_deltanet(l)
        else:
            self.phase_begin()
            zz = self.sb("zz", [128, 2, self.cfg.ntok], BF16)
            self.dve(lambda e: e.memset(zz[:], 0.0), [], ["zz"])
            self.dma("sp", self.YT[256:512, :].rearrange("(c p) t -> p c t", p=128), zz[:], ["zz"], ["YT"])
            self.phase_end()

    def wplan_tile(self, ph, first_of_seq):
        L = self.cfg.depth
        blocks = []
        lp, ln = ph - 1, ph
        if ph >= 1:
            if first_of_seq:
                for j in range(8):
                    blocks.append(("mem_wkv", lp, "col", j))
                for k in range(8):
                    blocks.append(("mem_wkv", lp, "row", k, [(0, 1024, 1024)]))
            for k in range(8):
                blocks.append(("w_out", lp, "row", k, [(0, 0, 1024)]))
            for j in range(8):
                blocks.append(("mem_wq", lp, "col", j))
            for k in range(8):
                blocks.append(("mem_wo", lp, "row", k, [(0, 0, 1024)]))
            blocks += self.ffn_blocks("ffn2", lp)
        if ph < L:
            blocks += self.ffn_blocks("ffn1", ln)
            for k in range(8):
                blocks.append(("w_in", ln, "row", k, [(0, 0, 256), (256, 1024, 272)]))
            for k in range(8):
                blocks.append(("w_in", ln, "row", k, [(0, 1296, 768)]))
            for j in range(2, 8):
                blocks.append(("w_in", ln, "col", j))
        return blocks

    def ffn_blocks(self, pre, l):
        out = []
        for half in range(2):
            for jj in range(11 * half, 11 * half + 11):
                out.append((pre + "_w_gu", l, "col", jj))
                out.append((pre + "_w_gu", l, "col", NFB + jj))
            for k in range(11 * half, 11 * half + 11):
                out.append((pre + "_w_down", l, "row", k, [(0, 0, 1024)]))
        return out

    def ws_init(self, plan):
        self.ws_plan = plan
        self.ws_next_dma = 0
        self.ws_next_acq = 0
        self.ws_released = 0
        self.ws_fill()

    def ws_fill(self):
        while self.ws_next_dma < len(self.ws_plan) and self.ws_next_dma < self.ws_released + self.NS:
            b = self.ws_plan[self.ws_next_dma]
            s = self.ws_next_dma % self.NS
            slot = self.wslots[s]
            w = self.inp[b[0]]
            if b[2] == "col":
                j = b[3]
                src = w[b[1], :, j * 128:(j + 1) * 128].rearrange("(kc p) c -> p kc c", p=128)
                self.dma("pool", slot[:].rearrange("p (kc c) -> p kc c", kc=8), src, [], ["ws%d" % s])
            else:
                k = b[3]
                for (d0, c0, n) in b[4]:
                    self.dma("pool", slot[:, d0:d0 + n], w[b[1], k * 128:(k + 1) * 128, c0:c0 + n], [], ["ws%d" % s])
            self.ws_next_dma += 1

    def ws_acquire(self, name, kind):
        i = self.ws_next_acq
        b = self.ws_plan[i]
        assert b[0] == name and b[2] == kind, (b, name, kind)
        assert i < self.ws_next_dma, "weight block not yet scheduled (ring too small)"
        self.ws_next_acq += 1
        s = i % self.NS
        return s, self.wslots[s], "ws%d" % s

    def ws_release(self, n=1):
        self.ws_released += n
        self.ws_fill()

    def t_phase(self, ph):
        cfg = self.cfg
        L = cfg.depth
        self.phase_begin()
        TT, NSUB, TG, NTG = cfg.tt, cfg.nsub, cfg.tg, cfg.ntg
        self.NS = 16
        self.wslots = [self.sb("ws%d" % s, [128, 1024], BF16) for s in range(self.NS)]
        self.xt = self.sb("xt", [128, NSUB, D], F32)
        self.nT = self.sb("nT", [128, 8, TT], BF16)
        self.ov = self.sb("ov", [128, 24 * TT], BF16)
        self.hT = self.ov[:, 0:11 * TT].rearrange("p (c t) -> p c t", c=11)
        self.qT = self.ov[:, 0:8 * TT].rearrange("p (c t) -> p c t", c=8)
        self.pT = self.ov[:, 8 * TT:16 * TT].rearrange("p (c t) -> p c t", c=8)
        self.oT = self.ov[:, 16 * TT:24 * TT].rearrange("p (c t) -> p c t", c=8)
        self.yTt = self.ov[:, 8 * TT:16 * TT].rearrange("p (c t) -> p c t", c=8)
        self.memkT = self.sb("memkT", [128, 8, MEMT], BF16)
        self.memV = self.sb("memV", [128, 2, D], BF16)
        self.memnT = self.sb("memnT", [128, 8, MEMT], BF16)
        self.rbc = self.sb("rbc", [128, 2, 512], F32)
        self.cos = self.sb("cos", [128, NSUB, 2, 16], F32)
        self.sin = self.sb("sin", [128, NSUB, 2, 16], F32)
        if ph == L:
            self.gfin = self.sb("gfin", [128, D], F32)
            self.dma("sp", self.gfin[:], self.inp["final_norm"].partition_broadcast(128), [], ["gfin"])
        self.xn = self.sb("xn", [128, 2, D], BF16)
        self.junk = self.sb("junk", [128, D], BF16)
        self.ss = self.sb("ss", [128, 8], F32)
        self.rstd = self.sb("rstd", [128, 8], F32)
        self.sg = self.sb("sg", [128, 2, 512], F32)
        self.zt = self.sb("zt", [128, 1296], F32)
        self.zq = self.sb("zq", [128, 768], F32)
        self.zqb = self.sb("zqb", [128, 768], BF16)
        self.zsq = self.sb("zsq", [128, 640], F32)
        self.zn = self.sb("zn", [128, 640], F32)
        self.zr = self.sb("zr", [128, 640], F32)
        self.rt = self.sb("rt", [128, 4, 320], F32)
        self.zss = self.sb("zss", [128, 32], F32)
        self.qkst = self.sb("qkst", [128, 6, TT], BF16)
        self.vst = self.sb("vst", [128, 128], BF16)
        self.ufst = self.sb("ufst", [128, 256], BF16)
        self.zfm = self.sb("zfm", [128, 2, 512], BF16)
        tiles = []
        for si, (st, S) in enumerate(zip(cfg.starts, cfg.seqs)):
            for t0 in range(0, S, TT):
                tiles.append((si, st, S, t0))
        plan = []
        for (si, st, S, t0) in tiles:
            plan += self.wplan_tile(ph, t0 == 0)
        self.ws_init(plan)
        xsrc = self.inp["x"] if ph == 0 else self.XS
        xdst = self.y if ph == L else self.XS
        for (si, st, S, t0) in tiles:
            g0 = st + t0
            self.dma("sp", self.xt[:], xsrc[g0:g0 + TT, :].rearrange("(j p) d -> p j d", p=128),
                     [], ["x%d" % j for j in range(NSUB)])
            if ph >= 1:
                if t0 == 0:
                    self.mem_kv(ph - 1, si)
                self.wout(ph - 1, g0)
                self.mca(ph - 1)
                self.ffn("ffn2", ph - 1)
            if ph == L:
                self.final_norm()
            if ph < L:
                b0 = t0 // 128
                self.dma("sp", self.cos[:], self.cin["c_cos"][:, b0:b0 + NSUB], [], ["cos"])
                self.dma("sp", self.sin[:], self.cin["c_sin"][:, b0:b0 + NSUB], [], ["sin"])
                if self.stop_after != "load":
                    self.ffn("ffn1", ph)
                if self.stop_after not in ("load", "ffn", "n1", "n2"):
                    self.win(ph, g0, t0)
            self.dma("sp", xdst[g0:g0 + TT, :].rearrange("(j p) d -> p j d", p=128), self.xt[:],
                     ["x%d" % j for j in range(NSUB)], ["xdst"])
        self.phase_end()

    def norm_to_nT(self, name, l):
        cfg = self.cfg
        NSUB, TT = cfg.nsub, cfg.tt
        for j in range(NSUB):
            self.act(lambda e, j=j: e.activation(self.junk[:], self.xt[:, j, :], AF.Square,
                                                 accum_out=self.ss[:, j:j + 1]),
                     ["x%d" % j], ["junk", "ss%d" % j])
        self.rsqrt(self.rstd[:, 0:NSUB], self.ss[:, 0:NSUB], 1.0 / D, ["ss%d" % j for j in range(NSUB)], "rstd")
        if self.stop_after == "n1":
            return
        for j in range(NSUB):
            b = j % 2
            self.dve(lambda e, j=j, b=b: e.tensor_scalar(self.xn[:, b, :], self.xt[:, j, :], self.rstd[:, j:j + 1], None, ALU.mult),
                     ["x%d" % j, "rstd"], ["xn%d" % b])
            pb = 6 + (j % 2)
            psv = self.psb[pb]
            for c in range(8):
                self.pe(lambda e, c=c, b=b, psv=psv: e.transpose(psv[:, c * 128:(c + 1) * 128], self.xn[:, b, c * 128:(c + 1) * 128], self.identb[:]),
                        ["xn%d" % b, "identb"], ["ps%d" % pb])
            for c in range(8):
                self.dve(lambda e, c=c, j=j, psv=psv: e.tensor_scalar(self.nT[:, c, j * 128:(j + 1) * 128], psv[:, c * 128:(c + 1) * 128],
                                                                      self.gain(name, l, c), None, ALU.mult),
                         ["ps%d" % pb, "gains"], ["nT%d" % c])

    def ffn(self, pre, l):
        cfg = self.cfg
        NSUB, TT, TG, NTG = cfg.nsub, cfg.tt, cfg.tg, cfg.ntg
        self.norm_to_nT(pre + "_norm", l)
        nTr = ["nT%d" % c for c in range(8)]
        bank = 0
        if self.stop_after in ("n1", "n2"):
            return
        for half in range(2):
            for jl in range(11):
                sg_, slg, rg = self.ws_acquire(pre + "_w_gu", "col")
                su_, slu, ru = self.ws_acquire(pre + "_w_gu", "col")
                wg = slg[:].rearrange("p (kc c) -> p kc c", kc=8)
                wu = slu[:].rearrange("p (kc c) -> p kc c", kc=8)
                for tg in range(NTG):
                    bg, bu = bank % 6, (bank + 1) % 6
                    bank += 2
                    tsl = slice(tg * TG, (tg + 1) * TG)
                    for kc in range(8):
                        self.pe(lambda e, kc=kc, wg=wg, bg=bg, tsl=tsl: e.matmul(self.ps[bg][:, 0:TG], wg[:, kc, :], self.nT[:, kc, tsl], start=(kc == 0), stop=(kc == 7)),
                                [rg, "nT%d" % kc], ["ps%d" % bg])
                    for kc in range(8):
                        self.pe(lambda e, kc=kc, wu=wu, bu=bu, tsl=tsl: e.matmul(self.ps[bu][:, 0:TG], wu[:, kc, :], self.nT[:, kc, tsl], start=(kc == 0), stop=(kc == 7)),
                                [ru, "nT%d" % kc], ["ps%d" % bu])
                    sb_ = tg % 2
                    self.act(lambda e, bg=bg, sb_=sb_: e.activation(self.sg[:, sb_, 0:TG], self.ps[bg][:, 0:TG], AF.Silu),
                             ["ps%d" % bg], ["sg%d" % sb_])
                    self.dve(lambda e, bu=bu, sb_=sb_, jl=jl, tsl=tsl: e.tensor_tensor(self.hT[:, jl, tsl], self.sg[:, sb_, 0:TG], self.ps[bu][:, 0:TG], ALU.mult),
                             ["sg%d" % sb_, "ps%d" % bu], ["h%d" % jl])
                self.ws_release(2)
            slots = [self.ws_acquire(pre + "_w_down", "row") for _ in range(11)]
            for j in range(NSUB):
                for ch in range(2):
                    b = bank % 6; bank += 1
                    for k in range(11):
                        _, sl, rk = slots[k]
                        self.pe(lambda e, k=k, sl=sl, b=b, j=j, ch=ch: e.matmul(self.ps[b][:, :], self.hT[:, k, j * 128:(j + 1) * 128], sl[:, ch * 512:(ch + 1) * 512], start=(k == 0), stop=(k == 10)),
                                [rk, "h%d" % k], ["ps%d" % b])
                    self.dve(lambda e, b=b, j=j, ch=ch: e.scalar_tensor_tensor(self.xt[:, j, ch * 512:(ch + 1) * 512], self.ps[b][:, :], 0.5, self.xt[:, j, ch * 512:(ch + 1) * 512], ALU.mult, ALU.add),
                             ["ps%d" % b, "x%d" % j], ["x%d" % j])
            self.ws_release(11)

    def win(self, l, g0, t0):
        cfg = self.cfg
        NSUB, TT, TG, NTG = cfg.nsub, cfg.tt, cfg.tg, cfg.ntg
        self.norm_to_nT("mix_norm", l)
        slots = [self.ws_acquire("w_in", "row") for _ in range(8)]
        for j in range(NSUB):
            for (c0, n, zo) in ((0, 256, 0), (256, 272, 256)):
                b = (j * 2 + (c0 > 0)) % 6
                for kc in range(8):
                    _, sl, rk = slots[kc]
                    self.pe(lambda e, kc=kc, sl=sl, b=b, j=j, c0=c0, n=n: e.matmul(self.ps[b][:, 0:n], self.nT[:, kc, j * 128:(j + 1) * 128], sl[:, c0:c0 + n], start=(kc == 0), stop=(kc == 7)),
                            [rk, "nT%d" % kc], ["ps%d" % b])
                if c0 == 0:
                    self.act(lambda e, b=b: e.copy(self.ufst[:], self.ps[b][:, 0:256]), ["ps%d" % b], ["ufst"])
                    self.dma("sp", self.UF[g0 + j * 128:g0 + (j + 1) * 128, :], self.ufst[:], ["ufst"], ["UF"])
                else:
                    self.dve(lambda e, b=b: e.tensor_copy(self.zt[:, 0:272], self.ps[b][:, 0:272]), ["ps%d" % b], ["zab"])
                    self.dma("sp", self.ZAB[g0 + j * 128:g0 + (j + 1) * 128, :], self.zt[:, 0:272], ["zab"], ["ZAB"])
        self.ws_release(8)
        slots = [self.ws_acquire("w_in", "row") for _ in range(8)]
        for j in range(NSUB):
            for (c0, n) in ((0, 512), (512, 256)):
                b = (j * 2 + (c0 > 0)) % 6
                for kc in range(8):
                    _, sl, rk = slots[kc]
                    self.pe(lambda e, kc=kc, sl=sl, b=b, j=j, c0=c0, n=n: e.matmul(self.ps[b][:, 0:n], self.nT[:, kc, j * 128:(j + 1) * 128], sl[:, c0:c0 + n], start=(kc == 0), stop=(kc == 7)),
                            [rk, "nT%d" % kc], ["ps%d" % b])
                if c0 == 0:
                    self.act(lambda e, b=b: e.copy(self.zq[:, 0:512], self.ps[b][:, 0:512]), ["ps%d" % b], ["zq"])
                else:
                    self.dve(lambda e, b=b: e.tensor_copy(self.zq[:, 512:768], self.ps[b][:, 0:256]), ["ps%d" % b], ["zk"])
            self.qk_post(l, j, g0, t0)
        self.ws_release(8)
        for jb in range(6):
            _, sl, rk = self.ws_acquire("w_in", "col")
            wv = sl[:].rearrange("p (kc c) -> p kc c", kc=8)
            for tg in range(NTG):
                b = (jb * NTG + tg) % 6
                tsl = slice(tg * TG, (tg + 1) * TG)
                for kc in range(8):
                    self.pe(lambda e, kc=kc, wv=wv, b=b, tsl=tsl: e.matmul(self.ps[b][:, 0:TG], wv[:, kc, :], self.nT[:, kc, tsl], start=(kc == 0), stop=(kc == 7)),
                            [rk, "nT%d" % kc], ["ps%d" % b])
                s2 = (jb * NTG + tg) % 2
                self.act(lambda e, b=b, s2=s2: e.copy(self.zfm[:, s2, 0:TG], self.ps[b][:, 0:TG]), ["ps%d" % b], ["zfm%d" % s2])
                self.dma("sp", self.ZT[jb * 128:(jb + 1) * 128, g0 + tg * TG:g0 + (tg + 1) * TG], self.zfm[:, s2, 0:TG], ["zfm%d" % s2], ["ZT"])
            self.ws_release(1)
        for pr in range(4):
            self.dma("sp", self.QT[pr, :, g0:g0 + TT], self.qkst[:, pr, :], ["qkst"], ["QT"])
        for g in range(2):
            self.dma("sp", self.KT[g, :, g0:g0 + TT], self.qkst[:, 4 + g, :], ["qkst"], ["KT"])

    def qk_post(self, l, j, g0, t0):
        blk = (t0 // 128) + j
        zq, zsq, zn, zr, zqb, tt_ = self.zq, self.zsq, self.zn, self.zr, self.zqb, self.rt
        hv = lambda ap: ap.rearrange("p (h d) -> p h d", d=64)
        self.dve(lambda e: e.tensor_tensor(zsq[:, 0:640], zq[:, 0:640], zq[:, 0:640], ALU.mult), ["zq", "zk"], ["zsq"])
        self.dve(lambda e: e.tensor_reduce(self.zss[:, 0:10], hv(zsq[:, 0:640]), AX.X, ALU.add), ["zsq"], ["zss"])
        self.rsqrt(self.zss[:, 16:26], self.zss[:, 0:10], 1.0 / 64, ["zss"], "zrs")
        rs = self.zss[:, 16:26]
        self.dve(lambda e: e.tensor_tensor(hv(zn[:, 0:640]), hv(zq[:, 0:640]), rs.unsqueeze(2).to_broadcast([128, 10, 64]), ALU.mult),
                 ["zq", "zk", "zrs"], ["zn"])
        self.dve(lambda e: e.tensor_tensor(hv(zn[:, 0:512]), hv(zn[:, 0:512]), self.gq[:, l:l + 1, :].to_broadcast([128, 8, 64]), ALU.mult),
                 ["zn", "gq"], ["zn"])
        self.dve(lambda e: e.tensor_tensor(hv(zn[:, 512:640]), hv(zn[:, 512:640]), self.gk[:, l:l + 1, :].to_broadcast([128, 2, 64]), ALU.mult),
                 ["zn", "gk"], ["zn"])
        xv = zn[:, 0:640].rearrange("p (h a t f) -> p h a t f", h=10, a=2, t=2)
        ov = zr[:, 0:640].rearrange("p (h a t f) -> p h a t f", h=10, a=2, t=2)
        x1, x2 = xv[:, :, :, 0, :], xv[:, :, :, 1, :]
        cb = self.cos[:, j, :, :].unsqueeze(1).to_broadcast([128, 10, 2, 16])
        sb_ = self.sin[:, j, :, :].unsqueeze(1).to_broadcast([128, 10, 2, 16])
        t = [tt_[:, i, :].rearrange("p (h a f) -> p h a f", h=10, a=2) for i in range(4)]
        self.dve(lambda e: e.tensor_tensor(t[0], x1, cb, ALU.mult), ["zn", "cos"], ["rt0"])
        self.dve(lambda e: e.tensor_tensor(t[1], x2, sb_, ALU.mult), ["zn", "sin"], ["rt1"])
        self.dve(lambda e: e.tensor_tensor(t[2], x2, cb, ALU.mult), ["zn", "cos"], ["rt2"])
        self.dve(lambda e: e.tensor_tensor(t[3], x1, sb_, ALU.mult), ["zn", "sin"], ["rt3"])
        self.dve(lambda e: e.tensor_tensor(ov[:, :, :, 0, :], t[0], t[1], ALU.subtract), ["rt0", "rt1"], ["zr"])
        self.dve(lambda e: e.tensor_tensor(ov[:, :, :, 1, :], t[2], t[3], ALU.add), ["rt2", "rt3"], ["zr"])
        self.act(lambda e: e.copy(zqb[:, 0:512], zr[:, 0:512]), ["zr"], ["zqb"])
        self.act(lambda e: e.copy(zqb[:, 512:768].rearrange("p (g u d) -> p g u d", g=2, u=2),
                                  zr[:, 512:640].rearrange("p (g d) -> p g d", g=2).unsqueeze(2).to_broadcast([128, 2, 2, 64])),
                 ["zr"], ["zqb"])
        self.act(lambda e: e.copy(self.vst[:], zq[:, 640:768]), ["zk"], ["vst"])
        self.dma("sp", self.VV[g0 + j * 128:g0 + (j + 1) * 128, :], self.vst[:], ["vst"], ["VV"])
        pb = 6 + (j % 2)
        psv = self.psb[pb]
        for i in range(6):
            self.pe(lambda e, i=i, psv=psv: e.transpose(psv[:, i * 128:(i + 1) * 128], zqb[:, i * 128:(i + 1) * 128], self.identb[:]),
                    ["zqb", "identb"], ["ps%d" % pb])
        self.dve(lambda e, psv=psv: e.tensor_copy(self.qkst[:, :, j * 128:(j + 1) * 128], psv[:, 0:768].rearrange("p (i t) -> p i t", i=6)),
                 ["ps%d" % pb], ["qkst"])

    def mem_kv(self, l, si):
        mt = self.zt
        mnT = self.qkst
        mnTv = self.memnT
        for j in range(2):
            self.dma("sp", mt[:, 0:D], self.inp["mem"][si * MEMT + j * 128: si * MEMT + (j + 1) * 128, :], [], ["zab"])
            self.act(lambda e: e.activation(self.junk[:], mt[:, 0:D], AF.Square, accum_out=self.ss[:, 0:1]), ["zab"], ["junk", "ss0"])
            self.rsqrt(self.rstd[:, 0:1], self.ss[:, 0:1], 1.0 / D, ["ss0"], "rstd")
            self.dve(lambda e: e.tensor_scalar(self.xn[:, 0, :], mt[:, 0:D], self.rstd[:, 0:1], None, ALU.mult), ["zab", "rstd"], ["xn0"])
            pb = 6 + j
            psv = self.psb[pb]
            for c in range(8):
                self.pe(lambda e, c=c, psv=psv: e.transpose(psv[:, c * 128:(c + 1) * 128], self.xn[:, 0, c * 128:(c + 1) * 128], self.identb[:]),
                        ["xn0", "identb"], ["ps%d" % pb])
            for c in range(8):
                self.dve(lambda e, c=c, j=j, psv=psv: e.tensor_scalar(mnTv[:, c, j * 128:(j + 1) * 128], psv[:, c * 128:(c + 1) * 128],
                                                                      self.gain("mem_norm_m", l, c), None, ALU.mult),
                         ["ps%d" % pb, "gains"], ["qkst"])
        for jb in range(8):
            _, sl, rk = self.ws_acquire("mem_wkv", "col")
            wv = sl[:].rearrange("p (kc c) -> p kc c", kc=8)
            b = jb % 6
            for kc in range(8):
                self.pe(lambda e, kc=kc, wv=wv, b=b: e.matmul(self.ps[b][:, 0:MEMT], wv[:, kc, :], mnTv[:, kc, :], start=(kc == 0), stop=(kc == 7)),
                        [rk, "qkst"], ["ps%d" % b])
            self.act(lambda e, b=b, jb=jb: e.copy(self.memkT[:, jb, :], self.ps[b][:, 0:MEMT]), ["ps%d" % b], ["memkT"])
            self.ws_release(1)
        slots = [self.ws_acquire("mem_wkv", "row") for _ in range(8)]
        for mc in range(2):
            for ch in range(2):
                b = (mc * 2 + ch) % 6
                for kc in range(8):
                    _, sl, rk = slots[kc]
                    self.pe(lambda e, kc=kc, sl=sl, b=b, mc=mc, ch=ch: e.matmul(self.ps[b][:, :], mnTv[:, kc, mc * 128:(mc + 1) * 128], sl[:, ch * 512:(ch + 1) * 512], start=(kc == 0), stop=(kc == 7)),
                            [rk, "qkst"], ["ps%d" % b])
                self.act(lambda e, b=b, mc=mc, ch=ch: e.copy(self.memV[:, mc, ch * 512:(ch + 1) * 512], self.ps[b][:, :]), ["ps%d" % b], ["memV"])
        self.ws_release(8)

    def tm_linear_acc(self, name, aT, akey, nk, scale):
        cfg = self.cfg
        slots = [self.ws_acquire(name, "row") for _ in range(nk)]
        for j in range(cfg.nsub):
            for ch in range(2):
                b = (j * 2 + ch) % 6
                for k in range(nk):
                    _, sl, rk = slots[k]
                    self.pe(lambda e, k=k, sl=sl, b=b, j=j, ch=ch: e.matmul(self.ps[b][:, :], aT[:, k, j * 128:(j + 1) * 128], sl[:, ch * 512:(ch + 1) * 512], start=(k == 0), stop=(k == nk - 1)),
                            [rk, akey], ["ps%d" % b])
                self.dve(lambda e, b=b, j=j, ch=ch: e.scalar_tensor_tensor(self.xt[:, j, ch * 512:(ch + 1) * 512], self.ps[b][:, :], scale, self.xt[:, j, ch * 512:(ch + 1) * 512], ALU.mult, ALU.add),
                         ["ps%d" % b, "x%d" % j], ["x%d" % j])
        self.ws_release(nk)

    def wout(self, l, g0):
        TT = self.cfg.tt
        self.dma("sp", self.yTt[:], self.YT[:, g0:g0 + TT].rearrange("(c p) t -> p c t", p=128), [], ["yTt"])
        self.tm_linear_acc("w_out", self.yTt, "yTt", 8, 1.0)

    def mca(self, l):
        cfg = self.cfg
        NSUB, TT, TG, NTG = cfg.nsub, cfg.tt, cfg.tg, cfg.ntg
        self.norm_to_nT("mem_norm_x", l)
        for jb in range(8):
            _, sl, rk = self.ws_acquire("mem_wq", "col")
            wv = sl[:].rearrange("p (kc c) -> p kc c", kc=8)
            for tg in range(NTG):
                b = (jb * NTG + tg) % 6
                tsl = slice(tg * TG, (tg + 1) * TG)
                for kc in range(8):
                    self.pe(lambda e, kc=kc, wv=wv, b=b, tsl=tsl: e.matmul(self.ps[b][:, 0:TG], wv[:, kc, :], self.nT[:, kc, tsl], start=(kc == 0), stop=(kc == 7)),
                            [rk, "nT%d" % kc], ["ps%d" % b])
                self.act(lambda e, b=b, jb=jb, tsl=tsl: e.copy(self.qT[:, jb, tsl], self.ps[b][:, 0:TG]), ["ps%d" % b], ["qT%d" % jb])
            self.ws_release(1)
        bank = 0
        for tg in range(NTG):
            tsl = slice(tg * TG, (tg + 1) * TG)
            for h in range(4):
                for mc in range(2):
                    b = bank % 6; bank += 1
                    for dc in range(2):
                        self.pe(lambda e, b=b, h=h, mc=mc, dc=dc, tsl=tsl: e.matmul(self.ps[b][:, 0:TG], self.memkT[:, 2 * h + dc, mc * 128:(mc + 1) * 128], self.qT[:, 2 * h + dc, tsl], start=(dc == 0), stop=(dc == 1)),
                                ["memkT", "qT%d" % (2 * h + dc)], ["ps%d" % b])
                    self.act(lambda e, b=b, h=h, mc=mc, tsl=tsl: e.activation(self.pT[:, 2 * h + mc, tsl], self.ps[b][:, 0:TG], AF.Exp, scale=1.0 / 16.0),
                             ["ps%d" % b], ["pT%d" % (2 * h + mc)])
                b = bank % 6; bank += 1
                for mc in range(2):
                    self.pe(lambda e, b=b, h=h, mc=mc, tsl=tsl: e.matmul(self.ps[b][:, 0:TG], self.onesb[:], self.pT[:, 2 * h + mc, tsl], start=(mc == 0), stop=(mc == 1)),
                            ["onesb", "pT%d" % (2 * h + mc)], ["ps%d" % b])
                rb = h % 2
                self.dve(lambda e, b=b, rb=rb: e.reciprocal(self.rbc[:, rb, 0:TG], self.ps[b][:, 0:TG]), ["ps%d" % b], ["rbc%d" % rb])
                for dc in range(2):
                    b = bank % 6; bank += 1
                    for mc in range(2):
                        self.pe(lambda e, b=b, h=h, mc=mc, dc=dc, tsl=tsl: e.matmul(self.ps[b][:, 0:TG], self.memV[:, mc, (2 * h + dc) * 128:(2 * h + dc + 1) * 128], self.pT[:, 2 * h + mc, tsl], start=(mc == 0), stop=(mc == 1)),
                                ["memV", "pT%d" % (2 * h + mc)], ["ps%d" % b])
                    self.dve(lambda e, b=b, h=h, dc=dc, rb=rb, tsl=tsl: e.tensor_tensor(self.oT[:, 2 * h + dc, tsl], self.ps[b][:, 0:TG], self.rbc[:, rb, 0:TG], ALU.mult),
                             ["ps%d" % b, "rbc%d" % rb], ["oT"])
        self.tm_linear_acc("mem_wo", self.oT, "oT", 8, 1.0)

    def final_norm(self):
        cfg = self.cfg
        NSUB = cfg.nsub
        for j in range(NSUB):
            self.act(lambda e, j=j: e.activation(self.junk[:], self.xt[:, j, :], AF.Square, accum_out=self.ss[:, j:j + 1]),
                     ["x%d" % j], ["junk", "ss%d" % j])
        self.rsqrt(self.rstd[:, 0:NSUB], self.ss[:, 0:NSUB], 1.0 / D, ["ss%d" % j for j in range(NSUB)], "rstd")
        for j in range(NSUB):
            self.dve(lambda e, j=j: e.scalar_tensor_tensor(self.xt[:, j, :], self.xt[:, j, :], self.rstd[:, j:j + 1], self.gfin[:], ALU.mult, ALU.mult),
                     ["x%d" % j, "rstd", "gfin"], ["x%d" % j])

    def m_gqa(self, l):
        cfg = self.cfg
        self.phase_begin()
        SM = max(cfg.seqs)
        NBM = SM // 128
        KTs = self.sb("KTs", [128, 2, SM], BF16)
        Vx = self.sb("Vx", [128, NBM, 2, 65], BF16)
        onesf = self.sb("onesf", [128, 64], F32)
        QW = min(512, min(cfg.seqs))
        qt = self.sb("qtile", [128, 4, QW], BF16)
        pT2 = self.sb("pTa", [128, 12, QW], BF16)
        rr2 = self.sb("rr", [128, 4, QW], F32)
        bcs2 = self.sb("bcs", [64, 4, QW], F32)
        yst = self.sb("yst", [64, 4, QW], BF16)
        self.dve(lambda e: e.memset(onesf[:], 1.0), [], ["onesf"])
        self.dve(lambda e: e.memset(Vx[:], 1.0), [], ["Vx"])
        it = 0
        for si, (st, S) in enumerate(zip(cfg.starts, cfg.seqs)):
            NB = S // 128
            for g in range(2):
                self.dma("sp", KTs[:, g, 0:S], self.KT[g, :, st:st + S], [], ["KTs"])
            for g in range(2):
                self.dma("sp", Vx[:, 0:NB, g, 0:64], self.VV[st:st + S, g * 64:(g + 1) * 64].rearrange("(kb p) d -> p kb d", p=128), [], ["Vx"])
            nqb = S // QW
            G = 2 if nqb % 2 == 0 else 1
            for pr in range(4):
                g = pr // 2
                for qb0 in range(0, nqb, G):
                    par = it % 2; it += 1
                    chains = []
                    for gi in range(G):
                        qs = par * 2 + gi
                        qb = qb0 + gi
                        self.dma("sp", qt[:, qs, :], self.QT[pr, :, st + qb * QW: st + (qb + 1) * QW], [], ["qt%d" % qs])
                        for hh in range(2):
                            chains.append((gi * 2 + hh, qs, qb, hh))
                    for kb in range(NB + 1):
                        if kb < NB:
                            for (c, qs, qb, hh) in chains:
                                base = hh * 64
                                pk = c * 3 + kb % 3
                                self.pe(lambda e, c=c, kb=kb, g=g, base=base, qs=qs: e.matmul(self.ps[c][:, 0:QW], KTs[base:base + 64, g, kb * 128:(kb + 1) * 128], qt[base:base + 64, qs, :], start=True, stop=True),
                                        ["KTs", "qt%d" % qs], ["ps%d" % c])
                                self.act(lambda e, c=c, pk=pk: e.activation(pT2[:, pk, :], self.ps[c][:, 0:QW], AF.Exp), ["ps%d" % c], ["pTa%d" % pk])
                        if kb >= 1:
                            k2 = kb - 1
                            for (c, qs, qb, hh) in chains:
                                ob = 4 + c
                                pk = c * 3 + k2 % 3
                                self.pe(lambda e, k2=k2, g=g, ob=ob, NB=NB, pk=pk: e.matmul(self.ps[ob][0:65, 0:QW], Vx[:, k2, g, :], pT2[:, pk, :], start=(k2 == 0), stop=(k2 == NB - 1)),
                                        ["Vx", "pTa%d" % pk], ["ps%d" % ob])
                    for (c, qs, qb, hh) in chains:
                        h = 2 * pr + hh
                        ob = 4 + c
                        self.dve(lambda e, ob=ob, c=c: e.reciprocal(rr2[64:65, c, :], self.ps[ob][64:65, 0:QW]), ["ps%d" % ob], ["rr%d" % c])
                        self.pe(lambda e, c=c: e.matmul(self.ps[c][0:64, 0:QW], onesf[64:65, 0:64], rr2[64:65, c, :], start=True, stop=True), ["onesf", "rr%d" % c], ["ps%d" % c])
                        self.act(lambda e, c=c: e.copy(bcs2[:, c, :], self.ps[c][0:64, 0:QW]), ["ps%d" % c], ["bcs%d" % c])
                        self.dve(lambda e, ob=ob, c=c: e.tensor_tensor(yst[:, c, :], self.ps[ob][0:64, 0:QW], bcs2[:, c, :], ALU.mult), ["ps%d" % ob, "bcs%d" % c], ["yst%d" % c])
                        self.dma("sp", self.YT[512 + h * 64: 512 + (h + 1) * 64, st + qb * QW: st + (qb + 1) * QW], yst[:, c, :], ["yst%d" % c], ["YT"])
        self.phase_end()

    def m_fourier(self, l):
        cfg = self.cfg
        self.phase_begin()
        SM = max(cfg.seqs)
        N1M = SM // 128
        fc = self.sb("fc", [64, 2, 64], BF16)
        wf = self.sb("wf", [64, 4, 64], BF16)
        AB = self.sb("AB", [64, 4, 2, 64], BF16)
        f1 = self.sb("f1", [64, 3 * N1M], BF16)
        Ht = self.sb("Ht", [128, N1M, 2, 128], BF16)
        Ut = self.sb("Ut", [64, 128, 64], BF16)
        Ot = self.sb("Ot", [128, N1M, 3, 64], BF16)
        PT = self.sb("PTf", [64, 2, SM], BF16)
        yst = self.sb("ystf", [64, 2, 512], BF16)
        self.dma("sp", fc[:], self.cin["c_fc"], [], ["fc"])
        self.dma("pool", wf[:], self.inp["four_w"][l].rearrange("g c d -> c g d"), [], ["wf"])
        for g in range(4):
            for x in range(2):
                self.pe(lambda e, g=g, x=x: e.matmul(self.ps[0][0:64, (g * 2 + x) * 64:(g * 2 + x + 1) * 64], fc[:, x, :], wf[:, g, :], start=True, stop=True),
                        ["fc", "wf"], ["ps0"])
        self.dve(lambda e: e.tensor_copy(AB[:].rearrange("c g x d -> c (g x d)"), self.ps[0][0:64, :]), ["ps0"], ["AB"])
        for si, (st, S) in enumerate(zip(cfg.starts, cfg.seqs)):
            N1 = S // 128
            self.dma("sp", f1[0:N1, 0:3 * N1], self.cin["c_f1_%d" % S], [], ["f1"])
            self.dma("sp", Ht[:, 0:N1, :, :], self.cin["c_h_%d" % S], [], ["Ht"])
            CB = 1
            while CB * 2 <= min(64, 512 // (3 * N1)):
                CB *= 2
            for g in range(4):
                self.dma("sp", Ut[0:N1, :, :], self.UF[st:st + S, g * 64:(g + 1) * 64].rearrange("(n1 n2) c -> n1 n2 c", n2=128), [], ["Ut"])
                for c0 in range(0, 64, CB):
                    b = (c0 // CB) % 3
                    for cl in range(CB):
                        c = c0 + cl
                        self.pe(lambda e, b=b, cl=cl, c=c, N1=N1: e.matmul(self.ps[b][:, cl * 3 * N1:(cl + 1) * 3 * N1], Ut[0:N1, :, c], f1[0:N1, 0:3 * N1], start=True, stop=True),
                                ["Ut", "f1"], ["ps%d" % b])
                    self.dve(lambda e, b=b, c0=c0, N1=N1, CB=CB: e.tensor_copy(Ot[:, 0:N1, :, c0:c0 + CB].rearrange("p k x c -> p c x k"),
                                                                             self.ps[b][:, 0:CB * 3 * N1].rearrange("p (c x k) -> p c x k", c=CB, x=3)),
                             ["ps%d" % b], ["Ot"])
                KB = min(4, N1)
                for k0 in range(0, N1, KB):
                    for x in range(2):
                        b = 3 + ((k0 // KB) * 2 + x) % 3
                        for kl in range(KB):
                            k1 = k0 + kl
                            o = self.ps[b][0:64, kl * 128:(kl + 1) * 128]
                            if x == 0:
                                self.pe(lambda e, o=o, k1=k1: e.matmul(o, Ot[:, k1, 0, :], Ht[:, k1, 0, :], start=True, stop=False), ["Ot", "Ht"], ["ps%d" % b])
                                self.pe(lambda e, o=o, k1=k1: e.matmul(o, Ot[:, k1, 2, :], Ht[:, k1, 1, :], start=False, stop=True), ["Ot", "Ht"], ["ps%d" % b])
                            else:
                                self.pe(lambda e, o=o, k1=k1: e.matmul(o, Ot[:, k1, 0, :], Ht[:, k1, 1, :], start=True, stop=False), ["Ot", "Ht"], ["ps%d" % b])
                                self.pe(lambda e, o=o, k1=k1: e.matmul(o, Ot[:, k1, 1, :], Ht[:, k1, 0, :], start=False, stop=True), ["Ot", "Ht"], ["ps%d" % b])
                        dst = PT[:, x, 0:S].rearrange("c (k2 k1) -> c k1 k2", k1=N1)[:, k0:k0 + KB, :]
                        eng = self.act if x == 0 else self.dve
                        if x == 0:
                            self.act(lambda e, b=b, dst=dst, KB=KB: e.copy(dst, self.ps[b][0:64, 0:KB * 128].rearrange("c (k f) -> c k f", k=KB)), ["ps%d" % b], ["PTf"])
                        else:
                            self.dve(lambda e, b=b, dst=dst, KB=KB: e.tensor_copy(dst, self.ps[b][0:64, 0:KB * 128].rearrange("c (k f) -> c k f", k=KB)), ["ps%d" % b], ["PTf"])
                SW = min(512, S)
                for s0 in range(0, S, SW):
                    b = 6 + (s0 // SW) % 2
                    ys = (s0 // SW) % 2
                    self.pe(lambda e, b=b, g=g, s0=s0, SW=SW: e.matmul(self.ps[b][0:64, 0:SW], AB[:, g, 0, :], PT[:, 0, s0:s0 + SW], start=True, stop=False), ["AB", "PTf"], ["ps%d" % b])
                    self.pe(lambda e, b=b, g=g, s0=s0, SW=SW: e.matmul(self.ps[b][0:64, 0:SW], AB[:, g, 1, :], PT[:, 1, s0:s0 + SW], start=False, stop=True), ["AB", "PTf"], ["ps%d" % b])
                    self.act(lambda e, b=b, ys=ys, SW=SW: e.copy(yst[:, ys, 0:SW], self.ps[b][0:64, 0:SW]), ["ps%d" % b], ["ystf%d" % ys])
                    self.dma("sp", self.YT[g * 64:(g + 1) * 64, st + s0: st + s0 + SW], yst[:, ys, 0:SW], ["ystf%d" % ys], ["YT"])
        self.phase_end()

    def nb(self):
        self._bank = (getattr(self, "_bank", -1) + 1) % 3
        return 4 + self._bank

    def nbt(self):
        return 7

    def m_deltanet(self, l):
        cfg = self.cfg
        self.phase_begin()
        W = min(512, min(cfg.seqs)); NCH = W // 64; NCP = NCH // 2; NU = NCH * 4
        idb = self.identb[0:64, 0:64]; idf = self.identf[0:64, 0:64]; onb = self.onesb[0:64, 0:64]
        sbt = lambda n, sh, dt: self.sb(n, sh, dt)
        cw = sbt("cw", [64, 3, 12], F32); dg = sbt("dg", [64, 12, 3, 64], BF16)
        mk = sbt("mk", [64, 4, 64], F32); tri = sbt("tri", [64, 3, 64], F32)
        nA = sbt("nA", [64, 8], F32); dtb = sbt("dtb", [64, 8], F32)
        zin = sbt("zin", [64, 12, W + 2], BF16); zab = sbt("zab", [64, NCH, 16], F32)
        cs = sbt("cs", [64, 2, W], F32); sq = sbt("sq", [64, 2, W], BF16); rs = sbt("rs", [64, 2, W], F32)
        qTn = sbt("qTn", [64, NU, 64], BF16); kTn = sbt("kTn", [64, NU, 64], BF16); vTn = sbt("vTn", [64, NU, 64], BF16)
        k_tm = sbt("k_tm", [64, NU, 64], BF16); v_tm = sbt("v_tm", [64, NU, 64], BF16)
        gt = sbt("gt", [64, 8, NU], F32)
        dgx = sbt("dgx", [64, 8, 64], F32); dl = sbt("dl", [64, 8, 64], F32); egr = sbt("egr", [64, 8, 64], F32)
        t1 = sbt("t1", [64, 2, 8, 64], F32); DTt = sbt("DTt", [64, 8, 64], F32); DLt = sbt("DLt", [64, 8, 64], F32)
        qdec = sbt("qdec", [64, NU, 64], BF16); qkTm = sbt("qkTm", [64, NU, 64], BF16)
        Lp = [sbt("Lp%d" % i, [64, NU, 64], BF16) for i in range(2)]
        Up = [sbt("Up%d" % i, [64, NU, 64], BF16) for i in range(2)]
        Rp = [sbt("Rp%d" % i, [64, NU, 64], BF16) for i in range(2)]
        vb = sbt("vb", [64, NU, 64], BF16); kbe = sbt("kbe", [64, NU, 64], BF16); ktl = sbt("ktl", [64, NU, 64], BF16)
        u_sb = sbt("u_sb", [64, NU, 64], F32); wT = sbt("wT", [64, NU, 64], BF16)
        Sf = sbt("Sf", [64, 4, 64], F32); Sb = sbt("Sb", [64, 4, 64], BF16)
        vnew = sbt("vnew", [64, 2, 4, 64], BF16); ost = sbt("ost", [64, 2, 256], F32)
        o2 = sbt("o2", [128, 2, 256], F32); gte = sbt("gte", [128, 256], F32); osum = sbt("osum", [128, 256], F32)
        pss = sbt("pss", [128, 8], F32); ydb = sbt("ydb", [128, 256], BF16); gdn = sbt("gdn", [128, 64], F32)
        ydT = sbt("ydT", [128, 2, W], BF16)
        self.dma("sp", cw[:], self.inp["dn_conv"][l].rearrange("t (xh d) -> d t xh", d=64), [], ["cw"])
        self.dma("sp", mk[:], self.cin["c_dnmask"], [], ["mk"])
        self.dma("sp", tri[:], self.cin["c_tri"], [], ["tri"])
        self.dma("sp", nA[:], self.inp["dn_A_log"][l].rearrange("a h -> (a h)").partition_broadcast(64), [], ["nA"])
        self.dma("sp", dtb[:], self.inp["dn_dt_bias"][l].rearrange("a h -> (a h)").partition_broadcast(64), [], ["dtb"])
        self.dma("sp", gdn[:], self.inp["dn_out_norm"][l].partition_broadcast(128), [], ["gdn"])
        self.act(lambda e: e.activation(nA[:], nA[:], AF.Exp), ["nA"], ["nA"])
        self.dve(lambda e: e.tensor_scalar(nA[:], nA[:], -1.0, None, ALU.mult), ["nA"], ["nA"])
        for xh in range(12):
            for t in range(3):
                self.dve(lambda e, xh=xh, t=t: e.tensor_scalar(dg[:, xh, t, :], idb, cw[:, t, xh:xh + 1], None, ALU.mult), ["cw", "identb"], ["dg"])
        one_col = tri[:, 2, 0:1]
        B8 = lambda ap: ap.unsqueeze(2).to_broadcast([64, 8, 64])
        B4 = lambda ap: ap.unsqueeze(2).to_broadcast([64, 4, 64])
        psv = lambda b: self.ps[b][0:64, :].rearrange("p (u f) -> p u f", f=64)
        psbv = lambda b: self.psb[b][0:64, 0:512].rearrange("p (u f) -> p u f", f=64)

        for si, (st, S) in enumerate(zip(cfg.starts, cfg.seqs)):
            NTL = S // W
            for d in range(2):
                kT_, kL_ = (0, 1) if d == 0 else (2, 3)
                self.dve(lambda e: e.memset(Sf[:], 0.0), [], ["Sf"])
                self.dve(lambda e: e.memset(Sb[:], 0.0), [], ["Sb"])
                order = range(NTL) if d == 0 else range(NTL - 1, -1, -1)
                for ti in order:
                    p0 = ti * W
                    lo, hi = max(p0 - 1, 0), min(p0 + W + 1, S)
                    c0 = lo - (p0 - 1)
                    if p0 == 0:
                        self.dve(lambda e: e.memset(zin[:, :, 0:1], 0.0), [], ["zin"])
                    if p0 + W == S:
                        self.dve(lambda e: e.memset(zin[:, :, W + 1:W + 2], 0.0), [], ["zin"])
                    for x in range(3):
                        self.dma("sp", zin[:, x * 4:(x + 1) * 4, c0:c0 + (hi - lo)],
                                 self.ZT[x * 256:(x + 1) * 256, st + lo:st + hi].rearrange("(h d) t -> d h t", d=64), [], ["zin"])
                    self.dma("sp", zab[:], self.ZAB[st + p0:st + p0 + W, 0:16].rearrange("(n c) x -> c n x", c=64), [], ["zab"])
                    for xh in range(12):
                        x, h = xh // 4, xh % 4
                        b = self.nb()
                        for t in range(3):
                            self.pe(lambda e, b=b, xh=xh, t=t: e.matmul(self.ps[b][0:64, 0:W], dg[:, xh, t, :], zin[:, xh, t:t + W], start=(t == 0), stop=(t == 2)),
                                    ["dg", "zin"], ["ps%d" % b])
                        if x == 2:
                            self.act(lambda e, b=b, h=h: e.activation(vTn[:].rearrange("p (n h) f -> p n h f", h=4)[:, :, h, :], self.ps[b][0:64, 0:W].rearrange("p (n f) -> p n f", f=64), AF.Silu),
                                     ["ps%d" % b], ["vTn"])
                            continue
                        s2 = xh % 2
                        self.act(lambda e, b=b, s2=s2: e.activation(cs[:, s2, :], self.ps[b][0:64, 0:W], AF.Silu), ["ps%d" % b], ["cs%d" % s2])
                        self.act(lambda e, s2=s2: e.activation(sq[:, s2, :], cs[:, s2, :], AF.Square), ["cs%d" % s2], ["sq%d" % s2])
                        b2 = self.nb()
                        self.pe(lambda e, b2=b2, s2=s2: e.matmul(self.ps[b2][0:64, 0:W], onb, sq[:, s2, :], start=True, stop=True), ["onesb", "sq%d" % s2], ["ps%d" % b2])
                        self.act(lambda e, b2=b2, s2=s2: e.activation(rs[:, s2, :], self.ps[b2][0:64, 0:W], AF.Sqrt, bias=self.epsc[0:64, 0:1], scale=1.0), ["ps%d" % b2, "epsc"], ["rs%d_s" % s2])
                        self.dve(lambda e, s2=s2: e.reciprocal(rs[:, s2, :], rs[:, s2, :]), ["rs%d_s" % s2], ["rs%d" % s2])
                        dstT = (qTn if x == 0 else kTn)[:].rearrange("p (n h) f -> p n h f", h=4)[:, :, h, :]
                        csv = cs[:, s2, :].rearrange("p (n f) -> p n f", f=64); rsv = rs[:, s2, :].rearrange("p (n f) -> p n f", f=64)
                        if x == 0:
                            self.dve(lambda e, dstT=dstT, csv=csv, rsv=rsv: e.scalar_tensor_tensor(dstT, csv, 0.125, rsv, ALU.mult, ALU.mult), ["cs%d" % s2, "rs%d" % s2], ["qTn"])
                        else:
                            self.dve(lambda e, dstT=dstT, csv=csv, rsv=rsv: e.tensor_tensor(dstT, csv, rsv, ALU.mult), ["cs%d" % s2, "rs%d" % s2], ["kTn"])
                    if self.dn_stage < 2:
                        continue
                    for cp in range(NCP):
                        for (srcT, dstm, key, dkey) in ((kTn, k_tm, "kTn", "k_tm"), (vTn, v_tm, "vTn", "v_tm")):
                            b = self.nbt()
                            for u in range(8):
                                self.pe(lambda e, b=b, u=u, cp=cp, srcT=srcT: e.transpose(self.psb[b][0:64, u * 64:(u + 1) * 64], srcT[:, cp * 8 + u, :], idb), [key, "identb"], ["ps%d" % b])
                            self.dve(lambda e, b=b, cp=cp, dstm=dstm: e.tensor_copy(dstm[:, cp * 8:(cp + 1) * 8, :], psbv(b)), ["ps%d" % b], [dkey])
                    if self.dn_stage < 3:
                        continue
                    G = lambda r: gt[:, r, :]
                    Gn = lambda r: gt[:, r, :].rearrange("p (n h) -> p n h", h=4)
                    self.dve(lambda e, d=d: e.tensor_tensor(Gn(0), zab[:, :, d * 4:(d + 1) * 4], dtb[:, d * 4:(d + 1) * 4].unsqueeze(1).to_broadcast([64, NCH, 4]), ALU.add), ["zab", "dtb"], ["g0"])
                    self.act(lambda e: e.activation(G(0), G(0), AF.Exp), ["g0"], ["g0"])
                    self.act(lambda e: e.activation(G(0), G(0), AF.Ln, bias=one_col, scale=1.0), ["g0", "tri"], ["g0"])
                    self.dve(lambda e, d=d: e.tensor_tensor(Gn(1), Gn(0), nA[:, d * 4:(d + 1) * 4].unsqueeze(1).to_broadcast([64, NCH, 4]), ALU.mult), ["g0", "nA"], ["g1"])
                    self.act(lambda e, d=d: e.activation(Gn(2), zab[:, :, 8 + d * 4:8 + (d + 1) * 4], AF.Sigmoid), ["zab"], ["g2"])
                    b = self.nb()
                    self.pe(lambda e, b=b, d=d: e.matmul(self.ps[b][0:64, 0:NU], tri[:, d, :], G(1), start=True, stop=True), ["tri", "g1"], ["ps%d" % b])
                    self.pe(lambda e, b=b: e.matmul(self.ps[b][0:64, 64:64 + NU], tri[:, 2, :], G(1), start=True, stop=True), ["tri", "g1"], ["ps%d" % b])
                    self.dve(lambda e, b=b: e.tensor_copy(G(3), self.ps[b][0:64, 0:NU]), ["ps%d" % b], ["g3"])
                    self.act(lambda e, b=b: e.activation(G(4), self.ps[b][0:64, 0:NU], AF.Exp), ["ps%d" % b], ["g4"])
                    self.act(lambda e, b=b: e.activation(G(5), self.ps[b][0:64, 64:64 + NU], AF.Exp), ["ps%d" % b], ["g5"])
                    self.dve(lambda e, b=b: e.tensor_tensor(G(6), self.ps[b][0:64, 64:64 + NU], G(3), ALU.subtract), ["ps%d" % b, "g3"], ["g6"])
                    self.act(lambda e: e.activation(G(6), G(6), AF.Exp), ["g6"], ["g6"])
                    self.dve(lambda e: e.tensor_tensor(G(7), G(2), G(4), ALU.mult), ["g2", "g4"], ["g7"])
                    if self.dn_stage < 4:
                        continue
                    for cp in range(NCP):
                        us = slice(cp * 8, (cp + 1) * 8)
                        bA = self.nb()
                        for u in range(8):
                            self.pe(lambda e, bA=bA, u=u, cp=cp: e.matmul(self.ps[bA][0:64, u * 64:(u + 1) * 64], kTn[:, cp * 8 + u, :], kTn[:, cp * 8 + u, :], start=True, stop=True), ["kTn"], ["ps%d" % bA])
                        bB = self.nb()
                        for u in range(8):
                            self.pe(lambda e, bB=bB, u=u, cp=cp: e.matmul(self.ps[bB][0:64, u * 64:(u + 1) * 64], kTn[:, cp * 8 + u, :], qTn[:, cp * 8 + u, :], start=True, stop=True), ["kTn", "qTn"], ["ps%d" % bB])
                        if self.dn_stage < 4.01:
                            continue
                        self.dve(lambda e, us=us: e.tensor_tensor(dgx[:], idf.unsqueeze(1).to_broadcast([64, 8, 64]), B8(gt[:, 3, us]), ALU.mult), ["identf", "g3"], ["dgx"])
                        if self.dn_stage < 4.02:
                            continue
                        bC = self.nb()
                        self.pe(lambda e, bC=bC: e.matmul(self.ps[bC][0:64, :], tri[:, 2, :], dgx[:].rearrange("p u f -> p (u f)"), start=True, stop=True), ["tri", "dgx"], ["ps%d" % bC])
                        if self.dn_stage < 4.03:
                            continue
                        self.dve(lambda e, bC=bC, us=us: e.tensor_tensor(dl[:], psv(bC), B8(gt[:, 3, us]), ALU.subtract), ["ps%d" % bC, "g3"], ["dl"])
                        if self.dn_stage < 4.04:
                            continue
                        self.dve(lambda e, bC=bC: e.tensor_copy(egr[:], psv(bC)), ["ps%d" % bC], ["egr"])
                        self.act(lambda e: e.activation(egr[:], egr[:], AF.Exp), ["egr"], ["egr"])
                        if self.dn_stage < 4.05:
                            continue
                        self.dve(lambda e, us=us: e.tensor_tensor(qdec[:, us, :], qTn[:, us, :], egr[:], ALU.mult), ["qTn", "egr"], ["qdec"])
                        if self.dn_stage < 4.25:
                            continue
                        self.dve(lambda e, kT_=kT_: e.scalar_tensor_tensor(t1[:, 0], dl[:], 0.0, mk[:, kT_, :].unsqueeze(1).to_broadcast([64, 8, 64]), ALU.min, ALU.add), ["dl", "mk"], ["t1a"])
                        self.act(lambda e: e.activation(DTt[:], t1[:, 0], AF.Exp), ["t1a"], ["DTt"])
                        self.dve(lambda e, kL_=kL_: e.scalar_tensor_tensor(t1[:, 1], dl[:], 0.0, mk[:, kL_, :].unsqueeze(1).to_broadcast([64, 8, 64]), ALU.max, ALU.subtract), ["dl", "mk"], ["t1b"])
                        self.act(lambda e: e.activation(DLt[:], t1[:, 1], AF.Exp, scale=-1.0), ["t1b"], ["DLt"])
                        self.dve(lambda e, bB=bB, us=us: e.tensor_tensor(qkTm[:, us, :], psv(bB), DTt[:], ALU.mult), ["ps%d" % bB, "DTt"], ["qkTm"])
                        self.dve(lambda e, bA=bA: e.tensor_tensor(DLt[:], psv(bA), DLt[:], ALU.mult), ["ps%d" % bA, "DLt"], ["DLt"])
                        self.dve(lambda e, us=us: e.tensor_tensor(Lp[0][:, us, :], DLt[:], B8(gt[:, 2, us]), ALU.mult), ["DLt", "g2"], ["L0"])
                        if self.dn_stage < 4.5:
                            continue
                        bD = self.nbt()
                        for u in range(8):
                            self.pe(lambda e, bD=bD, u=u, cp=cp: e.transpose(self.psb[bD][0:64, u * 64:(u + 1) * 64], Lp[0][:, cp * 8 + u, :], idb), ["L0", "identb"], ["ps%d" % bD])
                        self.dve(lambda e, bD=bD, us=us: e.tensor_copy(Up[0][:, us, :], psbv(bD)), ["ps%d" % bD], ["U0"])
                        self.dve(lambda e, us=us: e.tensor_tensor(Rp[0][:, us, :], idb.unsqueeze(1).to_broadcast([64, 8, 64]), Up[0][:, us, :], ALU.subtract), ["identb", "U0"], ["R0"])
                        if self.dn_stage < 4.75:
                            continue
                        self.dve(lambda e, us=us: e.tensor_tensor(vb[:, us, :], v_tm[:, us, :], B8(gt[:, 2, us]), ALU.mult), ["v_tm", "g2"], ["vb"])
                        self.dve(lambda e, us=us: e.tensor_tensor(kbe[:, us, :], k_tm[:, us, :], B8(gt[:, 7, us]), ALU.mult), ["k_tm", "g7"], ["kbe"])
                        self.dve(lambda e, us=us: e.tensor_tensor(ktl[:, us, :], k_tm[:, us, :], B8(gt[:, 6, us]), ALU.mult), ["k_tm", "g6"], ["ktl"])
                    if self.dn_stage < 5:
                        continue
                    cur = 0
                    for lev in range(5):
                        nxt = 1 - cur
                        for cp in range(NCP):
                            us = slice(cp * 8, (cp + 1) * 8)
                            b = self.nb()
                            for u in range(8):
                                self.pe(lambda e, b=b, u=u, cp=cp, cur=cur: e.matmul(self.ps[b][0:64, u * 64:(u + 1) * 64], Up[cur][:, cp * 8 + u, :], Lp[cur][:, cp * 8 + u, :], start=True, stop=True),
                                        ["U%d" % cur, "L%d" % cur], ["ps%d" % b])
                            self.act(lambda e, b=b, us=us, nxt=nxt: e.copy(Lp[nxt][:, us, :], psv(b)), ["ps%d" % b], ["L%d" % nxt])
                        if lev < 4:
                            for cp in range(NCP):
                                us = slice(cp * 8, (cp + 1) * 8)
                                b = self.nb()
                                for u in range(8):
                                    self.pe(lambda e, b=b, u=u, cp=cp, cur=cur: e.matmul(self.ps[b][0:64, u * 64:(u + 1) * 64], Lp[cur][:, cp * 8 + u, :], Up[cur][:, cp * 8 + u, :], start=True, stop=True),
                                            ["U%d" % cur, "L%d" % cur], ["ps%d" % b])
                                self.act(lambda e, b=b, us=us, nxt=nxt: e.copy(Up[nxt][:, us, :], psv(b)), ["ps%d" % b], ["U%d" % nxt])
                        for cp in range(NCP):
                            us = slice(cp * 8, (cp + 1) * 8)
                            b = self.nb()
                            for u in range(8):
                                self.pe(lambda e, b=b, u=u, cp=cp, cur=cur, nxt=nxt: e.matmul(self.ps[b][0:64, u * 64:(u + 1) * 64], Lp[nxt][:, cp * 8 + u, :], Rp[cur][:, cp * 8 + u, :], start=True, stop=True),
                                        ["L%d" % nxt, "R%d" % cur], ["ps%d" % b])
                            self.dve(lambda e, b=b, us=us, cur=cur, nxt=nxt: e.tensor_tensor(Rp[nxt][:, us, :], psv(b), Rp[cur][:, us, :], ALU.add), ["ps%d" % b, "R%d" % cur], ["R%d" % nxt])
                        cur = nxt
                    Rf_ = Rp[cur]; rkey = "R%d" % cur
                    if self.dn_stage < 6:
                        continue
                    for cp in range(NCP):
                        us = slice(cp * 8, (cp + 1) * 8)
                        b = self.nb()
                        for u in range(8):
                            self.pe(lambda e, b=b, u=u, cp=cp, Rf_=Rf_: e.matmul(self.ps[b][0:64, u * 64:(u + 1) * 64], Rf_[:, cp * 8 + u, :], vb[:, cp * 8 + u, :], start=True, stop=True), [rkey, "vb"], ["ps%d" % b])
                        self.act(lambda e, b=b, us=us: e.copy(u_sb[:, us, :], psv(b)), ["ps%d" % b], ["u_sb"])
                        b = self.nb()
                        for u in range(8):
                            self.pe(lambda e, b=b, u=u, cp=cp, Rf_=Rf_: e.matmul(self.ps[b][0:64, u * 64:(u + 1) * 64], kbe[:, cp * 8 + u, :], Rf_[:, cp * 8 + u, :], start=True, stop=True), [rkey, "kbe"], ["ps%d" % b])
                        self.dve(lambda e, b=b, us=us: e.tensor_copy(wT[:, us, :], psv(b)), ["ps%d" % b], ["wT"])
                    if self.dn_stage < 7:
                        continue
                    corder = range(NCH) if d == 0 else range(NCH - 1, -1, -1)
                    for n in corder:
                        hs = slice(n * 4, (n + 1) * 4)
                        par = n % 2
                        psh = lambda b: self.ps[b][0:64, 0:256].rearrange("p (h f) -> p h f", f=64)
                        bG = self.nb()
                        for h in range(4):
                            self.pe(lambda e, bG=bG, h=h, n=n: e.matmul(self.ps[bG][0:64, h * 64:(h + 1) * 64], wT[:, n * 4 + h, :], Sb[:, h, :], start=True, stop=True), ["wT", "Sb"], ["ps%d" % bG])
                        self.dve(lambda e, bG=bG, hs=hs, par=par: e.tensor_tensor(vnew[:, par], u_sb[:, hs, :], psh(bG), ALU.subtract), ["u_sb", "ps%d" % bG], ["vnew%d" % par])
                        bH = self.nb()
                        for h in range(4):
                            self.pe(lambda e, bH=bH, h=h, n=n: e.matmul(self.ps[bH][0:64, h * 64:(h + 1) * 64], qdec[:, n * 4 + h, :], Sb[:, h, :], start=True, stop=False), ["qdec", "Sb"], ["ps%d" % bH])
                            self.pe(lambda e, bH=bH, h=h, n=n, par=par: e.matmul(self.ps[bH][0:64, h * 64:(h + 1) * 64], qkTm[:, n * 4 + h, :], vnew[:, par, h, :], start=False, stop=True), ["qkTm", "vnew%d" % par], ["ps%d" % bH])
                        bI = self.nb()
                        for h in range(4):
                            self.pe(lambda e, bI=bI, h=h, n=n, par=par: e.matmul(self.ps[bI][0:64, h * 64:(h + 1) * 64], ktl[:, n * 4 + h, :], vnew[:, par, h, :], start=True, stop=True), ["ktl", "vnew%d" % par], ["ps%d" % bI])
                        self.dve(lambda e, hs=hs: e.tensor_tensor(Sf[:], Sf[:], B4(gt[:, 5, hs]), ALU.mult), ["Sf", "g5"], ["Sf"])
                        self.dve(lambda e, bI=bI: e.tensor_tensor(Sf[:], Sf[:], psh(bI), ALU.add), ["Sf", "ps%d" % bI], ["Sf"])
                        self.act(lambda e: e.copy(Sb[:], Sf[:]), ["Sf"], ["Sb"])
                        self.act(lambda e, bH=bH, par=par: e.copy(ost[:, par, :], self.ps[bH][0:64, 0:256]), ["ps%d" % bH], ["ost%d" % par])
                        self.dma("sp", self.OF[d, st + p0 + n * 64: st + p0 + (n + 1) * 64, :], ost[:, par, :], ["ost%d" % par], ["OF"])
            for p0 in (range(0, S, W) if self.dn_stage >= 8 else []):
                for jb in range(W // 128):
                    q0 = st + p0 + jb * 128
                    self.dma("sp", o2[:], self.OF[:, q0:q0 + 128, :].rearrange("a p f -> p a f"), ["OF"], ["o2"])
                    self.dma("sp", gte[:], self.ZAB[q0:q0 + 128, 16:272], [], ["gte"])
                    self.dve(lambda e: e.tensor_tensor(osum[:], o2[:, 0, :], o2[:, 1, :], ALU.add), ["o2"], ["osum"])
                    self.act(lambda e: e.activation(o2[:, 0, :], osum[:], AF.Square), ["osum"], ["o2"])
                    self.dve(lambda e: e.tensor_reduce(pss[:, 0:4], o2[:, 0, :].rearrange("p (h f) -> p h f", f=64), AX.X, ALU.add), ["o2"], ["pss"])
                    self.rsqrt(pss[:, 4:8], pss[:, 0:4], 1.0 / 64, ["pss"], "prs")
                    self.dve(lambda e: e.tensor_tensor(osum[:].rearrange("p (h f) -> p h f", f=64), osum[:].rearrange("p (h f) -> p h f", f=64), pss[:, 4:8].unsqueeze(2).to_broadcast([128, 4, 64]), ALU.mult), ["osum", "prs"], ["osum"])
                    self.dve(lambda e: e.tensor_tensor(osum[:].rearrange("p (h f) -> p h f", f=64), osum[:].rearrange("p (h f) -> p h f", f=64), gdn[:].unsqueeze(1).to_broadcast([128, 4, 64]), ALU.mult), ["osum", "gdn"], ["osum"])
                    self.act(lambda e: e.activation(gte[:], gte[:], AF.Silu), ["gte"], ["gte"])
                    self.dve(lambda e: e.tensor_tensor(ydb[:], osum[:], gte[:], ALU.mult), ["osum", "gte"], ["ydb"])
                    b = self.nbt()
                    for c in range(2):
                        self.pe(lambda e, b=b, c=c: e.transpose(self.psb[b][:, c * 128:(c + 1) * 128], ydb[:, c * 128:(c + 1) * 128], self.identb[:]), ["ydb", "identb"], ["ps%d" % b])
                    self.dve(lambda e, b=b, jb=jb: e.tensor_copy(ydT[:, :, jb * 128:(jb + 1) * 128], self.psb[b][:, 0:256].rearrange("p (c t) -> p c t", c=2)), ["ps%d" % b], ["ydT"])
                self.dma("sp", self.YT[256:512, st + p0: st + p0 + W].rearrange("(c p) t -> p c t", p=128), ydT[:], ["ydT"], ["YT"])
        self.phase_end()


_SKIP = set()


def kernel(**inputs):
    cfg = FULL
    b = Builder(cfg)
    b.skip = set(_SKIP)
    nc = b.build()
    consts = host_consts(cfg)
    xp = np.asarray(inputs["x_prompt"], dtype=np.float32)
    xs = np.asarray(inputs["x_sample"], dtype=np.float32)
    mp = np.asarray(inputs["mem_prompt"], dtype=np.float32)
    ms = np.asarray(inputs["mem_sample"], dtype=np.float32)
    w = {n: np.ascontiguousarray(np.asarray(inputs[n], dtype=np.float32)) for n in WNAMES}
    in_maps = []
    for c in range(NCORES):
        x = np.concatenate([xp[4 * c:4 * c + 4].reshape(-1, D), xs[c].reshape(-1, D)], axis=0)
        mem = np.concatenate([mp[4 * c:4 * c + 4].reshape(-1, D), ms[c].reshape(-1, D)], axis=0)
        m = {"x": np.ascontiguousarray(x), "mem": np.ascontiguousarray(mem)}
        m.update(w)
        m.update(consts)
        in_maps.append(m)
    res = run_bass_kernel_spmd(nc, in_maps, core_ids=list(range(NCORES)))
    yp = np.empty((32, 2048, D), np.float32)
    ys = np.empty((8, 8192, D), np.float32)
    for c in range(NCORES):
        y = np.asarray(res.results[c]["y"], dtype=np.float32)
        yp[4 * c:4 * c + 4] = y[0:8192].reshape(4, 2048, D)
        ys[c] = y[8192:16384]
    return (yp, ys)
```

```python
import math
from contextlib import ExitStack
import numpy as np
import ml_dtypes
import concourse.bass as bass
import concourse.mybir as mybir
from concourse.bass_utils import run_bass_kernel_spmd

F32 = mybir.dt.float32
BF16 = mybir.dt.bfloat16
AF = mybir.ActivationFunctionType
ALU = mybir.AluOpType
AX = mybir.AxisListType

D = 1024
DFF = 2816
NFB = DFF // 128
INW = 2064
EPS = 1e-6
NCORES = 8
MEMT = 256
ENGS = ("pe", "act", "dve", "pool", "sp")


class Op:
    __slots__ = ("eng", "fn", "deps", "sig", "cnt", "dma", "sem", "semval")

    def __init__(self, eng, fn, dma):
        self.eng = eng; self.fn = fn; self.dma = dma
        self.deps = []; self.sig = False; self.cnt = 0; self.sem = None; self.semval = 0


class Rec:
    NDMASEM = 14

    def __init__(self):
        self.ops = {e: [] for e in ENGS}
        self.lastw = {}
        self.readers = {}
        self.dma_uses = {}
        self.dma_rr = {}
        self.pending = {e: [] for e in ENGS}
        self.last_compute = {}
        self.n = 0
        self.seq = []
        self.sid = 0
        self.streams = None

    def add(self, eng, fn, reads=(), writes=(), dma=False):
        op = Op(eng, fn, dma)
        deps = {}
        for r in reads:
            w = self.lastw.get(r)
            if w is not None:
                deps[w] = "raw"
        for w_ in writes:
            lw = self.lastw.get(w_)
            if lw is not None and lw not in deps:
                deps[lw] = "waw"
            for rd in self.readers.get(w_, ()):
                if rd not in deps:
                    deps[rd] = "war"
        need = []
        for d, kind in deps.items():
            if d.dma or dma:
                need.append(d)
            elif d.eng == eng:
                if eng == "pe":
                    continue
                need.append(d)
            else:
                need.append(d)
        if dma:
            rk = (self.sid, eng)
            k = self.dma_rr.get(rk, 0) % self.NDMASEM + self.sid * self.NDMASEM
            self.dma_rr[rk] = self.dma_rr.get(rk, 0) + 1
            ent = self.dma_uses.setdefault((eng, k), [0, None])
            if ent[1] is not None:
                need.append(ent[1])
            ent[0] += 1
            op.sem = (eng, k); op.semval = 16 * ent[0]
            ent[1] = op
        if self.pending[eng]:
            need.extend(self.pending[eng]); self.pending[eng] = []
        for d in need:
            d.sig = True
        op.deps = need
        for r in reads:
            self.readers.setdefault(r, []).append(op)
        for w_ in writes:
            self.lastw[w_] = op
            self.readers[w_] = []
        self.seq.append(op)
        if not dma:
            self.last_compute[eng] = op
        self.n += 1
        return op

    def barrier(self):
        lasts = []
        for e in ENGS:
            if self.last_compute.get(e) is not None:
                lasts.append(self.last_compute[e])
        for ent in self.dma_uses.values():
            if ent[1] is not None:
                lasts.append(ent[1])
        for e in ENGS:
            self.pending[e] = [o for o in lasts if not (o.eng == e and not o.dma)]
        self.lastw = {}; self.readers = {}

    def fork(self, n=2):
        self.streams = []
        for i in range(n):
            self.streams.append({"seq": [], "lastw": {}, "readers": {}, "pending": {e: list(self.pending[e]) for e in ENGS}})
        self.main_seq = self.seq
        self.pending = {e: [] for e in ENGS}

    def use(self, i):
        st = self.streams[i]
        self.sid = i
        self.seq, self.lastw, self.readers, self.pending = st["seq"], st["lastw"], st["readers"], st["pending"]

    def join(self):
        lists = [s["seq"] for s in self.streams]
        idx = [0] * len(lists)
        merged = []
        total = sum(len(x) for x in lists)
        for _ in range(total):
            best, bf = None, None
            for i, lst in enumerate(lists):
                if idx[i] < len(lst):
                    f = idx[i] / len(lst)
                    if bf is None or f < bf:
                        best, bf = i, f
            merged.append(lists[best][idx[best]]); idx[best] += 1
        self.seq = self.main_seq
        self.seq.extend(merged)
        self.sid = 0
        self.lastw = {}; self.readers = {}
        self.pending = {e: [] for e in ENGS}
        for op in merged:
            if not op.dma:
                self.last_compute[op.eng] = op
        self.streams = None

    def setup_sems(self, nc, stack):
        self.esem = {e: stack.enter_context(nc.semaphore("s_" + e)) for e in ENGS}
        self.dsem = {}
        for q in ("sp", "pool"):
            for k in range(self.NDMASEM * 2):
                self.dsem[(q, k)] = stack.enter_context(nc.semaphore("d_%s%d" % (q, k)))
        self.cnts = {e: 0 for e in ENGS}
        self.waited = {e: {} for e in ENGS}

    def emit(self, nc, final=False):
        esem, dsem = self.esem, self.dsem
        for e in ENGS:
            if self.last_compute.get(e) is not None:
                self.last_compute[e].sig = True
        for op in self.seq:
            if op.sig and not op.dma:
                self.cnts[op.eng] += 1; op.cnt = self.cnts[op.eng]
        finals = [(dsem[k], 16 * ent[0]) for k, ent in self.dma_uses.items()] if final else []
        ops = {e: [] for e in ENGS}
        for op in self.seq:
            ops[op.eng].append(op)
        self.seq = []
        with nc.Block() as block:
            def run(engname, e):
                waited = self.waited[engname]
                for op in ops[engname]:
                    for d in op.deps:
                        if d.dma:
                            s, v = dsem[d.sem], d.semval
                        else:
                            s, v = esem[d.eng], d.cnt
                            assert v > 0
                        if waited.get(id(s), 0) < v:
                            e.wait_ge(s, v); waited[id(s)] = v
                    ins = op.fn(e)
                    if op.dma:
                        ins.then_inc(dsem[op.sem], 16)
                    elif op.sig:
                        ins.then_inc(esem[engname], 1)
                if engname == "sp":
                    for s, v in finals:
                        e.wait_ge(s, v)

            @block.tensor
            def _(e):
                run("pe", e)

            @block.scalar
            def _(e):
                run("act", e)

            @block.vector
            def _(e):
                run("dve", e)

            @block.gpsimd
            def _(e):
                run("pool", e)

            @block.sync
            def _(e):
                run("sp", e)


class Cfg:
    def __init__(self, seqs, depth=2):
        self.seqs = list(seqs)
        self.depth = depth
        self.ntok = sum(seqs)
        self.starts = [sum(seqs[:i]) for i in range(len(seqs))]
        self.tt = min(1024, min(seqs))
        self.nsub = self.tt // 128
        self.tg = min(512, self.tt)
        self.ntg = self.tt // self.tg


FULL = Cfg([2048, 2048, 2048, 2048, 8192])

WNAMES = ["ffn1_norm", "ffn1_w_gu", "ffn1_w_down", "mix_norm", "w_in", "four_w", "dn_conv", "dn_A_log",
          "dn_dt_bias", "dn_out_norm", "attn_q_norm", "attn_k_norm", "w_out", "mem_norm_x", "mem_norm_m",
          "mem_wq", "mem_wkv", "mem_wo", "ffn2_norm", "ffn2_w_gu", "ffn2_w_down", "final_norm"]
WSHAPES = {
    "ffn1_norm": (2, D), "ffn1_w_gu": (2, D, 2 * DFF), "ffn1_w_down": (2, DFF, D), "mix_norm": (2, D),
    "w_in": (2, D, INW), "four_w": (2, 4, 64, 64), "dn_conv": (2, 3, 768), "dn_A_log": (2, 2, 4),
    "dn_dt_bias": (2, 2, 4), "dn_out_norm": (2, 64), "attn_q_norm": (2, 64), "attn_k_norm": (2, 64),
    "w_out": (2, D, D), "mem_norm_x": (2, D), "mem_norm_m": (2, D), "mem_wq": (2, D, D),
    "mem_wkv": (2, D, 2 * D), "mem_wo": (2, D, D), "ffn2_norm": (2, D), "ffn2_w_gu": (2, D, 2 * DFF),
    "ffn2_w_down": (2, DFF, D), "final_norm": (D,),
}
NORM_IDS = {"ffn1_norm": 0, "mix_norm": 1, "mem_norm_x": 2, "mem_norm_m": 3, "ffn2_norm": 4}


def host_consts(cfg):
    c = {}
    c["c_identb"] = np.eye(128, dtype=np.float32).astype(ml_dtypes.bfloat16)
    c["c_identf"] = np.eye(128, dtype=np.float32)
    smax = max(cfg.seqs)
    nblk = smax // 128
    pos = np.arange(smax)
    row = (pos // 64).astype(np.float32); col = (pos % 64).astype(np.float32)
    inv = (1.0 / (10000.0 ** (np.arange(16, dtype=np.float32) * (2.0 / 32)))).astype(np.float32)
    ang = np.stack([row[:, None] * inv, col[:, None] * inv], axis=1).astype(np.float32)
    cos = np.cos(ang).astype(np.float32).reshape(nblk, 128, 2, 16).transpose(1, 0, 2, 3)
    sin = np.sin(ang).astype(np.float32).reshape(nblk, 128, 2, 16).transpose(1, 0, 2, 3)
    c["c_cos"] = np.ascontiguousarray(cos); c["c_sin"] = np.ascontiguousarray(sin)
    bf = ml_dtypes.bfloat16
    cc = np.arange(64)
    th = 2 * np.pi * np.outer(cc, cc) / 64.0
    c["c_fc"] = np.stack([np.cos(th) / 8.0, np.sin(th) / 8.0], axis=1).astype(np.float32).astype(bf)
    for S in sorted(set(cfg.seqs)):
        N1 = S // 128
        n1 = np.arange(N1); t1 = 2 * np.pi * np.outer(n1, n1) / N1
        sc = 1.0 / np.sqrt(S)
        f1 = np.concatenate([np.cos(t1), -np.sin(t1), np.sin(t1)], axis=1) * sc
        c["c_f1_%d" % S] = f1.astype(np.float32).astype(bf)
        n2 = np.arange(128)[:, None, None]; k1 = np.arange(N1)[None, :, None]; k2 = np.arange(128)[None, None, :]
        ang = 2 * np.pi * ((n2 * (k1 + N1 * k2)) % S) / S
        h = np.stack([np.cos(ang), -np.sin(ang)], axis=2)
        c["c_h_%d" % S] = h.astype(np.float32).astype(bf)
    p = np.arange(64)[:, None]; f = np.arange(64)[None, :]
    NEG = -1.0e4
    mk = np.stack([np.where(f >= p, 0.0, NEG), np.where(f < p, 0.0, NEG), np.where(f <= p, 0.0, NEG), np.where(f > p, 0.0, NEG)], axis=1)
    c["c_dnmask"] = mk.astype(np.float32)
    tri = np.stack([(p <= f).astype(np.float32), (p >= f).astype(np.float32), np.ones((64, 64), np.float32)], axis=1)
    c["c_tri"] = tri.astype(np.float32)
    return c


class Builder:
    def __init__(self, cfg, dump=(), stop_after=None):
        self.cfg = cfg
        self.dump = set(dump)
        self.stop_after = stop_after
        self.skip = set()
        self.dn_stage = 9
        self.overlap = False
        self.own_phase = True
        self.nc = bass.Bass("TRN2", target_bir_lowering=False)
        self.R = Rec()
        self.sb_off = 16512
        self.sb_mark = 16512

    def dram(self, name, shape, dt, kind=None):
        if kind is None:
            kind = "ExternalOutput" if name in self.dump else ("ExternalInput" if name in getattr(self, "ext_in", ()) else "Internal")
        return self.nc.dram_tensor(name, list(shape), dt, kind=kind).ap()

    def sb(self, name, shape, dt):
        self.tid = getattr(self, "tid", 0) + 1
        return self.pstack.enter_context(self.nc.sbuf_tensor("%s_%d" % (name, self.tid), list(shape), dt))

    def phase_begin(self):
        if not self.own_phase:
            return
        self.R.barrier()
        self.pstack = ExitStack()

    def phase_end(self, final=False):
        if not self.own_phase:
            return
        self.R.emit(self.nc, final=final)
        self.pstack.close()

    def pe(self, fn, r, w): return self.R.add("pe", fn, r, w)
    def act(self, fn, r, w): return self.R.add("act", fn, r, w)
    def dve(self, fn, r, w): return self.R.add("dve", fn, r, w)
    def pool(self, fn, r, w): return self.R.add("pool", fn, r, w)

    def dma(self, q, out, in_, r, w):
        return self.R.add(q, lambda e, o=out, i=in_: e.dma_start(out=o, in_=i), r, w, dma=True)

    def build(self):
        cfg = self.cfg; nc = self.nc
        NT = cfg.ntok
        self.inp = {}
        self.inp["x"] = self.dram("x", [NT, D], F32, "ExternalInput")
        self.inp["mem"] = self.dram("mem", [len(cfg.seqs) * MEMT, D], F32, "ExternalInput")
        for n in WNAMES:
            shp = WSHAPES[n]
            if getattr(self, "tiny_w", False) and int(np.prod(shp)) > 100000:
                shp = (2, 128, 128)
            self.inp[n] = self.dram(n, shp, F32, "ExternalInput")
        hc = host_consts(cfg)
        self.cin = {}
        for k, v in hc.items():
            self.cin[k] = self.dram(k, v.shape, BF16 if v.dtype == ml_dtypes.bfloat16 else F32, "ExternalInput")
        self.y = self.dram("y", [NT, D], F32, "ExternalOutput")
        self.XS = self.dram("XS", [NT, D], F32)
        self.YT = self.dram("YT", [D, NT], BF16)
        self.UF = self.dram("UF", [NT, 256], BF16)
        self.ZT = self.dram("ZT", [768, NT], BF16)
        self.ZAB = self.dram("ZAB", [NT, 272], F32)
        self.QT = self.dram("QT", [4, 128, NT], BF16)
        self.KT = self.dram("KT", [2, 128, NT], BF16)
        self.VV = self.dram("VV", [NT, 128], BF16)
        self.OF = self.dram("OF", [2, NT, 256], F32)

        with ExitStack() as stack:
            stack.enter_context(nc.allow_non_contiguous_dma(reason="strided weight/activation tiles"))
            self.ps = [stack.enter_context(nc.psum_tensor("ps%d" % b, [128, 512], F32)) for b in range(8)]
            self.psb = {b: self.ps[b][:].bitcast(BF16) for b in range(8)}
            self.R.setup_sems(nc, stack)
            self.pstack = stack
            self.setup_consts()
            self.R.emit(nc)
            self.program()
            self.pstack = stack
            self.dve(lambda e: e.memset(self.epsc[:, 0:1], EPS), [], ["epsc"])
            self.R.emit(nc, final=True)
        return nc

    def setup_consts(self):
        cfg = self.cfg
        self.identb = self.sb("identb", [128, 128], BF16)
        self.identf = self.sb("identf", [128, 128], F32)
        self.onesb = self.sb("onesb", [128, 128], BF16)
        self.gains = self.sb("gains", [128, 88], F32)
        self.gq = self.sb("gq", [128, 2, 64], F32)
        self.gk = self.sb("gk", [128, 2, 64], F32)
        gstage = self.sb("gstage", [88, 128], F32)
        self.dma("sp", self.identb[:], self.cin["c_identb"], [], ["identb"])
        self.dma("sp", self.identf[:], self.cin["c_identf"], [], ["identf"])
        self.dve(lambda e: e.memset(self.onesb[:], 1.0), [], ["onesb"])
        self.epsc = self.sb("epsc", [128, 8], F32)
        self.dve(lambda e: e.memset(self.epsc[:], EPS), [], ["epsc"])
        for nm, i in NORM_IDS.items():
            self.dma("sp", gstage[i * 16:(i + 1) * 16, :],
                     self.inp[nm].rearrange("l (c p) -> (l c) p", p=128), [], ["gstage"])
        self.dma("sp", gstage[80:88, :], self.inp["final_norm"].rearrange("(c p) -> c p", p=128), [], ["gstage"])
        self.pe(lambda e: e.transpose(self.ps[0][:, 0:88], gstage[0:88, :], self.identf[0:88, 0:88]),
                ["gstage", "identf"], ["ps0"])
        self.dve(lambda e: e.tensor_copy(self.gains[:], self.ps[0][:, 0:88]), ["ps0"], ["gains"])
        self.dma("sp", self.gq[:].rearrange("p l d -> p (l d)"),
                 self.inp["attn_q_norm"].rearrange("l d -> (l d)").partition_broadcast(128), [], ["gq"])
        self.dma("sp", self.gk[:].rearrange("p l d -> p (l d)"),
                 self.inp["attn_k_norm"].rearrange("l d -> (l d)").partition_broadcast(128), [], ["gk"])
        self.dve(lambda e: e.tensor_scalar(self.gq[:], self.gq[:], 0.125, None, ALU.mult), ["gq"], ["gq"])

    def rsqrt(self, out, in_, scale, rkeys, wkey):
        self.act(lambda e: e.activation(out, in_, AF.Sqrt, bias=self.epsc[:out.shape[0], 0:1], scale=scale), list(rkeys) + ["epsc"], [wkey + "_s"])
        self.dve(lambda e: e.reciprocal(out, out), [wkey + "_s"], [wkey])

    def gain(self, name, l, c):
        i = NORM_IDS[name] * 16 + l * 8 + c
        return self.gains[:, i:i + 1]

    def program(self):
        cfg = self.cfg
        L = cfg.depth
        sa = self.stop_after
        if sa == "consts":
            return
        if sa == "dnonly":
            self.m_deltanet(0)
            return
        self.t_phase(0)
        if sa in ("T0", "load", "ffn", "n1", "n2"):
            return
        for l in range(L):
            self.m_phase(l)
            if sa == "M%d" % l:
                return
            self.t_phase(l + 1)
            if sa == "T%d" % (l + 1):
                return

    def m_phase(self, l):
        if self.overlap and "gqa" not in self.skip and "dn" not in self.skip:
            self.phase_begin()
            self.R.fork(2)
            self.own_phase = False
            self.R.use(0); self.m_gqa(l)
            self.R.use(1); self.m_deltanet(l)
            self.own_phase = True
            self.R.join()
            self.phase_end()
            if "four" not in self.skip:
                self.m_fourier(l)
            return
        if "gqa" not in self.skip:
            self.m_gqa(l)
        if "four" not in self.skip:
            self.m_fourier(l)
        if "dn" not in self.skip:
            self.m_deltanet(l)
        else:
            self.phase_begin()
            zz = self.sb("zz", [128, 2, self.cfg.ntok], BF16)
            self.dve(lambda e: e.memset(zz[:], 0.0), [], ["zz"])
            self.dma("sp", self.YT[256:512, :].rearrange("(c p) t -> p c t", p=128), zz[:], ["zz"], ["YT"])
            self.phase_end()

    def wplan_tile(self, ph, first_of_seq):
        L = self.cfg.depth
        blocks = []
        lp, ln = ph - 1, ph
        if ph >= 1:
            if first_of_seq:
                for j in range(8):
                    blocks.append(("mem_wkv", lp, "col", j))
                for k in range(8):
                    blocks.append(("mem_wkv", lp, "row", k, [(0, 1024, 1024)]))
            for k in range(8):
                blocks.append(("w_out", lp, "row", k, [(0, 0, 1024)]))
            for j in range(8):
                blocks.append(("mem_wq", lp, "col", j))
            for k in range(8):
                blocks.append(("mem_wo", lp, "row", k, [(0, 0, 1024)]))
            blocks += self.ffn_blocks("ffn2", lp)
        if ph < L:
            blocks += self.ffn_blocks("ffn1", ln)
            for k in range(8):
                blocks.append(("w_in", ln, "row", k, [(0, 0, 256), (256, 1024, 272)]))
            for k in range(8):
                blocks.append(("w_in", ln, "row", k, [(0, 1296, 768)]))
            for j in range(2, 8):
                blocks.append(("w_in", ln, "col", j))
        return blocks

    def ffn_blocks(self, pre, l):
        out = []
        for half in range(2):
            for jj in range(11 * half, 11 * half + 11):
                out.append((pre + "_w_gu", l, "col", jj))
                out.append((pre + "_w_gu", l, "col", NFB + jj))
            for k in range(11 * half, 11 * half + 11):
                out.append((pre + "_w_down", l, "row", k, [(0, 0, 1024)]))
        return out

    def ws_init(self, plan):
        self.ws_plan = plan
        self.ws_next_dma = 0
        self.ws_next_acq = 0
        self.ws_released = 0
        self.ws_fill()

    def ws_fill(self):
        while self.ws_next_dma < len(self.ws_plan) and self.ws_next_dma < self.ws_released + self.NS:
            b = self.ws_plan[self.ws_next_dma]
            s = self.ws_next_dma % self.NS
            slot = self.wslots[s]
            w = self.inp[b[0]]
            if b[2] == "col":
                j = b[3]
                src = w[b[1], :, j * 128:(j + 1) * 128].rearrange("(kc p) c -> p kc c", p=128)
                self.dma("pool", slot[:].rearrange("p (kc c) -> p kc c", kc=8), src, [], ["ws%d" % s])
            else:
                k = b[3]
                for (d0, c0, n) in b[4]:
                    self.dma("pool", slot[:, d0:d0 + n], w[b[1], k * 128:(k + 1) * 128, c0:c0 + n], [], ["ws%d" % s])
            self.ws_next_dma += 1

    def ws_acquire(self, name, kind):
        i = self.ws_next_acq
        b = self.ws_plan[i]
        assert b[0] == name and b[2] == kind, (b, name, kind)
        assert i < self.ws_next_dma, "weight block not yet scheduled (ring too small)"
        self.ws_next_acq += 1
        s = i % self.NS
        return s, self.wslots[s], "ws%d" % s

    def ws_release(self, n=1):
        self.ws_released += n
        self.ws_fill()

    def t_phase(self, ph):
        cfg = self.cfg
        L = cfg.depth
        self.phase_begin()
        TT, NSUB, TG, NTG = cfg.tt, cfg.nsub, cfg.tg, cfg.ntg
        self.NS = 16
        self.wslots = [self.sb("ws%d" % s, [128, 1024], BF16) for s in range(self.NS)]
        self.xt = self.sb("xt", [128, NSUB, D], F32)
        self.nT = self.sb("nT", [128, 8, TT], BF16)
        self.ov = self.sb("ov", [128, 24 * TT], BF16)
        self.hT = self.ov[:, 0:11 * TT].rearrange("p (c t) -> p c t", c=11)
        self.qT = self.ov[:, 0:8 * TT].rearrange("p (c t) -> p c t", c=8)
        self.pT = self.ov[:, 8 * TT:16 * TT].rearrange("p (c t) -> p c t", c=8)
        self.oT = self.ov[:, 16 * TT:24 * TT].rearrange("p (c t) -> p c t", c=8)
        self.yTt = self.ov[:, 8 * TT:16 * TT].rearrange("p (c t) -> p c t", c=8)
        self.memkT = self.sb("memkT", [128, 8, MEMT], BF16)
        self.memV = self.sb("memV", [128, 2, D], BF16)
        self.memnT = self.sb("memnT", [128, 8, MEMT], BF16)
        self.rbc = self.sb("rbc", [128, 2, 512], F32)
        self.cos = self.sb("cos", [128, NSUB, 2, 16], F32)
        self.sin = self.sb("sin", [128, NSUB, 2, 16], F32)
        if ph == L:
            self.gfin = self.sb("gfin", [128, D], F32)
            self.dma("sp", self.gfin[:], self.inp["final_norm"].partition_broadcast(128), [], ["gfin"])
        self.xn = self.sb("xn", [128, 2, D], BF16)
        self.junk = self.sb("junk", [128, D], BF16)
        self.ss = self.sb("ss", [128, 8], F32)
        self.rstd = self.sb("rstd", [128, 8], F32)
        self.sg = self.sb("sg", [128, 2, 512], F32)
        self.zt = self.sb("zt", [128, 1296], F32)
        self.zq = self.sb("zq", [128, 768], F32)
        self.zqb = self.sb("zqb", [128, 768], BF16)
        self.zsq = self.sb("zsq", [128, 640], F32)
        self.zn = self.sb("zn", [128, 640], F32)
        self.zr = self.sb("zr", [128, 640], F32)
        self.rt = self.sb("rt", [128, 4, 320], F32)
        self.zss = self.sb("zss", [128, 32], F32)
        self.qkst = self.sb("qkst", [128, 6, TT], BF16)
        self.vst = self.sb("vst", [128, 128], BF16)
        self.ufst = self.sb("ufst", [128, 256], BF16)
        self.zfm = self.sb("zfm", [128, 2, 512], BF16)
        tiles = []
        for si, (st, S) in enumerate(zip(cfg.starts, cfg.seqs)):
            for t0 in range(0, S, TT):
                tiles.append((si, st, S, t0))
        plan = []
        for (si, st, S, t0) in tiles:
            plan += self.wplan_tile(ph, t0 == 0)
        self.ws_init(plan)
        xsrc = self.inp["x"] if ph == 0 else self.XS
        xdst = self.y if ph == L else self.XS
        for (si, st, S, t0) in tiles:
            g0 = st + t0
            self.dma("sp", self.xt[:], xsrc[g0:g0 + TT, :].rearrange("(j p) d -> p j d", p=128),
                     [], ["x%d" % j for j in range(NSUB)])
            if ph >= 1:
                if t0 == 0:
                    self.mem_kv(ph - 1, si)
                self.wout(ph - 1, g0)
                self.mca(ph - 1)
                self.ffn("ffn2", ph - 1)
            if ph == L:
                self.final_norm()
            if ph < L:
                b0 = t0 // 128
                self.dma("sp", self.cos[:], self.cin["c_cos"][:, b0:b0 + NSUB], [], ["cos"])
                self.dma("sp", self.sin[:], self.cin["c_sin"][:, b0:b0 + NSUB], [], ["sin"])
                if self.stop_after != "load":
                    self.ffn("ffn1", ph)
                if self.stop_after not in ("load", "ffn", "n1", "n2"):
                    self.win(ph, g0, t0)
            self.dma("sp", xdst[g0:g0 + TT, :].rearrange("(j p) d -> p j d", p=128), self.xt[:],
                     ["x%d" % j for j in range(NSUB)], ["xdst"])
        self.phase_end()

    def norm_to_nT(self, name, l):
        cfg = self.cfg
        NSUB, TT = cfg.nsub, cfg.tt
        for j in range(NSUB):
            self.act(lambda e, j=j: e.activation(self.junk[:], self.xt[:, j, :], AF.Square,
                                                 accum_out=self.ss[:, j:j + 1]),
                     ["x%d" % j], ["junk", "ss%d" % j])
        self.rsqrt(self.rstd[:, 0:NSUB], self.ss[:, 0:NSUB], 1.0 / D, ["ss%d" % j for j in range(NSUB)], "rstd")
        if self.stop_after == "n1":
            return
        for j in range(NSUB):
            b = j % 2
            self.dve(lambda e, j=j, b=b: e.tensor_scalar(self.xn[:, b, :], self.xt[:, j, :], self.rstd[:, j:j + 1], None, ALU.mult),
                     ["x%d" % j, "rstd"], ["xn%d" % b])
            pb = 6 + (j % 2)
            psv = self.psb[pb]
            for c in range(8):
                self.pe(lambda e, c=c, b=b, psv=psv: e.transpose(psv[:, c * 128:(c + 1) * 128], self.xn[:, b, c * 128:(c + 1) * 128], self.identb[:]),
                        ["xn%d" % b, "identb"], ["ps%d" % pb])
            for c in range(8):
                self.dve(lambda e, c=c, j=j, psv=psv: e.tensor_scalar(self.nT[:, c, j * 128:(j + 1) * 128], psv[:, c * 128:(c + 1) * 128],
                                                                      self.gain(name, l, c), None, ALU.mult),
                         ["ps%d" % pb, "gains"], ["nT%d" % c])

    def ffn(self, pre, l):
        cfg = self.cfg
        NSUB, TT, TG, NTG = cfg.nsub, cfg.tt, cfg.tg, cfg.ntg
        self.norm_to_nT(pre + "_norm", l)
        nTr = ["nT%d" % c for c in range(8)]
        bank = 0
        if self.stop_after in ("n1", "n2"):
            return
        for half in range(2):
            for jl in range(11):
                sg_, slg, rg = self.ws_acquire(pre + "_w_gu", "col")
                su_, slu, ru = self.ws_acquire(pre + "_w_gu", "col")
                wg = slg[:].rearrange("p (kc c) -> p kc c", kc=8)
                wu = slu[:].rearrange("p (kc c) -> p kc c", kc=8)
                for tg in range(NTG):
                    bg, bu = bank % 6, (bank + 1) % 6
                    bank += 2
                    tsl = slice(tg * TG, (tg + 1) * TG)
                    for kc in range(8):
                        self.pe(lambda e, kc=kc, wg=wg, bg=bg, tsl=tsl: e.matmul(self.ps[bg][:, 0:TG], wg[:, kc, :], self.nT[:, kc, tsl], start=(kc == 0), stop=(kc == 7)),
                                [rg, "nT%d" % kc], ["ps%d" % bg])
                    for kc in range(8):
                        self.pe(lambda e, kc=kc, wu=wu, bu=bu, tsl=tsl: e.matmul(self.ps[bu][:, 0:TG], wu[:, kc, :], self.nT[:, kc, tsl], start=(kc == 0), stop=(kc == 7)),
                                [ru, "nT%d" % kc], ["ps%d" % bu])
                    sb_ = tg % 2
                    self.act(lambda e, bg=bg, sb_=sb_: e.activation(self.sg[:, sb_, 0:TG], self.ps[bg][:, 0:TG], AF.Silu),
                             ["ps%d" % bg], ["sg%d" % sb_])
                    self.dve(lambda e, bu=bu, sb_=sb_, jl=jl, tsl=tsl: e.tensor_tensor(self.hT[:, jl, tsl], self.sg[:, sb_, 0:TG], self.ps[bu][:, 0:TG], ALU.mult),
                             ["sg%d" % sb_, "ps%d" % bu], ["h%d" % jl])
                self.ws_release(2)
            slots = [self.ws_acquire(pre + "_w_down", "row") for _ in range(11)]
            for j in range(NSUB):
                for ch in range(2):
                    b = bank % 6; bank += 1
                    for k in range(11):
                        _, sl, rk = slots[k]
                        self.pe(lambda e, k=k, sl=sl, b=b, j=j, ch=ch: e.matmul(self.ps[b][:, :], self.hT[:, k, j * 128:(j + 1) * 128], sl[:, ch * 512:(ch + 1) * 512], start=(k == 0), stop=(k == 10)),
                                [rk, "h%d" % k], ["ps%d" % b])
                    self.dve(lambda e, b=b, j=j, ch=ch: e.scalar_tensor_tensor(self.xt[:, j, ch * 512:(ch + 1) * 512], self.ps[b][:, :], 0.5, self.xt[:, j, ch * 512:(ch + 1) * 512], ALU.mult, ALU.add),
                             ["ps%d" % b, "x%d" % j], ["x%d" % j])
            self.ws_release(11)

    def win(self, l, g0, t0):
        cfg = self.cfg
        NSUB, TT, TG, NTG = cfg.nsub, cfg.tt, cfg.tg, cfg.ntg
        self.norm_to_nT("mix_norm", l)
        slots = [self.ws_acquire("w_in", "row") for _ in range(8)]
        for j in range(NSUB):
            for (c0, n, zo) in ((0, 256, 0), (256, 272, 256)):
                b = (j * 2 + (c0 > 0)) % 6
                for kc in range(8):
                    _, sl, rk = slots[kc]
                    self.pe(lambda e, kc=kc, sl=sl, b=b, j=j, c0=c0, n=n: e.matmul(self.ps[b][:, 0:n], self.nT[:, kc, j * 128:(j + 1) * 128], sl[:, c0:c0 + n], start=(kc == 0), stop=(kc == 7)),
                            [rk, "nT%d" % kc], ["ps%d" % b])
                if c0 == 0:
                    self.act(lambda e, b=b: e.copy(self.ufst[:], self.ps[b][:, 0:256]), ["ps%d" % b], ["ufst"])
                    self.dma("sp", self.UF[g0 + j * 128:g0 + (j + 1) * 128, :], self.ufst[:], ["ufst"], ["UF"])
                else:
                    self.dve(lambda e, b=b: e.tensor_copy(self.zt[:, 0:272], self.ps[b][:, 0:272]), ["ps%d" % b], ["zab"])
                    self.dma("sp", self.ZAB[g0 + j * 128:g0 + (j + 1) * 128, :], self.zt[:, 0:272], ["zab"], ["ZAB"])
        self.ws_release(8)
        slots = [self.ws_acquire("w_in", "row") for _ in range(8)]
        for j in range(NSUB):
            for (c0, n) in ((0, 512), (512, 256)):
                b = (j * 2 + (c0 > 0)) % 6
                for kc in range(8):
                    _, sl, rk = slots[kc]
                    self.pe(lambda e, kc=kc, sl=sl, b=b, j=j, c0=c0, n=n: e.matmul(self.ps[b][:, 0:n], self.nT[:, kc, j * 128:(j + 1) * 128], sl[:, c0:c0 + n], start=(kc == 0), stop=(kc == 7)),
                            [rk, "nT%d" % kc], ["ps%d" % b])
                if c0 == 0:
                    self.act(lambda e, b=b: e.copy(self.zq[:, 0:512], self.ps[b][:, 0:512]), ["ps%d" % b], ["zq"])
                else:
                    self.dve(lambda e, b=b: e.tensor_copy(self.zq[:, 512:768], self.ps[b][:, 0:256]), ["ps%d" % b], ["zk"])
            self.qk_post(l, j, g0, t0)
        self.ws_release(8)
        for jb in range(6):
            _, sl, rk = self.ws_acquire("w_in", "col")
            wv = sl[:].rearrange("p (kc c) -> p kc c", kc=8)
            for tg in range(NTG):
                b = (jb * NTG + tg) % 6
                tsl = slice(tg * TG, (tg + 1) * TG)
                for kc in range(8):
                    self.pe(lambda e, kc=kc, wv=wv, b=b, tsl=tsl: e.matmul(self.ps[b][:, 0:TG], wv[:, kc, :], self.nT[:, kc, tsl], start=(kc == 0), stop=(kc == 7)),
                            [rk, "nT%d" % kc], ["ps%d" % b])
                s2 = (jb * NTG + tg) % 2
                self.act(lambda e, b=b, s2=s2: e.copy(self.zfm[:, s2, 0:TG], self.ps[b][:, 0:TG]), ["ps%d" % b], ["zfm%d" % s2])
                self.dma("sp", self.ZT[jb * 128:(jb + 1) * 128, g0 + tg * TG:g0 + (tg + 1) * TG], self.zfm[:, s2, 0:TG], ["zfm%d" % s2], ["ZT"])
            self.ws_release(1)
        for pr in range(4):
            self.dma("sp", self.QT[pr, :, g0:g0 + TT], self.qkst[:, pr, :], ["qkst"], ["QT"])
        for g in range(2):
            self.dma("sp", self.KT[g, :, g0:g0 + TT], self.qkst[:, 4 + g, :], ["qkst"], ["KT"])

    def qk_post(self, l, j, g0, t0):
        blk = (t0 // 128) + j
        zq, zsq, zn, zr, zqb, tt_ = self.zq, self.zsq, self.zn, self.zr, self.zqb, self.rt
        hv = lambda ap: ap.rearrange("p (h d) -> p h d", d=64)
        self.dve(lambda e: e.tensor_tensor(zsq[:, 0:640], zq[:, 0:640], zq[:, 0:640], ALU.mult), ["zq", "zk"], ["zsq"])
        self.dve(lambda e: e.tensor_reduce(self.zss[:, 0:10], hv(zsq[:, 0:640]), AX.X, ALU.add), ["zsq"], ["zss"])
        self.rsqrt(self.zss[:, 16:26], self.zss[:, 0:10], 1.0 / 64, ["zss"], "zrs")
        rs = self.zss[:, 16:26]
        self.dve(lambda e: e.tensor_tensor(hv(zn[:, 0:640]), hv(zq[:, 0:640]), rs.unsqueeze(2).to_broadcast([128, 10, 64]), ALU.mult),
                 ["zq", "zk", "zrs"], ["zn"])
        self.dve(lambda e: e.tensor_tensor(hv(zn[:, 0:512]), hv(zn[:, 0:512]), self.gq[:, l:l + 1, :].to_broadcast([128, 8, 64]), ALU.mult),
                 ["zn", "gq"], ["zn"])
        self.dve(lambda e: e.tensor_tensor(hv(zn[:, 512:640]), hv(zn[:, 512:640]), self.gk[:, l:l + 1, :].to_broadcast([128, 2, 64]), ALU.mult),
                 ["zn", "gk"], ["zn"])
        xv = zn[:, 0:640].rearrange("p (h a t f) -> p h a t f", h=10, a=2, t=2)
        ov = zr[:, 0:640].rearrange("p (h a t f) -> p h a t f", h=10, a=2, t=2)
        x1, x2 = xv[:, :, :, 0, :], xv[:, :, :, 1, :]
        cb = self.cos[:, j, :, :].unsqueeze(1).to_broadcast([128, 10, 2, 16])
        sb_ = self.sin[:, j, :, :].unsqueeze(1).to_broadcast([128, 10, 2, 16])
        t = [tt_[:, i, :].rearrange("p (h a f) -> p h a f", h=10, a=2) for i in range(4)]
        self.dve(lambda e: e.tensor_tensor(t[0], x1, cb, ALU.mult), ["zn", "cos"], ["rt0"])
        self.dve(lambda e: e.tensor_tensor(t[1], x2, sb_, ALU.mult), ["zn", "sin"], ["rt1"])
        self.dve(lambda e: e.tensor_tensor(t[2], x2, cb, ALU.mult), ["zn", "cos"], ["rt2"])
        self.dve(lambda e: e.tensor_tensor(t[3], x1, sb_, ALU.mult), ["zn", "sin"], ["rt3"])
        self.dve(lambda e: e.tensor_tensor(ov[:, :, :, 0, :], t[0], t[1], ALU.subtract), ["rt0", "rt1"], ["zr"])
        self.dve(lambda e: e.tensor_tensor(ov[:, :, :, 1, :], t[2], t[3], ALU.add), ["rt2", "rt3"], ["zr"])
        self.act(lambda e: e.copy(zqb[:, 0:512], zr[:, 0:512]), ["zr"], ["zqb"])
        self.act(lambda e: e.copy(zqb[:, 512:768].rearrange("p (g u d) -> p g u d", g=2, u=2),
                                  zr[:, 512:640].rearrange("p (g d) -> p g d", g=2).unsqueeze(2).to_broadcast([128, 2, 2, 64])),
                 ["zr"], ["zqb"])
        self.act(lambda e: e.copy(self.vst[:], zq[:, 640:768]), ["zk"], ["vst"])
        self.dma("sp", self.VV[g0 + j * 128:g0 + (j + 1) * 128, :], self.vst[:], ["vst"], ["VV"])
        pb = 6 + (j % 2)
        psv = self.psb[pb]
        for i in range(6):
            self.pe(lambda e, i=i, psv=psv: e.transpose(psv[:, i * 128:(i + 1) * 128], zqb[:, i * 128:(i + 1) * 128], self.identb[:]),
                    ["zqb", "identb"], ["ps%d" % pb])
        self.dve(lambda e, psv=psv: e.tensor_copy(self.qkst[:, :, j * 128:(j + 1) * 128], psv[:, 0:768].rearrange("p (i t) -> p i t", i=6)),
                 ["ps%d" % pb], ["qkst"])

    def mem_kv(self, l, si):
        mt = self.zt
        mnT = self.qkst
        mnTv = self.memnT
        for j in range(2):
            self.dma("sp", mt[:, 0:D], self.inp["mem"][si * MEMT + j * 128: si * MEMT + (j + 1) * 128, :], [], ["zab"])
            self.act(lambda e: e.activation(self.junk[:], mt[:, 0:D], AF.Square, accum_out=self.ss[:, 0:1]), ["zab"], ["junk", "ss0"])
            self.rsqrt(self.rstd[:, 0:1], self.ss[:, 0:1], 1.0 / D, ["ss0"], "rstd")
            self.dve(lambda e: e.tensor_scalar(self.xn[:, 0, :], mt[:, 0:D], self.rstd[:, 0:1], None, ALU.mult), ["zab", "rstd"], ["xn0"])
            pb = 6 + j
            psv = self.psb[pb]
            for c in range(8):
                self.pe(lambda e, c=c, psv=psv: e.transpose(psv[:, c * 128:(c + 1) * 128], self.xn[:, 0, c * 128:(c + 1) * 128], self.identb[:]),
                        ["xn0", "identb"], ["ps%d" % pb])
            for c in range(8):
                self.dve(lambda e, c=c, j=j, psv=psv: e.tensor_scalar(mnTv[:, c, j * 128:(j + 1) * 128], psv[:, c * 128:(c + 1) * 128],
                                                                      self.gain("mem_norm_m", l, c), None, ALU.mult),
                         ["ps%d" % pb, "gains"], ["qkst"])
        for jb in range(8):
            _, sl, rk = self.ws_acquire("mem_wkv", "col")
            wv = sl[:].rearrange("p (kc c) -> p kc c", kc=8)
            b = jb % 6
            for kc in range(8):
                self.pe(lambda e, kc=kc, wv=wv, b=b: e.matmul(self.ps[b][:, 0:MEMT], wv[:, kc, :], mnTv[:, kc, :], start=(kc == 0), stop=(kc == 7)),
                        [rk, "qkst"], ["ps%d" % b])
            self.act(lambda e, b=b, jb=jb: e.copy(self.memkT[:, jb, :], self.ps[b][:, 0:MEMT]), ["ps%d" % b], ["memkT"])
            self.ws_release(1)
        slots = [self.ws_acquire("mem_wkv", "row") for _ in range(8)]
        for mc in range(2):
            for ch in range(2):
                b = (mc * 2 + ch) % 6
                for kc in range(8):
                    _, sl, rk = slots[kc]
                    self.pe(lambda e, kc=kc, sl=sl, b=b, mc=mc, ch=ch: e.matmul(self.ps[b][:, :], mnTv[:, kc, mc * 128:(mc + 1) * 128], sl[:, ch * 512:(ch + 1) * 512], start=(kc == 0), stop=(kc == 7)),
                            [rk, "qkst"], ["ps%d" % b])
                self.act(lambda e, b=b, mc=mc, ch=ch: e.copy(self.memV[:, mc, ch * 512:(ch + 1) * 512], self.ps[b][:, :]), ["ps%d" % b], ["memV"])
        self.ws_release(8)

    def tm_linear_acc(self, name, aT, akey, nk, scale):
        cfg = self.cfg
        slots = [self.ws_acquire(name, "row") for _ in range(nk)]
        for j in range(cfg.nsub):
            for ch in range(2):
                b = (j * 2 + ch) % 6
                for k in range(nk):
                    _, sl, rk = slots[k]
                    self.pe(lambda e, k=k, sl=sl, b=b, j=j, ch=ch: e.matmul(self.ps[b][:, :], aT[:, k, j * 128:(j + 1) * 128], sl[:, ch * 512:(ch + 1) * 512], start=(k == 0), stop=(k == nk - 1)),
                            [rk, akey], ["ps%d" % b])
                self.dve(lambda e, b=b, j=j, ch=ch: e.scalar_tensor_tensor(self.xt[:, j, ch * 512:(ch + 1) * 512], self.ps[b][:, :], scale, self.xt[:, j, ch * 512:(ch + 1) * 512], ALU.mult, ALU.add),
                         ["ps%d" % b, "x%d" % j], ["x%d" % j])
        self.ws_release(nk)

    def wout(self, l, g0):
        TT = self.cfg.tt
        self.dma("sp", self.yTt[:], self.YT[:, g0:g0 + TT].rearrange("(c p) t -> p c t", p=128), [], ["yTt"])
        self.tm_linear_acc("w_out", self.yTt, "yTt", 8, 1.0)

    def mca(self, l):
        cfg = self.cfg
        NSUB, TT, TG, NTG = cfg.nsub, cfg.tt, cfg.tg, cfg.ntg
        self.norm_to_nT("mem_norm_x", l)
        for jb in range(8):
            _, sl, rk = self.ws_acquire("mem_wq", "col")
            wv = sl[:].rearrange("p (kc c) -> p kc c", kc=8)
            for tg in range(NTG):
                b = (jb * NTG + tg) % 6
                tsl = slice(tg * TG, (tg + 1) * TG)
                for kc in range(8):
                    self.pe(lambda e, kc=kc, wv=wv, b=b, tsl=tsl: e.matmul(self.ps[b][:, 0:TG], wv[:, kc, :], self.nT[:, kc, tsl], start=(kc == 0), stop=(kc == 7)),
                            [rk, "nT%d" % kc], ["ps%d" % b])
                self.act(lambda e, b=b, jb=jb, tsl=tsl: e.copy(self.qT[:, jb, tsl], self.ps[b][:, 0:TG]), ["ps%d" % b], ["qT%d" % jb])
            self.ws_release(1)
        bank = 0
        for tg in range(NTG):
            tsl = slice(tg * TG, (tg + 1) * TG)
            for h in range(4):
                for mc in range(2):
                    b = bank % 6; bank += 1
                    for dc in range(2):
                        self.pe(lambda e, b=b, h=h, mc=mc, dc=dc, tsl=tsl: e.matmul(self.ps[b][:, 0:TG], self.memkT[:, 2 * h + dc, mc * 128:(mc + 1) * 128], self.qT[:, 2 * h + dc, tsl], start=(dc == 0), stop=(dc == 1)),
                                ["memkT", "qT%d" % (2 * h + dc)], ["ps%d" % b])
                    self.act(lambda e, b=b, h=h, mc=mc, tsl=tsl: e.activation(self.pT[:, 2 * h + mc, tsl], self.ps[b][:, 0:TG], AF.Exp, scale=1.0 / 16.0),
                             ["ps%d" % b], ["pT%d" % (2 * h + mc)])
                b = bank % 6; bank += 1
                for mc in range(2):
                    self.pe(lambda e, b=b, h=h, mc=mc, tsl=tsl: e.matmul(self.ps[b][:, 0:TG], self.onesb[:], self.pT[:, 2 * h + mc, tsl], start=(mc == 0), stop=(mc == 1)),
                            ["onesb", "pT%d" % (2 * h + mc)], ["ps%d" % b])
                rb = h % 2
                self.dve(lambda e, b=b, rb=rb: e.reciprocal(self.rbc[:, rb, 0:TG], self.ps[b][:, 0:TG]), ["ps%d" % b], ["rbc%d" % rb])
                for dc in range(2):
                    b = bank % 6; bank += 1
                    for mc in range(2):
                        self.pe(lambda e, b=b, h=h, mc=mc, dc=dc, tsl=tsl: e.matmul(self.ps[b][:, 0:TG], self.memV[:, mc, (2 * h + dc) * 128:(2 * h + dc + 1) * 128], self.pT[:, 2 * h + mc, tsl], start=(mc == 0), stop=(mc == 1)),
                                ["memV", "pT%d" % (2 * h + mc)], ["ps%d" % b])
                    self.dve(lambda e, b=b, h=h, dc=dc, rb=rb, tsl=tsl: e.tensor_tensor(self.oT[:, 2 * h + dc, tsl], self.ps[b][:, 0:TG], self.rbc[:, rb, 0:TG], ALU.mult),
                             ["ps%d" % b, "rbc%d" % rb], ["oT"])
        self.tm_linear_acc("mem_wo", self.oT, "oT", 8, 1.0)

    def final_norm(self):
        cfg = self.cfg
        NSUB = cfg.nsub
        for j in range(NSUB):
            self.act(lambda e, j=j: e.activation(self.junk[:], self.xt[:, j, :], AF.Square, accum_out=self.ss[:, j:j + 1]),
                     ["x%d" % j], ["junk", "ss%d" % j])
        self.rsqrt(self.rstd[:, 0:NSUB], self.ss[:, 0:NSUB], 1.0 / D, ["ss%d" % j for j in range(NSUB)], "rstd")
        for j in range(NSUB):
            self.dve(lambda e, j=j: e.scalar_tensor_tensor(self.xt[:, j, :], self.xt[:, j, :], self.rstd[:, j:j + 1], self.gfin[:], ALU.mult, ALU.mult),
                     ["x%d" % j, "rstd", "gfin"], ["x%d" % j])

    def m_gqa(self, l):
        cfg = self.cfg
        self.phase_begin()
        SM = max(cfg.seqs)
        NBM = SM // 128
        KTs = self.sb("KTs", [128, 2, SM], BF16)
        Vx = self.sb("Vx", [128, NBM, 2, 65], BF16)
        onesf = self.sb("onesf", [128, 64], F32)
        QW = min(512, min(cfg.seqs))
        qt = self.sb("qtile", [128, 4, QW], BF16)
        pT2 = self.sb("pTa", [128, 12, QW], BF16)
        rr2 = self.sb("rr", [128, 4, QW], F32)
        bcs2 = self.sb("bcs", [64, 4, QW], F32)
        yst = self.sb("yst", [64, 4, QW], BF16)
        self.dve(lambda e: e.memset(onesf[:], 1.0), [], ["onesf"])
        self.dve(lambda e: e.memset(Vx[:], 1.0), [], ["Vx"])
        it = 0
        for si, (st, S) in enumerate(zip(cfg.starts, cfg.seqs)):
            NB = S // 128
            for g in range(2):
                self.dma("sp", KTs[:, g, 0:S], self.KT[g, :, st:st + S], [], ["KTs"])
            for g in range(2):
                self.dma("sp", Vx[:, 0:NB, g, 0:64], self.VV[st:st + S, g * 64:(g + 1) * 64].rearrange("(kb p) d -> p kb d", p=128), [], ["Vx"])
            nqb = S // QW
            G = 2 if nqb % 2 == 0 else 1
            for pr in range(4):
                g = pr // 2
                for qb0 in range(0, nqb, G):
                    par = it % 2; it += 1
                    chains = []
                    for gi in range(G):
                        qs = par * 2 + gi
                        qb = qb0 + gi
                        self.dma("sp", qt[:, qs, :], self.QT[pr, :, st + qb * QW: st + (qb + 1) * QW], [], ["qt%d" % qs])
                        for hh in range(2):
                            chains.append((gi * 2 + hh, qs, qb, hh))
                    for kb in range(NB + 1):
                        if kb < NB:
                            for (c, qs, qb, hh) in chains:
                                base = hh * 64
                                pk = c * 3 + kb % 3
                                self.pe(lambda e, c=c, kb=kb, g=g, base=base, qs=qs: e.matmul(self.ps[c][:, 0:QW], KTs[base:base + 64, g, kb * 128:(kb + 1) * 128], qt[base:base + 64, qs, :], start=True, stop=True),
                                        ["KTs", "qt%d" % qs], ["ps%d" % c])
                                self.act(lambda e, c=c, pk=pk: e.activation(pT2[:, pk, :], self.ps[c][:, 0:QW], AF.Exp), ["ps%d" % c], ["pTa%d" % pk])
                        if kb >= 1:
                            k2 = kb - 1
                            for (c, qs, qb, hh) in chains:
                                ob = 4 + c
                                pk = c * 3 + k2 % 3
                                self.pe(lambda e, k2=k2, g=g, ob=ob, NB=NB, pk=pk: e.matmul(self.ps[ob][0:65, 0:QW], Vx[:, k2, g, :], pT2[:, pk, :], start=(k2 == 0), stop=(k2 == NB - 1)),
                                        ["Vx", "pTa%d" % pk], ["ps%d" % ob])
                    for (c, qs, qb, hh) in chains:
                        h = 2 * pr + hh
                        ob = 4 + c
                        self.dve(lambda e, ob=ob, c=c: e.reciprocal(rr2[64:65, c, :], self.ps[ob][64:65, 0:QW]), ["ps%d" % ob], ["rr%d" % c])
                        self.pe(lambda e, c=c: e.matmul(self.ps[c][0:64, 0:QW], onesf[64:65, 0:64], rr2[64:65, c, :], start=True, stop=True), ["onesf", "rr%d" % c], ["ps%d" % c])
                        self.act(lambda e, c=c: e.copy(bcs2[:, c, :], self.ps[c][0:64, 0:QW]), ["ps%d" % c], ["bcs%d" % c])
                        self.dve(lambda e, ob=ob, c=c: e.tensor_tensor(yst[:, c, :], self.ps[ob][0:64, 0:QW], bcs2[:, c, :], ALU.mult), ["ps%d" % ob, "bcs%d" % c], ["yst%d" % c])
                        self.dma("sp", self.YT[512 + h * 64: 512 + (h + 1) * 64, st + qb * QW: st + (qb + 1) * QW], yst[:, c, :], ["yst%d" % c], ["YT"])
        self.phase_end()

    def m_fourier(self, l):
        cfg = self.cfg
        self.phase_begin()
        SM = max(cfg.seqs)
        N1M = SM // 128
        fc = self.sb("fc", [64, 2, 64], BF16)
        wf = self.sb("wf", [64, 4, 64], BF16)
        AB = self.sb("AB", [64, 4, 2, 64], BF16)
        f1 = self.sb("f1", [64, 3 * N1M], BF16)
        Ht = self.sb("Ht", [128, N1M, 2, 128], BF16)
        Ut = self.sb("Ut", [64, 128, 64], BF16)
        Ot = self.sb("Ot", [128, N1M, 3, 64], BF16)
        PT = self.sb("PTf", [64, 2, SM], BF16)
        yst = self.sb("ystf", [64, 2, 512], BF16)
        self.dma("sp", fc[:], self.cin["c_fc"], [], ["fc"])
        self.dma("pool", wf[:], self.inp["four_w"][l].rearrange("g c d -> c g d"), [], ["wf"])
        for g in range(4):
            for x in range(2):
                self.pe(lambda e, g=g, x=x: e.matmul(self.ps[0][0:64, (g * 2 + x) * 64:(g * 2 + x + 1) * 64], fc[:, x, :], wf[:, g, :], start=True, stop=True),
                        ["fc", "wf"], ["ps0"])
        self.dve(lambda e: e.tensor_copy(AB[:].rearrange("c g x d -> c (g x d)"), self.ps[0][0:64, :]), ["ps0"], ["AB"])
        for si, (st, S) in enumerate(zip(cfg.starts, cfg.seqs)):
            N1 = S // 128
            self.dma("sp", f1[0:N1, 0:3 * N1], self.cin["c_f1_%d" % S], [], ["f1"])
            self.dma("sp", Ht[:, 0:N1, :, :], self.cin["c_h_%d" % S], [], ["Ht"])
            CB = 1
            while CB * 2 <= min(64, 512 // (3 * N1)):
                CB *= 2
            for g in range(4):
                self.dma("sp", Ut[0:N1, :, :], self.UF[st:st + S, g * 64:(g + 1) * 64].rearrange("(n1 n2) c -> n1 n2 c", n2=128), [], ["Ut"])
                for c0 in range(0, 64, CB):
                    b = (c0 // CB) % 3
                    for cl in range(CB):
                        c = c0 + cl
                        self.pe(lambda e, b=b, cl=cl, c=c, N1=N1: e.matmul(self.ps[b][:, cl * 3 * N1:(cl + 1) * 3 * N1], Ut[0:N1, :, c], f1[0:N1, 0:3 * N1], start=True, stop=True),
                                ["Ut", "f1"], ["ps%d" % b])
                    self.dve(lambda e, b=b, c0=c0, N1=N1, CB=CB: e.tensor_copy(Ot[:, 0:N1, :, c0:c0 + CB].rearrange("p k x c -> p c x k"),
                                                                             self.ps[b][:, 0:CB * 3 * N1].rearrange("p (c x k) -> p c x k", c=CB, x=3)),
                             ["ps%d" % b], ["Ot"])
                KB = min(4, N1)
                for k0 in range(0, N1, KB):
                    for x in range(2):
                        b = 3 + ((k0 // KB) * 2 + x) % 3
                        for kl in range(KB):
                            k1 = k0 + kl
                            o = self.ps[b][0:64, kl * 128:(kl + 1) * 128]
                            if x == 0:
                                self.pe(lambda e, o=o, k1=k1: e.matmul(o, Ot[:, k1, 0, :], Ht[:, k1, 0, :], start=True, stop=False), ["Ot", "Ht"], ["ps%d" % b])
                                self.pe(lambda e, o=o, k1=k1: e.matmul(o, Ot[:, k1, 2, :], Ht[:, k1, 1, :], start=False, stop=True), ["Ot", "Ht"], ["ps%d" % b])
                            else:
                                self.pe(lambda e, o=o, k1=k1: e.matmul(o, Ot[:, k1, 0, :], Ht[:, k1, 1, :], start=True, stop=False), ["Ot", "Ht"], ["ps%d" % b])
                                self.pe(lambda e, o=o, k1=k1: e.matmul(o, Ot[:, k1, 1, :], Ht[:, k1, 0, :], start=False, stop=True), ["Ot", "Ht"], ["ps%d" % b])
                        dst = PT[:, x, 0:S].rearrange("c (k2 k1) -> c k1 k2", k1=N1)[:, k0:k0 + KB, :]
                        eng = self.act if x == 0 else self.dve
                        if x == 0:
                            self.act(lambda e, b=b, dst=dst, KB=KB: e.copy(dst, self.ps[b][0:64, 0:KB * 128].rearrange("c (k f) -> c k f", k=KB)), ["ps%d" % b], ["PTf"])
                        else:
                            self.dve(lambda e, b=b, dst=dst, KB=KB: e.tensor_copy(dst, self.ps[b][0:64, 0:KB * 128].rearrange("c (k f) -> c k f", k=KB)), ["ps%d" % b], ["PTf"])
                SW = min(512, S)
                for s0 in range(0, S, SW):
                    b = 6 + (s0 // SW) % 2
                    ys = (s0 // SW) % 2
                    self.pe(lambda e, b=b, g=g, s0=s0, SW=SW: e.matmul(self.ps[b][0:64, 0:SW], AB[:, g, 0, :], PT[:, 0, s0:s0 + SW], start=True, stop=False), ["AB", "PTf"], ["ps%d" % b])
                    self.pe(lambda e, b=b, g=g, s0=s0, SW=SW: e.matmul(self.ps[b][0:64, 0:SW], AB[:, g, 1, :], PT[:, 1, s0:s0 + SW], start=False, stop=True), ["AB", "PTf"], ["ps%d" % b])
                    self.act(lambda e, b=b, ys=ys, SW=SW: e.copy(yst[:, ys, 0:SW], self.ps[b][0:64, 0:SW]), ["ps%d" % b], ["ystf%d" % ys])
                    self.dma("sp", self.YT[g * 64:(g + 1) * 64, st + s0: st + s0 + SW], yst[:, ys, 0:SW], ["ystf%d" % ys], ["YT"])
        self.phase_end()

    def nb(self):
        self._bank = (getattr(self, "_bank", -1) + 1) % 6
        return self._bank

    def nbt(self):
        self._bankt = (getattr(self, "_bankt", -1) + 1) % 2
        return 6 + self._bankt

    def m_deltanet(self, l):
        cfg = self.cfg
        self.phase_begin()
        W = min(512, min(cfg.seqs)); NCH = W // 64; NCP = NCH // 2; NU = NCH * 4
        idb = self.identb[0:64, 0:64]; idf = self.identf[0:64, 0:64]; onb = self.onesb[0:64, 0:64]
        sbt = lambda n, sh, dt: self.sb(n, sh, dt)
        cw = sbt("cw", [64, 3, 12], F32); dg = sbt("dg", [64, 12, 3, 64], BF16)
        mk = sbt("mk", [64, 4, 64], F32); tri = sbt("tri", [64, 3, 64], F32)
        nA = sbt("nA", [64, 8], F32); dtb = sbt("dtb", [64, 8], F32)
        zin = sbt("zin", [64, 12, W + 2], BF16); zab = sbt("zab", [64, NCH, 16], F32)
        cs = sbt("cs", [64, 2, W], F32); sq = sbt("sq", [64, 2, W], BF16); rs = sbt("rs", [64, 2, W], F32)
        qTn = sbt("qTn", [64, NU, 64], BF16); kTn = sbt("kTn", [64, NU, 64], BF16); vTn = sbt("vTn", [64, NU, 64], BF16)
        k_tm = sbt("k_tm", [64, NU, 64], BF16); v_tm = sbt("v_tm", [64, NU, 64], BF16)
        gt = sbt("gt", [64, 8, NU], F32)
        dgx = sbt("dgx", [64, 8, 64], F32); dl = sbt("dl", [64, 8, 64], F32); egr = sbt("egr", [64, 8, 64], F32)
        t1 = sbt("t1", [64, 2, 8, 64], F32); DTt = sbt("DTt", [64, 8, 64], F32); DLt = sbt("DLt", [64, 8, 64], F32)
        qdec = sbt("qdec", [64, NU, 64], BF16); qkTm = sbt("qkTm", [64, NU, 64], BF16)
        Lp = [sbt("Lp%d" % i, [64, NU, 64], BF16) for i in range(2)]
        Up = [sbt("Up%d" % i, [64, NU, 64], BF16) for i in range(2)]
        Rp = [sbt("Rp%d" % i, [64, NU, 64], BF16) for i in range(2)]
        vb = sbt("vb", [64, NU, 64], BF16); kbe = sbt("kbe", [64, NU, 64], BF16); ktl = sbt("ktl", [64, NU, 64], BF16)
        u_sb = sbt("u_sb", [64, NU, 64], F32); wT = sbt("wT", [64, NU, 64], BF16)
        Sf = sbt("Sf", [64, 4, 64], F32); Sb = sbt("Sb", [64, 4, 64], BF16)
        vnew = sbt("vnew", [64, 2, 4, 64], BF16); ost = sbt("ost", [64, 2, 256], F32)
        o2 = sbt("o2", [128, 2, 256], F32); gte = sbt("gte", [128, 256], F32); osum = sbt("osum", [128, 256], F32)
        pss = sbt("pss", [128, 8], F32); ydb = sbt("ydb", [128, 256], BF16); gdn = sbt("gdn", [128, 64], F32)
        ydT = sbt("ydT", [128, 2, W], BF16)
        self.dma("sp", cw[:], self.inp["dn_conv"][l].rearrange("t (xh d) -> d t xh", d=64), [], ["cw"])
        self.dma("sp", mk[:], self.cin["c_dnmask"], [], ["mk"])
        self.dma("sp", tri[:], self.cin["c_tri"], [], ["tri"])
        self.dma("sp", nA[:], self.inp["dn_A_log"][l].rearrange("a h -> (a h)").partition_broadcast(64), [], ["nA"])
        self.dma("sp", dtb[:], self.inp["dn_dt_bias"][l].rearrange("a h -> (a h)").partition_broadcast(64), [], ["dtb"])
        self.dma("sp", gdn[:], self.inp["dn_out_norm"][l].partition_broadcast(128), [], ["gdn"])
        self.act(lambda e: e.activation(nA[:], nA[:], AF.Exp), ["nA"], ["nA"])
        self.dve(lambda e: e.tensor_scalar(nA[:], nA[:], -1.0, None, ALU.mult), ["nA"], ["nA"])
        for xh in range(12):
            for t in range(3):
                self.dve(lambda e, xh=xh, t=t: e.tensor_scalar(dg[:, xh, t, :], idb, cw[:, t, xh:xh + 1], None, ALU.mult), ["cw", "identb"], ["dg"])
        one_col = tri[:, 2, 0:1]
        B8 = lambda ap: ap.unsqueeze(2).to_broadcast([64, 8, 64])
        B4 = lambda ap: ap.unsqueeze(2).to_broadcast([64, 4, 64])
        psv = lambda b: self.ps[b][0:64, :].rearrange("p (u f) -> p u f", f=64)
        psbv = lambda b: self.psb[b][0:64, 0:512].rearrange("p (u f) -> p u f", f=64)

        for si, (st, S) in enumerate(zip(cfg.starts, cfg.seqs)):
            NTL = S // W
            for d in range(2):
                kT_, kL_ = (0, 1) if d == 0 else (2, 3)
                self.dve(lambda e: e.memset(Sf[:], 0.0), [], ["Sf"])
                self.dve(lambda e: e.memset(Sb[:], 0.0), [], ["Sb"])
                order = range(NTL) if d == 0 else range(NTL - 1, -1, -1)
                for ti in order:
                    p0 = ti * W
                    lo, hi = max(p0 - 1, 0), min(p0 + W + 1, S)
                    c0 = lo - (p0 - 1)
                    if p0 == 0:
                        self.dve(lambda e: e.memset(zin[:, :, 0:1], 0.0), [], ["zin"])
                    if p0 + W == S:
                        self.dve(lambda e: e.memset(zin[:, :, W + 1:W + 2], 0.0), [], ["zin"])
                    for x in range(3):
                        self.dma("sp", zin[:, x * 4:(x + 1) * 4, c0:c0 + (hi - lo)],
                                 self.ZT[x * 256:(x + 1) * 256, st + lo:st + hi].rearrange("(h d) t -> d h t", d=64), [], ["zin"])
                    self.dma("sp", zab[:], self.ZAB[st + p0:st + p0 + W, 0:16].rearrange("(n c) x -> c n x", c=64), [], ["zab"])
                    for xh in range(12):
                        x, h = xh // 4, xh % 4
                        b = self.nb()
                        for t in range(3):
                            self.pe(lambda e, b=b, xh=xh, t=t: e.matmul(self.ps[b][0:64, 0:W], dg[:, xh, t, :], zin[:, xh, t:t + W], start=(t == 0), stop=(t == 2)),
                                    ["dg", "zin"], ["ps%d" % b])
                        if x == 2:
                            self.act(lambda e, b=b, h=h: e.activation(vTn[:].rearrange("p (n h) f -> p n h f", h=4)[:, :, h, :], self.ps[b][0:64, 0:W].rearrange("p (n f) -> p n f", f=64), AF.Silu),
                                     ["ps%d" % b], ["vTn"])
                            continue
                        s2 = xh % 2
                        self.act(lambda e, b=b, s2=s2: e.activation(cs[:, s2, :], self.ps[b][0:64, 0:W], AF.Silu), ["ps%d" % b], ["cs%d" % s2])
                        self.act(lambda e, s2=s2: e.activation(sq[:, s2, :], cs[:, s2, :], AF.Square), ["cs%d" % s2], ["sq%d" % s2])
                        b2 = self.nb()
                        self.pe(lambda e, b2=b2, s2=s2: e.matmul(self.ps[b2][0:64, 0:W], onb, sq[:, s2, :], start=True, stop=True), ["onesb", "sq%d" % s2], ["ps%d" % b2])
                        self.act(lambda e, b2=b2, s2=s2: e.activation(rs[:, s2, :], self.ps[b2][0:64, 0:W], AF.Sqrt, bias=self.epsc[0:64, 0:1], scale=1.0), ["ps%d" % b2, "epsc"], ["rs%d_s" % s2])
                        self.dve(lambda e, s2=s2: e.reciprocal(rs[:, s2, :], rs[:, s2, :]), ["rs%d_s" % s2], ["rs%d" % s2])
                        dstT = (qTn if x == 0 else kTn)[:].rearrange("p (n h) f -> p n h f", h=4)[:, :, h, :]
                        csv = cs[:, s2, :].rearrange("p (n f) -> p n f", f=64); rsv = rs[:, s2, :].rearrange("p (n f) -> p n f", f=64)
                        if x == 0:
                            self.dve(lambda e, dstT=dstT, csv=csv, rsv=rsv: e.scalar_tensor_tensor(dstT, csv, 0.125, rsv, ALU.mult, ALU.mult), ["cs%d" % s2, "rs%d" % s2], ["qTn"])
                        else:
                            self.dve(lambda e, dstT=dstT, csv=csv, rsv=rsv: e.tensor_tensor(dstT, csv, rsv, ALU.mult), ["cs%d" % s2, "rs%d" % s2], ["kTn"])
                    if self.dn_stage < 2:
                        continue
                    for cp in range(NCP):
                        for (srcT, dstm, key, dkey) in ((kTn, k_tm, "kTn", "k_tm"), (vTn, v_tm, "vTn", "v_tm")):
                            b = self.nbt()
                            for u in range(8):
                                self.pe(lambda e, b=b, u=u, cp=cp, srcT=srcT: e.transpose(self.psb[b][0:64, u * 64:(u + 1) * 64], srcT[:, cp * 8 + u, :], idb), [key, "identb"], ["ps%d" % b])
                            self.dve(lambda e, b=b, cp=cp, dstm=dstm: e.tensor_copy(dstm[:, cp * 8:(cp + 1) * 8, :], psbv(b)), ["ps%d" % b], [dkey])
                    if self.dn_stage < 3:
                        continue
                    G = lambda r: gt[:, r, :]
                    Gn = lambda r: gt[:, r, :].rearrange("p (n h) -> p n h", h=4)
                    self.dve(lambda e, d=d: e.tensor_tensor(Gn(0), zab[:, :, d * 4:(d + 1) * 4], dtb[:, d * 4:(d + 1) * 4].unsqueeze(1).to_broadcast([64, NCH, 4]), ALU.add), ["zab", "dtb"], ["g0"])
                    self.act(lambda e: e.activation(G(0), G(0), AF.Exp), ["g0"], ["g0"])
                    self.act(lambda e: e.activation(G(0), G(0), AF.Ln, bias=one_col, scale=1.0), ["g0", "tri"], ["g0"])
                    self.dve(lambda e, d=d: e.tensor_tensor(Gn(1), Gn(0), nA[:, d * 4:(d + 1) * 4].unsqueeze(1).to_broadcast([64, NCH, 4]), ALU.mult), ["g0", "nA"], ["g1"])
                    self.act(lambda e, d=d: e.activation(Gn(2), zab[:, :, 8 + d * 4:8 + (d + 1) * 4], AF.Sigmoid), ["zab"], ["g2"])
                    b = self.nb()
                    self.pe(lambda e, b=b, d=d: e.matmul(self.ps[b][0:64, 0:NU], tri[:, d, :], G(1), start=True, stop=True), ["tri", "g1"], ["ps%d" % b])
                    self.pe(lambda e, b=b: e.matmul(self.ps[b][0:64, 64:64 + NU], tri[:, 2, :], G(1), start=True, stop=True), ["tri", "g1"], ["ps%d" % b])
                    self.dve(lambda e, b=b: e.tensor_copy(G(3), self.ps[b][0:64, 0:NU]), ["ps%d" % b], ["g3"])
                    self.act(lambda e, b=b: e.activation(G(4), self.ps[b][0:64, 0:NU], AF.Exp), ["ps%d" % b], ["g4"])
                    self.act(lambda e, b=b: e.activation(G(5), self.ps[b][0:64, 64:64 + NU], AF.Exp), ["ps%d" % b], ["g5"])
                    self.dve(lambda e, b=b: e.tensor_tensor(G(6), self.ps[b][0:64, 64:64 + NU], G(3), ALU.subtract), ["ps%d" % b, "g3"], ["g6"])
                    self.act(lambda e: e.activation(G(6), G(6), AF.Exp), ["g6"], ["g6"])
                    self.dve(lambda e: e.tensor_tensor(G(7), G(2), G(4), ALU.mult), ["g2", "g4"], ["g7"])
                    if self.dn_stage < 4:
                        continue
                    for cp in range(NCP):
                        us = slice(cp * 8, (cp + 1) * 8)
                        bA = self.nb()
                        for u in range(8):
                            self.pe(lambda e, bA=bA, u=u, cp=cp: e.matmul(self.ps[bA][0:64, u * 64:(u + 1) * 64], kTn[:, cp * 8 + u, :], kTn[:, cp * 8 + u, :], start=True, stop=True), ["kTn"], ["ps%d" % bA])
                        bB = self.nb()
                        for u in range(8):
                            self.pe(lambda e, bB=bB, u=u, cp=cp: e.matmul(self.ps[bB][0:64, u * 64:(u + 1) * 64], kTn[:, cp * 8 + u, :], qTn[:, cp * 8 + u, :], start=True, stop=True), ["kTn", "qTn"], ["ps%d" % bB])
                        if self.dn_stage < 4.01:
                            continue
                        self.dve(lambda e, us=us: e.tensor_tensor(dgx[:], idf.unsqueeze(1).to_broadcast([64, 8, 64]), B8(gt[:, 3, us]), ALU.mult), ["identf", "g3"], ["dgx"])
                        if self.dn_stage < 4.02:
                            continue
                        bC = self.nb()
                        self.pe(lambda e, bC=bC: e.matmul(self.ps[bC][0:64, :], tri[:, 2, :], dgx[:].rearrange("p u f -> p (u f)"), start=True, stop=True), ["tri", "dgx"], ["ps%d" % bC])
                        if self.dn_stage < 4.03:
                            continue
                        self.dve(lambda e, bC=bC, us=us: e.tensor_tensor(dl[:], psv(bC), B8(gt[:, 3, us]), ALU.subtract), ["ps%d" % bC, "g3"], ["dl"])
                        if self.dn_stage < 4.04:
                            continue
                        self.dve(lambda e, bC=bC: e.tensor_copy(egr[:], psv(bC)), ["ps%d" % bC], ["egr"])
                        self.act(lambda e: e.activation(egr[:], egr[:], AF.Exp), ["egr"], ["egr"])
                        if self.dn_stage < 4.05:
                            continue
                        self.dve(lambda e, us=us: e.tensor_tensor(qdec[:, us, :], qTn[:, us, :], egr[:], ALU.mult), ["qTn", "egr"], ["qdec"])
                        if self.dn_stage < 4.25:
                            continue
                        self.dve(lambda e, kT_=kT_: e.scalar_tensor_tensor(t1[:, 0], dl[:], 0.0, mk[:, kT_, :].unsqueeze(1).to_broadcast([64, 8, 64]), ALU.min, ALU.add), ["dl", "mk"], ["t1a"])
                        self.act(lambda e: e.activation(DTt[:], t1[:, 0], AF.Exp), ["t1a"], ["DTt"])
                        self.dve(lambda e, kL_=kL_: e.scalar_tensor_tensor(t1[:, 1], dl[:], 0.0, mk[:, kL_, :].unsqueeze(1).to_broadcast([64, 8, 64]), ALU.max, ALU.subtract), ["dl", "mk"], ["t1b"])
                        self.act(lambda e: e.activation(DLt[:], t1[:, 1], AF.Exp, scale=-1.0), ["t1b"], ["DLt"])
                        self.dve(lambda e, bB=bB, us=us: e.tensor_tensor(qkTm[:, us, :], psv(bB), DTt[:], ALU.mult), ["ps%d" % bB, "DTt"], ["qkTm"])
                        self.dve(lambda e, bA=bA: e.tensor_tensor(DLt[:], psv(bA), DLt[:], ALU.mult), ["ps%d" % bA, "DLt"], ["DLt"])
                        self.dve(lambda e, us=us: e.tensor_tensor(Lp[0][:, us, :], DLt[:], B8(gt[:, 2, us]), ALU.mult), ["DLt", "g2"], ["L0"])
                        if self.dn_stage < 4.5:
                            continue
                        bD = self.nbt()
                        for u in range(8):
                            self.pe(lambda e, bD=bD, u=u, cp=cp: e.transpose(self.psb[bD][0:64, u * 64:(u + 1) * 64], Lp[0][:, cp * 8 + u, :], idb), ["L0", "identb"], ["ps%d" % bD])
                        self.dve(lambda e, bD=bD, us=us: e.tensor_copy(Up[0][:, us, :], psbv(bD)), ["ps%d" % bD], ["U0"])
                        self.dve(lambda e, us=us: e.tensor_tensor(Rp[0][:, us, :], idb.unsqueeze(1).to_broadcast([64, 8, 64]), Up[0][:, us, :], ALU.subtract), ["identb", "U0"], ["R0"])
                        if self.dn_stage < 4.75:
                            continue
                        self.dve(lambda e, us=us: e.tensor_tensor(vb[:, us, :], v_tm[:, us, :], B8(gt[:, 2, us]), ALU.mult), ["v_tm", "g2"], ["vb"])
                        self.dve(lambda e, us=us: e.tensor_tensor(kbe[:, us, :], k_tm[:, us, :], B8(gt[:, 7, us]), ALU.mult), ["k_tm", "g7"], ["kbe"])
                        self.dve(lambda e, us=us: e.tensor_tensor(ktl[:, us, :], k_tm[:, us, :], B8(gt[:, 6, us]), ALU.mult), ["k_tm", "g6"], ["ktl"])
                    if self.dn_stage < 5:
                        continue
                    cur = 0
                    for lev in range(5):
                        nxt = 1 - cur
                        for cp in range(NCP):
                            us = slice(cp * 8, (cp + 1) * 8)
                            b = self.nb()
                            for u in range(8):
                                self.pe(lambda e, b=b, u=u, cp=cp, cur=cur: e.matmul(self.ps[b][0:64, u * 64:(u + 1) * 64], Up[cur][:, cp * 8 + u, :], Lp[cur][:, cp * 8 + u, :], start=True, stop=True),
                                        ["U%d" % cur, "L%d" % cur], ["ps%d" % b])
                            self.act(lambda e, b=b, us=us, nxt=nxt: e.copy(Lp[nxt][:, us, :], psv(b)), ["ps%d" % b], ["L%d" % nxt])
                        if lev < 4:
                            for cp in range(NCP):
                                us = slice(cp * 8, (cp + 1) * 8)
                                b = self.nb()
                                for u in range(8):
                                    self.pe(lambda e, b=b, u=u, cp=cp, cur=cur: e.matmul(self.ps[b][0:64, u * 64:(u + 1) * 64], Lp[cur][:, cp * 8 + u, :], Up[cur][:, cp * 8 + u, :], start=True, stop=True),
                                            ["U%d" % cur, "L%d" % cur], ["ps%d" % b])
                                self.act(lambda e, b=b, us=us, nxt=nxt: e.copy(Up[nxt][:, us, :], psv(b)), ["ps%d" % b], ["U%d" % nxt])
                        for cp in range(NCP):
                            us = slice(cp * 8, (cp + 1) * 8)
                            b = self.nb()
                            for u in range(8):
                                self.pe(lambda e, b=b, u=u, cp=cp, cur=cur, nxt=nxt: e.matmul(self.ps[b][0:64, u * 64:(u + 1) * 64], Lp[nxt][:, cp * 8 + u, :], Rp[cur][:, cp * 8 + u, :], start=True, stop=True),
                                        ["L%d" % nxt, "R%d" % cur], ["ps%d" % b])
                            self.dve(lambda e, b=b, us=us, cur=cur, nxt=nxt: e.tensor_tensor(Rp[nxt][:, us, :], psv(b), Rp[cur][:, us, :], ALU.add), ["ps%d" % b, "R%d" % cur], ["R%d" % nxt])
                        cur = nxt
                    Rf_ = Rp[cur]; rkey = "R%d" % cur
                    if self.dn_stage < 6:
                        continue
                    for cp in range(NCP):
                        us = slice(cp * 8, (cp + 1) * 8)
                        b = self.nb()
                        for u in range(8):
                            self.pe(lambda e, b=b, u=u, cp=cp, Rf_=Rf_: e.matmul(self.ps[b][0:64, u * 64:(u + 1) * 64], Rf_[:, cp * 8 + u, :], vb[:, cp * 8 + u, :], start=True, stop=True), [rkey, "vb"], ["ps%d" % b])
                        self.act(lambda e, b=b, us=us: e.copy(u_sb[:, us, :], psv(b)), ["ps%d" % b], ["u_sb"])
                        b = self.nb()
                        for u in range(8):
                            self.pe(lambda e, b=b, u=u, cp=cp, Rf_=Rf_: e.matmul(self.ps[b][0:64, u * 64:(u + 1) * 64], kbe[:, cp * 8 + u, :], Rf_[:, cp * 8 + u, :], start=True, stop=True), [rkey, "kbe"], ["ps%d" % b])
                        self.dve(lambda e, b=b, us=us: e.tensor_copy(wT[:, us, :], psv(b)), ["ps%d" % b], ["wT"])
                    if self.dn_stage < 7:
                        continue
                    corder = range(NCH) if d == 0 else range(NCH - 1, -1, -1)
                    for n in corder:
                        hs = slice(n * 4, (n + 1) * 4)
                        par = n % 2
                        psh = lambda b: self.ps[b][0:64, 0:256].rearrange("p (h f) -> p h f", f=64)
                        bG = self.nb()
                        for h in range(4):
                            self.pe(lambda e, bG=bG, h=h, n=n: e.matmul(self.ps[bG][0:64, h * 64:(h + 1) * 64], wT[:, n * 4 + h, :], Sb[:, h, :], start=True, stop=True), ["wT", "Sb"], ["ps%d" % bG])
                        self.dve(lambda e, bG=bG, hs=hs, par=par: e.tensor_tensor(vnew[:, par], u_sb[:, hs, :], psh(bG), ALU.subtract), ["u_sb", "ps%d" % bG], ["vnew%d" % par])
                        bH = self.nb()
                        for h in range(4):
                            self.pe(lambda e, bH=bH, h=h, n=n: e.matmul(self.ps[bH][0:64, h * 64:(h + 1) * 64], qdec[:, n * 4 + h, :], Sb[:, h, :], start=True, stop=False), ["qdec", "Sb"], ["ps%d" % bH])
                            self.pe(lambda e, bH=bH, h=h, n=n, par=par: e.matmul(self.ps[bH][0:64, h * 64:(h + 1) * 64], qkTm[:, n * 4 + h, :], vnew[:, par, h, :], start=False, stop=True), ["qkTm", "vnew%d" % par], ["ps%d" % bH])
                        bI = self.nb()
                        for h in range(4):
                            self.pe(lambda e, bI=bI, h=h, n=n, par=par: e.matmul(self.ps[bI][0:64, h * 64:(h + 1) * 64], ktl[:, n * 4 + h, :], vnew[:, par, h, :], start=True, stop=True), ["ktl", "vnew%d" % par], ["ps%d" % bI])
                        self.dve(lambda e, hs=hs: e.tensor_tensor(Sf[:], Sf[:], B4(gt[:, 5, hs]), ALU.mult), ["Sf", "g5"], ["Sf"])
                        self.dve(lambda e, bI=bI: e.tensor_tensor(Sf[:], Sf[:], psh(bI), ALU.add), ["Sf", "ps%d" % bI], ["Sf"])
                        self.act(lambda e: e.copy(Sb[:], Sf[:]), ["Sf"], ["Sb"])
                        self.act(lambda e, bH=bH, par=par: e.copy(ost[:, par, :], self.ps[bH][0:64, 0:256]), ["ps%d" % bH], ["ost%d" % par])
                        self.dma("sp", self.OF[d, st + p0 + n * 64: st + p0 + (n + 1) * 64, :], ost[:, par, :], ["ost%d" % par], ["OF"])
            for p0 in (range(0, S, W) if self.dn_stage >= 8 else []):
                for jb in range(W // 128):
                    q0 = st + p0 + jb * 128
                    self.dma("sp", o2[:], self.OF[:, q0:q0 + 128, :].rearrange("a p f -> p a f"), ["OF"], ["o2"])
                    self.dma("sp", gte[:], self.ZAB[q0:q0 + 128, 16:272], [], ["gte"])
                    self.dve(lambda e: e.tensor_tensor(osum[:], o2[:, 0, :], o2[:, 1, :], ALU.add), ["o2"], ["osum"])
                    self.act(lambda e: e.activation(o2[:, 0, :], osum[:], AF.Square), ["osum"], ["o2"])
                    self.dve(lambda e: e.tensor_reduce(pss[:, 0:4], o2[:, 0, :].rearrange("p (h f) -> p h f", f=64), AX.X, ALU.add), ["o2"], ["pss"])
                    self.rsqrt(pss[:, 4:8], pss[:, 0:4], 1.0 / 64, ["pss"], "prs")
                    self.dve(lambda e: e.tensor_tensor(osum[:].rearrange("p (h f) -> p h f", f=64), osum[:].rearrange("p (h f) -> p h f", f=64), pss[:, 4:8].unsqueeze(2).to_broadcast([128, 4, 64]), ALU.mult), ["osum", "prs"], ["osum"])
                    self.dve(lambda e: e.tensor_tensor(osum[:].rearrange("p (h f) -> p h f", f=64), osum[:].rearrange("p (h f) -> p h f", f=64), gdn[:].unsqueeze(1).to_broadcast([128, 4, 64]), ALU.mult), ["osum", "gdn"], ["osum"])
                    self.act(lambda e: e.activation(gte[:], gte[:], AF.Silu), ["gte"], ["gte"])
                    self.dve(lambda e: e.tensor_tensor(ydb[:], osum[:], gte[:], ALU.mult), ["osum", "gte"], ["ydb"])
                    b = self.nbt()
                    for c in range(2):
                        self.pe(lambda e, b=b, c=c: e.transpose(self.psb[b][:, c * 128:(c + 1) * 128], ydb[:, c * 128:(c + 1) * 128], self.identb[:]), ["ydb", "identb"], ["ps%d" % b])
                    self.dve(lambda e, b=b, jb=jb: e.tensor_copy(ydT[:, :, jb * 128:(jb + 1) * 128], self.psb[b][:, 0:256].rearrange("p (c t) -> p c t", c=2)), ["ps%d" % b], ["ydT"])
                self.dma("sp", self.YT[256:512, st + p0: st + p0 + W].rearrange("(c p) t -> p c t", p=128), ydT[:], ["ydT"], ["YT"])
        self.phase_end()


_SKIP = set()


def kernel(**inputs):
    cfg = FULL
    b = Builder(cfg)
    b.skip = set(_SKIP)
    nc = b.build()
    consts = host_consts(cfg)
    xp = np.asarray(inputs["x_prompt"], dtype=np.float32)
    xs = np.asarray(inputs["x_sample"], dtype=np.float32)
    mp = np.asarray(inputs["mem_prompt"], dtype=np.float32)
    ms = np.asarray(inputs["mem_sample"], dtype=np.float32)
    w = {n: np.ascontiguousarray(np.asarray(inputs[n], dtype=np.float32)) for n in WNAMES}
    in_maps = []
    for c in range(NCORES):
        x = np.concatenate([xp[4 * c:4 * c + 4].reshape(-1, D), xs[c].reshape(-1, D)], axis=0)
        mem = np.concatenate([mp[4 * c:4 * c + 4].reshape(-1, D), ms[c].reshape(-1, D)], axis=0)
        m = {"x": np.ascontiguousarray(x), "mem": np.ascontiguousarray(mem)}
        m.update(w)
        m.update(consts)
        in_maps.append(m)
    res = run_bass_kernel_spmd(nc, in_maps, core_ids=list(range(NCORES)))
    yp = np.empty((32, 2048, D), np.float32)
    ys = np.empty((8, 8192, D), np.float32)
    for c in range(NCORES):
        y = np.asarray(res.results[c]["y"], dtype=np.float32)
        yp[4 * c:4 * c + 4] = y[0:8192].reshape(4, 2048, D)
        ys[c] = y[8192:16384]
    return (yp, ys)
```

```python
import math
from contextlib import ExitStack
import numpy as np
import ml_dtypes
import concourse.bass as bass
import concourse.mybir as mybir
from concourse.bass_utils import run_bass_kernel_spmd

F32 = mybir.dt.float32
BF16 = mybir.dt.bfloat16
AF = mybir.ActivationFunctionType
ALU = mybir.AluOpType
AX = mybir.AxisListType

D = 1024
DFF = 2816
NFB = DFF // 128
INW = 2064
EPS = 1e-6
NCORES = 8
MEMT = 256
ENGS = ("pe", "act", "dve", "pool", "sp")


class Op:
    __slots__ = ("eng", "fn", "deps", "sig", "cnt", "dma", "sem", "semval")

    def __init__(self, eng, fn, dma):
        self.eng = eng; self.fn = fn; self.dma = dma
        self.deps = []; self.sig = False; self.cnt = 0; self.sem = None; self.semval = 0


class Rec:
    NDMASEM = 14

    def __init__(self):
        self.ops = {e: [] for e in ENGS}
        self.lastw = {}
        self.readers = {}
        self.dma_uses = {}
        self.dma_rr = {}
        self.pending = {e: [] for e in ENGS}
        self.last_compute = {}
        self.n = 0
        self.seq = []
        self.sid = 0
        self.streams = None

    def add(self, eng, fn, reads=(), writes=(), dma=False):
        op = Op(eng, fn, dma)
        deps = {}
        for r in reads:
            w = self.lastw.get(r)
            if w is not None:
                deps[w] = "raw"
        for w_ in writes:
            lw = self.lastw.get(w_)
            if lw is not None and lw not in deps:
                deps[lw] = "waw"
            for rd in self.readers.get(w_, ()):
                if rd not in deps:
                    deps[rd] = "war"
        need = []
        for d, kind in deps.items():
            if d.dma or dma:
                need.append(d)
            elif d.eng == eng:
                if eng == "pe":
                    continue
                need.append(d)
            else:
                need.append(d)
        if dma:
            rk = (self.sid, eng)
            k = self.dma_rr.get(rk, 0) % self.NDMASEM + self.sid * self.NDMASEM
            self.dma_rr[rk] = self.dma_rr.get(rk, 0) + 1
            ent = self.dma_uses.setdefault((eng, k), [0, None])
            if ent[1] is not None:
                need.append(ent[1])
            ent[0] += 1
            op.sem = (eng, k); op.semval = 16 * ent[0]
            ent[1] = op
        if self.pending[eng]:
            need.extend(self.pending[eng]); self.pending[eng] = []
        for d in need:
            d.sig = True
        op.deps = need
        for r in reads:
            self.readers.setdefault(r, []).append(op)
        for w_ in writes:
            self.lastw[w_] = op
            self.readers[w_] = []
        self.seq.append(op)
        if not dma:
            self.last_compute[eng] = op
        self.n += 1
        return op

    def barrier(self):
        lasts = []
        for e in ENGS:
            if self.last_compute.get(e) is not None:
                lasts.append(self.last_compute[e])
        for ent in self.dma_uses.values():
            if ent[1] is not None:
                lasts.append(ent[1])
        for e in ENGS:
            self.pending[e] = [o for o in lasts if not (o.eng == e and not o.dma)]
        self.lastw = {}; self.readers = {}

    def fork(self, n=2):
        self.streams = []
        for i in range(n):
            self.streams.append({"seq": [], "lastw": {}, "readers": {}, "pending": {e: list(self.pending[e]) for e in ENGS}})
        self.main_seq = self.seq
        self.pending = {e: [] for e in ENGS}

    def use(self, i):
        st = self.streams[i]
        self.sid = i
        self.seq, self.lastw, self.readers, self.pending = st["seq"], st["lastw"], st["readers"], st["pending"]

    def join(self):
        lists = [s["seq"] for s in self.streams]
        idx = [0] * len(lists)
        merged = []
        total = sum(len(x) for x in lists)
        for _ in range(total):
            best, bf = None, None
            for i, lst in enumerate(lists):
                if idx[i] < len(lst):
                    f = idx[i] / len(lst)
                    if bf is None or f < bf:
                        best, bf = i, f
            merged.append(lists[best][idx[best]]); idx[best] += 1
        self.seq = self.main_seq
        self.seq.extend(merged)
        self.sid = 0
        self.lastw = {}; self.readers = {}
        self.pending = {e: [] for e in ENGS}
        for op in merged:
            if not op.dma:
                self.last_compute[op.eng] = op
        self.streams = None

    def setup_sems(self, nc, stack):
        self.esem = {e: stack.enter_context(nc.semaphore("s_" + e)) for e in ENGS}
        self.dsem = {}
        for q in ("sp", "pool"):
            for k in range(self.NDMASEM * 2):
                self.dsem[(q, k)] = stack.enter_context(nc.semaphore("d_%s%d" % (q, k)))
        self.cnts = {e: 0 for e in ENGS}
        self.waited = {e: {} for e in ENGS}

    def emit(self, nc, final=False):
        esem, dsem = self.esem, self.dsem
        for e in ENGS:
            if self.last_compute.get(e) is not None:
                self.last_compute[e].sig = True
        for op in self.seq:
            if op.sig and not op.dma:
                self.cnts[op.eng] += 1; op.cnt = self.cnts[op.eng]
        finals = [(dsem[k], 16 * ent[0]) for k, ent in self.dma_uses.items()] if final else []
        ops = {e: [] for e in ENGS}
        for op in self.seq:
            ops[op.eng].append(op)
        self.seq = []
        with nc.Block() as block:
            def run(engname, e):
                waited = self.waited[engname]
                for op in ops[engname]:
                    for d in op.deps:
                        if d.dma:
                            s, v = dsem[d.sem], d.semval
                        else:
                            s, v = esem[d.eng], d.cnt
                            assert v > 0
                        if waited.get(id(s), 0) < v:
                            e.wait_ge(s, v); waited[id(s)] = v
                    ins = op.fn(e)
                    if op.dma:
                        ins.then_inc(dsem[op.sem], 16)
                    elif op.sig:
                        ins.then_inc(esem[engname], 1)
                if engname == "sp":
                    for s, v in finals:
                        e.wait_ge(s, v)

            @block.tensor
            def _(e):
                run("pe", e)

            @block.scalar
            def _(e):
                run("act", e)

            @block.vector
            def _(e):
                run("dve", e)

            @block.gpsimd
            def _(e):
                run("pool", e)

            @block.sync
            def _(e):
                run("sp", e)


class Cfg:
    def __init__(self, seqs, depth=2):
        self.seqs = list(seqs)
        self.depth = depth
        self.ntok = sum(seqs)
        self.starts = [sum(seqs[:i]) for i in range(len(seqs))]
        self.tt = min(1024, min(seqs))
        self.nsub = self.tt // 128
        self.tg = min(512, self.tt)
        self.ntg = self.tt // self.tg


FULL = Cfg([2048, 2048, 2048, 2048, 8192])

WNAMES = ["ffn1_norm", "ffn1_w_gu", "ffn1_w_down", "mix_norm", "w_in", "four_w", "dn_conv", "dn_A_log",
          "dn_dt_bias", "dn_out_norm", "attn_q_norm", "attn_k_norm", "w_out", "mem_norm_x", "mem_norm_m",
          "mem_wq", "mem_wkv", "mem_wo", "ffn2_norm", "ffn2_w_gu", "ffn2_w_down", "final_norm"]
WSHAPES = {
    "ffn1_norm": (2, D), "ffn1_w_gu": (2, D, 2 * DFF), "ffn1_w_down": (2, DFF, D), "mix_norm": (2, D),
    "w_in": (2, D, INW), "four_w": (2, 4, 64, 64), "dn_conv": (2, 3, 768), "dn_A_log": (2, 2, 4),
    "dn_dt_bias": (2, 2, 4), "dn_out_norm": (2, 64), "attn_q_norm": (2, 64), "attn_k_norm": (2, 64),
    "w_out": (2, D, D), "mem_norm_x": (2, D), "mem_norm_m": (2, D), "mem_wq": (2, D, D),
    "mem_wkv": (2, D, 2 * D), "mem_wo": (2, D, D), "ffn2_norm": (2, D), "ffn2_w_gu": (2, D, 2 * DFF),
    "ffn2_w_down": (2, DFF, D), "final_norm": (D,),
}
NORM_IDS = {"ffn1_norm": 0, "mix_norm": 1, "mem_norm_x": 2, "mem_norm_m": 3, "ffn2_norm": 4}


def host_consts(cfg):
    c = {}
    c["c_identb"] = np.eye(128, dtype=np.float32).astype(ml_dtypes.bfloat16)
    c["c_identf"] = np.eye(128, dtype=np.float32)
    smax = max(cfg.seqs)
    nblk = smax // 128
    pos = np.arange(smax)
    row = (pos // 64).astype(np.float32); col = (pos % 64).astype(np.float32)
    inv = (1.0 / (10000.0 ** (np.arange(16, dtype=np.float32) * (2.0 / 32)))).astype(np.float32)
    ang = np.stack([row[:, None] * inv, col[:, None] * inv], axis=1).astype(np.float32)
    cos = np.cos(ang).astype(np.float32).reshape(nblk, 128, 2, 16).transpose(1, 0, 2, 3)
    sin = np.sin(ang).astype(np.float32).reshape(nblk, 128, 2, 16).transpose(1, 0, 2, 3)
    c["c_cos"] = np.ascontiguousarray(cos); c["c_sin"] = np.ascontiguousarray(sin)
    bf = ml_dtypes.bfloat16
    cc = np.arange(64)
    th = 2 * np.pi * np.outer(cc, cc) / 64.0
    c["c_fc"] = np.stack([np.cos(th) / 8.0, np.sin(th) / 8.0], axis=1).astype(np.float32).astype(bf)
    for S in sorted(set(cfg.seqs)):
        N1 = S // 128
        n1 = np.arange(N1); t1 = 2 * np.pi * np.outer(n1, n1) / N1
        sc = 1.0 / np.sqrt(S)
        f1 = np.concatenate([np.cos(t1), -np.sin(t1), np.sin(t1)], axis=1) * sc
        c["c_f1_%d" % S] = f1.astype(np.float32).astype(bf)
        n2 = np.arange(128)[:, None, None]; k1 = np.arange(N1)[None, :, None]; k2 = np.arange(128)[None, None, :]
        ang = 2 * np.pi * ((n2 * (k1 + N1 * k2)) % S) / S
        h = np.stack([np.cos(ang), -np.sin(ang)], axis=2)
        c["c_h_%d" % S] = h.astype(np.float32).astype(bf)
    p = np.arange(64)[:, None]; f = np.arange(64)[None, :]
    NEG = -1.0e4
    mk = np.stack([np.where(f >= p, 0.0, NEG), np.where(f < p, 0.0, NEG), np.where(f <= p, 0.0, NEG), np.where(f > p, 0.0, NEG)], axis=1)
    c["c_dnmask"] = mk.astype(np.float32)
    tri = np.stack([(p <= f).astype(np.float32), (p >= f).astype(np.float32), np.ones((64, 64), np.float32)], axis=1)
    c["c_tri"] = tri.astype(np.float32)
    return c


class Builder:
    def __init__(self, cfg, dump=(), stop_after=None):
        self.cfg = cfg
        self.dump = set(dump)
        self.stop_after = stop_after
        self.skip = set()
        self.dn_stage = 9
        self.overlap = False
        self.own_phase = True
        self.nc = bass.Bass("TRN2", target_bir_lowering=False)
        self.R = Rec()
        self.sb_off = 16512
        self.sb_mark = 16512

    def dram(self, name, shape, dt, kind=None):
        if kind is None:
            kind = "ExternalOutput" if name in self.dump else ("ExternalInput" if name in getattr(self, "ext_in", ()) else "Internal")
        return self.nc.dram_tensor(name, list(shape), dt, kind=kind).ap()

    def sb(self, name, shape, dt):
        self.tid = getattr(self, "tid", 0) + 1
        return self.pstack.enter_context(self.nc.sbuf_tensor("%s_%d" % (name, self.tid), list(shape), dt))

    def phase_begin(self):
        if not self.own_phase:
            return
        self.R.barrier()
        self.pstack = ExitStack()

    def phase_end(self, final=False):
        if not self.own_phase:
            return
        self.R.emit(self.nc, final=final)
        self.pstack.close()

    def pe(self, fn, r, w): return self.R.add("pe", fn, r, w)
    def act(self, fn, r, w): return self.R.add("act", fn, r, w)
    def dve(self, fn, r, w): return self.R.add("dve", fn, r, w)
    def pool(self, fn, r, w): return self.R.add("pool", fn, r, w)

    def dma(self, q, out, in_, r, w):
        return self.R.add(q, lambda e, o=out, i=in_: e.dma_start(out=o, in_=i), r, w, dma=True)

    def build(self):
        cfg = self.cfg; nc = self.nc
        NT = cfg.ntok
        self.inp = {}
        self.inp["x"] = self.dram("x", [NT, D], F32, "ExternalInput")
        self.inp["mem"] = self.dram("mem", [len(cfg.seqs) * MEMT, D], F32, "ExternalInput")
        for n in WNAMES:
            shp = WSHAPES[n]
            if getattr(self, "tiny_w", False) and int(np.prod(shp)) > 100000:
                shp = (2, 128, 128)
            self.inp[n] = self.dram(n, shp, F32, "ExternalInput")
        hc = host_consts(cfg)
        self.cin = {}
        for k, v in hc.items():
            self.cin[k] = self.dram(k, v.shape, BF16 if v.dtype == ml_dtypes.bfloat16 else F32, "ExternalInput")
        self.y = self.dram("y", [NT, D], F32, "ExternalOutput")
        self.XS = self.dram("XS", [NT, D], F32)
        self.YT = self.dram("YT", [D, NT], BF16)
        self.UF = self.dram("UF", [NT, 256], BF16)
        self.ZT = self.dram("ZT", [768, NT], BF16)
        self.ZAB = self.dram("ZAB", [NT, 272], F32)
        self.QT = self.dram("QT", [4, 128, NT], BF16)
        self.KT = self.dram("KT", [2, 128, NT], BF16)
        self.VV = self.dram("VV", [NT, 128], BF16)
        self.OF = self.dram("OF", [2, NT, 256], F32)

        with ExitStack() as stack:
            stack.enter_context(nc.allow_non_contiguous_dma(reason="strided weight/activation tiles"))
            self.ps = [stack.enter_context(nc.psum_tensor("ps%d" % b, [128, 512], F32)) for b in range(8)]
            self.psb = {b: self.ps[b][:].bitcast(BF16) for b in range(8)}
            self.R.setup_sems(nc, stack)
            self.pstack = stack
            self.setup_consts()
            self.R.emit(nc)
            self.program()
            self.pstack = stack
            self.dve(lambda e: e.memset(self.epsc[:, 0:1], EPS), [], ["epsc"])
            self.R.emit(nc, final=True)
        return nc

    def setup_consts(self):
        cfg = self.cfg
        self.identb = self.sb("identb", [128, 128], BF16)
        self.identf = self.sb("identf", [128, 128], F32)
        self.onesb = self.sb("onesb", [128, 128], BF16)
        self.gains = self.sb("gains", [128, 88], F32)
        self.gq = self.sb("gq", [128, 2, 64], F32)
        self.gk = self.sb("gk", [128, 2, 64], F32)
        gstage = self.sb("gstage", [88, 128], F32)
        self.dma("sp", self.identb[:], self.cin["c_identb"], [], ["identb"])
        self.dma("sp", self.identf[:], self.cin["c_identf"], [], ["identf"])
        self.dve(lambda e: e.memset(self.onesb[:], 1.0), [], ["onesb"])
        self.epsc = self.sb("epsc", [128, 8], F32)
        self.dve(lambda e: e.memset(self.epsc[:], EPS), [], ["epsc"])
        for nm, i in NORM_IDS.items():
            self.dma("sp", gstage[i * 16:(i + 1) * 16, :],
                     self.inp[nm].rearrange("l (c p) -> (l c) p", p=128), [], ["gstage"])
        self.dma("sp", gstage[80:88, :], self.inp["final_norm"].rearrange("(c p) -> c p", p=128), [], ["gstage"])
        self.pe(lambda e: e.transpose(self.ps[0][:, 0:88], gstage[0:88, :], self.identf[0:88, 0:88]),
                ["gstage", "identf"], ["ps0"])
        self.dve(lambda e: e.tensor_copy(self.gains[:], self.ps[0][:, 0:88]), ["ps0"], ["gains"])
        self.dma("sp", self.gq[:].rearrange("p l d -> p (l d)"),
                 self.inp["attn_q_norm"].rearrange("l d -> (l d)").partition_broadcast(128), [], ["gq"])
        self.dma("sp", self.gk[:].rearrange("p l d -> p (l d)"),
                 self.inp["attn_k_norm"].rearrange("l d -> (l d)").partition_broadcast(128), [], ["gk"])
        self.dve(lambda e: e.tensor_scalar(self.gq[:], self.gq[:], 0.125, None, ALU.mult), ["gq"], ["gq"])

    def rsqrt(self, out, in_, scale, rkeys, wkey):
        self.act(lambda e: e.activation(out, in_, AF.Sqrt, bias=self.epsc[:out.shape[0], 0:1], scale=scale), list(rkeys) + ["epsc"], [wkey + "_s"])
        self.dve(lambda e: e.reciprocal(out, out), [wkey + "_s"], [wkey])

    def gain(self, name, l, c):
        i = NORM_IDS[name] * 16 + l * 8 + c
        return self.gains[:, i:i + 1]

    def program(self):
        cfg = self.cfg
        L = cfg.depth
        sa = self.stop_after
        if sa == "consts":
            return
        if sa == "dnonly":
            self.m_deltanet(0)
            return
        self.t_phase(0)
        if sa in ("T0", "load", "ffn", "n1", "n2"):
            return
        for l in range(L):
            self.m_phase(l)
            if sa == "M%d" % l:
                return
            self.t_phase(l + 1)
            if sa == "T%d" % (l + 1):
                return

    def m_phase(self, l):
        if self.overlap and "gqa" not in self.skip and "dn" not in self.skip:
            self.phase_begin()
            self.R.fork(2)
            self.own_phase = False
            self.R.use(0); self.m_gqa(l)
            self.R.use(1); self.m_deltanet(l)
            self.own_phase = True
            self.R.join()
            self.phase_end()
            if "four" not in self.skip:
                self.m_fourier(l)
            return
        if "gqa" not in self.skip:
            self.m_gqa(l)
        if "four" not in self.skip:
            self.m_fourier(l)
        if "dn" not in self.skip:
            self.m_deltanet(l)
        else:
            self.phase_begin()
            zz = self.sb("zz", [128, 2, self.cfg.ntok], BF16)
            self.dve(lambda e: e.memset(zz[:], 0.0), [], ["zz"])
            self.dma("sp", self.YT[256:512, :].rearrange("(c p) t -> p c t", p=128), zz[:], ["zz"], ["YT"])
            self.phase_end()

    def wplan_tile(self, ph, first_of_seq):
        L = self.cfg.depth
        blocks = []
        lp, ln = ph - 1, ph
        if ph >= 1:
            if first_of_seq:
                for j in range(8):
                    blocks.append(("mem_wkv", lp, "col", j))
                for k in range(8):
                    blocks.append(("mem_wkv", lp, "row", k, [(0, 1024, 1024)]))
            for k in range(8):
                blocks.append(("w_out", lp, "row", k, [(0, 0, 1024)]))
            for j in range(8):
                blocks.append(("mem_wq", lp, "col", j))
            for k in range(8):
                blocks.append(("mem_wo", lp, "row", k, [(0, 0, 1024)]))
            blocks += self.ffn_blocks("ffn2", lp)
        if ph < L:
            blocks += self.ffn_blocks("ffn1", ln)
            for k in range(8):
                blocks.append(("w_in", ln, "row", k, [(0, 0, 256), (256, 1024, 272)]))
            for k in range(8):
                blocks.append(("w_in", ln, "row", k, [(0, 1296, 768)]))
            for j in range(2, 8):
                blocks.append(("w_in", ln, "col", j))
        return blocks

    def ffn_blocks(self, pre, l):
        out = []
        for half in range(2):
            for jj in range(11 * half, 11 * half + 11):
                out.append((pre + "_w_gu", l, "col", jj))
                out.append((pre + "_w_gu", l, "col", NFB + jj))
            for k in range(11 * half, 11 * half + 11):
                out.append((pre + "_w_down", l, "row", k, [(0, 0, 1024)]))
        return out

    def ws_init(self, plan):
        self.ws_plan = plan
        self.ws_next_dma = 0
        self.ws_next_acq = 0
        self.ws_released = 0
        self.ws_fill()

    def ws_fill(self):
        while self.ws_next_dma < len(self.ws_plan) and self.ws_next_dma < self.ws_released + self.NS:
            b = self.ws_plan[self.ws_next_dma]
            s = self.ws_next_dma % self.NS
            slot = self.wslots[s]
            w = self.inp[b[0]]
            if b[2] == "col":
                j = b[3]
                src = w[b[1], :, j * 128:(j + 1) * 128].rearrange("(kc p) c -> p kc c", p=128)
                self.dma("pool", slot[:].rearrange("p (kc c) -> p kc c", kc=8), src, [], ["ws%d" % s])
            else:
                k = b[3]
                for (d0, c0, n) in b[4]:
                    self.dma("pool", slot[:, d0:d0 + n], w[b[1], k * 128:(k + 1) * 128, c0:c0 + n], [], ["ws%d" % s])
            self.ws_next_dma += 1

    def ws_acquire(self, name, kind):
        i = self.ws_next_acq
        b = self.ws_plan[i]
        assert b[0] == name and b[2] == kind, (b, name, kind)
        assert i < self.ws_next_dma, "weight block not yet scheduled (ring too small)"
        self.ws_next_acq += 1
        s = i % self.NS
        return s, self.wslots[s], "ws%d" % s

    def ws_release(self, n=1):
        self.ws_released += n
        self.ws_fill()

    def t_phase(self, ph):
        cfg = self.cfg
        L = cfg.depth
        self.phase_begin()
        TT, NSUB, TG, NTG = cfg.tt, cfg.nsub, cfg.tg, cfg.ntg
        self.NS = 16
        self.wslots = [self.sb("ws%d" % s, [128, 1024], BF16) for s in range(self.NS)]
        self.xt = self.sb("xt", [128, NSUB, D], F32)
        self.nT = self.sb("nT", [128, 8, TT], BF16)
        self.ov = self.sb("ov", [128, 24 * TT], BF16)
        self.hT = self.ov[:, 0:11 * TT].rearrange("p (c t) -> p c t", c=11)
        self.qT = self.ov[:, 0:8 * TT].rearrange("p (c t) -> p c t", c=8)
        self.pT = self.ov[:, 8 * TT:16 * TT].rearrange("p (c t) -> p c t", c=8)
        self.oT = self.ov[:, 16 * TT:24 * TT].rearrange("p (c t) -> p c t", c=8)
        self.yTt = self.ov[:, 8 * TT:16 * TT].rearrange("p (c t) -> p c t", c=8)
        self.memkT = self.sb("memkT", [128, 8, MEMT], BF16)
        self.memV = self.sb("memV", [128, 2, D], BF16)
        self.memnT = self.sb("memnT", [128, 8, MEMT], BF16)
        self.rbc = self.sb("rbc", [128, 2, 512], F32)
        self.cos = self.sb("cos", [128, NSUB, 2, 16], F32)
        self.sin = self.sb("sin", [128, NSUB, 2, 16], F32)
        if ph == L:
            self.gfin = self.sb("gfin", [128, D], F32)
            self.dma("sp", self.gfin[:], self.inp["final_norm"].partition_broadcast(128), [], ["gfin"])
        self.xn = self.sb("xn", [128, 2, D], BF16)
        self.junk = self.sb("junk", [128, D], BF16)
        self.ss = self.sb("ss", [128, 8], F32)
        self.rstd = self.sb("rstd", [128, 8], F32)
        self.sg = self.sb("sg", [128, 2, 512], F32)
        self.zt = self.sb("zt", [128, 1296], F32)
        self.zq = self.sb("zq", [128, 768], F32)
        self.zqb = self.sb("zqb", [128, 768], BF16)
        self.zsq = self.sb("zsq", [128, 640], F32)
        self.zn = self.sb("zn", [128, 640], F32)
        self.zr = self.sb("zr", [128, 640], F32)
        self.rt = self.sb("rt", [128, 4, 320], F32)
        self.zss = self.sb("zss", [128, 32], F32)
        self.qkst = self.sb("qkst", [128, 6, TT], BF16)
        self.vst = self.sb("vst", [128, 128], BF16)
        self.ufst = self.sb("ufst", [128, 256], BF16)
        self.zfm = self.sb("zfm", [128, 2, 512], BF16)
        tiles = []
        for si, (st, S) in enumerate(zip(cfg.starts, cfg.seqs)):
            for t0 in range(0, S, TT):
                tiles.append((si, st, S, t0))
        plan = []
        for (si, st, S, t0) in tiles:
            plan += self.wplan_tile(ph, t0 == 0)
        self.ws_init(plan)
        xsrc = self.inp["x"] if ph == 0 else self.XS
        xdst = self.y if ph == L else self.XS
        for (si, st, S, t0) in tiles:
            g0 = st + t0
            self.dma("sp", self.xt[:], xsrc[g0:g0 + TT, :].rearrange("(j p) d -> p j d", p=128),
                     [], ["x%d" % j for j in range(NSUB)])
            if ph >= 1:
                if t0 == 0:
                    self.mem_kv(ph - 1, si)
                self.wout(ph - 1, g0)
                self.mca(ph - 1)
                self.ffn("ffn2", ph - 1)
            if ph == L:
                self.final_norm()
            if ph < L:
                b0 = t0 // 128
                self.dma("sp", self.cos[:], self.cin["c_cos"][:, b0:b0 + NSUB], [], ["cos"])
                self.dma("sp", self.sin[:], self.cin["c_sin"][:, b0:b0 + NSUB], [], ["sin"])
                if self.stop_after != "load":
                    self.ffn("ffn1", ph)
                if self.stop_after not in ("load", "ffn", "n1", "n2"):
                    self.win(ph, g0, t0)
            self.dma("sp", xdst[g0:g0 + TT, :].rearrange("(j p) d -> p j d", p=128), self.xt[:],
                     ["x%d" % j for j in range(NSUB)], ["xdst"])
        self.phase_end()

    def norm_to_nT(self, name, l):
        cfg = self.cfg
        NSUB, TT = cfg.nsub, cfg.tt
        for j in range(NSUB):
            self.act(lambda e, j=j: e.activation(self.junk[:], self.xt[:, j, :], AF.Square,
                                                 accum_out=self.ss[:, j:j + 1]),
                     ["x%d" % j], ["junk", "ss%d" % j])
        self.rsqrt(self.rstd[:, 0:NSUB], self.ss[:, 0:NSUB], 1.0 / D, ["ss%d" % j for j in range(NSUB)], "rstd")
        if self.stop_after == "n1":
            return
        for j in range(NSUB):
            b = j % 2
            self.act(lambda e, j=j, b=b: e.mul(self.xn[:, b, :], self.xt[:, j, :], self.rstd[:, j:j + 1]),
                     ["x%d" % j, "rstd"], ["xn%d" % b])
            pb = 6 + (j % 2)
            psv = self.psb[pb]
            for c in range(8):
                self.pe(lambda e, c=c, b=b, psv=psv: e.transpose(psv[:, c * 128:(c + 1) * 128], self.xn[:, b, c * 128:(c + 1) * 128], self.identb[:]),
                        ["xn%d" % b, "identb"], ["ps%d" % pb])
            for c in range(8):
                self.dve(lambda e, c=c, j=j, psv=psv: e.tensor_scalar(self.nT[:, c, j * 128:(j + 1) * 128], psv[:, c * 128:(c + 1) * 128],
                                                                      self.gain(name, l, c), None, ALU.mult),
                         ["ps%d" % pb, "gains"], ["nT%d" % c])

    def ffn(self, pre, l):
        cfg = self.cfg
        NSUB, TT, TG, NTG = cfg.nsub, cfg.tt, cfg.tg, cfg.ntg
        self.norm_to_nT(pre + "_norm", l)
        nTr = ["nT%d" % c for c in range(8)]
        bank = 0
        if self.stop_after in ("n1", "n2"):
            return
        for half in range(2):
            for jl in range(11):
                sg_, slg, rg = self.ws_acquire(pre + "_w_gu", "col")
                su_, slu, ru = self.ws_acquire(pre + "_w_gu", "col")
                wg = slg[:].rearrange("p (kc c) -> p kc c", kc=8)
                wu = slu[:].rearrange("p (kc c) -> p kc c", kc=8)
                for tg in range(NTG):
                    bg, bu = bank % 6, (bank + 1) % 6
                    bank += 2
                    tsl = slice(tg * TG, (tg + 1) * TG)
                    for kc in range(8):
                        self.pe(lambda e, kc=kc, wg=wg, bg=bg, tsl=tsl: e.matmul(self.ps[bg][:, 0:TG], wg[:, kc, :], self.nT[:, kc, tsl], start=(kc == 0), stop=(kc == 7)),
                                [rg, "nT%d" % kc], ["ps%d" % bg])
                    for kc in range(8):
                        self.pe(lambda e, kc=kc, wu=wu, bu=bu, tsl=tsl: e.matmul(self.ps[bu][:, 0:TG], wu[:, kc, :], self.nT[:, kc, tsl], start=(kc == 0), stop=(kc == 7)),
                                [ru, "nT%d" % kc], ["ps%d" % bu])
                    sb_ = tg % 2
                    self.act(lambda e, bg=bg, sb_=sb_: e.activation(self.sg[:, sb_, 0:TG], self.ps[bg][:, 0:TG], AF.Silu),
                             ["ps%d" % bg], ["sg%d" % sb_])
                    self.dve(lambda e, bu=bu, sb_=sb_, jl=jl, tsl=tsl: e.tensor_tensor(self.hT[:, jl, tsl], self.sg[:, sb_, 0:TG], self.ps[bu][:, 0:TG], ALU.mult),
                             ["sg%d" % sb_, "ps%d" % bu], ["h%d" % jl])
                self.ws_release(2)
            slots = [self.ws_acquire(pre + "_w_down", "row") for _ in range(11)]
            for j in range(NSUB):
                for ch in range(2):
                    b = bank % 6; bank += 1
                    for k in range(11):
                        _, sl, rk = slots[k]
                        self.pe(lambda e, k=k, sl=sl, b=b, j=j, ch=ch: e.matmul(self.ps[b][:, :], self.hT[:, k, j * 128:(j + 1) * 128], sl[:, ch * 512:(ch + 1) * 512], start=(k == 0), stop=(k == 10)),
                                [rk, "h%d" % k], ["ps%d" % b])
                    self.dve(lambda e, b=b, j=j, ch=ch: e.scalar_tensor_tensor(self.xt[:, j, ch * 512:(ch + 1) * 512], self.ps[b][:, :], 0.5, self.xt[:, j, ch * 512:(ch + 1) * 512], ALU.mult, ALU.add),
                             ["ps%d" % b, "x%d" % j], ["x%d" % j])
            self.ws_release(11)

    def win(self, l, g0, t0):
        cfg = self.cfg
        NSUB, TT, TG, NTG = cfg.nsub, cfg.tt, cfg.tg, cfg.ntg
        self.norm_to_nT("mix_norm", l)
        slots = [self.ws_acquire("w_in", "row") for _ in range(8)]
        for j in range(NSUB):
            for (c0, n, zo) in ((0, 256, 0), (256, 272, 256)):
                b = (j * 2 + (c0 > 0)) % 6
                for kc in range(8):
                    _, sl, rk = slots[kc]
                    self.pe(lambda e, kc=kc, sl=sl, b=b, j=j, c0=c0, n=n: e.matmul(self.ps[b][:, 0:n], self.nT[:, kc, j * 128:(j + 1) * 128], sl[:, c0:c0 + n], start=(kc == 0), stop=(kc == 7)),
                            [rk, "nT%d" % kc], ["ps%d" % b])
                if c0 == 0:
                    self.act(lambda e, b=b: e.copy(self.ufst[:], self.ps[b][:, 0:256]), ["ps%d" % b], ["ufst"])
                    self.dma("sp", self.UF[g0 + j * 128:g0 + (j + 1) * 128, :], self.ufst[:], ["ufst"], ["UF"])
                else:
                    self.dve(lambda e, b=b: e.tensor_copy(self.zt[:, 0:272], self.ps[b][:, 0:272]), ["ps%d" % b], ["zab"])
                    self.dma("sp", self.ZAB[g0 + j * 128:g0 + (j + 1) * 128, :], self.zt[:, 0:272], ["zab"], ["ZAB"])
        self.ws_release(8)
        slots = [self.ws_acquire("w_in", "row") for _ in range(8)]
        for j in range(NSUB):
            for (c0, n) in ((0, 512), (512, 256)):
                b = (j * 2 + (c0 > 0)) % 6
                for kc in range(8):
                    _, sl, rk = slots[kc]
                    self.pe(lambda e, kc=kc, sl=sl, b=b, j=j, c0=c0, n=n: e.matmul(self.ps[b][:, 0:n], self.nT[:, kc, j * 128:(j + 1) * 128], sl[:, c0:c0 + n], start=(kc == 0), stop=(kc == 7)),
                            [rk, "nT%d" % kc], ["ps%d" % b])
                if c0 == 0:
                    self.act(lambda e, b=b: e.copy(self.zq[:, 0:512], self.ps[b][:, 0:512]), ["ps%d" % b], ["zq"])
                else:
                    self.dve(lambda e, b=b: e.tensor_copy(self.zq[:, 512:768], self.ps[b][:, 0:256]), ["ps%d" % b], ["zk"])
            self.qk_post(l, j, g0, t0)
        self.ws_release(8)
        for jb in range(6):
            _, sl, rk = self.ws_acquire("w_in", "col")
            wv = sl[:].rearrange("p (kc c) -> p kc c", kc=8)
            for tg in range(NTG):
                b = (jb * NTG + tg) % 6
                tsl = slice(tg * TG, (tg + 1) * TG)
                for kc in range(8):
                    self.pe(lambda e, kc=kc, wv=wv, b=b, tsl=tsl: e.matmul(self.ps[b][:, 0:TG], wv[:, kc, :], self.nT[:, kc, tsl], start=(kc == 0), stop=(kc == 7)),
                            [rk, "nT%d" % kc], ["ps%d" % b])
                s2 = (jb * NTG + tg) % 2
                self.act(lambda e, b=b, s2=s2: e.copy(self.zfm[:, s2, 0:TG], self.ps[b][:, 0:TG]), ["ps%d" % b], ["zfm%d" % s2])
                self.dma("sp", self.ZT[jb * 128:(jb + 1) * 128, g0 + tg * TG:g0 + (tg + 1) * TG], self.zfm[:, s2, 0:TG], ["zfm%d" % s2], ["ZT"])
            self.ws_release(1)
        for pr in range(4):
            self.dma("sp", self.QT[pr, :, g0:g0 + TT], self.qkst[:, pr, :], ["qkst"], ["QT"])
        for g in range(2):
            self.dma("sp", self.KT[g, :, g0:g0 + TT], self.qkst[:, 4 + g, :], ["qkst"], ["KT"])

    def qk_post(self, l, j, g0, t0):
        blk = (t0 // 128) + j
        zq, zsq, zn, zr, zqb, tt_ = self.zq, self.zsq, self.zn, self.zr, self.zqb, self.rt
        hv = lambda ap: ap.rearrange("p (h d) -> p h d", d=64)
        self.dve(lambda e: e.tensor_tensor(zsq[:, 0:640], zq[:, 0:640], zq[:, 0:640], ALU.mult), ["zq", "zk"], ["zsq"])
        self.dve(lambda e: e.tensor_reduce(self.zss[:, 0:10], hv(zsq[:, 0:640]), AX.X, ALU.add), ["zsq"], ["zss"])
        self.rsqrt(self.zss[:, 16:26], self.zss[:, 0:10], 1.0 / 64, ["zss"], "zrs")
        rs = self.zss[:, 16:26]
        self.dve(lambda e: e.tensor_tensor(hv(zn[:, 0:640]), hv(zq[:, 0:640]), rs.unsqueeze(2).to_broadcast([128, 10, 64]), ALU.mult),
                 ["zq", "zk", "zrs"], ["zn"])
        self.dve(lambda e: e.tensor_tensor(hv(zn[:, 0:512]), hv(zn[:, 0:512]), self.gq[:, l:l + 1, :].to_broadcast([128, 8, 64]), ALU.mult),
                 ["zn", "gq"], ["zn"])
        self.dve(lambda e: e.tensor_tensor(hv(zn[:, 512:640]), hv(zn[:, 512:640]), self.gk[:, l:l + 1, :].to_broadcast([128, 2, 64]), ALU.mult),
                 ["zn", "gk"], ["zn"])
        xv = zn[:, 0:640].rearrange("p (h a t f) -> p h a t f", h=10, a=2, t=2)
        ov = zr[:, 0:640].rearrange("p (h a t f) -> p h a t f", h=10, a=2, t=2)
        x1, x2 = xv[:, :, :, 0, :], xv[:, :, :, 1, :]
        cb = self.cos[:, j, :, :].unsqueeze(1).to_broadcast([128, 10, 2, 16])
        sb_ = self.sin[:, j, :, :].unsqueeze(1).to_broadcast([128, 10, 2, 16])
        t = [tt_[:, i, :].rearrange("p (h a f) -> p h a f", h=10, a=2) for i in range(4)]
        self.dve(lambda e: e.tensor_tensor(t[0], x1, cb, ALU.mult), ["zn", "cos"], ["rt0"])
        self.dve(lambda e: e.tensor_tensor(t[1], x2, sb_, ALU.mult), ["zn", "sin"], ["rt1"])
        self.dve(lambda e: e.tensor_tensor(t[2], x2, cb, ALU.mult), ["zn", "cos"], ["rt2"])
        self.dve(lambda e: e.tensor_tensor(t[3], x1, sb_, ALU.mult), ["zn", "sin"], ["rt3"])
        self.dve(lambda e: e.tensor_tensor(ov[:, :, :, 0, :], t[0], t[1], ALU.subtract), ["rt0", "rt1"], ["zr"])
        self.dve(lambda e: e.tensor_tensor(ov[:, :, :, 1, :], t[2], t[3], ALU.add), ["rt2", "rt3"], ["zr"])
        self.act(lambda e: e.copy(zqb[:, 0:512], zr[:, 0:512]), ["zr"], ["zqb"])
        self.act(lambda e: e.copy(zqb[:, 512:768].rearrange("p (g u d) -> p g u d", g=2, u=2),
                                  zr[:, 512:640].rearrange("p (g d) -> p g d", g=2).unsqueeze(2).to_broadcast([128, 2, 2, 64])),
                 ["zr"], ["zqb"])
        self.act(lambda e: e.copy(self.vst[:], zq[:, 640:768]), ["zk"], ["vst"])
        self.dma("sp", self.VV[g0 + j * 128:g0 + (j + 1) * 128, :], self.vst[:], ["vst"], ["VV"])
        pb = 6 + (j % 2)
        psv = self.psb[pb]
        for i in range(6):
            self.pe(lambda e, i=i, psv=psv: e.transpose(psv[:, i * 128:(i + 1) * 128], zqb[:, i * 128:(i + 1) * 128], self.identb[:]),
                    ["zqb", "identb"], ["ps%d" % pb])
        self.dve(lambda e, psv=psv: e.tensor_copy(self.qkst[:, :, j * 128:(j + 1) * 128], psv[:, 0:768].rearrange("p (i t) -> p i t", i=6)),
                 ["ps%d" % pb], ["qkst"])

    def mem_kv(self, l, si):
        mt = self.zt
        mnT = self.qkst
        mnTv = self.memnT
        for j in range(2):
            self.dma("sp", mt[:, 0:D], self.inp["mem"][si * MEMT + j * 128: si * MEMT + (j + 1) * 128, :], [], ["zab"])
            self.act(lambda e: e.activation(self.junk[:], mt[:, 0:D], AF.Square, accum_out=self.ss[:, 0:1]), ["zab"], ["junk", "ss0"])
            self.rsqrt(self.rstd[:, 0:1], self.ss[:, 0:1], 1.0 / D, ["ss0"], "rstd")
            self.dve(lambda e: e.tensor_scalar(self.xn[:, 0, :], mt[:, 0:D], self.rstd[:, 0:1], None, ALU.mult), ["zab", "rstd"], ["xn0"])
            pb = 6 + j
            psv = self.psb[pb]
            for c in range(8):
                self.pe(lambda e, c=c, psv=psv: e.transpose(psv[:, c * 128:(c + 1) * 128], self.xn[:, 0, c * 128:(c + 1) * 128], self.identb[:]),
                        ["xn0", "identb"], ["ps%d" % pb])
            for c in range(8):
                self.dve(lambda e, c=c, j=j, psv=psv: e.tensor_scalar(mnTv[:, c, j * 128:(j + 1) * 128], psv[:, c * 128:(c + 1) * 128],
                                                                      self.gain("mem_norm_m", l, c), None, ALU.mult),
                         ["ps%d" % pb, "gains"], ["qkst"])
        for jb in range(8):
            _, sl, rk = self.ws_acquire("mem_wkv", "col")
            wv = sl[:].rearrange("p (kc c) -> p kc c", kc=8)
            b = jb % 6
            for kc in range(8):
                self.pe(lambda e, kc=kc, wv=wv, b=b: e.matmul(self.ps[b][:, 0:MEMT], wv[:, kc, :], mnTv[:, kc, :], start=(kc == 0), stop=(kc == 7)),
                        [rk, "qkst"], ["ps%d" % b])
            self.act(lambda e, b=b, jb=jb: e.copy(self.memkT[:, jb, :], self.ps[b][:, 0:MEMT]), ["ps%d" % b], ["memkT"])
            self.ws_release(1)
        slots = [self.ws_acquire("mem_wkv", "row") for _ in range(8)]
        for mc in range(2):
            for ch in range(2):
                b = (mc * 2 + ch) % 6
                for kc in range(8):
                    _, sl, rk = slots[kc]
                    self.pe(lambda e, kc=kc, sl=sl, b=b, mc=mc, ch=ch: e.matmul(self.ps[b][:, :], mnTv[:, kc, mc * 128:(mc + 1) * 128], sl[:, ch * 512:(ch + 1) * 512], start=(kc == 0), stop=(kc == 7)),
                            [rk, "qkst"], ["ps%d" % b])
                self.act(lambda e, b=b, mc=mc, ch=ch: e.copy(self.memV[:, mc, ch * 512:(ch + 1) * 512], self.ps[b][:, :]), ["ps%d" % b], ["memV"])
        self.ws_release(8)

    def tm_linear_acc(self, name, aT, akey, nk, scale):
        cfg = self.cfg
        slots = [self.ws_acquire(name, "row") for _ in range(nk)]
        for j in range(cfg.nsub):
            for ch in range(2):
                b = (j * 2 + ch) % 6
                for k in range(nk):
                    _, sl, rk = slots[k]
                    self.pe(lambda e, k=k, sl=sl, b=b, j=j, ch=ch: e.matmul(self.ps[b][:, :], aT[:, k, j * 128:(j + 1) * 128], sl[:, ch * 512:(ch + 1) * 512], start=(k == 0), stop=(k == nk - 1)),
                            [rk, akey], ["ps%d" % b])
                self.dve(lambda e, b=b, j=j, ch=ch: e.scalar_tensor_tensor(self.xt[:, j, ch * 512:(ch + 1) * 512], self.ps[b][:, :], scale, self.xt[:, j, ch * 512:(ch + 1) * 512], ALU.mult, ALU.add),
                         ["ps%d" % b, "x%d" % j], ["x%d" % j])
        self.ws_release(nk)

    def wout(self, l, g0):
        TT = self.cfg.tt
        self.dma("sp", self.yTt[:], self.YT[:, g0:g0 + TT].rearrange("(c p) t -> p c t", p=128), [], ["yTt"])
        self.tm_linear_acc("w_out", self.yTt, "yTt", 8, 1.0)

    def mca(self, l):
        cfg = self.cfg
        NSUB, TT, TG, NTG = cfg.nsub, cfg.tt, cfg.tg, cfg.ntg
        self.norm_to_nT("mem_norm_x", l)
        for jb in range(8):
            _, sl, rk = self.ws_acquire("mem_wq", "col")
            wv = sl[:].rearrange("p (kc c) -> p kc c", kc=8)
            for tg in range(NTG):
                b = (jb * NTG + tg) % 6
                tsl = slice(tg * TG, (tg + 1) * TG)
                for kc in range(8):
                    self.pe(lambda e, kc=kc, wv=wv, b=b, tsl=tsl: e.matmul(self.ps[b][:, 0:TG], wv[:, kc, :], self.nT[:, kc, tsl], start=(kc == 0), stop=(kc == 7)),
                            [rk, "nT%d" % kc], ["ps%d" % b])
                self.act(lambda e, b=b, jb=jb, tsl=tsl: e.copy(self.qT[:, jb, tsl], self.ps[b][:, 0:TG]), ["ps%d" % b], ["qT%d" % jb])
            self.ws_release(1)
        bank = 0
        for tg in range(NTG):
            tsl = slice(tg * TG, (tg + 1) * TG)
            for h in range(4):
                for mc in range(2):
                    b = bank % 6; bank += 1
                    for dc in range(2):
                        self.pe(lambda e, b=b, h=h, mc=mc, dc=dc, tsl=tsl: e.matmul(self.ps[b][:, 0:TG], self.memkT[:, 2 * h + dc, mc * 128:(mc + 1) * 128], self.qT[:, 2 * h + dc, tsl], start=(dc == 0), stop=(dc == 1)),
                                ["memkT", "qT%d" % (2 * h + dc)], ["ps%d" % b])
                    self.act(lambda e, b=b, h=h, mc=mc, tsl=tsl: e.activation(self.pT[:, 2 * h + mc, tsl], self.ps[b][:, 0:TG], AF.Exp, scale=1.0 / 16.0),
                             ["ps%d" % b], ["pT%d" % (2 * h + mc)])
                b = bank % 6; bank += 1
                for mc in range(2):
                    self.pe(lambda e, b=b, h=h, mc=mc, tsl=tsl: e.matmul(self.ps[b][:, 0:TG], self.onesb[:], self.pT[:, 2 * h + mc, tsl], start=(mc == 0), stop=(mc == 1)),
                            ["onesb", "pT%d" % (2 * h + mc)], ["ps%d" % b])
                rb = h % 2
                self.dve(lambda e, b=b, rb=rb: e.reciprocal(self.rbc[:, rb, 0:TG], self.ps[b][:, 0:TG]), ["ps%d" % b], ["rbc%d" % rb])
                for dc in range(2):
                    b = bank % 6; bank += 1
                    for mc in range(2):
                        self.pe(lambda e, b=b, h=h, mc=mc, dc=dc, tsl=tsl: e.matmul(self.ps[b][:, 0:TG], self.memV[:, mc, (2 * h + dc) * 128:(2 * h + dc + 1) * 128], self.pT[:, 2 * h + mc, tsl], start=(mc == 0), stop=(mc == 1)),
                                ["memV", "pT%d" % (2 * h + mc)], ["ps%d" % b])
                    self.dve(lambda e, b=b, h=h, dc=dc, rb=rb, tsl=tsl: e.tensor_tensor(self.oT[:, 2 * h + dc, tsl], self.ps[b][:, 0:TG], self.rbc[:, rb, 0:TG], ALU.mult),
                             ["ps%d" % b, "rbc%d" % rb], ["oT"])
        self.tm_linear_acc("mem_wo", self.oT, "oT", 8, 1.0)

    def final_norm(self):
        cfg = self.cfg
        NSUB = cfg.nsub
        for j in range(NSUB):
            self.act(lambda e, j=j: e.activation(self.junk[:], self.xt[:, j, :], AF.Square, accum_out=self.ss[:, j:j + 1]),
                     ["x%d" % j], ["junk", "ss%d" % j])
        self.rsqrt(self.rstd[:, 0:NSUB], self.ss[:, 0:NSUB], 1.0 / D, ["ss%d" % j for j in range(NSUB)], "rstd")
        for j in range(NSUB):
            self.dve(lambda e, j=j: e.scalar_tensor_tensor(self.xt[:, j, :], self.xt[:, j, :], self.rstd[:, j:j + 1], self.gfin[:], ALU.mult, ALU.mult),
                     ["x%d" % j, "rstd", "gfin"], ["x%d" % j])

    def m_gqa(self, l):
        cfg = self.cfg
        self.phase_begin()
        SM = max(cfg.seqs)
        NBM = SM // 128
        KTs = self.sb("KTs", [128, 2, SM], BF16)
        Vx = self.sb("Vx", [128, NBM, 2, 65], BF16)
        onesf = self.sb("onesf", [128, 64], F32)
        QW = min(512, min(cfg.seqs))
        qt = self.sb("qtile", [128, 4, QW], BF16)
        pT2 = self.sb("pTa", [128, 12, QW], BF16)
        rr2 = self.sb("rr", [128, 4, QW], F32)
        bcs2 = self.sb("bcs", [64, 4, QW], F32)
        yst = self.sb("yst", [64, 4, QW], BF16)
        self.dve(lambda e: e.memset(onesf[:], 1.0), [], ["onesf"])
        self.dve(lambda e: e.memset(Vx[:], 1.0), [], ["Vx"])
        it = 0
        for si, (st, S) in enumerate(zip(cfg.starts, cfg.seqs)):
            NB = S // 128
            for g in range(2):
                self.dma("sp", KTs[:, g, 0:S], self.KT[g, :, st:st + S], [], ["KTs"])
            for g in range(2):
                self.dma("sp", Vx[:, 0:NB, g, 0:64], self.VV[st:st + S, g * 64:(g + 1) * 64].rearrange("(kb p) d -> p kb d", p=128), [], ["Vx"])
            nqb = S // QW
            G = 2 if nqb % 2 == 0 else 1
            for pr in range(4):
                g = pr // 2
                for qb0 in range(0, nqb, G):
                    par = it % 2; it += 1
                    chains = []
                    for gi in range(G):
                        qs = par * 2 + gi
                        qb = qb0 + gi
                        self.dma("sp", qt[:, qs, :], self.QT[pr, :, st + qb * QW: st + (qb + 1) * QW], [], ["qt%d" % qs])
                        for hh in range(2):
                            chains.append((gi * 2 + hh, qs, qb, hh))
                    for kb in range(NB + 1):
                        if kb < NB:
                            for (c, qs, qb, hh) in chains:
                                base = hh * 64
                                pk = c * 3 + kb % 3
                                self.pe(lambda e, c=c, kb=kb, g=g, base=base, qs=qs: e.matmul(self.ps[c][:, 0:QW], KTs[base:base + 64, g, kb * 128:(kb + 1) * 128], qt[base:base + 64, qs, :], start=True, stop=True),
                                        ["KTs", "qt%d" % qs], ["ps%d" % c])
                                self.act(lambda e, c=c, pk=pk: e.activation(pT2[:, pk, :], self.ps[c][:, 0:QW], AF.Exp), ["ps%d" % c], ["pTa%d" % pk])
                        if kb >= 1:
                            k2 = kb - 1
                            for (c, qs, qb, hh) in chains:
                                ob = 4 + c
                                pk = c * 3 + k2 % 3
                                self.pe(lambda e, k2=k2, g=g, ob=ob, NB=NB, pk=pk: e.matmul(self.ps[ob][0:65, 0:QW], Vx[:, k2, g, :], pT2[:, pk, :], start=(k2 == 0), stop=(k2 == NB - 1)),
                                        ["Vx", "pTa%d" % pk], ["ps%d" % ob])
                    for (c, qs, qb, hh) in chains:
                        h = 2 * pr + hh
                        ob = 4 + c
                        self.dve(lambda e, ob=ob, c=c: e.reciprocal(rr2[64:65, c, :], self.ps[ob][64:65, 0:QW]), ["ps%d" % ob], ["rr%d" % c])
                        self.pe(lambda e, c=c: e.matmul(self.ps[c][0:64, 0:QW], onesf[64:65, 0:64], rr2[64:65, c, :], start=True, stop=True), ["onesf", "rr%d" % c], ["ps%d" % c])
                        self.act(lambda e, c=c: e.copy(bcs2[:, c, :], self.ps[c][0:64, 0:QW]), ["ps%d" % c], ["bcs%d" % c])
                        self.dve(lambda e, ob=ob, c=c: e.tensor_tensor(yst[:, c, :], self.ps[ob][0:64, 0:QW], bcs2[:, c, :], ALU.mult), ["ps%d" % ob, "bcs%d" % c], ["yst%d" % c])
                        self.dma("sp", self.YT[512 + h * 64: 512 + (h + 1) * 64, st + qb * QW: st + (qb + 1) * QW], yst[:, c, :], ["yst%d" % c], ["YT"])
        self.phase_end()

    def m_fourier(self, l):
        cfg = self.cfg
        self.phase_begin()
        SM = max(cfg.seqs)
        N1M = SM // 128
        fc = self.sb("fc", [64, 2, 64], BF16)
        wf = self.sb("wf", [64, 4, 64], BF16)
        AB = self.sb("AB", [64, 4, 2, 64], BF16)
        f1 = self.sb("f1", [64, 3 * N1M], BF16)
        Ht = self.sb("Ht", [128, N1M, 2, 128], BF16)
        Ut = self.sb("Ut", [64, 128, 64], BF16)
        Ot = self.sb("Ot", [128, N1M, 3, 64], BF16)
        PT = self.sb("PTf", [64, 2, SM], BF16)
        yst = self.sb("ystf", [64, 2, 512], BF16)
        self.dma("sp", fc[:], self.cin["c_fc"], [], ["fc"])
        self.dma("pool", wf[:], self.inp["four_w"][l].rearrange("g c d -> c g d"), [], ["wf"])
        for g in range(4):
            for x in range(2):
                self.pe(lambda e, g=g, x=x: e.matmul(self.ps[0][0:64, (g * 2 + x) * 64:(g * 2 + x + 1) * 64], fc[:, x, :], wf[:, g, :], start=True, stop=True),
                        ["fc", "wf"], ["ps0"])
        self.dve(lambda e: e.tensor_copy(AB[:].rearrange("c g x d -> c (g x d)"), self.ps[0][0:64, :]), ["ps0"], ["AB"])
        for si, (st, S) in enumerate(zip(cfg.starts, cfg.seqs)):
            N1 = S // 128
            self.dma("sp", f1[0:N1, 0:3 * N1], self.cin["c_f1_%d" % S], [], ["f1"])
            self.dma("sp", Ht[:, 0:N1, :, :], self.cin["c_h_%d" % S], [], ["Ht"])
            CB = 1
            while CB * 2 <= min(64, 512 // (3 * N1)):
                CB *= 2
            for g in range(4):
                self.dma("sp", Ut[0:N1, :, :], self.UF[st:st + S, g * 64:(g + 1) * 64].rearrange("(n1 n2) c -> n1 n2 c", n2=128), [], ["Ut"])
                for c0 in range(0, 64, CB):
                    b = (c0 // CB) % 3
                    for cl in range(CB):
                        c = c0 + cl
                        self.pe(lambda e, b=b, cl=cl, c=c, N1=N1: e.matmul(self.ps[b][:, cl * 3 * N1:(cl + 1) * 3 * N1], Ut[0:N1, :, c], f1[0:N1, 0:3 * N1], start=True, stop=True),
                                ["Ut", "f1"], ["ps%d" % b])
                    self.dve(lambda e, b=b, c0=c0, N1=N1, CB=CB: e.tensor_copy(Ot[:, 0:N1, :, c0:c0 + CB].rearrange("p k x c -> p c x k"),
                                                                             self.ps[b][:, 0:CB * 3 * N1].rearrange("p (c x k) -> p c x k", c=CB, x=3)),
                             ["ps%d" % b], ["Ot"])
                KB = min(4, N1)
                for k0 in range(0, N1, KB):
                    for x in range(2):
                        b = 3 + ((k0 // KB) * 2 + x) % 3
                        for kl in range(KB):
                            k1 = k0 + kl
                            o = self.ps[b][0:64, kl * 128:(kl + 1) * 128]
                            if x == 0:
                                self.pe(lambda e, o=o, k1=k1: e.matmul(o, Ot[:, k1, 0, :], Ht[:, k1, 0, :], start=True, stop=False), ["Ot", "Ht"], ["ps%d" % b])
                                self.pe(lambda e, o=o, k1=k1: e.matmul(o, Ot[:, k1, 2, :], Ht[:, k1, 1, :], start=False, stop=True), ["Ot", "Ht"], ["ps%d" % b])
                            else:
                                self.pe(lambda e, o=o, k1=k1: e.matmul(o, Ot[:, k1, 0, :], Ht[:, k1, 1, :], start=True, stop=False), ["Ot", "Ht"], ["ps%d" % b])
                                self.pe(lambda e, o=o, k1=k1: e.matmul(o, Ot[:, k1, 1, :], Ht[:, k1, 0, :], start=False, stop=True), ["Ot", "Ht"], ["ps%d" % b])
                        dst = PT[:, x, 0:S].rearrange("c (k2 k1) -> c k1 k2", k1=N1)[:, k0:k0 + KB, :]
                        eng = self.act if x == 0 else self.dve
                        if x == 0:
                            self.act(lambda e, b=b, dst=dst, KB=KB: e.copy(dst, self.ps[b][0:64, 0:KB * 128].rearrange("c (k f) -> c k f", k=KB)), ["ps%d" % b], ["PTf"])
                        else:
                            self.dve(lambda e, b=b, dst=dst, KB=KB: e.tensor_copy(dst, self.ps[b][0:64, 0:KB * 128].rearrange("c (k f) -> c k f", k=KB)), ["ps%d" % b], ["PTf"])
                SW = min(512, S)
                for s0 in range(0, S, SW):
                    b = 6 + (s0 // SW) % 2
                    ys = (s0 // SW) % 2
                    self.pe(lambda e, b=b, g=g, s0=s0, SW=SW: e.matmul(self.ps[b][0:64, 0:SW], AB[:, g, 0, :], PT[:, 0, s0:s0 + SW], start=True, stop=False), ["AB", "PTf"], ["ps%d" % b])
                    self.pe(lambda e, b=b, g=g, s0=s0, SW=SW: e.matmul(self.ps[b][0:64, 0:SW], AB[:, g, 1, :], PT[:, 1, s0:s0 + SW], start=False, stop=True), ["AB", "PTf"], ["ps%d" % b])
                    self.act(lambda e, b=b, ys=ys, SW=SW: e.copy(yst[:, ys, 0:SW], self.ps[b][0:64, 0:SW]), ["ps%d" % b], ["ystf%d" % ys])
                    self.dma("sp", self.YT[g * 64:(g + 1) * 64, st + s0: st + s0 + SW], yst[:, ys, 0:SW], ["ystf%d" % ys], ["YT"])
        self.phase_end()

    def nb(self):
        self._bank = (getattr(self, "_bank", -1) + 1) % 6
        return self._bank

    def nbt(self):
        self._bankt = (getattr(self, "_bankt", -1) + 1) % 2
        return 6 + self._bankt

    def m_deltanet(self, l):
        cfg = self.cfg
        self.phase_begin()
        W = min(512, min(cfg.seqs)); NCH = W // 64; NCP = NCH // 2; NU = NCH * 4
        idb = self.identb[0:64, 0:64]; idf = self.identf[0:64, 0:64]; onb = self.onesb[0:64, 0:64]
        sbt = lambda n, sh, dt: self.sb(n, sh, dt)
        cw = sbt("cw", [64, 3, 12], F32); dg = sbt("dg", [64, 12, 3, 64], BF16)
        mk = sbt("mk", [64, 4, 64], F32); tri = sbt("tri", [64, 3, 64], F32)
        nA = sbt("nA", [64, 8], F32); dtb = sbt("dtb", [64, 8], F32)
        zin = sbt("zin", [64, 12, W + 2], BF16); zab = sbt("zab", [64, NCH, 16], F32)
        cs = sbt("cs", [64, 2, W], F32); sq = sbt("sq", [64, 2, W], BF16); rs = sbt("rs", [64, 2, W], F32)
        qTn = sbt("qTn", [64, NU, 64], BF16); kTn = sbt("kTn", [64, NU, 64], BF16); vTn = sbt("vTn", [64, NU, 64], BF16)
        k_tm = sbt("k_tm", [64, NU, 64], BF16); v_tm = sbt("v_tm", [64, NU, 64], BF16)
        gt = sbt("gt", [64, 8, NU], F32)
        dgx = sbt("dgx", [64, 8, 64], F32); dl = sbt("dl", [64, 8, 64], F32); egr = sbt("egr", [64, 8, 64], F32)
        t1 = sbt("t1", [64, 2, 8, 64], F32); DTt = sbt("DTt", [64, 8, 64], F32); DLt = sbt("DLt", [64, 8, 64], F32)
        qdec = sbt("qdec", [64, NU, 64], BF16); qkTm = sbt("qkTm", [64, NU, 64], BF16)
        Lp = [sbt("Lp%d" % i, [64, NU, 64], BF16) for i in range(2)]
        Up = [sbt("Up%d" % i, [64, NU, 64], BF16) for i in range(2)]
        Rp = [sbt("Rp%d" % i, [64, NU, 64], BF16) for i in range(2)]
        vb = sbt("vb", [64, NU, 64], BF16); kbe = sbt("kbe", [64, NU, 64], BF16); ktl = sbt("ktl", [64, NU, 64], BF16)
        u_sb = sbt("u_sb", [64, NU, 64], F32); wT = sbt("wT", [64, NU, 64], BF16)
        Sf = sbt("Sf", [64, 4, 64], F32); Sb = sbt("Sb", [64, 4, 64], BF16)
        vnew = sbt("vnew", [64, 2, 4, 64], BF16); ost = sbt("ost", [64, 2, 256], F32)
        o2 = sbt("o2", [128, 2, 256], F32); gte = sbt("gte", [128, 256], F32); osum = sbt("osum", [128, 256], F32)
        pss = sbt("pss", [128, 8], F32); ydb = sbt("ydb", [128, 256], BF16); gdn = sbt("gdn", [128, 64], F32)
        ydT = sbt("ydT", [128, 2, W], BF16)
        self.dma("sp", cw[:], self.inp["dn_conv"][l].rearrange("t (xh d) -> d t xh", d=64), [], ["cw"])
        self.dma("sp", mk[:], self.cin["c_dnmask"], [], ["mk"])
        self.dma("sp", tri[:], self.cin["c_tri"], [], ["tri"])
        self.dma("sp", nA[:], self.inp["dn_A_log"][l].rearrange("a h -> (a h)").partition_broadcast(64), [], ["nA"])
        self.dma("sp", dtb[:], self.inp["dn_dt_bias"][l].rearrange("a h -> (a h)").partition_broadcast(64), [], ["dtb"])
        self.dma("sp", gdn[:], self.inp["dn_out_norm"][l].partition_broadcast(128), [], ["gdn"])
        self.act(lambda e: e.activation(nA[:], nA[:], AF.Exp), ["nA"], ["nA"])
        self.dve(lambda e: e.tensor_scalar(nA[:], nA[:], -1.0, None, ALU.mult), ["nA"], ["nA"])
        for xh in range(12):
            for t in range(3):
                self.dve(lambda e, xh=xh, t=t: e.tensor_scalar(dg[:, xh, t, :], idb, cw[:, t, xh:xh + 1], None, ALU.mult), ["cw", "identb"], ["dg"])
        one_col = tri[:, 2, 0:1]
        B8 = lambda ap: ap.unsqueeze(2).to_broadcast([64, 8, 64])
        B4 = lambda ap: ap.unsqueeze(2).to_broadcast([64, 4, 64])
        psv = lambda b: self.ps[b][0:64, :].rearrange("p (u f) -> p u f", f=64)
        psbv = lambda b: self.psb[b][0:64, 0:512].rearrange("p (u f) -> p u f", f=64)

        for si, (st, S) in enumerate(zip(cfg.starts, cfg.seqs)):
            NTL = S // W
            for d in range(2):
                kT_, kL_ = (0, 1) if d == 0 else (2, 3)
                self.dve(lambda e: e.memset(Sf[:], 0.0), [], ["Sf"])
                self.dve(lambda e: e.memset(Sb[:], 0.0), [], ["Sb"])
                order = range(NTL) if d == 0 else range(NTL - 1, -1, -1)
                for ti in order:
                    p0 = ti * W
                    lo, hi = max(p0 - 1, 0), min(p0 + W + 1, S)
                    c0 = lo - (p0 - 1)
                    if p0 == 0:
                        self.dve(lambda e: e.memset(zin[:, :, 0:1], 0.0), [], ["zin"])
                    if p0 + W == S:
                        self.dve(lambda e: e.memset(zin[:, :, W + 1:W + 2], 0.0), [], ["zin"])
                    for x in range(3):
                        self.dma("sp", zin[:, x * 4:(x + 1) * 4, c0:c0 + (hi - lo)],
                                 self.ZT[x * 256:(x + 1) * 256, st + lo:st + hi].rearrange("(h d) t -> d h t", d=64), [], ["zin"])
                    self.dma("sp", zab[:], self.ZAB[st + p0:st + p0 + W, 0:16].rearrange("(n c) x -> c n x", c=64), [], ["zab"])
                    for xh in range(12):
                        x, h = xh // 4, xh % 4
                        b = self.nb()
                        for t in range(3):
                            self.pe(lambda e, b=b, xh=xh, t=t: e.matmul(self.ps[b][0:64, 0:W], dg[:, xh, t, :], zin[:, xh, t:t + W], start=(t == 0), stop=(t == 2)),
                                    ["dg", "zin"], ["ps%d" % b])
                        if x == 2:
                            self.act(lambda e, b=b, h=h: e.activation(vTn[:].rearrange("p (n h) f -> p n h f", h=4)[:, :, h, :], self.ps[b][0:64, 0:W].rearrange("p (n f) -> p n f", f=64), AF.Silu),
                                     ["ps%d" % b], ["vTn"])
                            continue
                        s2 = xh % 2
                        self.act(lambda e, b=b, s2=s2: e.activation(cs[:, s2, :], self.ps[b][0:64, 0:W], AF.Silu), ["ps%d" % b], ["cs%d" % s2])
                        self.act(lambda e, s2=s2: e.activation(sq[:, s2, :], cs[:, s2, :], AF.Square), ["cs%d" % s2], ["sq%d" % s2])
                        b2 = self.nb()
                        self.pe(lambda e, b2=b2, s2=s2: e.matmul(self.ps[b2][0:64, 0:W], onb, sq[:, s2, :], start=True, stop=True), ["onesb", "sq%d" % s2], ["ps%d" % b2])
                        self.act(lambda e, b2=b2, s2=s2: e.activation(rs[:, s2, :], self.ps[b2][0:64, 0:W], AF.Sqrt, bias=self.epsc[0:64, 0:1], scale=1.0), ["ps%d" % b2, "epsc"], ["rs%d_s" % s2])
                        self.dve(lambda e, s2=s2: e.reciprocal(rs[:, s2, :], rs[:, s2, :]), ["rs%d_s" % s2], ["rs%d" % s2])
                        dstT = (qTn if x == 0 else kTn)[:].rearrange("p (n h) f -> p n h f", h=4)[:, :, h, :]
                        csv = cs[:, s2, :].rearrange("p (n f) -> p n f", f=64); rsv = rs[:, s2, :].rearrange("p (n f) -> p n f", f=64)
                        if x == 0:
                            self.dve(lambda e, dstT=dstT, csv=csv, rsv=rsv: e.scalar_tensor_tensor(dstT, csv, 0.125, rsv, ALU.mult, ALU.mult), ["cs%d" % s2, "rs%d" % s2], ["qTn"])
                        else:
                            self.dve(lambda e, dstT=dstT, csv=csv, rsv=rsv: e.tensor_tensor(dstT, csv, rsv, ALU.mult), ["cs%d" % s2, "rs%d" % s2], ["kTn"])
                    if self.dn_stage < 2:
                        continue
                    for cp in range(NCP):
                        for (srcT, dstm, key, dkey) in ((kTn, k_tm, "kTn", "k_tm"), (vTn, v_tm, "vTn", "v_tm")):
                            b = self.nbt()
                            for u in range(8):
                                self.pe(lambda e, b=b, u=u, cp=cp, srcT=srcT: e.transpose(self.psb[b][0:64, u * 64:(u + 1) * 64], srcT[:, cp * 8 + u, :], idb), [key, "identb"], ["ps%d" % b])
                            self.dve(lambda e, b=b, cp=cp, dstm=dstm: e.tensor_copy(dstm[:, cp * 8:(cp + 1) * 8, :], psbv(b)), ["ps%d" % b], [dkey])
                    if self.dn_stage < 3:
                        continue
                    G = lambda r: gt[:, r, :]
                    Gn = lambda r: gt[:, r, :].rearrange("p (n h) -> p n h", h=4)
                    self.dve(lambda e, d=d: e.tensor_tensor(Gn(0), zab[:, :, d * 4:(d + 1) * 4], dtb[:, d * 4:(d + 1) * 4].unsqueeze(1).to_broadcast([64, NCH, 4]), ALU.add), ["zab", "dtb"], ["g0"])
                    self.act(lambda e: e.activation(G(0), G(0), AF.Exp), ["g0"], ["g0"])
                    self.act(lambda e: e.activation(G(0), G(0), AF.Ln, bias=one_col, scale=1.0), ["g0", "tri"], ["g0"])
                    self.dve(lambda e, d=d: e.tensor_tensor(Gn(1), Gn(0), nA[:, d * 4:(d + 1) * 4].unsqueeze(1).to_broadcast([64, NCH, 4]), ALU.mult), ["g0", "nA"], ["g1"])
                    self.act(lambda e, d=d: e.activation(Gn(2), zab[:, :, 8 + d * 4:8 + (d + 1) * 4], AF.Sigmoid), ["zab"], ["g2"])
                    b = self.nb()
                    self.pe(lambda e, b=b, d=d: e.matmul(self.ps[b][0:64, 0:NU], tri[:, d, :], G(1), start=True, stop=True), ["tri", "g1"], ["ps%d" % b])
                    self.pe(lambda e, b=b: e.matmul(self.ps[b][0:64, 64:64 + NU], tri[:, 2, :], G(1), start=True, stop=True), ["tri", "g1"], ["ps%d" % b])
                    self.dve(lambda e, b=b: e.tensor_copy(G(3), self.ps[b][0:64, 0:NU]), ["ps%d" % b], ["g3"])
                    self.act(lambda e, b=b: e.activation(G(4), self.ps[b][0:64, 0:NU], AF.Exp), ["ps%d" % b], ["g4"])
                    self.act(lambda e, b=b: e.activation(G(5), self.ps[b][0:64, 64:64 + NU], AF.Exp), ["ps%d" % b], ["g5"])
                    self.dve(lambda e, b=b: e.tensor_tensor(G(6), self.ps[b][0:64, 64:64 + NU], G(3), ALU.subtract), ["ps%d" % b, "g3"], ["g6"])
                    self.act(lambda e: e.activation(G(6), G(6), AF.Exp), ["g6"], ["g6"])
                    self.dve(lambda e: e.tensor_tensor(G(7), G(2), G(4), ALU.mult), ["g2", "g4"], ["g7"])
                    if self.dn_stage < 4:
                        continue
                    for cp in range(NCP):
                        us = slice(cp * 8, (cp + 1) * 8)
                        bA = self.nb()
                        for u in range(8):
                            self.pe(lambda e, bA=bA, u=u, cp=cp: e.matmul(self.ps[bA][0:64, u * 64:(u + 1) * 64], kTn[:, cp * 8 + u, :], kTn[:, cp * 8 + u, :], start=True, stop=True), ["kTn"], ["ps%d" % bA])
                        bB = self.nb()
                        for u in range(8):
                            self.pe(lambda e, bB=bB, u=u, cp=cp: e.matmul(self.ps[bB][0:64, u * 64:(u + 1) * 64], kTn[:, cp * 8 + u, :], qTn[:, cp * 8 + u, :], start=True, stop=True), ["kTn", "qTn"], ["ps%d" % bB])
                        if self.dn_stage < 4.01:
                            continue
                        self.dve(lambda e, us=us: e.tensor_tensor(dgx[:], idf.unsqueeze(1).to_broadcast([64, 8, 64]), B8(gt[:, 3, us]), ALU.mult), ["identf", "g3"], ["dgx"])
                        if self.dn_stage < 4.02:
                            continue
                        bC = self.nb()
                        self.pe(lambda e, bC=bC: e.matmul(self.ps[bC][0:64, :], tri[:, 2, :], dgx[:].rearrange("p u f -> p (u f)"), start=True, stop=True), ["tri", "dgx"], ["ps%d" % bC])
                        if self.dn_stage < 4.03:
                            continue
                        self.dve(lambda e, bC=bC, us=us: e.tensor_tensor(dl[:], psv(bC), B8(gt[:, 3, us]), ALU.subtract), ["ps%d" % bC, "g3"], ["dl"])
                        if self.dn_stage < 4.04:
                            continue
                        self.dve(lambda e, bC=bC: e.tensor_copy(egr[:], psv(bC)), ["ps%d" % bC], ["egr"])
                        self.act(lambda e: e.activation(egr[:], egr[:], AF.Exp), ["egr"], ["egr"])
                        if self.dn_stage < 4.05:
                            continue
                        self.dve(lambda e, us=us: e.tensor_tensor(qdec[:, us, :], qTn[:, us, :], egr[:], ALU.mult), ["qTn", "egr"], ["qdec"])
                        if self.dn_stage < 4.25:
                            continue
                        self.dve(lambda e, kT_=kT_: e.scalar_tensor_tensor(t1[:, 0], dl[:], 0.0, mk[:, kT_, :].unsqueeze(1).to_broadcast([64, 8, 64]), ALU.min, ALU.add), ["dl", "mk"], ["t1a"])
                        self.act(lambda e: e.activation(DTt[:], t1[:, 0], AF.Exp), ["t1a"], ["DTt"])
                        self.dve(lambda e, kL_=kL_: e.scalar_tensor_tensor(t1[:, 1], dl[:], 0.0, mk[:, kL_, :].unsqueeze(1).to_broadcast([64, 8, 64]), ALU.max, ALU.subtract), ["dl", "mk"], ["t1b"])
                        self.act(lambda e: e.activation(DLt[:], t1[:, 1], AF.Exp, scale=-1.0), ["t1b"], ["DLt"])
                        self.dve(lambda e, bB=bB, us=us: e.tensor_tensor(qkTm[:, us, :], psv(bB), DTt[:], ALU.mult), ["ps%d" % bB, "DTt"], ["qkTm"])
                        self.dve(lambda e, bA=bA: e.tensor_tensor(DLt[:], psv(bA), DLt[:], ALU.mult), ["ps%d" % bA, "DLt"], ["DLt"])
                        self.dve(lambda e, us=us: e.tensor_tensor(Lp[0][:, us, :], DLt[:], B8(gt[:, 2, us]), ALU.mult), ["DLt", "g2"], ["L0"])
                        if self.dn_stage < 4.5:
                            continue
                        bD = self.nbt()
                        for u in range(8):
                            self.pe(lambda e, bD=bD, u=u, cp=cp: e.transpose(self.psb[bD][0:64, u * 64:(u + 1) * 64], Lp[0][:, cp * 8 + u, :], idb), ["L0", "identb"], ["ps%d" % bD])
                        self.dve(lambda e, bD=bD, us=us: e.tensor_copy(Up[0][:, us, :], psbv(bD)), ["ps%d" % bD], ["U0"])
                        self.dve(lambda e, us=us: e.tensor_tensor(Rp[0][:, us, :], idb.unsqueeze(1).to_broadcast([64, 8, 64]), Up[0][:, us, :], ALU.subtract), ["identb", "U0"], ["R0"])
                        if self.dn_stage < 4.75:
                            continue
                        self.dve(lambda e, us=us: e.tensor_tensor(vb[:, us, :], v_tm[:, us, :], B8(gt[:, 2, us]), ALU.mult), ["v_tm", "g2"], ["vb"])
                        self.dve(lambda e, us=us: e.tensor_tensor(kbe[:, us, :], k_tm[:, us, :], B8(gt[:, 7, us]), ALU.mult), ["k_tm", "g7"], ["kbe"])
                        self.dve(lambda e, us=us: e.tensor_tensor(ktl[:, us, :], k_tm[:, us, :], B8(gt[:, 6, us]), ALU.mult), ["k_tm", "g6"], ["ktl"])
                    if self.dn_stage < 5:
                        continue
                    cur = 0
                    for lev in range(5):
                        nxt = 1 - cur
                        for cp in range(NCP):
                            us = slice(cp * 8, (cp + 1) * 8)
                            b = self.nb()
                            for u in range(8):
                                self.pe(lambda e, b=b, u=u, cp=cp, cur=cur: e.matmul(self.ps[b][0:64, u * 64:(u + 1) * 64], Up[cur][:, cp * 8 + u, :], Lp[cur][:, cp * 8 + u, :], start=True, stop=True),
                                        ["U%d" % cur, "L%d" % cur], ["ps%d" % b])
                            self.act(lambda e, b=b, us=us, nxt=nxt: e.copy(Lp[nxt][:, us, :], psv(b)), ["ps%d" % b], ["L%d" % nxt])
                        if lev < 4:
                            for cp in range(NCP):
                                us = slice(cp * 8, (cp + 1) * 8)
                                b = self.nb()
                                for u in range(8):
                                    self.pe(lambda e, b=b, u=u, cp=cp, cur=cur: e.matmul(self.ps[b][0:64, u * 64:(u + 1) * 64], Lp[cur][:, cp * 8 + u, :], Up[cur][:, cp * 8 + u, :], start=True, stop=True),
                                            ["U%d" % cur, "L%d" % cur], ["ps%d" % b])
                                self.act(lambda e, b=b, us=us, nxt=nxt: e.copy(Up[nxt][:, us, :], psv(b)), ["ps%d" % b], ["U%d" % nxt])
                        for cp in range(NCP):
                            us = slice(cp * 8, (cp + 1) * 8)
                            b = self.nb()
                            for u in range(8):
                                self.pe(lambda e, b=b, u=u, cp=cp, cur=cur, nxt=nxt: e.matmul(self.ps[b][0:64, u * 64:(u + 1) * 64], Lp[nxt][:, cp * 8 + u, :], Rp[cur][:, cp * 8 + u, :], start=True, stop=True),
                                        ["L%d" % nxt, "R%d" % cur], ["ps%d" % b])
                            self.dve(lambda e, b=b, us=us, cur=cur, nxt=nxt: e.tensor_tensor(Rp[nxt][:, us, :], psv(b), Rp[cur][:, us, :], ALU.add), ["ps%d" % b, "R%d" % cur], ["R%d" % nxt])
                        cur = nxt
                    Rf_ = Rp[cur]; rkey = "R%d" % cur
                    if self.dn_stage < 6:
                        continue
                    for cp in range(NCP):
                        us = slice(cp * 8, (cp + 1) * 8)
                        b = self.nb()
                        for u in range(8):
                            self.pe(lambda e, b=b, u=u, cp=cp, Rf_=Rf_: e.matmul(self.ps[b][0:64, u * 64:(u + 1) * 64], Rf_[:, cp * 8 + u, :], vb[:, cp * 8 + u, :], start=True, stop=True), [rkey, "vb"], ["ps%d" % b])
                        self.act(lambda e, b=b, us=us: e.copy(u_sb[:, us, :], psv(b)), ["ps%d" % b], ["u_sb"])
                        b = self.nb()
                        for u in range(8):
                            self.pe(lambda e, b=b, u=u, cp=cp, Rf_=Rf_: e.matmul(self.ps[b][0:64, u * 64:(u + 1) * 64], kbe[:, cp * 8 + u, :], Rf_[:, cp * 8 + u, :], start=True, stop=True), [rkey, "kbe"], ["ps%d" % b])
                        self.dve(lambda e, b=b, us=us: e.tensor_copy(wT[:, us, :], psv(b)), ["ps%d" % b], ["wT"])
                    if self.dn_stage < 7:
                        continue
                    corder = range(NCH) if d == 0 else range(NCH - 1, -1, -1)
                    for n in corder:
                        hs = slice(n * 4, (n + 1) * 4)
                        par = n % 2
                        psh = lambda b: self.ps[b][0:64, 0:256].rearrange("p (h f) -> p h f", f=64)
                        bG = self.nb()
                        for h in range(4):
                            self.pe(lambda e, bG=bG, h=h, n=n: e.matmul(self.ps[bG][0:64, h * 64:(h + 1) * 64], wT[:, n * 4 + h, :], Sb[:, h, :], start=True, stop=True), ["wT", "Sb"], ["ps%d" % bG])
                        self.dve(lambda e, bG=bG, hs=hs, par=par: e.tensor_tensor(vnew[:, par], u_sb[:, hs, :], psh(bG), ALU.subtract), ["u_sb", "ps%d" % bG], ["vnew%d" % par])
                        bH = self.nb()
                        for h in range(4):
                            self.pe(lambda e, bH=bH, h=h, n=n: e.matmul(self.ps[bH][0:64, h * 64:(h + 1) * 64], qdec[:, n * 4 + h, :], Sb[:, h, :], start=True, stop=False), ["qdec", "Sb"], ["ps%d" % bH])
                            self.pe(lambda e, bH=bH, h=h, n=n, par=par: e.matmul(self.ps[bH][0:64, h * 64:(h + 1) * 64], qkTm[:, n * 4 + h, :], vnew[:, par, h, :], start=False, stop=True), ["qkTm", "vnew%d" % par], ["ps%d" % bH])
                        bI = self.nb()
                        for h in range(4):
                            self.pe(lambda e, bI=bI, h=h, n=n, par=par: e.matmul(self.ps[bI][0:64, h * 64:(h + 1) * 64], ktl[:, n * 4 + h, :], vnew[:, par, h, :], start=True, stop=True), ["ktl", "vnew%d" % par], ["ps%d" % bI])
                        self.dve(lambda e, hs=hs: e.tensor_tensor(Sf[:], Sf[:], B4(gt[:, 5, hs]), ALU.mult), ["Sf", "g5"], ["Sf"])
                        self.dve(lambda e, bI=bI: e.tensor_tensor(Sf[:], Sf[:], psh(bI), ALU.add), ["Sf", "ps%d" % bI], ["Sf"])
                        self.act(lambda e: e.copy(Sb[:], Sf[:]), ["Sf"], ["Sb"])
                        self.act(lambda e, bH=bH, par=par: e.copy(ost[:, par, :], self.ps[bH][0:64, 0:256]), ["ps%d" % bH], ["ost%d" % par])
                        self.dma("sp", self.OF[d, st + p0 + n * 64: st + p0 + (n + 1) * 64, :], ost[:, par, :], ["ost%d" % par], ["OF"])
            for p0 in (range(0, S, W) if self.dn_stage >= 8 else []):
                for jb in range(W // 128):
                    q0 = st + p0 + jb * 128
                    self.dma("sp", o2[:], self.OF[:, q0:q0 + 128, :].rearrange("a p f -> p a f"), ["OF"], ["o2"])
                    self.dma("sp", gte[:], self.ZAB[q0:q0 + 128, 16:272], [], ["gte"])
                    self.dve(lambda e: e.tensor_tensor(osum[:], o2[:, 0, :], o2[:, 1, :], ALU.add), ["o2"], ["osum"])
                    self.act(lambda e: e.activation(o2[:, 0, :], osum[:], AF.Square), ["osum"], ["o2"])
                    self.dve(lambda e: e.tensor_reduce(pss[:, 0:4], o2[:, 0, :].rearrange("p (h f) -> p h f", f=64), AX.X, ALU.add), ["o2"], ["pss"])
                    self.rsqrt(pss[:, 4:8], pss[:, 0:4], 1.0 / 64, ["pss"], "prs")
                    self.dve(lambda e: e.tensor_tensor(osum[:].rearrange("p (h f) -> p h f", f=64), osum[:].rearrange("p (h f) -> p h f", f=64), pss[:, 4:8].unsqueeze(2).to_broadcast([128, 4, 64]), ALU.mult), ["osum", "prs"], ["osum"])
                    self.dve(lambda e: e.tensor_tensor(osum[:].rearrange("p (h f) -> p h f", f=64), osum[:].rearrange("p (h f) -> p h f", f=64), gdn[:].unsqueeze(1).to_broadcast([128, 4, 64]), ALU.mult), ["osum", "gdn"], ["osum"])
                    self.act(lambda e: e.activation(gte[:], gte[:], AF.Silu), ["gte"], ["gte"])
                    self.dve(lambda e: e.tensor_tensor(ydb[:], osum[:], gte[:], ALU.mult), ["osum", "gte"], ["ydb"])
                    b = self.nbt()
                    for c in range(2):
                        self.pe(lambda e, b=b, c=c: e.transpose(self.psb[b][:, c * 128:(c + 1) * 128], ydb[:, c * 128:(c + 1) * 128], self.identb[:]), ["ydb", "identb"], ["ps%d" % b])
                    self.dve(lambda e, b=b, jb=jb: e.tensor_copy(ydT[:, :, jb * 128:(jb + 1) * 128], self.psb[b][:, 0:256].rearrange("p (c t) -> p c t", c=2)), ["ps%d" % b], ["ydT"])
                self.dma("sp", self.YT[256:512, st + p0: st + p0 + W].rearrange("(c p) t -> p c t", p=128), ydT[:], ["ydT"], ["YT"])
        self.phase_end()


_SKIP = set()


def kernel(**inputs):
    cfg = FULL
    b = Builder(cfg)
    b.skip = set(_SKIP)
    nc = b.build()
    consts = host_consts(cfg)
    xp = np.asarray(inputs["x_prompt"], dtype=np.float32)
    xs = np.asarray(inputs["x_sample"], dtype=np.float32)
    mp = np.asarray(inputs["mem_prompt"], dtype=np.float32)
    ms = np.asarray(inputs["mem_sample"], dtype=np.float32)
    w = {n: np.ascontiguousarray(np.asarray(inputs[n], dtype=np.float32)) for n in WNAMES}
    in_maps = []
    for c in range(NCORES):
        x = np.concatenate([xp[4 * c:4 * c + 4].reshape(-1, D), xs[c].reshape(-1, D)], axis=0)
        mem = np.concatenate([mp[4 * c:4 * c + 4].reshape(-1, D), ms[c].reshape(-1, D)], axis=0)
        m = {"x": np.ascontiguousarray(x), "mem": np.ascontiguousarray(mem)}
        m.update(w)
        m.update(consts)
        in_maps.append(m)
    res = run_bass_kernel_spmd(nc, in_maps, core_ids=list(range(NCORES)))
    yp = np.empty((32, 2048, D), np.float32)
    ys = np.empty((8, 8192, D), np.float32)
    for c in range(NCORES):
        y = np.asarray(res.results[c]["y"], dtype=np.float32)
        yp[4 * c:4 * c + 4] = y[0:8192].reshape(4, 2048, D)
        ys[c] = y[8192:16384]
    return (yp, ys)
```
